# Optimizing a Trainium2 kernel written in Bass

```python
import math
import jax, jax.numpy as jnp
from jax import lax
import numpy as np

D_MODEL = 2048
BATCH = 4
SEQ = 2048
DEPTH = 1

MIX_WIDTH = D_MODEL
ATTN_WIDTH = MIX_WIDTH // 2
SSM_WIDTH = MIX_WIDTH - ATTN_WIDTH
ATTN_HEAD_DIM = 64
ATTN_VALUE_DIM = 2 * ATTN_HEAD_DIM
ATTN_HEADS = ATTN_WIDTH // ATTN_VALUE_DIM
SSM_GROUP = 16
SSM_GROUPS = SSM_WIDTH // SSM_GROUP
SSM_STATE = 64
D_FF = ((8 * D_MODEL // 3 + 255) // 256) * 256
IN_COLS = 3 * ATTN_WIDTH + SSM_WIDTH
QBLOCK = 128
NORM_EPS = 1e-6
DT_MIN = 1e-3
DT_MAX = 1e-1

kernel_name = "hybrid_diffattn_s5_macaron_encoder"


def rms_norm(x, g):
    xf = x.astype(jnp.float32)
    y = xf * lax.rsqrt(jnp.mean(xf * xf, axis=-1, keepdims=True) + NORM_EPS)
    return (y * g.astype(jnp.float32)).astype(x.dtype)


def swiglu(x, w_gate, w_up, w_down):
    return (jax.nn.silu(x @ w_gate) * (x @ w_up)) @ w_down


def alibi_slopes():
    h = np.arange(ATTN_HEADS) + 1
    return jnp.asarray(2.0 ** (-8.0 * h / ATTN_HEADS), dtype=jnp.float32)


def diff_attention(q, k, v, lam, slopes):
    B, S, H, _, dh = q.shape
    nblk = S // QBLOCK
    qb = (q * (dh ** -0.5)).reshape(B, nblk, QBLOCK, H, 2, dh).transpose(1, 0, 2, 3, 4, 5)
    starts = jnp.arange(nblk, dtype=jnp.int32) * QBLOCK
    kpos = jnp.arange(S, dtype=jnp.int32)

    def block(args):
        qblk, start = args
        s = jnp.einsum('bqhcd,bkhcd->bhcqk', qblk, k).astype(jnp.float32)
        qpos = start + jnp.arange(QBLOCK, dtype=jnp.int32)
        dist = jnp.abs(qpos[:, None] - kpos[None, :]).astype(jnp.float32)
        s = s - slopes[None, :, None, None, None] * dist[None, None, None]
        p = jax.nn.softmax(s, axis=-1)
        w = p[:, :, 0] - lam * p[:, :, 1]
        return jnp.einsum('bhqk,bkhe->bqhe', w.astype(v.dtype), v)

    out = lax.map(block, (qb, starts))
    return out.transpose(1, 0, 2, 3, 4).reshape(B, S, H, v.shape[-1])


def _complex_affine_combine(e1, e2):
    a1r, a1i, x1r, x1i = e1
    a2r, a2i, x2r, x2i = e2
    return (a2r * a1r - a2i * a1i,
            a2r * a1i + a2i * a1r,
            a2r * x1r - a2i * x1i + x2r,
            a2r * x1i + a2i * x1r + x2i)


def s5_direction(u, lam_re, lam_im, log_dt, b_re, b_im, c_re, c_im, reverse):
    f32 = jnp.float32
    lam_re = lam_re.astype(f32); lam_im = lam_im.astype(f32)
    dt = jnp.exp(log_dt.astype(f32))[:, None]
    mag = jnp.exp(lam_re * dt)
    a_re = mag * jnp.cos(lam_im * dt)
    a_im = mag * jnp.sin(lam_im * dt)
    den = lam_re * lam_re + lam_im * lam_im
    nr = a_re - 1.0
    f_re = (nr * lam_re + a_im * lam_im) / den
    f_im = (a_im * lam_re - nr * lam_im) / den
    b_re = b_re.astype(f32); b_im = b_im.astype(f32)
    bb_re = f_re[..., None] * b_re - f_im[..., None] * b_im
    bb_im = f_re[..., None] * b_im + f_im[..., None] * b_re
    bu_re = jnp.einsum('bsgp,gnp->bsgn', u, bb_re)
    bu_im = jnp.einsum('bsgp,gnp->bsgn', u, bb_im)
    ar = jnp.broadcast_to(a_re, bu_re.shape)
    ai = jnp.broadcast_to(a_im, bu_re.shape)
    _, _, h_re, h_im = lax.associative_scan(
        _complex_affine_combine, (ar, ai, bu_re, bu_im), reverse=reverse, axis=1)
    return (jnp.einsum('gpn,bsgn->bsgp', c_re.astype(f32), h_re)
            - jnp.einsum('gpn,bsgn->bsgp', c_im.astype(f32), h_im))


def s5_mixer(u, lam_re, lam_im, log_dt, b_re, b_im, c_re, c_im, d, w_glu, b_glu, out_g):
    B, S, _ = u.shape
    uf = u.astype(jnp.float32).reshape(B, S, SSM_GROUPS, SSM_GROUP)
    y = (s5_direction(uf, lam_re[0], lam_im[0], log_dt[0], b_re[0], b_im[0], c_re[0], c_im[0], False)
         + s5_direction(uf, lam_re[1], lam_im[1], log_dt[1], b_re[1], b_im[1], c_re[1], c_im[1], True)
         + d.astype(jnp.float32) * uf)
    y = y.reshape(B, S, SSM_WIDTH).astype(u.dtype)
    g = jax.nn.gelu(y)
    g = g * jax.nn.sigmoid(g @ w_glu + b_glu)
    return rms_norm(g, out_g)


def setup_inputs(seed: int = 0) -> dict:
    key = jax.random.key(seed)
    ks = iter(jax.random.split(key, 40))
    f32 = jnp.float32

    def nrm(shape, scale):
        return jax.random.normal(next(ks), shape, f32) * scale

    def gain(shape):
        return 1.0 + nrm(shape, 0.02)

    L, G, N, P = DEPTH, SSM_GROUPS, SSM_STATE, SSM_GROUP
    n_idx = jnp.arange(N, dtype=f32)
    return {
        "x": nrm((BATCH, SEQ, D_MODEL), 1.0),
        "ff1_pre_g": gain((L, D_MODEL)),
        "ff1_w_gate": nrm((L, D_MODEL, D_FF), D_MODEL ** -0.5),
        "ff1_w_up": nrm((L, D_MODEL, D_FF), D_MODEL ** -0.5),
        "ff1_w_down": nrm((L, D_FF, D_MODEL), D_FF ** -0.5),
        "ff1_post_g": gain((L, D_MODEL)),
        "mix_pre_g": gain((L, D_MODEL)),
        "w_in": nrm((L, D_MODEL, IN_COLS), D_MODEL ** -0.5),
        "lam_q1": nrm((L, ATTN_HEAD_DIM), 0.1),
        "lam_k1": nrm((L, ATTN_HEAD_DIM), 0.1),
        "lam_q2": nrm((L, ATTN_HEAD_DIM), 0.1),
        "lam_k2": nrm((L, ATTN_HEAD_DIM), 0.1),
        "attn_head_g": gain((L, ATTN_VALUE_DIM)),
        "ssm_lam_re": -0.5 + nrm((L, 2, G, N), 0.01),
        "ssm_lam_im": jnp.pi * n_idx + nrm((L, 2, G, N), 0.01),
        "ssm_log_dt": jax.random.uniform(next(ks), (L, 2, G), f32, math.log(DT_MIN), math.log(DT_MAX)),
        "ssm_b_re": nrm((L, 2, G, N, P), P ** -0.5),
        "ssm_b_im": nrm((L, 2, G, N, P), P ** -0.5),
        "ssm_c_re": nrm((L, 2, G, P, N), N ** -0.5),
        "ssm_c_im": nrm((L, 2, G, P, N), N ** -0.5),
        "ssm_d": nrm((L, G, P), 0.5),
        "ssm_w_glu": nrm((L, SSM_WIDTH, SSM_WIDTH), SSM_WIDTH ** -0.5),
        "ssm_b_glu": nrm((L, SSM_WIDTH), 0.01),
        "ssm_out_g": gain((L, SSM_WIDTH)),
        "w_out": nrm((L, MIX_WIDTH, D_MODEL), MIX_WIDTH ** -0.5),
        "mix_post_g": gain((L, D_MODEL)),
        "ff2_pre_g": gain((L, D_MODEL)),
        "ff2_w_gate": nrm((L, D_MODEL, D_FF), D_MODEL ** -0.5),
        "ff2_w_up": nrm((L, D_MODEL, D_FF), D_MODEL ** -0.5),
        "ff2_w_down": nrm((L, D_FF, D_MODEL), D_FF ** -0.5),
        "ff2_post_g": gain((L, D_MODEL)),
    }


def reference(x, ff1_pre_g, ff1_w_gate, ff1_w_up, ff1_w_down, ff1_post_g,
              mix_pre_g, w_in, lam_q1, lam_k1, lam_q2, lam_k2, attn_head_g,
              ssm_lam_re, ssm_lam_im, ssm_log_dt, ssm_b_re, ssm_b_im, ssm_c_re, ssm_c_im,
              ssm_d, ssm_w_glu, ssm_b_glu, ssm_out_g, w_out, mix_post_g,
              ff2_pre_g, ff2_w_gate, ff2_w_up, ff2_w_down, ff2_post_g):
    B, S, _ = x.shape
    slopes = alibi_slopes()
    f32 = jnp.float32
    for l in range(DEPTH):
        h = rms_norm(x, ff1_pre_g[l])
        x = x + 0.5 * rms_norm(swiglu(h, ff1_w_gate[l], ff1_w_up[l], ff1_w_down[l]), ff1_post_g[l])

        h = rms_norm(x, mix_pre_g[l])
        proj = h @ w_in[l]
        q, k, v, u = jnp.split(proj, [ATTN_WIDTH, 2 * ATTN_WIDTH, 3 * ATTN_WIDTH], axis=-1)
        q = q.reshape(B, S, ATTN_HEADS, 2, ATTN_HEAD_DIM)
        k = k.reshape(B, S, ATTN_HEADS, 2, ATTN_HEAD_DIM)
        v = v.reshape(B, S, ATTN_HEADS, ATTN_VALUE_DIM)
        lam_init = 0.8 - 0.6 * math.exp(-0.3 * l)
        lam = (jnp.exp(jnp.sum(lam_q1[l].astype(f32) * lam_k1[l].astype(f32)))
               - jnp.exp(jnp.sum(lam_q2[l].astype(f32) * lam_k2[l].astype(f32))) + lam_init)
        a = diff_attention(q, k, v, lam, slopes)
        a = (rms_norm(a, attn_head_g[l]) * (1.0 - lam_init)).reshape(B, S, ATTN_WIDTH)
        s = s5_mixer(u, ssm_lam_re[l], ssm_lam_im[l], ssm_log_dt[l], ssm_b_re[l], ssm_b_im[l],
                     ssm_c_re[l], ssm_c_im[l], ssm_d[l], ssm_w_glu[l], ssm_b_glu[l], ssm_out_g[l])
        mixed = jnp.concatenate([a, s], axis=-1) @ w_out[l]
        x = x + rms_norm(mixed, mix_post_g[l])

        h = rms_norm(x, ff2_pre_g[l])
        x = x + 0.5 * rms_norm(swiglu(h, ff2_w_gate[l], ff2_w_up[l], ff2_w_down[l]), ff2_post_g[l])
    return x
```

```python
import numpy as np
from contextlib import ExitStack
import concourse.bass as bass
import concourse.mybir as mybir
from concourse.bass_utils import run_bass_kernel_spmd

F32 = mybir.dt.float32
BF16 = mybir.dt.bfloat16
I32 = mybir.dt.int32
AF = mybir.ActivationFunctionType
ALU = mybir.AluOpType

D = 2048
S = 2048
TOWN = 1024
DFF = 5632
NFF = DFF // 128
KD = D // 128
EPS = 1e-6
NH = 8
NG = 64


class Buf:
    __slots__ = ("name", "w", "r")

    def __init__(self, name=""):
        self.name = name
        self.w = {}
        self.r = {}


def _merge(d, ev):
    sem, val = ev
    k = id(sem)
    if k not in d or d[k][1] < val:
        d[k] = (sem, val)


class FW:
    NDMA = 8

    def __init__(self, nc, es):
        self.nc = nc
        self.es = es
        self.eng = {"pe": nc.tensor, "act": nc.scalar, "dve": nc.vector,
                    "pool": nc.gpsimd, "sp": nc.sync}
        self.csem = {}
        self.ccnt = {}
        for k in ("pe", "act", "dve", "pool"):
            self.csem[k] = es.enter_context(nc.semaphore("c_" + k))
            self.ccnt[k] = 0
        self.dsem = {}
        self.dcnt = {}
        self.di = {}
        for q in ("sp", "act", "pool"):
            self.dsem[q] = [es.enter_context(nc.semaphore(f"d_{q}{i}")) for i in range(self.NDMA)]
            self.dcnt[q] = [0] * self.NDMA
            self.di[q] = 0
        self.waited = {k: {} for k in self.eng}

    def sb(self, es, name, shape, dt):
        self.nalloc = getattr(self, "nalloc", 0) + 1
        return es.enter_context(self.nc.sbuf_tensor(f"{name}_{self.nalloc}", list(shape), dt))

    def ps(self, es, name, shape, dt):
        self.nalloc = getattr(self, "nalloc", 0) + 1
        return es.enter_context(self.nc.psum_tensor(f"{name}_{self.nalloc}", list(shape), dt))

    def _wait(self, ek, ev):
        sem, val = ev
        key = id(sem)
        d = self.waited[ek]
        if d.get(key, 0) >= val:
            return
        d[key] = val
        self.eng[ek].wait_ge(sem, val)

    def _deps(self, ek, reads, writes):
        best = {}
        for b in reads:
            for ev in b.w.values():
                _merge(best, ev)
        for b in writes:
            for ev in b.w.values():
                _merge(best, ev)
            for ev in b.r.values():
                _merge(best, ev)
        for ev in best.values():
            self._wait(ek, ev)

    def _commit(self, ev, reads, writes):
        for b in reads:
            _merge(b.r, ev)
        for b in writes:
            _merge(b.w, ev)

    def op(self, ek, fn, reads=(), writes=(), signal=True):
        self._deps(ek, reads, writes)
        ins = fn()
        if signal:
            self.ccnt[ek] += 1
            ins.then_inc(self.csem[ek], 1)
            ev = (self.csem[ek], self.ccnt[ek])
            self._commit(ev, reads, writes)
            return ev
        return None

    def dma(self, q, out, in_, reads=(), writes=(), **kw):
        i = self.di[q]
        self.di[q] = (i + 1) % self.NDMA
        sem = self.dsem[q][i]
        if self.dcnt[q][i] > 0:
            self._wait(q, (sem, self.dcnt[q][i]))
        self._deps(q, reads, writes)
        ins = self.eng[q].dma_start(out=out, in_=in_, **kw)
        self.dcnt[q][i] += 16
        ins.then_inc(sem, 16)
        ev = (sem, self.dcnt[q][i])
        self._commit(ev, reads, writes)
        return ev

    def all_events(self):
        evs = []
        for k in self.csem:
            if self.ccnt[k] > 0:
                evs.append((self.csem[k], self.ccnt[k]))
        for q in self.dsem:
            for i in range(self.NDMA):
                if self.dcnt[q][i] > 0:
                    evs.append((self.dsem[q][i], self.dcnt[q][i]))
        return evs

    def barrier(self, engines=("pe", "act", "dve", "pool", "sp")):
        evs = self.all_events()
        for ek in engines:
            for ev in evs:
                self._wait(ek, ev)


def bc_last(t, off, nparts, mid, last, pstride, mid_stride=1):
    return bass.AP(t, off, [[pstride, nparts], [mid_stride, mid], [0, last]])


class Ctx:
    pass


def setup_consts(fw, nc, C, es, dram):
    C.ident = fw.sb(es, "ident", [128, 128], BF16)
    C.b_ident = Buf()
    C.identf = fw.sb(es, "identf", [128, 128], F32)
    C.b_identf = Buf()
    fw.dma("sp", C.identf[:], dram["c_ident"][:, :], writes=[C.b_identf])
    fw.op("dve", lambda: nc.vector.tensor_copy(C.ident[:], C.identf[:]), reads=[C.b_identf], writes=[C.b_ident])
    C.ones = fw.sb(es, "ones", [128, 128], BF16)
    C.b_ones = Buf()
    fw.op("dve", lambda: nc.vector.memset(C.ones[:], 1.0), writes=[C.b_ones])
    C.bank = [fw.ps(es, f"bank{i}", [128, 512], F32) for i in range(8)]
    C.b_bank = [Buf(f"bank{i}") for i in range(8)]


def rms_stats(fw, nc, src_tile, b_src, junk, b_junk, ss_col, b_ss, rs_col, b_rs, n, mult=1.0):
    fw.op("act", lambda: nc.scalar.activation(junk, src_tile, AF.Square, accum_out=ss_col),
          reads=[b_src], writes=[b_junk, b_ss])
    fw.op("act", lambda: nc.scalar.activation(rs_col, ss_col, AF.Sqrt, bias=EPS, scale=1.0 / n),
          reads=[b_ss], writes=[b_rs])
    fw.op("dve", lambda: nc.vector.reciprocal(rs_col, rs_col), reads=[b_rs], writes=[b_rs])
    if mult != 1.0:
        fw.op("dve", lambda: nc.vector.tensor_scalar(rs_col, rs_col, float(mult), None, ALU.mult),
              reads=[b_rs], writes=[b_rs])


def load_gT(fw, nc, gT, b_gT, g_dram):
    with nc.allow_non_contiguous_dma("tiny gain transpose load"):
        fw.dma("sp", gT[:], g_dram[0, :].rearrange("(k p) -> p k", p=128), writes=[b_gT])


def norm_transpose(fw, nc, C, xt, b_xt, hb, b_hb, ss_col, b_ss, rs_col, b_rs, gT, b_gT, hT, b_hT, col0, ncols=128,
                   col_step=1):
    rms_stats(fw, nc, xt, b_xt, hb, b_hb, ss_col, b_ss, rs_col, b_rs, D)
    fw.op("dve", lambda: nc.vector.tensor_scalar(hb, xt, rs_col, None, ALU.mult),
          reads=[b_xt, b_rs], writes=[b_hb])
    for half in range(2):
        bk = C.bank[half]
        bb = C.b_bank[half]
        pst = bk[:].bitcast(BF16)
        for k8 in range(8):
            k = half * 8 + k8
            fw.op("pe", lambda k=k, k8=k8, pst=pst: nc.tensor.transpose(
                pst[:, k8 * 128:(k8 + 1) * 128], hb[:, k * 128:(k + 1) * 128], C.ident[:]),
                reads=[b_hb, C.b_ident], writes=[bb], signal=(k8 == 7))
        src3 = pst.rearrange("p (k t) -> p k t", k=8)
        if col_step == 1:
            dst3 = hT[:, half * 8:(half + 1) * 8, col0:col0 + ncols]
        else:
            dst3 = hT[:, half * 8:(half + 1) * 8, col0:col0 + ncols * col_step:col_step]
        gb = bc_last(gT, half * 8, 128, 8, 128, KD)
        fw.op("dve", lambda src3=src3, dst3=dst3, gb=gb: nc.vector.tensor_tensor(dst3, src3, gb, ALU.mult),
              reads=[bb, b_gT], writes=[b_hT])


def ffn_alloc(fw, nc, es):
    A = Ctx()
    A.hT = fw.sb(es, "f_hT", [128, KD, TOWN], BF16); A.b_hT = Buf()
    A.actT = fw.sb(es, "f_actT", [128, NFF, TOWN], BF16); A.b_actT = Buf()
    A.NW = 2
    A.wg = [fw.sb(es, f"f_wg{i}", [128, KD, 128], BF16) for i in range(A.NW)]
    A.wu = [fw.sb(es, f"f_wu{i}", [128, KD, 128], BF16) for i in range(A.NW)]
    A.b_wg = [Buf() for _ in range(A.NW)]
    A.b_wu = [Buf() for _ in range(A.NW)]
    A.NWD = 2
    A.wd = [fw.sb(es, f"f_wd{i}", [128, 11, 512], BF16) for i in range(A.NWD)]
    A.b_wd = [Buf() for _ in range(A.NWD)]
    A.xt = [fw.sb(es, f"f_xt{i}", [128, D], F32) for i in range(2)]
    A.b_xt = [Buf() for _ in range(2)]
    A.hb = [fw.sb(es, f"f_hb{i}", [128, D], BF16) for i in range(2)]
    A.b_hb = [Buf() for _ in range(2)]
    A.ss = fw.sb(es, "f_ss", [128, 64], F32); A.b_ss = Buf()
    A.rs = fw.sb(es, "f_rs", [128, 64], F32); A.b_rs = Buf()
    A.gT = fw.sb(es, "f_gT", [128, KD], F32); A.b_gT = Buf()
    A.gpost = fw.sb(es, "f_gpost", [128, D], F32); A.b_gpost = Buf()
    A.sg = [fw.sb(es, f"f_sg{i}", [128, 512], F32) for i in range(2)]
    A.b_sg = [Buf() for _ in range(2)]
    A.yev = [fw.sb(es, f"f_yev{i}", [128, 512], F32) for i in range(4)]
    A.b_yev = [Buf() for _ in range(4)]
    A.cnt = 0
    return A


def ffn_pass(fw, nc, C, A, src, dst, wg_d, wu_d, wd_d, pre_g, post_g, y_d, b_yd, b_src, b_dst):
    NT = TOWN // 128
    load_gT(fw, nc, A.gT, A.b_gT, pre_g)
    fw.dma("sp", A.gpost[:], post_g[0:1, :].partition_broadcast(128), writes=[A.b_gpost])
    for tt in range(NT):
        s = tt % 2
        fw.dma("sp", A.xt[s][:], src[tt * 128:(tt + 1) * 128, :], reads=[b_src], writes=[A.b_xt[s]])
        col = A.cnt % 64
        A.cnt += 1
        norm_transpose(fw, nc, C, A.xt[s][:], A.b_xt[s], A.hb[s][:], A.b_hb[s],
                       A.ss[:, col:col + 1], A.b_ss, A.rs[:, col:col + 1], A.b_rs,
                       A.gT, A.b_gT, A.hT, A.b_hT, tt * 128)
    wgv = wg_d.rearrange("(k p) m -> p k m", p=128)
    wuv = wu_d.rearrange("(k p) m -> p k m", p=128)
    for f in range(NFF):
        s = f % A.NW
        fw.dma("pool", A.wg[s][:], wgv[:, :, f * 128:(f + 1) * 128], writes=[A.b_wg[s]])
        fw.dma("pool", A.wu[s][:], wuv[:, :, f * 128:(f + 1) * 128], writes=[A.b_wu[s]])
        for c in range(2):
            bi = (f % 2) * 4 + c * 2
            pg, pu = C.bank[bi], C.bank[bi + 1]
            bpg, bpu = C.b_bank[bi], C.b_bank[bi + 1]
            for k in range(KD):
                fw.op("pe", lambda k=k, pg=pg, s=s, c=c: nc.tensor.matmul(
                    pg[:], A.wg[s][:, k, :], A.hT[:, k, c * 512:(c + 1) * 512], start=(k == 0), stop=(k == KD - 1)),
                    reads=[A.b_wg[s], A.b_hT], writes=[bpg], signal=(k == KD - 1))
            for k in range(KD):
                fw.op("pe", lambda k=k, pu=pu, s=s, c=c: nc.tensor.matmul(
                    pu[:], A.wu[s][:, k, :], A.hT[:, k, c * 512:(c + 1) * 512], start=(k == 0), stop=(k == KD - 1)),
                    reads=[A.b_wu[s], A.b_hT], writes=[bpu], signal=(k == KD - 1))
            sgs = c
            fw.op("act", lambda pg=pg, sgs=sgs: nc.scalar.activation(A.sg[sgs][:], pg[:], AF.Silu),
                  reads=[bpg], writes=[A.b_sg[sgs]])
            fw.op("dve", lambda pu=pu, sgs=sgs, f=f, c=c: nc.vector.tensor_tensor(
                A.actT[:, f, c * 512:(c + 1) * 512], A.sg[sgs][:], pu[:], ALU.mult),
                reads=[A.b_sg[sgs], bpu], writes=[A.b_actT])
    wdv = wd_d.rearrange("(f p) m -> p f m", p=128)
    ig = 0
    iy = 0
    for n in range(4):
        for g4 in range(4):
            s = ig % A.NWD
            ig += 1
            fw.dma("pool", A.wd[s][:], wdv[:, g4 * 11:(g4 + 1) * 11, n * 512:(n + 1) * 512], writes=[A.b_wd[s]])
            for tt in range(NT):
                for fi in range(11):
                    f = g4 * 11 + fi
                    last = (g4 == 3 and fi == 10)
                    fw.op("pe", lambda tt=tt, f=f, fi=fi, s=s, g4=g4: nc.tensor.matmul(
                        C.bank[tt][:], A.actT[:, f, tt * 128:(tt + 1) * 128], A.wd[s][:, fi, :],
                        start=(g4 == 0 and fi == 0), stop=(g4 == 3 and fi == 10)),
                        reads=[A.b_actT, A.b_wd[s]], writes=[C.b_bank[tt]], signal=(fi == 10))
        for tt in range(NT):
            ys = iy % 4
            iy += 1
            ek = "act" if tt % 2 == 0 else "dve"
            if ek == "act":
                fw.op("act", lambda tt=tt, ys=ys: nc.scalar.copy(A.yev[ys][:], C.bank[tt][:]),
                      reads=[C.b_bank[tt]], writes=[A.b_yev[ys]])
            else:
                fw.op("dve", lambda tt=tt, ys=ys: nc.vector.tensor_copy(A.yev[ys][:], C.bank[tt][:]),
                      reads=[C.b_bank[tt]], writes=[A.b_yev[ys]])
            fw.dma("sp", y_d[tt * 128:(tt + 1) * 128, n * 512:(n + 1) * 512], A.yev[ys][:],
                   reads=[A.b_yev[ys]], writes=[b_yd])
    act32 = A.actT[:].bitcast(F32)
    NDB = 3
    dby = [act32[:, 4 * i:4 * i + 4, :].rearrange("p a b -> p (a b)") for i in range(NDB)]
    dbx = [act32[:, 4 * (NDB + i):4 * (NDB + i) + 4, :].rearrange("p a b -> p (a b)") for i in range(NDB)]
    b_dby = [Buf() for _ in range(NDB)]
    b_dbx = [Buf() for _ in range(NDB)]
    for tt in range(NT):
        s = tt % 2
        d = tt % NDB
        yt, b_yt = dby[d], b_dby[d]
        xt, b_xt = dbx[d], b_dbx[d]
        fw.dma("sp", yt, y_d[tt * 128:(tt + 1) * 128, :], reads=[b_yd], writes=[b_yt, A.b_actT])
        fw.dma("sp", xt, src[tt * 128:(tt + 1) * 128, :], reads=[b_src], writes=[b_xt, A.b_actT])
        col = A.cnt % 64
        A.cnt += 1
        ssc = A.ss[:, col:col + 1]
        rsc = A.rs[:, col:col + 1]
        rms_stats(fw, nc, yt, b_yt, A.hb[s][:], A.b_hb[s], ssc, A.b_ss, rsc, A.b_rs, D, mult=0.5)
        fw.op("dve", lambda rsc=rsc, yt=yt: nc.vector.scalar_tensor_tensor(
            yt, yt, rsc, A.gpost[:], ALU.mult, ALU.mult),
            reads=[b_yt, A.b_rs, A.b_gpost], writes=[b_yt])
        fw.op("pool", lambda yt=yt, xt=xt: nc.gpsimd.tensor_tensor(yt, yt, xt, ALU.add),
              reads=[b_yt, b_xt], writes=[b_yt])
        fw.dma("pool", dst[tt * 128:(tt + 1) * 128, :], yt, reads=[b_yt, A.b_actT], writes=[b_dst])


SLOPES = [2.0 ** (-(h + 1)) for h in range(NH)]
LAM_INIT = 0.2


def mixer_common_alloc(fw, nc, es):
    M = Ctx()
    M.hmT = fw.sb(es, "m_hmT", [128, KD, S], BF16); M.b_hmT = Buf()
    M.es = es
    M.ss = fw.sb(es, "m_ss", [128, 64], F32); M.b_ss = Buf()
    M.rs = fw.sb(es, "m_rs", [128, 64], F32); M.b_rs = Buf()
    M.gT = fw.sb(es, "m_gT", [128, KD], F32); M.b_gT = Buf()
    M.cnt = 0
    return M


def build_hmT(fw, nc, C, M, x1_src, b_x1, g_dram):
    with ExitStack() as es:
        xt = [fw.sb(es, f"h_xt{i}", [128, D], F32) for i in range(2)]
        b_xt = [Buf() for _ in range(2)]
        hb = [fw.sb(es, f"h_hb{i}", [128, D], BF16) for i in range(2)]
        b_hb = [Buf() for _ in range(2)]
        load_gT(fw, nc, M.gT, M.b_gT, g_dram)
        for tt in range(S // 128):
            s = tt % 2
            fw.dma("sp", xt[s][:], x1_src[tt * 128:(tt + 1) * 128, :], reads=[b_x1], writes=[b_xt[s]])
            col = M.cnt % 64
            M.cnt += 1
            norm_transpose(fw, nc, C, xt[s][:], b_xt[s], hb[s][:], b_hb[s],
                           M.ss[:, col:col + 1], M.b_ss, M.rs[:, col:col + 1], M.b_rs,
                           M.gT, M.b_gT, M.hmT, M.b_hmT, tt * 128)
        fw.barrier()


def attention_alloc(fw, nc, es):
    A = Ctx()
    A.QT = [fw.sb(es, f"a_QT{c}", [128, NH, TOWN], BF16) for c in range(2)]
    A.b_QT = Buf()
    A.KT = fw.sb(es, "a_KT", [128, NH, S], BF16); A.b_KT = Buf()
    A.V = fw.sb(es, "a_V", [128, S // 128, 1024], BF16); A.b_V = Buf()
    for c in range(2):
        fw.op("pool", lambda c=c: nc.gpsimd.memset(A.QT[c][:], 0.0), writes=[A.b_QT])
    return A


def attention_proj(fw, nc, C, M, A, dram, u_d, b_ud, only_u=False):
    w_in = dram["w_in"]
    if A is not None:
        QT, KT, V, b_QT, b_KT, b_V = A.QT, A.KT, A.V, A.b_QT, A.b_KT, A.b_V
    with ExitStack() as es:
        wp = [fw.sb(es, f"a_wp{i}", [128, KD, 256], BF16) for i in range(2)]
        b_wp = [Buf() for _ in range(2)]
        wv = w_in.rearrange("(k p) m -> p k m", p=128)
        iw = 0
        ib = 0
        ust = [fw.sb(es, f"a_ust{i}", [128, 16, 8, 16], BF16) for i in range(2)]
        b_ust = [Buf() for _ in range(2)]
        iu = 0
        for blk in range(4):
            s = iw % 2
            iw += 1
            fw.dma("pool", wp[s][:], wv[:, :, 3072 + blk * 256:3072 + (blk + 1) * 256], writes=[b_wp[s]])
            for hh in range(2):
                us = iu % 2
                iu += 1
                for j in range(8):
                    bi = ib % 8
                    ib += 1
                    t0 = 1024 * hh + j
                    for k in range(KD):
                        fw.op("pe", lambda k=k, s=s, t0=t0, bi=bi: nc.tensor.matmul(
                            C.bank[bi][:, 0:256], M.hmT[:, k, t0:t0 + 1017:8], wp[s][:, k, :],
                            start=(k == 0), stop=(k == KD - 1)),
                            reads=[b_wp[s], M.b_hmT], writes=[C.b_bank[bi]], signal=(k == KD - 1))
                    src = C.bank[bi][:, 0:256].rearrange("c (g q) -> c g q", g=16)
                    if j % 2 == 0:
                        fw.op("act", lambda us=us, j=j, src=src: nc.scalar.copy(ust[us][:, :, j, :], src),
                              reads=[C.b_bank[bi]], writes=[b_ust[us]])
                    else:
                        fw.op("dve", lambda us=us, j=j, src=src: nc.vector.tensor_copy(ust[us][:, :, j, :], src),
                              reads=[C.b_bank[bi]], writes=[b_ust[us]])
                fw.dma("sp", u_d[:, hh, blk * 16:(blk + 1) * 16, :],
                       ust[us][:].rearrange("c g j q -> c g (j q)"), reads=[b_ust[us]], writes=[b_ud])
        if only_u:
            fw.barrier()
            return
        for blk in range(8):
            s = iw % 2
            iw += 1
            fw.dma("pool", wp[s][:], wv[:, :, blk * 256:(blk + 1) * 256], writes=[b_wp[s]])
            isq = blk < 4
            nch = 2 if isq else 4
            for h4 in range(2):
                h = (blk % 4) * 2 + h4
                for ch in range(nch):
                    bi = ib % 8
                    ib += 1
                    for k in range(KD):
                        fw.op("pe", lambda k=k, s=s, h4=h4, ch=ch, bi=bi: nc.tensor.matmul(
                            C.bank[bi][:], wp[s][:, k, h4 * 128:(h4 + 1) * 128], M.hmT[:, k, ch * 512:(ch + 1) * 512],
                            start=(k == 0), stop=(k == KD - 1)),
                            reads=[b_wp[s], M.b_hmT], writes=[C.b_bank[bi]], signal=(k == KD - 1))
                    if isq:
                        for c in range(2):
                            fw.op("act", lambda h=h, ch=ch, bi=bi, c=c: nc.scalar.mul(
                                QT[c][64 * c:64 * c + 64, h, ch * 512:(ch + 1) * 512],
                                C.bank[bi][64 * c:64 * c + 64, :], 0.125),
                                reads=[C.b_bank[bi]], writes=[b_QT])
                    else:
                        fw.op("dve", lambda h=h, ch=ch, bi=bi: nc.vector.tensor_copy(
                            KT[:, h, ch * 512:(ch + 1) * 512], C.bank[bi][:]),
                            reads=[C.b_bank[bi]], writes=[b_KT])
        for blk in range(4):
            s = iw % 2
            iw += 1
            fw.dma("pool", wp[s][:], wv[:, :, 2048 + blk * 256:2048 + (blk + 1) * 256], writes=[b_wp[s]])
            for tt in range(S // 128):
                bi = ib % 8
                ib += 1
                for k in range(KD):
                    fw.op("pe", lambda k=k, s=s, tt=tt, bi=bi: nc.tensor.matmul(
                        C.bank[bi][:, 0:256], M.hmT[:, k, tt * 128:(tt + 1) * 128], wp[s][:, k, :],
                        start=(k == 0), stop=(k == KD - 1)),
                        reads=[b_wp[s], M.b_hmT], writes=[C.b_bank[bi]], signal=(k == KD - 1))
                if tt % 2 == 0:
                    fw.op("act", lambda tt=tt, blk=blk, bi=bi: nc.scalar.copy(
                        V[:, tt, blk * 256:(blk + 1) * 256], C.bank[bi][:, 0:256]),
                        reads=[C.b_bank[bi]], writes=[b_V])
                else:
                    fw.op("dve", lambda tt=tt, blk=blk, bi=bi: nc.vector.tensor_copy(
                        V[:, tt, blk * 256:(blk + 1) * 256], C.bank[bi][:, 0:256]),
                        reads=[C.b_bank[bi]], writes=[b_V])
        fw.barrier()


def attention_core(fw, nc, C, M, A, dram):
    QT, KT, V, b_QT, b_KT, b_V = A.QT, A.KT, A.V, A.b_QT, A.b_KT, A.b_V
    with ExitStack() as es:
        G = fw.sb(es, "a_G", [128, 3072], F32); b_G = Buf()
        fw.dma("sp", G[:], dram["c_alibi"][:, :], writes=[b_G])
        GD = [fw.sb(es, f"a_GD{i}", [128, 3072], BF16) for i in range(2)]
        b_GD = [Buf(), Buf()]
        lq = fw.sb(es, "a_lq", [128, 4, 64], F32); b_lq = Buf()
        for i, nm in enumerate(("lam_q1", "lam_k1", "lam_q2", "lam_k2")):
            fw.dma("sp", lq[:, i, :], dram[nm][0:1, :].partition_broadcast(128), writes=[b_lq])
        sc4 = fw.sb(es, "a_sc4", [128, 8], F32); b_sc4 = Buf()
        junk = fw.sb(es, "a_junk", [128, 64], F32); b_junk = Buf()
        fw.op("dve", lambda: nc.vector.scalar_tensor_tensor(junk[:], lq[:, 0, :], 1.0, lq[:, 1, :], ALU.mult, ALU.mult,
                                                            accum_out=sc4[:, 0:1]), reads=[b_lq], writes=[b_junk, b_sc4])
        fw.op("dve", lambda: nc.vector.scalar_tensor_tensor(junk[:], lq[:, 2, :], 1.0, lq[:, 3, :], ALU.mult, ALU.mult,
                                                            accum_out=sc4[:, 1:2]), reads=[b_lq], writes=[b_junk, b_sc4])
        fw.op("act", lambda: nc.scalar.activation(sc4[:, 2:4], sc4[:, 0:2], AF.Exp), reads=[b_sc4], writes=[b_sc4])
        fw.op("dve", lambda: nc.vector.tensor_tensor(sc4[:, 4:5], sc4[:, 3:4], sc4[:, 2:3], ALU.subtract),
              reads=[b_sc4], writes=[b_sc4])
        fw.op("dve", lambda: nc.vector.tensor_scalar(sc4[:, 5:6], sc4[:, 4:5], -LAM_INIT, None, ALU.add),
              reads=[b_sc4], writes=[b_sc4])
        neg_lam = sc4[:, 5:6]
        gh = fw.sb(es, "a_gh", [128, 2], F32); b_gh = Buf()
        with nc.allow_non_contiguous_dma("tiny"):
            fw.dma("sp", gh[:, 0:1], dram["attn_head_g"][0, :].rearrange("(p o) -> p o", o=1), writes=[b_gh])
        fw.op("dve", lambda: nc.vector.tensor_scalar(gh[:, 1:2], gh[:, 0:1], 1.0 - LAM_INIT, None, ALU.mult),
              reads=[b_gh], writes=[b_gh])
        scb = [fw.sb(es, f"a_scb{i}", [128, 512], BF16) for i in range(4)]
        b_scb = [Buf() for _ in range(4)]
        pT = [fw.sb(es, f"a_pT{i}", [128, 512], BF16) for i in range(6)]
        b_pT = [Buf() for _ in range(6)]
        rz = [fw.sb(es, f"a_rz{i}", [128, 512], F32) for i in range(2)]
        b_rz = [Buf() for _ in range(2)]
        ot = [fw.sb(es, f"a_ot{i}", [128, 512], F32) for i in range(2)]
        b_ot = [Buf() for _ in range(2)]
        sqa = fw.sb(es, "a_sqa", [128, 512], BF16); b_sqa = Buf()
        iters = [(h, qc, kb, c) for h in range(NH) for qc in range(2) for kb in range(S // 128) for c in range(2)]
        SB = [0, 1, 7, 6]
        DEPTH = 3
        NKB = S // 128

        def stage1(i):
            h, qc, kb, c = iters[i]
            off = 512 * qc - 128 * kb + 1920
            sb_i = SB[i % 4]
            sbf = i % 4
            pi = i % 6
            pbank = C.bank[sb_i]
            fw.op("pe", lambda: nc.tensor.matmul(
                pbank[:], KT[:, h, kb * 128:(kb + 1) * 128],
                QT[c][:, h, qc * 512:(qc + 1) * 512], start=True, stop=True),
                reads=[b_KT, b_QT], writes=[C.b_bank[sb_i]])
            if qc == 0 and kb == 0 and c == 0:
                fw.op("act", lambda: nc.scalar.activation(GD[h % 2][:], G[:], AF.Exp, scale=-SLOPES[h]),
                      reads=[b_G], writes=[b_GD[h % 2]])
            fw.op("act", lambda: nc.scalar.activation(scb[sbf][:], pbank[:], AF.Exp),
                  reads=[C.b_bank[sb_i]], writes=[b_scb[sbf]])
            fw.op("dve", lambda: nc.vector.tensor_tensor(pT[pi][:], scb[sbf][:], GD[h % 2][:, off:off + 512], ALU.mult),
                  reads=[b_scb[sbf], b_GD[h % 2]], writes=[b_pT[pi]])

        def stage2(i):
            h, qc, kb, c = iters[i]
            pi = i % 6
            fw.op("pe", lambda: nc.tensor.matmul(
                C.bank[2 + c][:], V[:, kb, h * 128:(h + 1) * 128], pT[pi][:],
                start=(kb == 0), stop=(kb == NKB - 1)),
                reads=[b_V, b_pT[pi]], writes=[C.b_bank[2 + c]], signal=False)
            fw.op("pe", lambda: nc.tensor.matmul(
                C.bank[4 + c][:], C.ones[:], pT[pi][:],
                start=(kb == 0), stop=(kb == NKB - 1)),
                reads=[C.b_ones, b_pT[pi], b_V], writes=[C.b_bank[2 + c], C.b_bank[4 + c]])
            if kb == NKB - 1 and c == 1:
                for cc in range(2):
                    fw.op("dve", lambda cc=cc: nc.vector.reciprocal(rz[cc][:], C.bank[4 + cc][:]),
                          reads=[C.b_bank[4 + cc]], writes=[b_rz[cc]])
                    fw.op("dve", lambda cc=cc: nc.vector.tensor_tensor(ot[cc][:], C.bank[2 + cc][:], rz[cc][:], ALU.mult),
                          reads=[C.b_bank[2 + cc], b_rz[cc]], writes=[b_ot[cc]])
                fw.op("dve", lambda: nc.vector.scalar_tensor_tensor(ot[0][:], ot[1][:], neg_lam, ot[0][:], ALU.mult, ALU.add),
                      reads=[b_ot[0], b_ot[1], b_sc4], writes=[b_ot[0]])
                fw.op("act", lambda: nc.scalar.activation(sqa[:], ot[0][:], AF.Square), reads=[b_ot[0]], writes=[b_sqa])
                fw.op("pe", lambda: nc.tensor.matmul(C.bank[6][:], C.ones[:], sqa[:], start=True, stop=True),
                      reads=[C.b_ones, b_sqa], writes=[C.b_bank[6]])
                fw.op("act", lambda: nc.scalar.activation(rz[0][:], C.bank[6][:], AF.Sqrt, bias=EPS, scale=1.0 / 128),
                      reads=[C.b_bank[6]], writes=[b_rz[0]])
                fw.op("dve", lambda: nc.vector.reciprocal(rz[0][:], rz[0][:]), reads=[b_rz[0]], writes=[b_rz[0]])
                fw.op("dve", lambda: nc.vector.scalar_tensor_tensor(
                    M.aT[:, h, qc * 512:(qc + 1) * 512], ot[0][:], gh[:, 1:2], rz[0][:], ALU.mult, ALU.mult),
                    reads=[b_ot[0], b_gh, b_rz[0]], writes=[M.b_aT])

        NI = len(iters)
        for i in range(NI + DEPTH):
            if i < NI:
                stage1(i)
            if i >= DEPTH:
                stage2(i - DEPTH)
        fw.barrier()


TWO_PI = 6.283185307179586
PI_C = 3.1415925


def ap4(t, off, pstride, nparts, dims):
    return bass.AP(t, off, [[pstride, nparts]] + [[st, n] for st, n in dims])


def s5_phase(fw, nc, C, dram, Yown, b_Y, u_d, b_ud):
    with ExitStack() as es:
        U2 = fw.sb(es, "s_U2", [128, 2, 64, 128], BF16); b_U2 = Buf()
        for hh in range(2):
            fw.dma("sp", U2[:, hh, :, :], u_d[:, hh, :, :], reads=[b_ud], writes=[b_U2])
        NF = 33
        PWR = fw.sb(es, "s_PWR", [128, NF, 64], F32); b_PW = Buf()
        PWI = fw.sb(es, "s_PWI", [128, NF, 64], F32)
        NAI = fw.sb(es, "s_NAI", [128, 8, 64], F32)
        BBR = fw.sb(es, "s_BBR", [128, 64, 16], F32); b_BB = Buf()
        BBI = fw.sb(es, "s_BBI", [128, 64, 16], F32)
        CTR = fw.sb(es, "s_CTR", [128, 64, 16], F32); b_CT = Buf()
        CTI = fw.sb(es, "s_CTI", [128, 64, 16], F32)
        MF = fw.sb(es, "s_MF", [128, 128], F32); b_MK = Buf()
        MB = fw.sb(es, "s_MB", [128, 128], F32)
        fw.dma("sp", MF[:], dram["c_maskF"][:, :], writes=[b_MK])
        fw.dma("sp", MB[:], dram["c_maskB"][:, :], writes=[b_MK])
        dB = fw.sb(es, "s_dB", [128, 1024], F32); b_dB = Buf()
        fw.dma("sp", dB[:], dram["ssm_d"][0:1, :].partition_broadcast(128), writes=[b_dB])
        dve = nc.vector
        with ExitStack() as pes:
            LL = fw.sb(pes, "s_LL", [64, 2, 128], F32); b_LL = Buf()
            for i, nm in enumerate(("ssm_lam_re", "ssm_lam_im")):
                fw.dma("sp", LL[:, i, :].rearrange("g (d n) -> g d n", d=2), dram[nm].rearrange("d g n -> g d n"),
                       writes=[b_LL])
            LRI = fw.sb(pes, "s_LRI", [128, 2, 64], F32); b_LRI = Buf()
            for i in range(2):
                fw.op("pe", lambda i=i: nc.tensor.transpose(C.bank[0][:, i * 64:(i + 1) * 64], LL[:, i, :],
                                                            C.identf[0:64, 0:64]),
                      reads=[b_LL, C.b_identf], writes=[C.b_bank[0]])
            fw.op("dve", lambda: dve.tensor_copy(LRI[:].rearrange("p a g -> p (a g)"), C.bank[0][:, 0:128]),
                  reads=[C.b_bank[0]], writes=[b_LRI])
            LR = LRI[:, 0, :]
            LI = LRI[:, 1, :]
            DT = fw.sb(pes, "s_DT", [128, 64], F32); b_DT = Buf()
            for d in range(2):
                fw.dma("sp", DT[64 * d:64 * d + 64, :], dram["ssm_log_dt"][d:d + 1, :].partition_broadcast(64),
                       writes=[b_DT])
            fw.op("act", lambda: nc.scalar.activation(DT[:], DT[:], AF.Exp), reads=[b_DT], writes=[b_DT])
            LD = fw.sb(pes, "s_LD", [128, 2, 64], F32); b_LD = Buf()
            for i in range(2):
                fw.op("dve", lambda i=i: dve.tensor_tensor(LD[:, i, :], LRI[:, i, :], DT[:], ALU.mult),
                      reads=[b_LRI, b_DT], writes=[b_LD])
            EXPS = fw.sb(pes, "s_EXPS", [128, NF], F32); b_EX = Buf()
            fw.dma("sp", EXPS[:], dram["c_exps"][:, :], writes=[b_EX])
            ANG = fw.sb(pes, "s_ANG", [128, NF, 64], F32); b_ANG = Buf()
            MAG = fw.sb(pes, "s_MAG", [128, NF, 64], F32); b_MAG = Buf()
            KF = fw.sb(pes, "s_KF", [128, NF, 64], F32); b_KF = Buf()
            KI = fw.sb(pes, "s_KI", [128, NF, 64], I32); b_KI = Buf()
            ex_b = ap4(EXPS, 0, NF, 128, [(1, NF), (0, 64)])
            lid_b = ap4(LD, 64, 128, 128, [(0, NF), (1, 64)])
            lrd_b = ap4(LD, 0, 128, 128, [(0, NF), (1, 64)])
            fw.op("dve", lambda: dve.tensor_tensor(ANG[:], lid_b, ex_b, ALU.mult), reads=[b_LD, b_EX], writes=[b_ANG])
            fw.op("dve", lambda: dve.tensor_tensor(MAG[:], lrd_b, ex_b, ALU.mult), reads=[b_LD, b_EX], writes=[b_MAG])
            fw.op("act", lambda: nc.scalar.activation(MAG[:], MAG[:], AF.Exp), reads=[b_MAG], writes=[b_MAG])

            def reduce_clamp(A, b_A):
                fw.op("dve", lambda: dve.tensor_scalar(KF[:], A[:], 1.0 / TWO_PI, None, ALU.mult),
                      reads=[b_A], writes=[b_KF])
                fw.op("dve", lambda: dve.tensor_copy(KI[:], KF[:]), reads=[b_KF], writes=[b_KI])
                fw.op("dve", lambda: dve.tensor_copy(KF[:], KI[:]), reads=[b_KI], writes=[b_KF])
                fw.op("dve", lambda: dve.scalar_tensor_tensor(A[:], KF[:], -TWO_PI, A[:], ALU.mult, ALU.add),
                      reads=[b_KF, b_A], writes=[b_A])
                fw.op("dve", lambda: dve.tensor_scalar(A[:], A[:], PI_C, -PI_C, ALU.min, ALU.max),
                      reads=[b_A], writes=[b_A])

            reduce_clamp(ANG, b_ANG)
            fw.op("act", lambda: nc.scalar.activation(PWI[:], ANG[:], AF.Sin), reads=[b_ANG], writes=[b_PW])
            fw.op("dve", lambda: dve.tensor_scalar(ANG[:], ANG[:], 1.5707963267948966, None, ALU.add),
                  reads=[b_ANG], writes=[b_ANG])
            reduce_clamp(ANG, b_ANG)
            fw.op("act", lambda: nc.scalar.activation(PWR[:], ANG[:], AF.Sin), reads=[b_ANG], writes=[b_PW])
            fw.op("dve", lambda: dve.tensor_tensor(PWR[:], PWR[:], MAG[:], ALU.mult), reads=[b_PW, b_MAG], writes=[b_PW])
            fw.op("dve", lambda: dve.tensor_tensor(PWI[:], PWI[:], MAG[:], ALU.mult), reads=[b_PW, b_MAG], writes=[b_PW])
            fw.op("dve", lambda: dve.tensor_scalar(NAI[:], PWI[:, 24:32, :], -1.0, None, ALU.mult),
                  reads=[b_PW], writes=[b_PW])
            FT = fw.sb(pes, "s_FT", [128, 8, 64], F32); b_FT = Buf()
            a_re = PWR[:, 32, :]
            a_im = PWI[:, 32, :]
            den, t2, nr, fre, fim, tt_ = (FT[:, i, :] for i in range(6))
            ops = [
                (den, LR, LR, ALU.mult), (t2, LI, LI, ALU.mult), (den, den, t2, ALU.add),
            ]
            for o, a, b, op_ in ops:
                fw.op("dve", lambda o=o, a=a, b=b, op_=op_: dve.tensor_tensor(o, a, b, op_),
                      reads=[b_LRI, b_FT, b_PW], writes=[b_FT])
            fw.op("dve", lambda: dve.reciprocal(den, den), reads=[b_FT], writes=[b_FT])
            fw.op("dve", lambda: dve.tensor_scalar(nr, a_re, -1.0, None, ALU.add), reads=[b_PW], writes=[b_FT])
            ops = [
                (fre, nr, LR, ALU.mult), (tt_, a_im, LI, ALU.mult), (fre, fre, tt_, ALU.add), (fre, fre, den, ALU.mult),
                (fim, a_im, LR, ALU.mult), (tt_, nr, LI, ALU.mult), (fim, fim, tt_, ALU.subtract),
                (fim, fim, den, ALU.mult),
            ]
            for o, a, b, op_ in ops:
                fw.op("dve", lambda o=o, a=a, b=b, op_=op_: dve.tensor_tensor(o, a, b, op_),
                      reads=[b_LRI, b_FT, b_PW], writes=[b_FT])
            BR = fw.sb(pes, "s_BR", [128, 64, 16], F32); b_BRI = Buf()
            BI = fw.sb(pes, "s_BI", [128, 64, 16], F32)
            TB_ = fw.sb(pes, "s_TB", [128, 64, 16], F32); b_TB = Buf()
            for d in range(2):
                fw.dma("sp", BR[64 * d:64 * d + 64, :, :], dram["ssm_b_re"][d].rearrange("g n q -> n g q"), writes=[b_BRI])
                fw.dma("sp", BI[64 * d:64 * d + 64, :, :], dram["ssm_b_im"][d].rearrange("g n q -> n g q"), writes=[b_BRI])
            fre_b = ap4(FT, 3 * 64, 8 * 64, 128, [(1, 64), (0, 16)])
            fim_b = ap4(FT, 4 * 64, 8 * 64, 128, [(1, 64), (0, 16)])
            seq = [
                (BBR[:], fre_b, BR[:], ALU.mult), (TB_[:], fim_b, BI[:], ALU.mult), (BBR[:], BBR[:], TB_[:], ALU.subtract),
                (BBI[:], fre_b, BI[:], ALU.mult), (TB_[:], fim_b, BR[:], ALU.mult), (BBI[:], BBI[:], TB_[:], ALU.add),
            ]
            for o, a, b, op_ in seq:
                fw.op("dve", lambda o=o, a=a, b=b, op_=op_: dve.tensor_tensor(o, a, b, op_),
                      reads=[b_FT, b_BRI, b_TB, b_BB], writes=[b_TB, b_BB])
            Cin = fw.sb(pes, "s_Cin", [128, 2, 8, 128], F32); b_Cin = Buf()
            for i, nm in enumerate(("ssm_c_re", "ssm_c_im")):
                for d in range(2):
                    fw.dma("sp", Cin[:, i, :, 64 * d:64 * d + 64],
                           dram[nm][d].rearrange("(gb g8) p n -> (g8 p) gb n", g8=8), writes=[b_Cin])
            for i, CT in enumerate((CTR, CTI)):
                for half in range(2):
                    bk = C.bank[half]
                    for q4 in range(4):
                        gb_ = half * 4 + q4
                        fw.op("pe", lambda i=i, gb_=gb_, q4=q4, bk=bk: nc.tensor.transpose(
                            bk[:, q4 * 128:(q4 + 1) * 128], Cin[:, i, gb_, :], C.identf[:]),
                            reads=[b_Cin, C.b_identf], writes=[C.b_bank[half]], signal=(q4 == 3))
                    fw.op("dve", lambda CT=CT, half=half, bk=bk: dve.tensor_copy(
                        CT[:].rearrange("p g q -> p (g q)")[:, half * 512:(half + 1) * 512], bk[:]),
                        reads=[C.b_bank[half]], writes=[b_CT])
            fw.barrier()
        GB = 16
        CO_re = fw.sb(es, "s_COre", [128, GB, 128], BF16); b_CO = Buf()
        CO_imn = fw.sb(es, "s_COim", [128, GB, 128], BF16)
        WX_re = fw.sb(es, "s_WXre", [128, GB, 128], BF16); b_WX = Buf()
        WX_im = fw.sb(es, "s_WXim", [128, GB, 128], BF16)
        W2_re = fw.sb(es, "s_W2re", [128, GB, 128], BF16); b_W2 = Buf()
        W2_im = fw.sb(es, "s_W2im", [128, GB, 128], BF16)
        WXT_re = fw.sb(es, "s_WXTre", [128, GB, 128], BF16); b_WXT = Buf()
        WXT_im = fw.sb(es, "s_WXTim", [128, GB, 128], BF16)
        TT = fw.sb(es, "s_TT", [128, GB, 128], BF16); b_TT = Buf()
        T1 = fw.sb(es, "s_T1", [128, GB * 128], F32); b_T1 = Buf()
        T2 = fw.sb(es, "s_T2", [128, GB * 128], F32); b_T2 = Buf()
        U8b = fw.sb(es, "s_U8b", [128, GB, 256], BF16); b_U8 = Buf()
        NI = 4
        ST = [[fw.sb(es, f"s_ST{a_}{b_}", [128, 2, 256], F32) for b_ in range(2)] for a_ in range(NI)]
        b_ST = [[Buf(), Buf()] for _ in range(NI)]
        SS = [fw.sb(es, f"s_SS{a_}", [128, 2, 256], F32) for a_ in range(NI)]
        b_SS = [Buf() for _ in range(NI)]
        E = [fw.sb(es, f"s_E{a_}", [128, 2, 128], BF16) for a_ in range(NI)]
        b_E = [Buf() for _ in range(NI)]
        for a_ in range(NI):
            for b_ in range(2):
                fw.op("dve", lambda a_=a_, b_=b_: dve.memset(ST[a_][b_][:], 0.0), writes=[b_ST[a_][b_]])
            fw.op("dve", lambda a_=a_: dve.memset(E[a_][:], 0.0), writes=[b_E[a_]])
            fw.op("dve", lambda a_=a_: dve.memset(SS[a_][:], 0.0), writes=[b_SS[a_]])
        ig = 0
        for gb in range(NG // GB):
            g0 = gb * GB

            def expand(fam, XR, XI, b_X, o_re, o_im, b_o, neg_im):
                pwr = ap4(PWR, fam * 8 * 64 + g0, NF * 64, 128, [(1, GB), (64, 8), (0, 16)])
                pwi = ap4(PWI, fam * 8 * 64 + g0, NF * 64, 128, [(1, GB), (64, 8), (0, 16)])
                xr = ap4(XR, g0 * 16, 1024, 128, [(16, GB), (0, 8), (1, 16)])
                xi = ap4(XI, g0 * 16, 1024, 128, [(16, GB), (0, 8), (1, 16)])
                t1 = T1[:].rearrange("p (g j q) -> p g j q", g=GB, j=8)
                t2 = T2[:].rearrange("p (g j q) -> p g j q", g=GB, j=8)
                ore = o_re[:].rearrange("p g (j q) -> p g j q", j=8)
                oim = o_im[:].rearrange("p g (j q) -> p g j q", j=8)
                fw.op("dve", lambda: dve.tensor_tensor(t1, pwr, xr, ALU.mult), reads=[b_PW, b_X], writes=[b_T1])
                fw.op("pool", lambda: nc.gpsimd.tensor_tensor(t2, pwi, xi, ALU.mult), reads=[b_PW, b_X], writes=[b_T2])
                fw.op("dve", lambda: dve.tensor_tensor(ore, t1, t2, ALU.subtract), reads=[b_T1, b_T2], writes=[b_o])
                fw.op("dve", lambda: dve.tensor_tensor(t1, pwr, xi, ALU.mult), reads=[b_PW, b_X], writes=[b_T1])
                fw.op("pool", lambda: nc.gpsimd.tensor_tensor(t2, pwi, xr, ALU.mult), reads=[b_PW, b_X], writes=[b_T2])
                if neg_im:
                    fw.op("dve", lambda: dve.scalar_tensor_tensor(oim, t1, -1.0, t2, ALU.mult, ALU.subtract),
                          reads=[b_T1, b_T2], writes=[b_o])
                else:
                    fw.op("dve", lambda: dve.tensor_tensor(oim, t1, t2, ALU.add), reads=[b_T1, b_T2], writes=[b_o])

            expand(0, CTR, CTI, b_CT, CO_re, CO_imn, b_CO, True)
            expand(1, BBR, BBI, b_BB, WX_re, WX_im, b_WX, False)
            expand(2, BBR, BBI, b_BB, W2_re, W2_im, b_W2, False)
            for src, dstT in ((WX_re, WXT_re), (WX_im, WXT_im)):
                for half in range(2):
                    bk = C.bank[half]
                    pst = bk[:].bitcast(BF16)
                    for q8 in range(8):
                        gl = half * 8 + q8
                        fw.op("pe", lambda src=src, gl=gl, q8=q8, pst=pst: nc.tensor.transpose(
                            pst[:, q8 * 128:(q8 + 1) * 128], src[:, gl, :], C.ident[:]),
                            reads=[b_WX, C.b_ident], writes=[C.b_bank[half]], signal=(q8 == 7))
                    fw.op("act", lambda dstT=dstT, half=half, pst=pst: nc.scalar.copy(
                        dstT[:, half * 8:(half + 1) * 8, :].rearrange("p g m -> p (g m)"), pst),
                        reads=[C.b_bank[half]], writes=[b_WXT])
            for q in range(GB // 4):
                for g4 in range(4):
                    gl = q * 4 + g4
                    for (lo, bki) in ((0, 2), (64, 3)):
                        fw.op("pe", lambda gl=gl, g4=g4, lo=lo, bki=bki: nc.tensor.matmul(
                            C.bank[bki][:, g4 * 128:(g4 + 1) * 128], W2_re[lo:lo + 64, gl, :], CO_re[lo:lo + 64, gl, :],
                            start=True, stop=False),
                            reads=[b_W2, b_CO], writes=[C.b_bank[bki]], signal=False)
                        fw.op("pe", lambda gl=gl, g4=g4, lo=lo, bki=bki: nc.tensor.matmul(
                            C.bank[bki][:, g4 * 128:(g4 + 1) * 128], W2_im[lo:lo + 64, gl, :], CO_imn[lo:lo + 64, gl, :],
                            start=False, stop=True),
                            reads=[b_W2, b_CO], writes=[C.b_bank[bki]], signal=(g4 == 3))
                mf = ap4(MF, 0, 128, 128, [(0, 4), (1, 128)])
                mb = ap4(MB, 0, 128, 128, [(0, 4), (1, 128)])
                t1v = T1[:, 0:512].rearrange("p (g m) -> p g m", g=4)
                t2v = T2[:, 0:512].rearrange("p (g m) -> p g m", g=4)
                fw.op("dve", lambda t1v=t1v, mf=mf: dve.tensor_tensor(
                    t1v, C.bank[2][:].rearrange("p (g m) -> p g m", g=4), mf, ALU.mult),
                    reads=[C.b_bank[2], b_MK], writes=[b_T1])
                fw.op("dve", lambda t2v=t2v, mb=mb: dve.tensor_tensor(
                    t2v, C.bank[3][:].rearrange("p (g m) -> p g m", g=4), mb, ALU.mult),
                    reads=[C.b_bank[3], b_MK], writes=[b_T2])
                fw.op("dve", lambda q=q, t1v=t1v, t2v=t2v: dve.tensor_tensor(
                    TT[:, q * 4:(q + 1) * 4, :], t1v, t2v, ALU.add), reads=[b_T1, b_T2], writes=[b_TT])
            for hh in range(2):
                for half in range(2):
                    bk = C.bank[half]
                    pst = bk[:].bitcast(BF16)
                    for q8 in range(8):
                        g = g0 + half * 8 + q8
                        fw.op("pe", lambda hh=hh, g=g, q8=q8, pst=pst: nc.tensor.transpose(
                            pst[:, q8 * 128:(q8 + 1) * 128], U2[:, hh, g, :], C.ident[:]),
                            reads=[b_U2, C.b_ident], writes=[C.b_bank[half]], signal=(q8 == 7))
                    fw.op("act", lambda hh=hh, half=half, pst=pst: nc.scalar.copy(
                        U8b[:, half * 8:(half + 1) * 8, hh * 128:(hh + 1) * 128],
                        pst.rearrange("p (g c) -> p g c", g=8)),
                        reads=[C.b_bank[half]], writes=[b_U8])
            for gp in range(GB // NI):
                info = []
                for st in range(NI):
                    gl = gp * NI + st
                    g = g0 + gl
                    xb = 2 + st
                    yb = 6 + ((ig // 4) % 2)
                    g4 = ig % 4
                    ig += 1
                    info.append((gl, g, xb, yb, g4))
                    fw.op("pe", lambda gl=gl, xb=xb: nc.tensor.matmul(
                        C.bank[xb][:, 0:256], WXT_re[:, gl, :], U8b[:, gl, :], start=True, stop=True),
                        reads=[b_WXT, b_U8], writes=[C.b_bank[xb]], signal=False)
                    fw.op("pe", lambda gl=gl, xb=xb: nc.tensor.matmul(
                        C.bank[xb][:, 256:512], WXT_im[:, gl, :], U8b[:, gl, :], start=True, stop=True),
                        reads=[b_WXT, b_U8], writes=[C.b_bank[xb]])
                    bx = C.bank[xb]
                    S0 = ST[st][0]
                    fw.op("act", lambda bx=bx, S0=S0: nc.scalar.copy(
                        S0[0:64, :, :], bx[0:64, :].rearrange("p (a c) -> p a c", a=2)),
                        reads=[C.b_bank[xb]], writes=[b_ST[st][0]])
                    fw.op("act", lambda bx=bx, S0=S0: nc.scalar.copy(S0[64:128, 0, :], bx[64:128, 255::-1]),
                          reads=[C.b_bank[xb]], writes=[b_ST[st][0]])
                    fw.op("act", lambda bx=bx, S0=S0: nc.scalar.copy(S0[64:128, 1, :], bx[64:128, 511:255:-1]),
                          reads=[C.b_bank[xb]], writes=[b_ST[st][0]])
                cur = 0
                for k in range(8):
                    sft = 1 << k
                    nxt = 1 - cur
                    n = 256 - sft
                    for st in range(NI):
                        gl, g, xb, yb, g4 = info[st]
                        Sc, Sn, Sx = ST[st][cur], ST[st][nxt], SS[st]
                        ar = PWR[:, 24 + k, g:g + 1]
                        ai = PWI[:, 24 + k, g:g + 1]
                        nai = NAI[:, k, g:g + 1]
                        fw.op("dve", lambda Sc=Sc, Sx=Sx, ar=ar, sft=sft, n=n: dve.scalar_tensor_tensor(
                            Sx[:, :, sft:256], Sc[:, :, 0:n], ar, Sc[:, :, sft:256], ALU.mult, ALU.add),
                            reads=[b_ST[st][cur], b_PW], writes=[b_SS[st]])
                        fw.op("dve", lambda Sc=Sc, Sn=Sn, Sx=Sx, nai=nai, sft=sft, n=n: dve.scalar_tensor_tensor(
                            Sn[:, 0, sft:256], Sc[:, 1, 0:n], nai, Sx[:, 0, sft:256], ALU.mult, ALU.add),
                            reads=[b_ST[st][cur], b_SS[st], b_PW], writes=[b_ST[st][nxt]])
                        fw.op("dve", lambda Sc=Sc, Sn=Sn, Sx=Sx, ai=ai, sft=sft, n=n: dve.scalar_tensor_tensor(
                            Sn[:, 1, sft:256], Sc[:, 0, 0:n], ai, Sx[:, 1, sft:256], ALU.mult, ALU.add),
                            reads=[b_ST[st][cur], b_SS[st], b_PW], writes=[b_ST[st][nxt]])
                        fw.op("act", lambda Sc=Sc, Sn=Sn, sft=sft: nc.scalar.copy(Sn[:, :, 0:sft], Sc[:, :, 0:sft]),
                              reads=[b_ST[st][cur]], writes=[b_ST[st][nxt]])
                    cur = nxt
                for st in range(NI):
                    gl, g, xb, yb, g4 = info[st]
                    Sf = ST[st][cur]
                    Es = E[st]
                    fw.op("act", lambda Sf=Sf, Es=Es: nc.scalar.copy(Es[0:64, :, 1:128], Sf[0:64, :, 0:127]),
                          reads=[b_ST[st][cur]], writes=[b_E[st]])
                    fw.op("pool", lambda Sf=Sf, Es=Es: nc.gpsimd.tensor_copy(Es[64:128, 0, :], Sf[64:128, 0, 254:126:-1]),
                          reads=[b_ST[st][cur]], writes=[b_E[st]])
                    fw.op("pool", lambda Sf=Sf, Es=Es: nc.gpsimd.tensor_copy(Es[64:128, 1, :], Sf[64:128, 1, 254:126:-1]),
                          reads=[b_ST[st][cur]], writes=[b_E[st]])
                    yo = C.bank[yb][:, g4 * 128:(g4 + 1) * 128]
                    fw.op("pe", lambda gl=gl, yo=yo: nc.tensor.matmul(yo, U8b[:, gl, 0:128], TT[:, gl, :], start=True, stop=False),
                          reads=[b_U8, b_TT], writes=[C.b_bank[yb]], signal=False)
                    fw.op("pe", lambda gl=gl, yo=yo, Es=Es: nc.tensor.matmul(yo, Es[:, 0, :], CO_re[:, gl, :], start=False, stop=False),
                          reads=[b_E[st], b_CO], writes=[C.b_bank[yb]], signal=False)
                    fw.op("pe", lambda gl=gl, yo=yo, Es=Es: nc.tensor.matmul(yo, Es[:, 1, :], CO_imn[:, gl, :], start=False, stop=True),
                          reads=[b_E[st], b_CO, b_U8, b_TT], writes=[C.b_bank[yb]])
                    if g4 == 3:
                        gbase = g - 3
                        fw.op("dve", lambda yb=yb, gbase=gbase: dve.tensor_copy(
                            Yown[:, :, gbase * 16:(gbase + 4) * 16].rearrange("c j (g p) -> c g j p", g=4),
                            C.bank[yb][:].rearrange("c (g j p) -> c g j p", g=4, j=8)),
                            reads=[C.b_bank[yb]], writes=[b_Y])
        for j in range(8):
            fw.op("pool", lambda j=j: nc.gpsimd.tensor_tensor(
                T1[:, 0:1024].rearrange("c (g p) -> c g p", g=64), dB[:].rearrange("c (g p) -> c g p", g=64),
                U2[:, 0, :, j * 16:(j + 1) * 16], ALU.mult),
                  reads=[b_dB, b_U2], writes=[b_T1])
            fw.op("dve", lambda j=j: dve.tensor_tensor(Yown[:, j, :], Yown[:, j, :], T1[:, 0:1024], ALU.add),
                  reads=[b_T1, b_Y], writes=[b_Y])
        fw.barrier()


GELU_K0 = 0.7978845608028654
GELU_K1 = 0.7978845608028654 * 0.044715


def post_phase(fw, nc, C, dram, Yown, b_Y, aT, b_aT, x1_d, b_x1d, y_d, b_yd, x2_d, b_x2d):
    dve = nc.vector
    with ExitStack() as es:
        sT = fw.sb(es, "p_sT", [128, 8, TOWN], BF16); b_sT = Buf()
        with ExitStack() as es1:
            wglu = fw.sb(es1, "p_wglu", [128, 8, 1024], BF16); b_wglu = Buf()
            wgv = dram["ssm_w_glu"].rearrange("(k p) m -> p k m", p=128)
            for h2 in range(2):
                fw.dma("pool", wglu[:, :, h2 * 512:(h2 + 1) * 512], wgv[:, :, h2 * 512:(h2 + 1) * 512], writes=[b_wglu])
            bglu = fw.sb(es1, "p_bglu", [128, 1024], F32); b_bg = Buf()
            outg = fw.sb(es1, "p_outg", [128, 1024], F32)
            fw.dma("sp", bglu[:], dram["ssm_b_glu"][0:1, :].partition_broadcast(128), writes=[b_bg])
            fw.dma("sp", outg[:], dram["ssm_out_g"][0:1, :].partition_broadcast(128), writes=[b_bg])
            tas = [fw.sb(es1, f"p_ta{i}", [128, 1024], F32) for i in range(2)]; b_tas = [Buf(), Buf()]
            tbs = [fw.sb(es1, f"p_tb{i}", [128, 1024], F32) for i in range(2)]; b_tbs = [Buf(), Buf()]
            g1s = [fw.sb(es1, f"p_g1{i}", [128, 1024], F32) for i in range(2)]; b_g1s = [Buf(), Buf()]
            g1bs = [fw.sb(es1, f"p_g1b{i}", [128, 1024], BF16) for i in range(2)]; b_g1bs = [Buf(), Buf()]
            g1Ts = [fw.sb(es1, f"p_g1T{i}", [128, 8, 128], BF16) for i in range(2)]; b_g1Ts = [Buf(), Buf()]
            ss = fw.sb(es1, "p_ss", [128, 16], F32); b_ss = Buf()
            rs = fw.sb(es1, "p_rs", [128, 16], F32); b_rs = Buf()
            for j in range(8):
                pj = j % 2
                ta, b_ta, tb, b_tb, g1, b_g1 = tas[pj], b_tas[pj], tbs[pj], b_tbs[pj], g1s[pj], b_g1s[pj]
                g1b, b_g1b, g1T, b_g1T = g1bs[pj], b_g1bs[pj], g1Ts[pj], b_g1Ts[pj]
                bk0 = 4 * pj
                y = Yown[:, j, :]
                fw.op("act", lambda y=y, ta=ta: nc.scalar.activation(ta[:], y, AF.Square), reads=[b_Y], writes=[b_ta])
                fw.op("dve", lambda ta=ta: dve.tensor_scalar(ta[:], ta[:], GELU_K1, GELU_K0, ALU.mult, ALU.add),
                      reads=[b_ta], writes=[b_ta])
                fw.op("dve", lambda y=y, ta=ta: dve.tensor_tensor(ta[:], ta[:], y, ALU.mult), reads=[b_ta, b_Y], writes=[b_ta])
                fw.op("act", lambda ta=ta, tb=tb: nc.scalar.activation(tb[:], ta[:], AF.Sigmoid, scale=2.0),
                      reads=[b_ta], writes=[b_tb])
                fw.op("dve", lambda y=y, tb=tb, g1=g1: dve.tensor_tensor(g1[:], y, tb[:], ALU.mult),
                      reads=[b_tb, b_Y], writes=[b_g1])
                fw.op("pool", lambda g1=g1, g1b=g1b: nc.gpsimd.tensor_copy(g1b[:], g1[:]), reads=[b_g1], writes=[b_g1b])
                pst = C.bank[bk0][:].bitcast(BF16)
                for kf in range(8):
                    fw.op("pe", lambda kf=kf, pst=pst, g1b=g1b: nc.tensor.transpose(
                        pst[:, kf * 128:(kf + 1) * 128], g1b[:, kf * 128:(kf + 1) * 128], C.ident[:]),
                        reads=[b_g1b, C.b_ident], writes=[C.b_bank[bk0]], signal=(kf == 7))
                fw.op("act", lambda pst=pst, g1T=g1T: nc.scalar.copy(g1T[:].rearrange("p k c -> p (k c)"), pst),
                      reads=[C.b_bank[bk0]], writes=[b_g1T])
                for n in range(2):
                    bk = bk0 + 2 + n
                    for kf in range(8):
                        fw.op("pe", lambda kf=kf, n=n, bk=bk, g1T=g1T: nc.tensor.matmul(
                            C.bank[bk][:], g1T[:, kf, :], wglu[:, kf, n * 512:(n + 1) * 512],
                            start=(kf == 0), stop=(kf == 7)),
                            reads=[b_g1T, b_wglu], writes=[C.b_bank[bk]], signal=(kf == 7))
                    fw.op("dve", lambda n=n, bk=bk, ta=ta: dve.tensor_tensor(
                        ta[:, n * 512:(n + 1) * 512], C.bank[bk][:], bglu[:, n * 512:(n + 1) * 512], ALU.add),
                        reads=[C.b_bank[bk], b_bg], writes=[b_ta])
                fw.op("act", lambda ta=ta, tb=tb: nc.scalar.activation(tb[:], ta[:], AF.Sigmoid), reads=[b_ta], writes=[b_tb])
                fw.op("dve", lambda g1=g1, tb=tb: dve.tensor_tensor(g1[:], g1[:], tb[:], ALU.mult),
                      reads=[b_tb, b_g1], writes=[b_g1])
                rms_stats(fw, nc, g1[:], b_g1, ta[:], b_ta, ss[:, j:j + 1], b_ss, rs[:, j:j + 1], b_rs, 1024)
                fw.op("dve", lambda j=j, g1=g1, g1b=g1b: dve.scalar_tensor_tensor(
                    g1b[:], g1[:], rs[:, j:j + 1], outg[:], ALU.mult, ALU.mult),
                    reads=[b_g1, b_rs, b_bg], writes=[b_g1b])
                pst2 = C.bank[bk0 + 1][:].bitcast(BF16)
                for kf in range(8):
                    fw.op("pe", lambda kf=kf, pst2=pst2, g1b=g1b: nc.tensor.transpose(
                        pst2[:, kf * 128:(kf + 1) * 128], g1b[:, kf * 128:(kf + 1) * 128], C.ident[:]),
                        reads=[b_g1b, C.b_ident], writes=[C.b_bank[bk0 + 1]], signal=(kf == 7))
                fw.op("act", lambda pst2=pst2, j=j: nc.scalar.copy(
                    sT[:, :, j * 128:(j + 1) * 128], pst2.rearrange("p (k c) -> p k c", k=8)),
                    reads=[C.b_bank[bk0 + 1]], writes=[b_sT])
            fw.barrier()
        wo = [fw.sb(es, f"p_wo{i}", [128, KD, 512], BF16) for i in range(2)]
        b_wo = [Buf(), Buf()]
        yev = [fw.sb(es, f"p_yev{i}", [128, 512], F32) for i in range(4)]
        b_yev = [Buf() for _ in range(4)]
        wov = dram["w_out"].rearrange("(k p) m -> p k m", p=128)
        iy = 0
        for n in range(4):
            s = n % 2
            fw.dma("pool", wo[s][:], wov[:, :, n * 512:(n + 1) * 512], writes=[b_wo[s]])
            for j in range(8):
                for k in range(KD):
                    if k < 8:
                        lhsT = aT[:, k, j:j + 1017:8]
                    else:
                        lhsT = sT[:, k - 8, j * 128:(j + 1) * 128]
                    fw.op("pe", lambda k=k, j=j, s=s, lhsT=lhsT: nc.tensor.matmul(
                        C.bank[j][:], lhsT, wo[s][:, k, :], start=(k == 0), stop=(k == KD - 1)),
                        reads=[b_aT, b_sT, b_wo[s]], writes=[C.b_bank[j]], signal=(k == KD - 1))
                ys = iy % 4
                iy += 1
                if j % 2 == 0:
                    fw.op("act", lambda j=j, ys=ys: nc.scalar.copy(yev[ys][:], C.bank[j][:]),
                          reads=[C.b_bank[j]], writes=[b_yev[ys]])
                else:
                    fw.op("dve", lambda j=j, ys=ys: dve.tensor_copy(yev[ys][:], C.bank[j][:]),
                          reads=[C.b_bank[j]], writes=[b_yev[ys]])
                fw.dma("sp", y_d[j * 128:(j + 1) * 128, n * 512:(n + 1) * 512], yev[ys][:],
                       reads=[b_yev[ys]], writes=[b_yd])
        gpost = fw.sb(es, "p_gpost", [128, D], F32); b_gp = Buf()
        fw.dma("sp", gpost[:], dram["mix_post_g"][0:1, :].partition_broadcast(128), writes=[b_gp])
        xts = [fw.sb(es, f"p_xt{i}", [128, D], F32) for i in range(2)]; b_xts = [Buf(), Buf()]
        yts = [fw.sb(es, f"p_yt{i}", [128, D], F32) for i in range(2)]; b_yts = [Buf(), Buf()]
        junks = [fw.sb(es, f"p_junk{i}", [128, D], BF16) for i in range(2)]; b_junks = [Buf(), Buf()]
        ss2 = fw.sb(es, "p_ss2", [128, 16], F32); b_ss2 = Buf()
        rs2 = fw.sb(es, "p_rs2", [128, 16], F32); b_rs2 = Buf()
        for j in range(8):
            pj = j % 2
            xt, b_xt, yt, b_yt, junk, b_junk = xts[pj], b_xts[pj], yts[pj], b_yts[pj], junks[pj], b_junks[pj]
            fw.dma("sp", yt[:], y_d[j * 128:(j + 1) * 128, :], reads=[b_yd], writes=[b_yt])
            fw.dma("sp", xt[:], x1_d[j:j + 1017:8, :], reads=[b_x1d], writes=[b_xt])
            rms_stats(fw, nc, yt[:], b_yt, junk[:], b_junk, ss2[:, j:j + 1], b_ss2, rs2[:, j:j + 1], b_rs2, D)
            fw.op("dve", lambda j=j, yt=yt: dve.scalar_tensor_tensor(yt[:], yt[:], rs2[:, j:j + 1], gpost[:], ALU.mult, ALU.mult),
                  reads=[b_yt, b_rs2, b_gp], writes=[b_yt])
            fw.op("pool", lambda yt=yt, xt=xt: nc.gpsimd.tensor_tensor(yt[:], yt[:], xt[:], ALU.add),
                  reads=[b_yt, b_xt], writes=[b_yt])
            fw.dma("pool", x2_d[j:j + 1017:8, :], yt[:], reads=[b_yt], writes=[b_x2d])
        fw.barrier()

def build(stage="full"):
    nc = bass.Bass("TRN2", target_bir_lowering=False)
    dram = {}

    def din(name, shape, dt=F32):
        dram[name] = nc.dram_tensor(name, list(shape), dt, kind="ExternalInput").ap()

    din("xs", [S, D])
    din("c_ident", [128, 128])
    for p in ("ff1", "ff2"):
        din(p + "_pre_g", [1, D]); din(p + "_post_g", [1, D])
        din(p + "_w_gate", [D, DFF]); din(p + "_w_up", [D, DFF]); din(p + "_w_down", [DFF, D])
    din("c_alibi", [128, 3072])
    din("mix_pre_g", [1, D]); din("mix_post_g", [1, D])
    din("w_in", [D, 4096]); din("w_out", [D, D])
    for nm in ("lam_q1", "lam_k1", "lam_q2", "lam_k2"):
        din(nm, [1, 64])
    din("attn_head_g", [1, 128])
    din("c_exps", [128, 33]); din("c_maskF", [128, 128]); din("c_maskB", [128, 128])
    din("ssm_lam_re", [2, 64, 64]); din("ssm_lam_im", [2, 64, 64]); din("ssm_log_dt", [2, 64])
    din("ssm_b_re", [2, 64, 64, 16]); din("ssm_b_im", [2, 64, 64, 16])
    din("ssm_c_re", [2, 64, 16, 64]); din("ssm_c_im", [2, 64, 16, 64])
    din("ssm_d", [1, 1024])
    u_d = nc.dram_tensor("u_d", [128, 2, 64, 128], BF16, kind="Internal").ap()
    b_ud = Buf()
    din("ssm_w_glu", [1024, 1024]); din("ssm_b_glu", [1, 1024]); din("ssm_out_g", [1, 1024])
    x2_d = nc.dram_tensor("x2_d", [TOWN, D], F32, kind="Internal").ap()
    b_x2d = Buf()
    if stage == "s5":
        dbg_Y = nc.dram_tensor("dbg_Y", [128, 8, 1024], F32, kind="ExternalOutput").ap()
    out = nc.dram_tensor("out", [TOWN, D], F32, kind="ExternalOutput").ap()
    if stage == "attn":
        dbg_aT = nc.dram_tensor("dbg_aT", [128, 8, TOWN], BF16, kind="ExternalOutput").ap()
    x1_d = nc.dram_tensor("x1_d", [S, D], F32, kind="Internal").ap()
    y_d = nc.dram_tensor("y_d", [TOWN, D], F32, kind="Internal").ap()
    b_x1d, b_yd, b_xs, b_out = Buf(), Buf(), Buf(), Buf()

    with ExitStack() as es:
        fw = FW(nc, es)
        C = Ctx()
        setup_consts(fw, nc, C, es, dram)
        if stage == "ffn":
            with ExitStack() as pes:
                A = ffn_alloc(fw, nc, pes)
                ffn_pass(fw, nc, C, A, dram["xs"][0:TOWN, :], out, dram["ff1_w_gate"], dram["ff1_w_up"],
                         dram["ff1_w_down"], dram["ff1_pre_g"], dram["ff1_post_g"], y_d, b_yd, b_xs, b_out)
                fw.barrier()
        if stage == "attn":
            with ExitStack() as mes:
                aT = fw.sb(mes, "m_aT", [128, 8, TOWN], BF16); b_aT = Buf()
                AA = attention_alloc(fw, nc, mes)
                with ExitStack() as mes3:
                    M = mixer_common_alloc(fw, nc, mes3)
                    M.aT, M.b_aT = aT, b_aT
                    build_hmT(fw, nc, C, M, dram["xs"], b_xs, dram["mix_pre_g"])
                    attention_proj(fw, nc, C, M, AA, dram, u_d, b_ud)
                attention_core(fw, nc, C, M, AA, dram)
                fw.dma("sp", dbg_aT[:, :, :], M.aT[:], reads=[M.b_aT], writes=[b_out])
                fw.barrier()
        if stage == "s5":
            with ExitStack() as mes:
                Yown = fw.sb(mes, "m_Yown", [128, 8, 1024], F32); b_Y = Buf()
                with ExitStack() as mes2:
                    M = mixer_common_alloc(fw, nc, mes2)
                    build_hmT(fw, nc, C, M, dram["xs"], b_xs, dram["mix_pre_g"])
                    attention_proj(fw, nc, C, M, None, dram, u_d, b_ud, only_u=True)
                s5_phase(fw, nc, C, dram, Yown, b_Y, u_d, b_ud)
                fw.dma("sp", dbg_Y[:, :, :], Yown[:], reads=[b_Y], writes=[b_out])
                fw.barrier()
        if stage in ("full", "mix"):
            if stage == "full":
                with ExitStack() as pes:
                    A = ffn_alloc(fw, nc, pes)
                    for ps_ in range(2):
                        ffn_pass(fw, nc, C, A, dram["xs"][ps_ * TOWN:(ps_ + 1) * TOWN, :],
                                 x1_d[ps_ * TOWN:(ps_ + 1) * TOWN, :], dram["ff1_w_gate"], dram["ff1_w_up"],
                                 dram["ff1_w_down"], dram["ff1_pre_g"], dram["ff1_post_g"], y_d, b_yd, b_xs, b_x1d)
                    fw.barrier()
                x1_src = x1_d
            else:
                x1_src = dram["xs"]
            with ExitStack() as mes:
                aT = fw.sb(mes, "m_aT", [128, 8, TOWN], BF16); b_aT = Buf()
                with ExitStack() as mes2:
                    AA = attention_alloc(fw, nc, mes2)
                    with ExitStack() as mes3:
                        M = mixer_common_alloc(fw, nc, mes3)
                        M.aT, M.b_aT = aT, b_aT
                        build_hmT(fw, nc, C, M, x1_src, b_x1d, dram["mix_pre_g"])
                        attention_proj(fw, nc, C, M, AA, dram, u_d, b_ud)
                    attention_core(fw, nc, C, M, AA, dram)
                Yown = fw.sb(mes, "m_Yown", [128, 8, 1024], F32); b_Y = Buf()
                s5_phase(fw, nc, C, dram, Yown, b_Y, u_d, b_ud)
                post_phase(fw, nc, C, dram, Yown, b_Y, aT, b_aT, x1_src, b_x1d, y_d, b_yd, x2_d, b_x2d)
            if stage == "full":
                with ExitStack() as pes:
                    A = ffn_alloc(fw, nc, pes)
                    ffn_pass(fw, nc, C, A, x2_d, out, dram["ff2_w_gate"], dram["ff2_w_up"],
                             dram["ff2_w_down"], dram["ff2_pre_g"], dram["ff2_post_g"], y_d, b_yd, b_x2d, b_out)
                    fw.barrier()
            else:
                with ExitStack() as pes:
                    xt = fw.sb(pes, "o_xt", [128, D], F32); b_xt = Buf()
                    for tt in range(8):
                        fw.dma("sp", xt[:], x2_d[tt * 128:(tt + 1) * 128, :], reads=[b_x2d], writes=[b_xt])
                        fw.dma("sp", out[tt * 128:(tt + 1) * 128, :], xt[:], reads=[b_xt], writes=[b_out])
                    fw.barrier()
        fw.barrier(engines=("sp",))
    return nc


def common_inputs(inp):
    m = {}
    m["c_ident"] = np.eye(128, dtype=np.float32)
    jj = np.arange(128)[:, None]
    mm = np.arange(3072)[None, :]
    m["c_alibi"] = np.abs(mm - jj - 1920).astype(np.float32)
    ex = np.zeros((128, 33), np.float32)
    j8 = np.arange(8)
    ex[:64, 0:8] = j8 + 1; ex[:64, 8:16] = 7 - j8; ex[:64, 16:24] = -1 - j8
    ex[64:, 0:8] = 8 - j8; ex[64:, 8:16] = j8; ex[64:, 16:24] = j8 - 8
    ex[:, 24:32] = 8 * (2 ** j8); ex[:, 32] = 1
    m["c_exps"] = ex
    jrow = (np.arange(128) // 16)[:, None]
    jcol = (np.arange(128) // 16)[None, :]
    m["c_maskF"] = (jcol >= jrow).astype(np.float32)
    m["c_maskB"] = (jcol <= jrow).astype(np.float32)
    m["ssm_d"] = np.ascontiguousarray(np.asarray(inp["ssm_d"], dtype=np.float32).reshape(1, -1))
    for p in ("ff1", "ff2"):
        for n in ("_pre_g", "_post_g"):
            m[p + n] = np.ascontiguousarray(np.asarray(inp[p + n], dtype=np.float32).reshape(1, -1))
        for n in ("_w_gate", "_w_up", "_w_down"):
            m[p + n] = np.ascontiguousarray(np.asarray(inp[p + n], dtype=np.float32)[0])
    for n in ("mix_pre_g", "mix_post_g", "lam_q1", "lam_k1", "lam_q2", "lam_k2", "attn_head_g"):
        m[n] = np.ascontiguousarray(np.asarray(inp[n], dtype=np.float32).reshape(1, -1))
    m["ssm_w_glu"] = np.ascontiguousarray(np.asarray(inp["ssm_w_glu"], dtype=np.float32)[0])
    for n in ("ssm_b_glu", "ssm_out_g"):
        m[n] = np.ascontiguousarray(np.asarray(inp[n], dtype=np.float32).reshape(1, -1))
    m["w_in"] = np.ascontiguousarray(np.asarray(inp["w_in"], dtype=np.float32)[0])
    m["w_out"] = np.ascontiguousarray(np.asarray(inp["w_out"], dtype=np.float32)[0])
    return m


SSM_KEYS = ("ssm_lam_re", "ssm_lam_im", "ssm_log_dt", "ssm_b_re", "ssm_b_im", "ssm_c_re", "ssm_c_im")


def ssm_inputs(inp, r):
    m = {}
    for k in SSM_KEYS:
        a = np.asarray(inp[k], dtype=np.float32)[0]
        if r == 1:
            a = a[::-1]
        m[k] = np.ascontiguousarray(a)
    return m


_NC_CACHE = {}


def kernel(**inputs):
    x = np.asarray(inputs["x"], dtype=np.float32)
    B = x.shape[0]
    common = common_inputs(inputs)
    ssm = [ssm_inputs(inputs, r) for r in range(2)]
    in_maps = []
    for core in range(8):
        b, r = core // 2, core % 2
        m = dict(common)
        m.update(ssm[r])
        xs = x[b] if r == 0 else x[b][::-1]
        m["xs"] = np.ascontiguousarray(xs)
        in_maps.append(m)
    if "full" not in _NC_CACHE:
        _NC_CACHE["full"] = build("full")
    nc = _NC_CACHE["full"]
    res = run_bass_kernel_spmd(nc, in_maps, core_ids=list(range(8)))
    out = np.empty((B, S, D), dtype=np.float32)
    for core in range(8):
        b, r = core // 2, core % 2
        o = np.asarray(res.results[core]["out"], dtype=np.float32)
        if r == 0:
            out[b, :TOWN] = o
        else:
            out[b, TOWN:] = o[::-1]
    return out
```

```python
import numpy as np
from contextlib import ExitStack
import concourse.bass as bass
import concourse.mybir as mybir
from concourse.bass_utils import run_bass_kernel_spmd

F32 = mybir.dt.float32
BF16 = mybir.dt.bfloat16
I32 = mybir.dt.int32
AF = mybir.ActivationFunctionType
ALU = mybir.AluOpType

D = 2048
S = 2048
TOWN = 1024
DFF = 5632
NFF = DFF // 128
KD = D // 128
EPS = 1e-6
NH = 8
NG = 64


class Buf:
    __slots__ = ("name", "w", "r")

    def __init__(self, name=""):
        self.name = name
        self.w = {}
        self.r = {}


def _merge(d, ev):
    sem, val = ev
    k = id(sem)
    if k not in d or d[k][1] < val:
        d[k] = (sem, val)


class FW:
    NDMA = 8

    def __init__(self, nc, es):
        self.nc = nc
        self.es = es
        self.eng = {"pe": nc.tensor, "act": nc.scalar, "dve": nc.vector,
                    "pool": nc.gpsimd, "sp": nc.sync}
        self.csem = {}
        self.ccnt = {}
        for k in ("pe", "act", "dve", "pool"):
            self.csem[k] = es.enter_context(nc.semaphore("c_" + k))
            self.ccnt[k] = 0
        self.dsem = {}
        self.dcnt = {}
        self.di = {}
        for q in ("sp", "act", "pool"):
            self.dsem[q] = [es.enter_context(nc.semaphore(f"d_{q}{i}")) for i in range(self.NDMA)]
            self.dcnt[q] = [0] * self.NDMA
            self.di[q] = 0
        self.waited = {k: {} for k in self.eng}

    def sb(self, es, name, shape, dt):
        self.nalloc = getattr(self, "nalloc", 0) + 1
        return es.enter_context(self.nc.sbuf_tensor(f"{name}_{self.nalloc}", list(shape), dt))

    def ps(self, es, name, shape, dt):
        self.nalloc = getattr(self, "nalloc", 0) + 1
        return es.enter_context(self.nc.psum_tensor(f"{name}_{self.nalloc}", list(shape), dt))

    def _wait(self, ek, ev):
        sem, val = ev
        key = id(sem)
        d = self.waited[ek]
        if d.get(key, 0) >= val:
            return
        d[key] = val
        self.eng[ek].wait_ge(sem, val)

    def _deps(self, ek, reads, writes):
        best = {}
        for b in reads:
            for ev in b.w.values():
                _merge(best, ev)
        for b in writes:
            for ev in b.w.values():
                _merge(best, ev)
            for ev in b.r.values():
                _merge(best, ev)
        for ev in best.values():
            self._wait(ek, ev)

    def _commit(self, ev, reads, writes):
        for b in reads:
            _merge(b.r, ev)
        for b in writes:
            _merge(b.w, ev)

    def op(self, ek, fn, reads=(), writes=(), signal=True):
        self._deps(ek, reads, writes)
        ins = fn()
        if signal:
            self.ccnt[ek] += 1
            ins.then_inc(self.csem[ek], 1)
            ev = (self.csem[ek], self.ccnt[ek])
            self._commit(ev, reads, writes)
            return ev
        return None

    def dma(self, q, out, in_, reads=(), writes=(), **kw):
        i = self.di[q]
        self.di[q] = (i + 1) % self.NDMA
        sem = self.dsem[q][i]
        if self.dcnt[q][i] > 0:
            self._wait(q, (sem, self.dcnt[q][i]))
        self._deps(q, reads, writes)
        ins = self.eng[q].dma_start(out=out, in_=in_, **kw)
        self.dcnt[q][i] += 16
        ins.then_inc(sem, 16)
        ev = (sem, self.dcnt[q][i])
        self._commit(ev, reads, writes)
        return ev

    def all_events(self):
        evs = []
        for k in self.csem:
            if self.ccnt[k] > 0:
                evs.append((self.csem[k], self.ccnt[k]))
        for q in self.dsem:
            for i in range(self.NDMA):
                if self.dcnt[q][i] > 0:
                    evs.append((self.dsem[q][i], self.dcnt[q][i]))
        return evs

    def barrier(self, engines=("pe", "act", "dve", "pool", "sp")):
        evs = self.all_events()
        for ek in engines:
            for ev in evs:
                self._wait(ek, ev)


def bc_last(t, off, nparts, mid, last, pstride, mid_stride=1):
    return bass.AP(t, off, [[pstride, nparts], [mid_stride, mid], [0, last]])


class Ctx:
    pass


def setup_consts(fw, nc, C, es, dram):
    C.ident = fw.sb(es, "ident", [128, 128], BF16)
    C.b_ident = Buf()
    C.identf = fw.sb(es, "identf", [128, 128], F32)
    C.b_identf = Buf()
    fw.dma("sp", C.identf[:], dram["c_ident"][:, :], writes=[C.b_identf])
    fw.op("dve", lambda: nc.vector.tensor_copy(C.ident[:], C.identf[:]), reads=[C.b_identf], writes=[C.b_ident])
    C.ones = fw.sb(es, "ones", [128, 128], BF16)
    C.b_ones = Buf()
    fw.op("dve", lambda: nc.vector.memset(C.ones[:], 1.0), writes=[C.b_ones])
    C.bank = [fw.ps(es, f"bank{i}", [128, 512], F32) for i in range(8)]
    C.b_bank = [Buf(f"bank{i}") for i in range(8)]


def rms_stats(fw, nc, src_tile, b_src, junk, b_junk, ss_col, b_ss, rs_col, b_rs, n, mult=1.0):
    fw.op("act", lambda: nc.scalar.activation(junk, src_tile, AF.Square, accum_out=ss_col),
          reads=[b_src], writes=[b_junk, b_ss])
    fw.op("act", lambda: nc.scalar.activation(rs_col, ss_col, AF.Sqrt, bias=EPS, scale=1.0 / n),
          reads=[b_ss], writes=[b_rs])
    fw.op("dve", lambda: nc.vector.reciprocal(rs_col, rs_col), reads=[b_rs], writes=[b_rs])
    if mult != 1.0:
        fw.op("dve", lambda: nc.vector.tensor_scalar(rs_col, rs_col, float(mult), None, ALU.mult),
              reads=[b_rs], writes=[b_rs])


def load_gT(fw, nc, gT, b_gT, g_dram):
    with nc.allow_non_contiguous_dma("tiny gain transpose load"):
        fw.dma("sp", gT[:], g_dram[0, :].rearrange("(k p) -> p k", p=128), writes=[b_gT])


def norm_transpose(fw, nc, C, xt, b_xt, hb, b_hb, ss_col, b_ss, rs_col, b_rs, gT, b_gT, hT, b_hT, col0, ncols=128,
                   col_step=1):
    rms_stats(fw, nc, xt, b_xt, hb, b_hb, ss_col, b_ss, rs_col, b_rs, D)
    fw.op("dve", lambda: nc.vector.tensor_scalar(hb, xt, rs_col, None, ALU.mult),
          reads=[b_xt, b_rs], writes=[b_hb])
    for half in range(2):
        bk = C.bank[half]
        bb = C.b_bank[half]
        pst = bk[:].bitcast(BF16)
        for k8 in range(8):
            k = half * 8 + k8
            fw.op("pe", lambda k=k, k8=k8, pst=pst: nc.tensor.transpose(
                pst[:, k8 * 128:(k8 + 1) * 128], hb[:, k * 128:(k + 1) * 128], C.ident[:]),
                reads=[b_hb, C.b_ident], writes=[bb], signal=(k8 == 7))
        src3 = pst.rearrange("p (k t) -> p k t", k=8)
        if col_step == 1:
            dst3 = hT[:, half * 8:(half + 1) * 8, col0:col0 + ncols]
        else:
            dst3 = hT[:, half * 8:(half + 1) * 8, col0:col0 + ncols * col_step:col_step]
        gb = bc_last(gT, half * 8, 128, 8, 128, KD)
        fw.op("dve", lambda src3=src3, dst3=dst3, gb=gb: nc.vector.tensor_tensor(dst3, src3, gb, ALU.mult),
              reads=[bb, b_gT], writes=[b_hT])


def ffn_alloc(fw, nc, es):
    A = Ctx()
    A.hT = fw.sb(es, "f_hT", [128, KD, TOWN], BF16); A.b_hT = Buf()
    A.actT = fw.sb(es, "f_actT", [128, NFF, TOWN], BF16); A.b_actT = Buf()
    A.NW = 2
    A.wg = [fw.sb(es, f"f_wg{i}", [128, KD, 128], BF16) for i in range(A.NW)]
    A.wu = [fw.sb(es, f"f_wu{i}", [128, KD, 128], BF16) for i in range(A.NW)]
    A.b_wg = [Buf() for _ in range(A.NW)]
    A.b_wu = [Buf() for _ in range(A.NW)]
    A.NWD = 2
    A.wd = [fw.sb(es, f"f_wd{i}", [128, 11, 512], BF16) for i in range(A.NWD)]
    A.b_wd = [Buf() for _ in range(A.NWD)]
    A.xt = [fw.sb(es, f"f_xt{i}", [128, D], F32) for i in range(2)]
    A.b_xt = [Buf() for _ in range(2)]
    A.hb = [fw.sb(es, f"f_hb{i}", [128, D], BF16) for i in range(2)]
    A.b_hb = [Buf() for _ in range(2)]
    A.ss = fw.sb(es, "f_ss", [128, 64], F32); A.b_ss = Buf()
    A.rs = fw.sb(es, "f_rs", [128, 64], F32); A.b_rs = Buf()
    A.gT = fw.sb(es, "f_gT", [128, KD], F32); A.b_gT = Buf()
    A.gpost = fw.sb(es, "f_gpost", [128, D], F32); A.b_gpost = Buf()
    A.sg = [fw.sb(es, f"f_sg{i}", [128, 512], F32) for i in range(2)]
    A.b_sg = [Buf() for _ in range(2)]
    A.yev = [fw.sb(es, f"f_yev{i}", [128, 512], F32) for i in range(4)]
    A.b_yev = [Buf() for _ in range(4)]
    A.cnt = 0
    return A


def ffn_pass(fw, nc, C, A, src, dst, wg_d, wu_d, wd_d, pre_g, post_g, y_d, b_yd, b_src, b_dst):
    NT = TOWN // 128
    load_gT(fw, nc, A.gT, A.b_gT, pre_g)
    fw.dma("sp", A.gpost[:], post_g[0:1, :].partition_broadcast(128), writes=[A.b_gpost])
    for tt in range(NT):
        s = tt % 2
        fw.dma("sp", A.xt[s][:], src[tt * 128:(tt + 1) * 128, :], reads=[b_src], writes=[A.b_xt[s]])
        col = A.cnt % 64
        A.cnt += 1
        norm_transpose(fw, nc, C, A.xt[s][:], A.b_xt[s], A.hb[s][:], A.b_hb[s],
                       A.ss[:, col:col + 1], A.b_ss, A.rs[:, col:col + 1], A.b_rs,
                       A.gT, A.b_gT, A.hT, A.b_hT, tt * 128)
    wgv = wg_d.rearrange("(k p) m -> p k m", p=128)
    wuv = wu_d.rearrange("(k p) m -> p k m", p=128)
    for f in range(NFF):
        s = f % A.NW
        fw.dma("pool", A.wg[s][:], wgv[:, :, f * 128:(f + 1) * 128], writes=[A.b_wg[s]])
        fw.dma("pool", A.wu[s][:], wuv[:, :, f * 128:(f + 1) * 128], writes=[A.b_wu[s]])
        for c in range(2):
            bi = (f % 2) * 4 + c * 2
            pg, pu = C.bank[bi], C.bank[bi + 1]
            bpg, bpu = C.b_bank[bi], C.b_bank[bi + 1]
            for k in range(KD):
                fw.op("pe", lambda k=k, pg=pg, s=s, c=c: nc.tensor.matmul(
                    pg[:], A.wg[s][:, k, :], A.hT[:, k, c * 512:(c + 1) * 512], start=(k == 0), stop=(k == KD - 1)),
                    reads=[A.b_wg[s], A.b_hT], writes=[bpg], signal=(k == KD - 1))
            for k in range(KD):
                fw.op("pe", lambda k=k, pu=pu, s=s, c=c: nc.tensor.matmul(
                    pu[:], A.wu[s][:, k, :], A.hT[:, k, c * 512:(c + 1) * 512], start=(k == 0), stop=(k == KD - 1)),
                    reads=[A.b_wu[s], A.b_hT], writes=[bpu], signal=(k == KD - 1))
            sgs = c
            fw.op("act", lambda pg=pg, sgs=sgs: nc.scalar.activation(A.sg[sgs][:], pg[:], AF.Silu),
                  reads=[bpg], writes=[A.b_sg[sgs]])
            fw.op("dve", lambda pu=pu, sgs=sgs, f=f, c=c: nc.vector.tensor_tensor(
                A.actT[:, f, c * 512:(c + 1) * 512], A.sg[sgs][:], pu[:], ALU.mult),
                reads=[A.b_sg[sgs], bpu], writes=[A.b_actT])
    wdv = wd_d.rearrange("(f p) m -> p f m", p=128)
    ig = 0
    iy = 0
    for n in range(4):
        for g4 in range(4):
            s = ig % A.NWD
            ig += 1
            fw.dma("pool", A.wd[s][:], wdv[:, g4 * 11:(g4 + 1) * 11, n * 512:(n + 1) * 512], writes=[A.b_wd[s]])
            for tt in range(NT):
                for fi in range(11):
                    f = g4 * 11 + fi
                    last = (g4 == 3 and fi == 10)
                    fw.op("pe", lambda tt=tt, f=f, fi=fi, s=s, g4=g4: nc.tensor.matmul(
                        C.bank[tt][:], A.actT[:, f, tt * 128:(tt + 1) * 128], A.wd[s][:, fi, :],
                        start=(g4 == 0 and fi == 0), stop=(g4 == 3 and fi == 10)),
                        reads=[A.b_actT, A.b_wd[s]], writes=[C.b_bank[tt]], signal=(fi == 10))
        for tt in range(NT):
            ys = iy % 4
            iy += 1
            ek = "act" if tt % 2 == 0 else "dve"
            if ek == "act":
                fw.op("act", lambda tt=tt, ys=ys: nc.scalar.copy(A.yev[ys][:], C.bank[tt][:]),
                      reads=[C.b_bank[tt]], writes=[A.b_yev[ys]])
            else:
                fw.op("dve", lambda tt=tt, ys=ys: nc.vector.tensor_copy(A.yev[ys][:], C.bank[tt][:]),
                      reads=[C.b_bank[tt]], writes=[A.b_yev[ys]])
            fw.dma("sp", y_d[tt * 128:(tt + 1) * 128, n * 512:(n + 1) * 512], A.yev[ys][:],
                   reads=[A.b_yev[ys]], writes=[b_yd])
    act32 = A.actT[:].bitcast(F32)
    NDB = 3
    dby = [act32[:, 4 * i:4 * i + 4, :].rearrange("p a b -> p (a b)") for i in range(NDB)]
    dbx = [act32[:, 4 * (NDB + i):4 * (NDB + i) + 4, :].rearrange("p a b -> p (a b)") for i in range(NDB)]
    b_dby = [Buf() for _ in range(NDB)]
    b_dbx = [Buf() for _ in range(NDB)]
    for tt in range(NT):
        s = tt % 2
        d = tt % NDB
        yt, b_yt = dby[d], b_dby[d]
        xt, b_xt = dbx[d], b_dbx[d]
        fw.dma("sp", yt, y_d[tt * 128:(tt + 1) * 128, :], reads=[b_yd], writes=[b_yt, A.b_actT])
        fw.dma("sp", xt, src[tt * 128:(tt + 1) * 128, :], reads=[b_src], writes=[b_xt, A.b_actT])
        col = A.cnt % 64
        A.cnt += 1
        ssc = A.ss[:, col:col + 1]
        rsc = A.rs[:, col:col + 1]
        rms_stats(fw, nc, yt, b_yt, A.hb[s][:], A.b_hb[s], ssc, A.b_ss, rsc, A.b_rs, D, mult=0.5)
        fw.op("dve", lambda rsc=rsc, yt=yt: nc.vector.scalar_tensor_tensor(
            yt, yt, rsc, A.gpost[:], ALU.mult, ALU.mult),
            reads=[b_yt, A.b_rs, A.b_gpost], writes=[b_yt])
        fw.op("pool", lambda yt=yt, xt=xt: nc.gpsimd.tensor_tensor(yt, yt, xt, ALU.add),
              reads=[b_yt, b_xt], writes=[b_yt])
        fw.dma("pool", dst[tt * 128:(tt + 1) * 128, :], yt, reads=[b_yt, A.b_actT], writes=[b_dst])


SLOPES = [2.0 ** (-(h + 1)) for h in range(NH)]
LAM_INIT = 0.2


def mixer_common_alloc(fw, nc, es):
    M = Ctx()
    M.hmT = fw.sb(es, "m_hmT", [128, KD, S], BF16); M.b_hmT = Buf()
    M.es = es
    M.ss = fw.sb(es, "m_ss", [128, 64], F32); M.b_ss = Buf()
    M.rs = fw.sb(es, "m_rs", [128, 64], F32); M.b_rs = Buf()
    M.gT = fw.sb(es, "m_gT", [128, KD], F32); M.b_gT = Buf()
    M.cnt = 0
    return M


def build_hmT(fw, nc, C, M, x1_src, b_x1, g_dram):
    with ExitStack() as es:
        xt = [fw.sb(es, f"h_xt{i}", [128, D], F32) for i in range(2)]
        b_xt = [Buf() for _ in range(2)]
        hb = [fw.sb(es, f"h_hb{i}", [128, D], BF16) for i in range(2)]
        b_hb = [Buf() for _ in range(2)]
        load_gT(fw, nc, M.gT, M.b_gT, g_dram)
        for tt in range(S // 128):
            s = tt % 2
            fw.dma("sp", xt[s][:], x1_src[tt * 128:(tt + 1) * 128, :], reads=[b_x1], writes=[b_xt[s]])
            col = M.cnt % 64
            M.cnt += 1
            norm_transpose(fw, nc, C, xt[s][:], b_xt[s], hb[s][:], b_hb[s],
                           M.ss[:, col:col + 1], M.b_ss, M.rs[:, col:col + 1], M.b_rs,
                           M.gT, M.b_gT, M.hmT, M.b_hmT, tt * 128)
        fw.barrier()


def attention_alloc(fw, nc, es):
    A = Ctx()
    A.QT = [fw.sb(es, f"a_QT{c}", [128, NH, TOWN], BF16) for c in range(2)]
    A.b_QT = Buf()
    A.KT = fw.sb(es, "a_KT", [128, NH, S], BF16); A.b_KT = Buf()
    A.V = fw.sb(es, "a_V", [128, S // 128, 1024], BF16); A.b_V = Buf()
    for c in range(2):
        fw.op("pool", lambda c=c: nc.gpsimd.memset(A.QT[c][:], 0.0), writes=[A.b_QT])
    return A


def attention_proj(fw, nc, C, M, A, dram, u_d, b_ud, only_u=False):
    w_in = dram["w_in"]
    if A is not None:
        QT, KT, V, b_QT, b_KT, b_V = A.QT, A.KT, A.V, A.b_QT, A.b_KT, A.b_V
    with ExitStack() as es:
        wp = [fw.sb(es, f"a_wp{i}", [128, KD, 256], BF16) for i in range(2)]
        b_wp = [Buf() for _ in range(2)]
        wv = w_in.rearrange("(k p) m -> p k m", p=128)
        iw = 0
        ib = 0
        ust = [fw.sb(es, f"a_ust{i}", [128, 16, 8, 16], BF16) for i in range(2)]
        b_ust = [Buf() for _ in range(2)]
        iu = 0
        for blk in range(4):
            s = iw % 2
            iw += 1
            fw.dma("pool", wp[s][:], wv[:, :, 3072 + blk * 256:3072 + (blk + 1) * 256], writes=[b_wp[s]])
            for hh in range(2):
                us = iu % 2
                iu += 1
                for j in range(8):
                    bi = ib % 8
                    ib += 1
                    t0 = 1024 * hh + j
                    for k in range(KD):
                        fw.op("pe", lambda k=k, s=s, t0=t0, bi=bi: nc.tensor.matmul(
                            C.bank[bi][:, 0:256], M.hmT[:, k, t0:t0 + 1017:8], wp[s][:, k, :],
                            start=(k == 0), stop=(k == KD - 1)),
                            reads=[b_wp[s], M.b_hmT], writes=[C.b_bank[bi]], signal=(k == KD - 1))
                    src = C.bank[bi][:, 0:256].rearrange("c (g q) -> c g q", g=16)
                    if j % 2 == 0:
                        fw.op("act", lambda us=us, j=j, src=src: nc.scalar.copy(ust[us][:, :, j, :], src),
                              reads=[C.b_bank[bi]], writes=[b_ust[us]])
                    else:
                        fw.op("dve", lambda us=us, j=j, src=src: nc.vector.tensor_copy(ust[us][:, :, j, :], src),
                              reads=[C.b_bank[bi]], writes=[b_ust[us]])
                fw.dma("sp", u_d[:, hh, blk * 16:(blk + 1) * 16, :],
                       ust[us][:].rearrange("c g j q -> c g (j q)"), reads=[b_ust[us]], writes=[b_ud])
        if only_u:
            fw.barrier()
            return
        for blk in range(8):
            s = iw % 2
            iw += 1
            fw.dma("pool", wp[s][:], wv[:, :, blk * 256:(blk + 1) * 256], writes=[b_wp[s]])
            isq = blk < 4
            nch = 2 if isq else 4
            for h4 in range(2):
                h = (blk % 4) * 2 + h4
                for ch in range(nch):
                    bi = ib % 8
                    ib += 1
                    for k in range(KD):
                        fw.op("pe", lambda k=k, s=s, h4=h4, ch=ch, bi=bi: nc.tensor.matmul(
                            C.bank[bi][:], wp[s][:, k, h4 * 128:(h4 + 1) * 128], M.hmT[:, k, ch * 512:(ch + 1) * 512],
                            start=(k == 0), stop=(k == KD - 1)),
                            reads=[b_wp[s], M.b_hmT], writes=[C.b_bank[bi]], signal=(k == KD - 1))
                    if isq:
                        for c in range(2):
                            fw.op("act", lambda h=h, ch=ch, bi=bi, c=c: nc.scalar.mul(
                                QT[c][64 * c:64 * c + 64, h, ch * 512:(ch + 1) * 512],
                                C.bank[bi][64 * c:64 * c + 64, :], 0.125),
                                reads=[C.b_bank[bi]], writes=[b_QT])
                    else:
                        fw.op("dve", lambda h=h, ch=ch, bi=bi: nc.vector.tensor_copy(
                            KT[:, h, ch * 512:(ch + 1) * 512], C.bank[bi][:]),
                            reads=[C.b_bank[bi]], writes=[b_KT])
        for blk in range(4):
            s = iw % 2
            iw += 1
            fw.dma("pool", wp[s][:], wv[:, :, 2048 + blk * 256:2048 + (blk + 1) * 256], writes=[b_wp[s]])
            for tt in range(S // 128):
                bi = ib % 8
                ib += 1
                for k in range(KD):
                    fw.op("pe", lambda k=k, s=s, tt=tt, bi=bi: nc.tensor.matmul(
                        C.bank[bi][:, 0:256], M.hmT[:, k, tt * 128:(tt + 1) * 128], wp[s][:, k, :],
                        start=(k == 0), stop=(k == KD - 1)),
                        reads=[b_wp[s], M.b_hmT], writes=[C.b_bank[bi]], signal=(k == KD - 1))
                if tt % 2 == 0:
                    fw.op("act", lambda tt=tt, blk=blk, bi=bi: nc.scalar.copy(
                        V[:, tt, blk * 256:(blk + 1) * 256], C.bank[bi][:, 0:256]),
                        reads=[C.b_bank[bi]], writes=[b_V])
                else:
                    fw.op("dve", lambda tt=tt, blk=blk, bi=bi: nc.vector.tensor_copy(
                        V[:, tt, blk * 256:(blk + 1) * 256], C.bank[bi][:, 0:256]),
                        reads=[C.b_bank[bi]], writes=[b_V])
        fw.barrier()


def attention_core(fw, nc, C, M, A, dram):
    QT, KT, V, b_QT, b_KT, b_V = A.QT, A.KT, A.V, A.b_QT, A.b_KT, A.b_V
    with ExitStack() as es:
        G = fw.sb(es, "a_G", [128, 3072], F32); b_G = Buf()
        fw.dma("sp", G[:], dram["c_alibi"][:, :], writes=[b_G])
        GD = [fw.sb(es, f"a_GD{i}", [128, 3072], BF16) for i in range(2)]
        b_GD = [Buf(), Buf()]
        lq = fw.sb(es, "a_lq", [128, 4, 64], F32); b_lq = Buf()
        for i, nm in enumerate(("lam_q1", "lam_k1", "lam_q2", "lam_k2")):
            fw.dma("sp", lq[:, i, :], dram[nm][0:1, :].partition_broadcast(128), writes=[b_lq])
        sc4 = fw.sb(es, "a_sc4", [128, 8], F32); b_sc4 = Buf()
        junk = fw.sb(es, "a_junk", [128, 64], F32); b_junk = Buf()
        fw.op("dve", lambda: nc.vector.scalar_tensor_tensor(junk[:], lq[:, 0, :], 1.0, lq[:, 1, :], ALU.mult, ALU.mult,
                                                            accum_out=sc4[:, 0:1]), reads=[b_lq], writes=[b_junk, b_sc4])
        fw.op("dve", lambda: nc.vector.scalar_tensor_tensor(junk[:], lq[:, 2, :], 1.0, lq[:, 3, :], ALU.mult, ALU.mult,
                                                            accum_out=sc4[:, 1:2]), reads=[b_lq], writes=[b_junk, b_sc4])
        fw.op("act", lambda: nc.scalar.activation(sc4[:, 2:4], sc4[:, 0:2], AF.Exp), reads=[b_sc4], writes=[b_sc4])
        fw.op("dve", lambda: nc.vector.tensor_tensor(sc4[:, 4:5], sc4[:, 3:4], sc4[:, 2:3], ALU.subtract),
              reads=[b_sc4], writes=[b_sc4])
        fw.op("dve", lambda: nc.vector.tensor_scalar(sc4[:, 5:6], sc4[:, 4:5], -LAM_INIT, None, ALU.add),
              reads=[b_sc4], writes=[b_sc4])
        neg_lam = sc4[:, 5:6]
        gh = fw.sb(es, "a_gh", [128, 2], F32); b_gh = Buf()
        with nc.allow_non_contiguous_dma("tiny"):
            fw.dma("sp", gh[:, 0:1], dram["attn_head_g"][0, :].rearrange("(p o) -> p o", o=1), writes=[b_gh])
        fw.op("dve", lambda: nc.vector.tensor_scalar(gh[:, 1:2], gh[:, 0:1], 1.0 - LAM_INIT, None, ALU.mult),
              reads=[b_gh], writes=[b_gh])
        scb = [fw.sb(es, f"a_scb{i}", [128, 512], BF16) for i in range(4)]
        b_scb = [Buf() for _ in range(4)]
        pT = [fw.sb(es, f"a_pT{i}", [128, 512], BF16) for i in range(6)]
        b_pT = [Buf() for _ in range(6)]
        rz = [fw.sb(es, f"a_rz{i}", [128, 512], F32) for i in range(2)]
        b_rz = [Buf() for _ in range(2)]
        ot = [fw.sb(es, f"a_ot{i}", [128, 512], F32) for i in range(2)]
        b_ot = [Buf() for _ in range(2)]
        an = fw.sb(es, "a_an", [128, 2 * NH, 512], F32); b_an = Buf()
        sqas = [fw.sb(es, f"a_sqa{i}", [128, 512], BF16) for i in range(2)]; b_sqas = [Buf(), Buf()]
        rns = [fw.sb(es, f"a_rn{i}", [128, 512], F32) for i in range(2)]; b_rns = [Buf(), Buf()]
        iters = [(h, qc, kb, c) for h in range(NH) for qc in range(2) for kb in range(S // 128) for c in range(2)]
        SB = [0, 1, 7, 6]
        DEPTH = 3
        NKB = S // 128

        def stage1(i):
            h, qc, kb, c = iters[i]
            off = 512 * qc - 128 * kb + 1920
            sb_i = SB[i % 4]
            sbf = i % 4
            pi = i % 6
            pbank = C.bank[sb_i]
            fw.op("pe", lambda: nc.tensor.matmul(
                pbank[:], KT[:, h, kb * 128:(kb + 1) * 128],
                QT[c][:, h, qc * 512:(qc + 1) * 512], start=True, stop=True),
                reads=[b_KT, b_QT], writes=[C.b_bank[sb_i]])
            if (h == 0 and qc == 0 and kb == 0 and c == 0) or (qc == 1 and kb == 0 and c == 0 and h + 1 < NH):
                hn = 0 if (h == 0 and qc == 0) else h + 1
                fw.op("act", lambda hn=hn: nc.scalar.activation(GD[hn % 2][:], G[:], AF.Exp, scale=-SLOPES[hn]),
                      reads=[b_G], writes=[b_GD[hn % 2]])
            fw.op("act", lambda: nc.scalar.activation(scb[sbf][:], pbank[:], AF.Exp),
                  reads=[C.b_bank[sb_i]], writes=[b_scb[sbf]])
            fw.op("dve", lambda: nc.vector.tensor_tensor(pT[pi][:], scb[sbf][:], GD[h % 2][:, off:off + 512], ALU.mult),
                  reads=[b_scb[sbf], b_GD[h % 2]], writes=[b_pT[pi]])

        def stage2(i):
            h, qc, kb, c = iters[i]
            pi = i % 6
            fw.op("pe", lambda: nc.tensor.matmul(
                C.bank[2 + c][:], V[:, kb, h * 128:(h + 1) * 128], pT[pi][:],
                start=(kb == 0), stop=(kb == NKB - 1)),
                reads=[b_V, b_pT[pi]], writes=[C.b_bank[2 + c]], signal=False)
            fw.op("pe", lambda: nc.tensor.matmul(
                C.bank[4 + c][:], C.ones[:], pT[pi][:],
                start=(kb == 0), stop=(kb == NKB - 1)),
                reads=[C.b_ones, b_pT[pi], b_V], writes=[C.b_bank[2 + c], C.b_bank[4 + c]])
            if kb == NKB - 1 and c == 1:
                for cc in range(2):
                    fw.op("dve", lambda cc=cc: nc.vector.reciprocal(rz[cc][:], C.bank[4 + cc][:]),
                          reads=[C.b_bank[4 + cc]], writes=[b_rz[cc]])
                    fw.op("dve", lambda cc=cc: nc.vector.tensor_tensor(ot[cc][:], C.bank[2 + cc][:], rz[cc][:], ALU.mult),
                          reads=[C.b_bank[2 + cc], b_rz[cc]], writes=[b_ot[cc]])
                u = h * 2 + qc
                fw.op("dve", lambda u=u: nc.vector.scalar_tensor_tensor(an[:, u, :], ot[1][:], neg_lam, ot[0][:], ALU.mult, ALU.add),
                      reads=[b_ot[0], b_ot[1], b_sc4], writes=[b_an])

        def head_norm(u):
            h, qc = u // 2, u % 2
            sq_, bsq_ = sqas[u % 2], b_sqas[u % 2]
            r_, br_ = rns[u % 2], b_rns[u % 2]
            bk = 2 + (u % 4)
            fw.op("act", lambda: nc.scalar.activation(sq_[:], an[:, u, :], AF.Square), reads=[b_an], writes=[bsq_])
            fw.op("pe", lambda: nc.tensor.matmul(C.bank[bk][:], C.ones[:], sq_[:], start=True, stop=True),
                  reads=[C.b_ones, bsq_], writes=[C.b_bank[bk]])
            fw.op("act", lambda: nc.scalar.activation(r_[:], C.bank[bk][:], AF.Sqrt, bias=EPS, scale=1.0 / 128),
                  reads=[C.b_bank[bk]], writes=[br_])
            fw.op("dve", lambda: nc.vector.reciprocal(r_[:], r_[:]), reads=[br_], writes=[br_])
            fw.op("dve", lambda: nc.vector.scalar_tensor_tensor(
                M.aT[:, h, qc * 512:(qc + 1) * 512], an[:, u, :], gh[:, 1:2], r_[:], ALU.mult, ALU.mult),
                reads=[b_an, b_gh, br_], writes=[M.b_aT])

        NI = len(iters)
        for i in range(NI + DEPTH):
            if i < NI:
                stage1(i)
            if i >= DEPTH:
                stage2(i - DEPTH)
        for u in range(2 * NH):
            head_norm(u)
        fw.barrier()


TWO_PI = 6.283185307179586
PI_C = 3.1415925


def ap4(t, off, pstride, nparts, dims):
    return bass.AP(t, off, [[pstride, nparts]] + [[st, n] for st, n in dims])


def s5_phase(fw, nc, C, dram, Yown, b_Y, u_d, b_ud):
    with ExitStack() as es:
        U2 = fw.sb(es, "s_U2", [128, 2, 64, 128], BF16); b_U2 = Buf()
        for hh in range(2):
            fw.dma("sp", U2[:, hh, :, :], u_d[:, hh, :, :], reads=[b_ud], writes=[b_U2])
        NF = 33
        PWR = fw.sb(es, "s_PWR", [128, NF, 64], F32); b_PW = Buf()
        PWI = fw.sb(es, "s_PWI", [128, NF, 64], F32)
        NAI = fw.sb(es, "s_NAI", [128, 8, 64], F32)
        BBR = fw.sb(es, "s_BBR", [128, 64, 16], F32); b_BB = Buf()
        BBI = fw.sb(es, "s_BBI", [128, 64, 16], F32)
        CTR = fw.sb(es, "s_CTR", [128, 64, 16], F32); b_CT = Buf()
        CTI = fw.sb(es, "s_CTI", [128, 64, 16], F32)
        MF = fw.sb(es, "s_MF", [128, 128], F32); b_MK = Buf()
        MB = fw.sb(es, "s_MB", [128, 128], F32)
        fw.dma("sp", MF[:], dram["c_maskF"][:, :], writes=[b_MK])
        fw.dma("sp", MB[:], dram["c_maskB"][:, :], writes=[b_MK])
        dB = fw.sb(es, "s_dB", [128, 1024], F32); b_dB = Buf()
        fw.dma("sp", dB[:], dram["ssm_d"][0:1, :].partition_broadcast(128), writes=[b_dB])
        dve = nc.vector
        with ExitStack() as pes:
            LL = fw.sb(pes, "s_LL", [64, 2, 128], F32); b_LL = Buf()
            for i, nm in enumerate(("ssm_lam_re", "ssm_lam_im")):
                fw.dma("sp", LL[:, i, :].rearrange("g (d n) -> g d n", d=2), dram[nm].rearrange("d g n -> g d n"),
                       writes=[b_LL])
            LRI = fw.sb(pes, "s_LRI", [128, 2, 64], F32); b_LRI = Buf()
            for i in range(2):
                fw.op("pe", lambda i=i: nc.tensor.transpose(C.bank[0][:, i * 64:(i + 1) * 64], LL[:, i, :],
                                                            C.identf[0:64, 0:64]),
                      reads=[b_LL, C.b_identf], writes=[C.b_bank[0]])
            fw.op("dve", lambda: dve.tensor_copy(LRI[:].rearrange("p a g -> p (a g)"), C.bank[0][:, 0:128]),
                  reads=[C.b_bank[0]], writes=[b_LRI])
            LR = LRI[:, 0, :]
            LI = LRI[:, 1, :]
            DT = fw.sb(pes, "s_DT", [128, 64], F32); b_DT = Buf()
            for d in range(2):
                fw.dma("sp", DT[64 * d:64 * d + 64, :], dram["ssm_log_dt"][d:d + 1, :].partition_broadcast(64),
                       writes=[b_DT])
            fw.op("act", lambda: nc.scalar.activation(DT[:], DT[:], AF.Exp), reads=[b_DT], writes=[b_DT])
            LD = fw.sb(pes, "s_LD", [128, 2, 64], F32); b_LD = Buf()
            for i in range(2):
                fw.op("dve", lambda i=i: dve.tensor_tensor(LD[:, i, :], LRI[:, i, :], DT[:], ALU.mult),
                      reads=[b_LRI, b_DT], writes=[b_LD])
            EXPS = fw.sb(pes, "s_EXPS", [128, NF], F32); b_EX = Buf()
            fw.dma("sp", EXPS[:], dram["c_exps"][:, :], writes=[b_EX])
            ANG = fw.sb(pes, "s_ANG", [128, NF, 64], F32); b_ANG = Buf()
            MAG = fw.sb(pes, "s_MAG", [128, NF, 64], F32); b_MAG = Buf()
            KF = fw.sb(pes, "s_KF", [128, NF, 64], F32); b_KF = Buf()
            KI = fw.sb(pes, "s_KI", [128, NF, 64], I32); b_KI = Buf()
            ex_b = ap4(EXPS, 0, NF, 128, [(1, NF), (0, 64)])
            lid_b = ap4(LD, 64, 128, 128, [(0, NF), (1, 64)])
            lrd_b = ap4(LD, 0, 128, 128, [(0, NF), (1, 64)])
            fw.op("dve", lambda: dve.tensor_tensor(ANG[:], lid_b, ex_b, ALU.mult), reads=[b_LD, b_EX], writes=[b_ANG])
            fw.op("dve", lambda: dve.tensor_tensor(MAG[:], lrd_b, ex_b, ALU.mult), reads=[b_LD, b_EX], writes=[b_MAG])
            fw.op("act", lambda: nc.scalar.activation(MAG[:], MAG[:], AF.Exp), reads=[b_MAG], writes=[b_MAG])

            def reduce_clamp(A, b_A):
                fw.op("dve", lambda: dve.tensor_scalar(KF[:], A[:], 1.0 / TWO_PI, None, ALU.mult),
                      reads=[b_A], writes=[b_KF])
                fw.op("dve", lambda: dve.tensor_copy(KI[:], KF[:]), reads=[b_KF], writes=[b_KI])
                fw.op("dve", lambda: dve.tensor_copy(KF[:], KI[:]), reads=[b_KI], writes=[b_KF])
                fw.op("dve", lambda: dve.scalar_tensor_tensor(A[:], KF[:], -TWO_PI, A[:], ALU.mult, ALU.add),
                      reads=[b_KF, b_A], writes=[b_A])
                fw.op("dve", lambda: dve.tensor_scalar(A[:], A[:], PI_C, -PI_C, ALU.min, ALU.max),
                      reads=[b_A], writes=[b_A])

            reduce_clamp(ANG, b_ANG)
            fw.op("act", lambda: nc.scalar.activation(PWI[:], ANG[:], AF.Sin), reads=[b_ANG], writes=[b_PW])
            fw.op("dve", lambda: dve.tensor_scalar(ANG[:], ANG[:], 1.5707963267948966, None, ALU.add),
                  reads=[b_ANG], writes=[b_ANG])
            reduce_clamp(ANG, b_ANG)
            fw.op("act", lambda: nc.scalar.activation(PWR[:], ANG[:], AF.Sin), reads=[b_ANG], writes=[b_PW])
            fw.op("dve", lambda: dve.tensor_tensor(PWR[:], PWR[:], MAG[:], ALU.mult), reads=[b_PW, b_MAG], writes=[b_PW])
            fw.op("dve", lambda: dve.tensor_tensor(PWI[:], PWI[:], MAG[:], ALU.mult), reads=[b_PW, b_MAG], writes=[b_PW])
            fw.op("dve", lambda: dve.tensor_scalar(NAI[:], PWI[:, 24:32, :], -1.0, None, ALU.mult),
                  reads=[b_PW], writes=[b_PW])
            FT = fw.sb(pes, "s_FT", [128, 8, 64], F32); b_FT = Buf()
            a_re = PWR[:, 32, :]
            a_im = PWI[:, 32, :]
            den, t2, nr, fre, fim, tt_ = (FT[:, i, :] for i in range(6))
            ops = [
                (den, LR, LR, ALU.mult), (t2, LI, LI, ALU.mult), (den, den, t2, ALU.add),
            ]
            for o, a, b, op_ in ops:
                fw.op("dve", lambda o=o, a=a, b=b, op_=op_: dve.tensor_tensor(o, a, b, op_),
                      reads=[b_LRI, b_FT, b_PW], writes=[b_FT])
            fw.op("dve", lambda: dve.reciprocal(den, den), reads=[b_FT], writes=[b_FT])
            fw.op("dve", lambda: dve.tensor_scalar(nr, a_re, -1.0, None, ALU.add), reads=[b_PW], writes=[b_FT])
            ops = [
                (fre, nr, LR, ALU.mult), (tt_, a_im, LI, ALU.mult), (fre, fre, tt_, ALU.add), (fre, fre, den, ALU.mult),
                (fim, a_im, LR, ALU.mult), (tt_, nr, LI, ALU.mult), (fim, fim, tt_, ALU.subtract),
                (fim, fim, den, ALU.mult),
            ]
            for o, a, b, op_ in ops:
                fw.op("dve", lambda o=o, a=a, b=b, op_=op_: dve.tensor_tensor(o, a, b, op_),
                      reads=[b_LRI, b_FT, b_PW], writes=[b_FT])
            BR = fw.sb(pes, "s_BR", [128, 64, 16], F32); b_BRI = Buf()
            BI = fw.sb(pes, "s_BI", [128, 64, 16], F32)
            TB_ = fw.sb(pes, "s_TB", [128, 64, 16], F32); b_TB = Buf()
            for d in range(2):
                fw.dma("sp", BR[64 * d:64 * d + 64, :, :], dram["ssm_b_re"][d].rearrange("g n q -> n g q"), writes=[b_BRI])
                fw.dma("sp", BI[64 * d:64 * d + 64, :, :], dram["ssm_b_im"][d].rearrange("g n q -> n g q"), writes=[b_BRI])
            fre_b = ap4(FT, 3 * 64, 8 * 64, 128, [(1, 64), (0, 16)])
            fim_b = ap4(FT, 4 * 64, 8 * 64, 128, [(1, 64), (0, 16)])
            seq = [
                (BBR[:], fre_b, BR[:], ALU.mult), (TB_[:], fim_b, BI[:], ALU.mult), (BBR[:], BBR[:], TB_[:], ALU.subtract),
                (BBI[:], fre_b, BI[:], ALU.mult), (TB_[:], fim_b, BR[:], ALU.mult), (BBI[:], BBI[:], TB_[:], ALU.add),
            ]
            for o, a, b, op_ in seq:
                fw.op("dve", lambda o=o, a=a, b=b, op_=op_: dve.tensor_tensor(o, a, b, op_),
                      reads=[b_FT, b_BRI, b_TB, b_BB], writes=[b_TB, b_BB])
            Cin = fw.sb(pes, "s_Cin", [128, 2, 8, 128], F32); b_Cin = Buf()
            for i, nm in enumerate(("ssm_c_re", "ssm_c_im")):
                for d in range(2):
                    fw.dma("sp", Cin[:, i, :, 64 * d:64 * d + 64],
                           dram[nm][d].rearrange("(gb g8) p n -> (g8 p) gb n", g8=8), writes=[b_Cin])
            for i, CT in enumerate((CTR, CTI)):
                for half in range(2):
                    bk = C.bank[half]
                    for q4 in range(4):
                        gb_ = half * 4 + q4
                        fw.op("pe", lambda i=i, gb_=gb_, q4=q4, bk=bk: nc.tensor.transpose(
                            bk[:, q4 * 128:(q4 + 1) * 128], Cin[:, i, gb_, :], C.identf[:]),
                            reads=[b_Cin, C.b_identf], writes=[C.b_bank[half]], signal=(q4 == 3))
                    fw.op("dve", lambda CT=CT, half=half, bk=bk: dve.tensor_copy(
                        CT[:].rearrange("p g q -> p (g q)")[:, half * 512:(half + 1) * 512], bk[:]),
                        reads=[C.b_bank[half]], writes=[b_CT])
            fw.barrier()
        GB = 16
        CO_re = fw.sb(es, "s_COre", [128, GB, 128], BF16); b_CO = Buf()
        CO_imn = fw.sb(es, "s_COim", [128, GB, 128], BF16)
        WX_re = fw.sb(es, "s_WXre", [128, GB, 128], BF16); b_WX = Buf()
        WX_im = fw.sb(es, "s_WXim", [128, GB, 128], BF16)
        W2_re = fw.sb(es, "s_W2re", [128, GB, 128], BF16); b_W2 = Buf()
        W2_im = fw.sb(es, "s_W2im", [128, GB, 128], BF16)
        WXT_re = fw.sb(es, "s_WXTre", [128, GB, 128], BF16); b_WXT = Buf()
        WXT_im = fw.sb(es, "s_WXTim", [128, GB, 128], BF16)
        TT = fw.sb(es, "s_TT", [128, GB, 128], BF16); b_TT = Buf()
        T1 = fw.sb(es, "s_T1", [128, GB * 128], F32); b_T1 = Buf()
        T2 = fw.sb(es, "s_T2", [128, GB * 128], F32); b_T2 = Buf()
        U8b = fw.sb(es, "s_U8b", [128, GB, 256], BF16); b_U8 = Buf()
        NI = 4
        ST = [[fw.sb(es, f"s_ST{a_}{b_}", [128, 2, 256], F32) for b_ in range(2)] for a_ in range(NI)]
        b_ST = [[Buf(), Buf()] for _ in range(NI)]
        SS = [fw.sb(es, f"s_SS{a_}", [128, 2, 256], F32) for a_ in range(NI)]
        b_SS = [Buf() for _ in range(NI)]
        E = [fw.sb(es, f"s_E{a_}", [128, 2, 128], BF16) for a_ in range(NI)]
        b_E = [Buf() for _ in range(NI)]
        for a_ in range(NI):
            for b_ in range(2):
                fw.op("dve", lambda a_=a_, b_=b_: dve.memset(ST[a_][b_][:], 0.0), writes=[b_ST[a_][b_]])
            fw.op("dve", lambda a_=a_: dve.memset(E[a_][:], 0.0), writes=[b_E[a_]])
            fw.op("dve", lambda a_=a_: dve.memset(SS[a_][:], 0.0), writes=[b_SS[a_]])
        ig = 0
        for gb in range(NG // GB):
            g0 = gb * GB

            def expand(fam, XR, XI, b_X, o_re, o_im, b_o, neg_im):
                pwr = ap4(PWR, fam * 8 * 64 + g0, NF * 64, 128, [(1, GB), (64, 8), (0, 16)])
                pwi = ap4(PWI, fam * 8 * 64 + g0, NF * 64, 128, [(1, GB), (64, 8), (0, 16)])
                xr = ap4(XR, g0 * 16, 1024, 128, [(16, GB), (0, 8), (1, 16)])
                xi = ap4(XI, g0 * 16, 1024, 128, [(16, GB), (0, 8), (1, 16)])
                t1 = T1[:].rearrange("p (g j q) -> p g j q", g=GB, j=8)
                t2 = T2[:].rearrange("p (g j q) -> p g j q", g=GB, j=8)
                ore = o_re[:].rearrange("p g (j q) -> p g j q", j=8)
                oim = o_im[:].rearrange("p g (j q) -> p g j q", j=8)
                fw.op("dve", lambda: dve.tensor_tensor(t1, pwr, xr, ALU.mult), reads=[b_PW, b_X], writes=[b_T1])
                fw.op("pool", lambda: nc.gpsimd.tensor_tensor(t2, pwi, xi, ALU.mult), reads=[b_PW, b_X], writes=[b_T2])
                fw.op("dve", lambda: dve.tensor_tensor(ore, t1, t2, ALU.subtract), reads=[b_T1, b_T2], writes=[b_o])
                fw.op("dve", lambda: dve.tensor_tensor(t1, pwr, xi, ALU.mult), reads=[b_PW, b_X], writes=[b_T1])
                fw.op("pool", lambda: nc.gpsimd.tensor_tensor(t2, pwi, xr, ALU.mult), reads=[b_PW, b_X], writes=[b_T2])
                if neg_im:
                    fw.op("dve", lambda: dve.scalar_tensor_tensor(oim, t1, -1.0, t2, ALU.mult, ALU.subtract),
                          reads=[b_T1, b_T2], writes=[b_o])
                else:
                    fw.op("dve", lambda: dve.tensor_tensor(oim, t1, t2, ALU.add), reads=[b_T1, b_T2], writes=[b_o])

            expand(0, CTR, CTI, b_CT, CO_re, CO_imn, b_CO, True)
            expand(1, BBR, BBI, b_BB, WX_re, WX_im, b_WX, False)
            expand(2, BBR, BBI, b_BB, W2_re, W2_im, b_W2, False)
            for src, dstT in ((WX_re, WXT_re), (WX_im, WXT_im)):
                for half in range(2):
                    bk = C.bank[half]
                    pst = bk[:].bitcast(BF16)
                    for q8 in range(8):
                        gl = half * 8 + q8
                        fw.op("pe", lambda src=src, gl=gl, q8=q8, pst=pst: nc.tensor.transpose(
                            pst[:, q8 * 128:(q8 + 1) * 128], src[:, gl, :], C.ident[:]),
                            reads=[b_WX, C.b_ident], writes=[C.b_bank[half]], signal=(q8 == 7))
                    fw.op("act", lambda dstT=dstT, half=half, pst=pst: nc.scalar.copy(
                        dstT[:, half * 8:(half + 1) * 8, :].rearrange("p g m -> p (g m)"), pst),
                        reads=[C.b_bank[half]], writes=[b_WXT])
            for q in range(GB // 4):
                for g4 in range(4):
                    gl = q * 4 + g4
                    for (lo, bki) in ((0, 2), (64, 3)):
                        fw.op("pe", lambda gl=gl, g4=g4, lo=lo, bki=bki: nc.tensor.matmul(
                            C.bank[bki][:, g4 * 128:(g4 + 1) * 128], W2_re[lo:lo + 64, gl, :], CO_re[lo:lo + 64, gl, :],
                            start=True, stop=False),
                            reads=[b_W2, b_CO], writes=[C.b_bank[bki]], signal=False)
                        fw.op("pe", lambda gl=gl, g4=g4, lo=lo, bki=bki: nc.tensor.matmul(
                            C.bank[bki][:, g4 * 128:(g4 + 1) * 128], W2_im[lo:lo + 64, gl, :], CO_imn[lo:lo + 64, gl, :],
                            start=False, stop=True),
                            reads=[b_W2, b_CO], writes=[C.b_bank[bki]], signal=(g4 == 3))
                mf = ap4(MF, 0, 128, 128, [(0, 4), (1, 128)])
                mb = ap4(MB, 0, 128, 128, [(0, 4), (1, 128)])
                t1v = T1[:, 0:512].rearrange("p (g m) -> p g m", g=4)
                t2v = T2[:, 0:512].rearrange("p (g m) -> p g m", g=4)
                fw.op("dve", lambda t1v=t1v, mf=mf: dve.tensor_tensor(
                    t1v, C.bank[2][:].rearrange("p (g m) -> p g m", g=4), mf, ALU.mult),
                    reads=[C.b_bank[2], b_MK], writes=[b_T1])
                fw.op("dve", lambda t2v=t2v, mb=mb: dve.tensor_tensor(
                    t2v, C.bank[3][:].rearrange("p (g m) -> p g m", g=4), mb, ALU.mult),
                    reads=[C.b_bank[3], b_MK], writes=[b_T2])
                fw.op("dve", lambda q=q, t1v=t1v, t2v=t2v: dve.tensor_tensor(
                    TT[:, q * 4:(q + 1) * 4, :], t1v, t2v, ALU.add), reads=[b_T1, b_T2], writes=[b_TT])
            for hh in range(2):
                for half in range(2):
                    bk = C.bank[half]
                    pst = bk[:].bitcast(BF16)
                    for q8 in range(8):
                        g = g0 + half * 8 + q8
                        fw.op("pe", lambda hh=hh, g=g, q8=q8, pst=pst: nc.tensor.transpose(
                            pst[:, q8 * 128:(q8 + 1) * 128], U2[:, hh, g, :], C.ident[:]),
                            reads=[b_U2, C.b_ident], writes=[C.b_bank[half]], signal=(q8 == 7))
                    fw.op("act", lambda hh=hh, half=half, pst=pst: nc.scalar.copy(
                        U8b[:, half * 8:(half + 1) * 8, hh * 128:(hh + 1) * 128],
                        pst.rearrange("p (g c) -> p g c", g=8)),
                        reads=[C.b_bank[half]], writes=[b_U8])
            for gp in range(GB // NI):
                info = []
                for st in range(NI):
                    gl = gp * NI + st
                    g = g0 + gl
                    xb = 2 + st
                    yb = 6 + ((ig // 4) % 2)
                    g4 = ig % 4
                    ig += 1
                    info.append((gl, g, xb, yb, g4))
                    fw.op("pe", lambda gl=gl, xb=xb: nc.tensor.matmul(
                        C.bank[xb][:, 0:256], WXT_re[:, gl, :], U8b[:, gl, :], start=True, stop=True),
                        reads=[b_WXT, b_U8], writes=[C.b_bank[xb]], signal=False)
                    fw.op("pe", lambda gl=gl, xb=xb: nc.tensor.matmul(
                        C.bank[xb][:, 256:512], WXT_im[:, gl, :], U8b[:, gl, :], start=True, stop=True),
                        reads=[b_WXT, b_U8], writes=[C.b_bank[xb]])
                    bx = C.bank[xb]
                    S0 = ST[st][0]
                    fw.op("act", lambda bx=bx, S0=S0: nc.scalar.copy(
                        S0[0:64, :, :], bx[0:64, :].rearrange("p (a c) -> p a c", a=2)),
                        reads=[C.b_bank[xb]], writes=[b_ST[st][0]])
                    fw.op("act", lambda bx=bx, S0=S0: nc.scalar.copy(S0[64:128, 0, :], bx[64:128, 255::-1]),
                          reads=[C.b_bank[xb]], writes=[b_ST[st][0]])
                    fw.op("act", lambda bx=bx, S0=S0: nc.scalar.copy(S0[64:128, 1, :], bx[64:128, 511:255:-1]),
                          reads=[C.b_bank[xb]], writes=[b_ST[st][0]])
                cur = 0
                for k in range(8):
                    sft = 1 << k
                    nxt = 1 - cur
                    n = 256 - sft
                    for st in range(NI):
                        gl, g, xb, yb, g4 = info[st]
                        Sc, Sn, Sx = ST[st][cur], ST[st][nxt], SS[st]
                        ar = PWR[:, 24 + k, g:g + 1]
                        ai = PWI[:, 24 + k, g:g + 1]
                        nai = NAI[:, k, g:g + 1]
                        fw.op("dve", lambda Sc=Sc, Sx=Sx, ar=ar, sft=sft, n=n: dve.scalar_tensor_tensor(
                            Sx[:, :, sft:256], Sc[:, :, 0:n], ar, Sc[:, :, sft:256], ALU.mult, ALU.add),
                            reads=[b_ST[st][cur], b_PW], writes=[b_SS[st]])
                        fw.op("dve", lambda Sc=Sc, Sn=Sn, Sx=Sx, nai=nai, sft=sft, n=n: dve.scalar_tensor_tensor(
                            Sn[:, 0, sft:256], Sc[:, 1, 0:n], nai, Sx[:, 0, sft:256], ALU.mult, ALU.add),
                            reads=[b_ST[st][cur], b_SS[st], b_PW], writes=[b_ST[st][nxt]])
                        fw.op("dve", lambda Sc=Sc, Sn=Sn, Sx=Sx, ai=ai, sft=sft, n=n: dve.scalar_tensor_tensor(
                            Sn[:, 1, sft:256], Sc[:, 0, 0:n], ai, Sx[:, 1, sft:256], ALU.mult, ALU.add),
                            reads=[b_ST[st][cur], b_SS[st], b_PW], writes=[b_ST[st][nxt]])
                        fw.op("act", lambda Sc=Sc, Sn=Sn, sft=sft: nc.scalar.copy(Sn[:, :, 0:sft], Sc[:, :, 0:sft]),
                              reads=[b_ST[st][cur]], writes=[b_ST[st][nxt]])
                    cur = nxt
                for st in range(NI):
                    gl, g, xb, yb, g4 = info[st]
                    Sf = ST[st][cur]
                    Es = E[st]
                    fw.op("act", lambda Sf=Sf, Es=Es: nc.scalar.copy(Es[0:64, :, 1:128], Sf[0:64, :, 0:127]),
                          reads=[b_ST[st][cur]], writes=[b_E[st]])
                    fw.op("pool", lambda Sf=Sf, Es=Es: nc.gpsimd.tensor_copy(Es[64:128, 0, :], Sf[64:128, 0, 254:126:-1]),
                          reads=[b_ST[st][cur]], writes=[b_E[st]])
                    fw.op("pool", lambda Sf=Sf, Es=Es: nc.gpsimd.tensor_copy(Es[64:128, 1, :], Sf[64:128, 1, 254:126:-1]),
                          reads=[b_ST[st][cur]], writes=[b_E[st]])
                    yo = C.bank[yb][:, g4 * 128:(g4 + 1) * 128]
                    fw.op("pe", lambda gl=gl, yo=yo: nc.tensor.matmul(yo, U8b[:, gl, 0:128], TT[:, gl, :], start=True, stop=False),
                          reads=[b_U8, b_TT], writes=[C.b_bank[yb]], signal=False)
                    fw.op("pe", lambda gl=gl, yo=yo, Es=Es: nc.tensor.matmul(yo, Es[:, 0, :], CO_re[:, gl, :], start=False, stop=False),
                          reads=[b_E[st], b_CO], writes=[C.b_bank[yb]], signal=False)
                    fw.op("pe", lambda gl=gl, yo=yo, Es=Es: nc.tensor.matmul(yo, Es[:, 1, :], CO_imn[:, gl, :], start=False, stop=True),
                          reads=[b_E[st], b_CO, b_U8, b_TT], writes=[C.b_bank[yb]])
                    if g4 == 3:
                        gbase = g - 3
                        fw.op("dve", lambda yb=yb, gbase=gbase: dve.tensor_copy(
                            Yown[:, :, gbase * 16:(gbase + 4) * 16].rearrange("c j (g p) -> c g j p", g=4),
                            C.bank[yb][:].rearrange("c (g j p) -> c g j p", g=4, j=8)),
                            reads=[C.b_bank[yb]], writes=[b_Y])
        for j in range(8):
            fw.op("pool", lambda j=j: nc.gpsimd.tensor_tensor(
                T1[:, 0:1024].rearrange("c (g p) -> c g p", g=64), dB[:].rearrange("c (g p) -> c g p", g=64),
                U2[:, 0, :, j * 16:(j + 1) * 16], ALU.mult),
                  reads=[b_dB, b_U2], writes=[b_T1])
            fw.op("dve", lambda j=j: dve.tensor_tensor(Yown[:, j, :], Yown[:, j, :], T1[:, 0:1024], ALU.add),
                  reads=[b_T1, b_Y], writes=[b_Y])
        fw.barrier()


GELU_K0 = 0.7978845608028654
GELU_K1 = 0.7978845608028654 * 0.044715


def post_phase(fw, nc, C, dram, Yown, b_Y, aT, b_aT, x1_d, b_x1d, y_d, b_yd, x2_d, b_x2d):
    dve = nc.vector
    with ExitStack() as es:
        sT = fw.sb(es, "p_sT", [128, 8, TOWN], BF16); b_sT = Buf()
        with ExitStack() as es1:
            wglu = fw.sb(es1, "p_wglu", [128, 8, 1024], BF16); b_wglu = Buf()
            wgv = dram["ssm_w_glu"].rearrange("(k p) m -> p k m", p=128)
            for h2 in range(2):
                fw.dma("pool", wglu[:, :, h2 * 512:(h2 + 1) * 512], wgv[:, :, h2 * 512:(h2 + 1) * 512], writes=[b_wglu])
            bglu = fw.sb(es1, "p_bglu", [128, 1024], F32); b_bg = Buf()
            outg = fw.sb(es1, "p_outg", [128, 1024], F32)
            fw.dma("sp", bglu[:], dram["ssm_b_glu"][0:1, :].partition_broadcast(128), writes=[b_bg])
            fw.dma("sp", outg[:], dram["ssm_out_g"][0:1, :].partition_broadcast(128), writes=[b_bg])
            tas = [fw.sb(es1, f"p_ta{i}", [128, 1024], F32) for i in range(2)]; b_tas = [Buf(), Buf()]
            tbs = [fw.sb(es1, f"p_tb{i}", [128, 1024], F32) for i in range(2)]; b_tbs = [Buf(), Buf()]
            g1s = [fw.sb(es1, f"p_g1{i}", [128, 1024], F32) for i in range(2)]; b_g1s = [Buf(), Buf()]
            g1bs = [fw.sb(es1, f"p_g1b{i}", [128, 1024], BF16) for i in range(2)]; b_g1bs = [Buf(), Buf()]
            g1Ts = [fw.sb(es1, f"p_g1T{i}", [128, 8, 128], BF16) for i in range(2)]; b_g1Ts = [Buf(), Buf()]
            ss = fw.sb(es1, "p_ss", [128, 16], F32); b_ss = Buf()
            rs = fw.sb(es1, "p_rs", [128, 16], F32); b_rs = Buf()
            for j in range(8):
                pj = j % 2
                ta, b_ta, tb, b_tb, g1, b_g1 = tas[pj], b_tas[pj], tbs[pj], b_tbs[pj], g1s[pj], b_g1s[pj]
                g1b, b_g1b, g1T, b_g1T = g1bs[pj], b_g1bs[pj], g1Ts[pj], b_g1Ts[pj]
                bk0 = 4 * pj
                y = Yown[:, j, :]
                fw.op("act", lambda y=y, ta=ta: nc.scalar.activation(ta[:], y, AF.Square), reads=[b_Y], writes=[b_ta])
                fw.op("dve", lambda ta=ta: dve.tensor_scalar(ta[:], ta[:], GELU_K1, GELU_K0, ALU.mult, ALU.add),
                      reads=[b_ta], writes=[b_ta])
                fw.op("dve", lambda y=y, ta=ta: dve.tensor_tensor(ta[:], ta[:], y, ALU.mult), reads=[b_ta, b_Y], writes=[b_ta])
                fw.op("act", lambda ta=ta, tb=tb: nc.scalar.activation(tb[:], ta[:], AF.Sigmoid, scale=2.0),
                      reads=[b_ta], writes=[b_tb])
                fw.op("dve", lambda y=y, tb=tb, g1=g1: dve.tensor_tensor(g1[:], y, tb[:], ALU.mult),
                      reads=[b_tb, b_Y], writes=[b_g1])
                fw.op("pool", lambda g1=g1, g1b=g1b: nc.gpsimd.tensor_copy(g1b[:], g1[:]), reads=[b_g1], writes=[b_g1b])
                pst = C.bank[bk0][:].bitcast(BF16)
                for kf in range(8):
                    fw.op("pe", lambda kf=kf, pst=pst, g1b=g1b: nc.tensor.transpose(
                        pst[:, kf * 128:(kf + 1) * 128], g1b[:, kf * 128:(kf + 1) * 128], C.ident[:]),
                        reads=[b_g1b, C.b_ident], writes=[C.b_bank[bk0]], signal=(kf == 7))
                fw.op("act", lambda pst=pst, g1T=g1T: nc.scalar.copy(g1T[:].rearrange("p k c -> p (k c)"), pst),
                      reads=[C.b_bank[bk0]], writes=[b_g1T])
                for n in range(2):
                    bk = bk0 + 2 + n
                    for kf in range(8):
                        fw.op("pe", lambda kf=kf, n=n, bk=bk, g1T=g1T: nc.tensor.matmul(
                            C.bank[bk][:], g1T[:, kf, :], wglu[:, kf, n * 512:(n + 1) * 512],
                            start=(kf == 0), stop=(kf == 7)),
                            reads=[b_g1T, b_wglu], writes=[C.b_bank[bk]], signal=(kf == 7))
                    fw.op("dve", lambda n=n, bk=bk, ta=ta: dve.tensor_tensor(
                        ta[:, n * 512:(n + 1) * 512], C.bank[bk][:], bglu[:, n * 512:(n + 1) * 512], ALU.add),
                        reads=[C.b_bank[bk], b_bg], writes=[b_ta])
                fw.op("act", lambda ta=ta, tb=tb: nc.scalar.activation(tb[:], ta[:], AF.Sigmoid), reads=[b_ta], writes=[b_tb])
                fw.op("dve", lambda g1=g1, tb=tb: dve.tensor_tensor(g1[:], g1[:], tb[:], ALU.mult),
                      reads=[b_tb, b_g1], writes=[b_g1])
                rms_stats(fw, nc, g1[:], b_g1, ta[:], b_ta, ss[:, j:j + 1], b_ss, rs[:, j:j + 1], b_rs, 1024)
                fw.op("dve", lambda j=j, g1=g1, g1b=g1b: dve.scalar_tensor_tensor(
                    g1b[:], g1[:], rs[:, j:j + 1], outg[:], ALU.mult, ALU.mult),
                    reads=[b_g1, b_rs, b_bg], writes=[b_g1b])
                pst2 = C.bank[bk0 + 1][:].bitcast(BF16)
                for kf in range(8):
                    fw.op("pe", lambda kf=kf, pst2=pst2, g1b=g1b: nc.tensor.transpose(
                        pst2[:, kf * 128:(kf + 1) * 128], g1b[:, kf * 128:(kf + 1) * 128], C.ident[:]),
                        reads=[b_g1b, C.b_ident], writes=[C.b_bank[bk0 + 1]], signal=(kf == 7))
                fw.op("act", lambda pst2=pst2, j=j: nc.scalar.copy(
                    sT[:, :, j * 128:(j + 1) * 128], pst2.rearrange("p (k c) -> p k c", k=8)),
                    reads=[C.b_bank[bk0 + 1]], writes=[b_sT])
            fw.barrier()
        wo = [fw.sb(es, f"p_wo{i}", [128, KD, 512], BF16) for i in range(2)]
        b_wo = [Buf(), Buf()]
        yev = [fw.sb(es, f"p_yev{i}", [128, 512], F32) for i in range(4)]
        b_yev = [Buf() for _ in range(4)]
        wov = dram["w_out"].rearrange("(k p) m -> p k m", p=128)
        iy = 0
        for n in range(4):
            s = n % 2
            fw.dma("pool", wo[s][:], wov[:, :, n * 512:(n + 1) * 512], writes=[b_wo[s]])
            for j in range(8):
                for k in range(KD):
                    if k < 8:
                        lhsT = aT[:, k, j:j + 1017:8]
                    else:
                        lhsT = sT[:, k - 8, j * 128:(j + 1) * 128]
                    fw.op("pe", lambda k=k, j=j, s=s, lhsT=lhsT: nc.tensor.matmul(
                        C.bank[j][:], lhsT, wo[s][:, k, :], start=(k == 0), stop=(k == KD - 1)),
                        reads=[b_aT, b_sT, b_wo[s]], writes=[C.b_bank[j]], signal=(k == KD - 1))
                ys = iy % 4
                iy += 1
                if j % 2 == 0:
                    fw.op("act", lambda j=j, ys=ys: nc.scalar.copy(yev[ys][:], C.bank[j][:]),
                          reads=[C.b_bank[j]], writes=[b_yev[ys]])
                else:
                    fw.op("dve", lambda j=j, ys=ys: dve.tensor_copy(yev[ys][:], C.bank[j][:]),
                          reads=[C.b_bank[j]], writes=[b_yev[ys]])
                fw.dma("sp", y_d[j * 128:(j + 1) * 128, n * 512:(n + 1) * 512], yev[ys][:],
                       reads=[b_yev[ys]], writes=[b_yd])
        gpost = fw.sb(es, "p_gpost", [128, D], F32); b_gp = Buf()
        fw.dma("sp", gpost[:], dram["mix_post_g"][0:1, :].partition_broadcast(128), writes=[b_gp])
        xts = [fw.sb(es, f"p_xt{i}", [128, D], F32) for i in range(2)]; b_xts = [Buf(), Buf()]
        yts = [fw.sb(es, f"p_yt{i}", [128, D], F32) for i in range(2)]; b_yts = [Buf(), Buf()]
        junks = [fw.sb(es, f"p_junk{i}", [128, D], BF16) for i in range(2)]; b_junks = [Buf(), Buf()]
        ss2 = fw.sb(es, "p_ss2", [128, 16], F32); b_ss2 = Buf()
        rs2 = fw.sb(es, "p_rs2", [128, 16], F32); b_rs2 = Buf()
        for j in range(8):
            pj = j % 2
            xt, b_xt, yt, b_yt, junk, b_junk = xts[pj], b_xts[pj], yts[pj], b_yts[pj], junks[pj], b_junks[pj]
            fw.dma("sp", yt[:], y_d[j * 128:(j + 1) * 128, :], reads=[b_yd], writes=[b_yt])
            fw.dma("sp", xt[:], x1_d[j:j + 1017:8, :], reads=[b_x1d], writes=[b_xt])
            rms_stats(fw, nc, yt[:], b_yt, junk[:], b_junk, ss2[:, j:j + 1], b_ss2, rs2[:, j:j + 1], b_rs2, D)
            fw.op("dve", lambda j=j, yt=yt: dve.scalar_tensor_tensor(yt[:], yt[:], rs2[:, j:j + 1], gpost[:], ALU.mult, ALU.mult),
                  reads=[b_yt, b_rs2, b_gp], writes=[b_yt])
            fw.op("pool", lambda yt=yt, xt=xt: nc.gpsimd.tensor_tensor(yt[:], yt[:], xt[:], ALU.add),
                  reads=[b_yt, b_xt], writes=[b_yt])
            fw.dma("pool", x2_d[j:j + 1017:8, :], yt[:], reads=[b_yt], writes=[b_x2d])
        fw.barrier()

def build(stage="full"):
    nc = bass.Bass("TRN2", target_bir_lowering=False)
    dram = {}

    def din(name, shape, dt=F32):
        dram[name] = nc.dram_tensor(name, list(shape), dt, kind="ExternalInput").ap()

    din("xs", [S, D])
    din("c_ident", [128, 128])
    for p in ("ff1", "ff2"):
        din(p + "_pre_g", [1, D]); din(p + "_post_g", [1, D])
        din(p + "_w_gate", [D, DFF]); din(p + "_w_up", [D, DFF]); din(p + "_w_down", [DFF, D])
    din("c_alibi", [128, 3072])
    din("mix_pre_g", [1, D]); din("mix_post_g", [1, D])
    din("w_in", [D, 4096]); din("w_out", [D, D])
    for nm in ("lam_q1", "lam_k1", "lam_q2", "lam_k2"):
        din(nm, [1, 64])
    din("attn_head_g", [1, 128])
    din("c_exps", [128, 33]); din("c_maskF", [128, 128]); din("c_maskB", [128, 128])
    din("ssm_lam_re", [2, 64, 64]); din("ssm_lam_im", [2, 64, 64]); din("ssm_log_dt", [2, 64])
    din("ssm_b_re", [2, 64, 64, 16]); din("ssm_b_im", [2, 64, 64, 16])
    din("ssm_c_re", [2, 64, 16, 64]); din("ssm_c_im", [2, 64, 16, 64])
    din("ssm_d", [1, 1024])
    u_d = nc.dram_tensor("u_d", [128, 2, 64, 128], BF16, kind="Internal").ap()
    b_ud = Buf()
    din("ssm_w_glu", [1024, 1024]); din("ssm_b_glu", [1, 1024]); din("ssm_out_g", [1, 1024])
    x2_d = nc.dram_tensor("x2_d", [TOWN, D], F32, kind="Internal").ap()
    b_x2d = Buf()
    if stage == "s5":
        dbg_Y = nc.dram_tensor("dbg_Y", [128, 8, 1024], F32, kind="ExternalOutput").ap()
    out = nc.dram_tensor("out", [TOWN, D], F32, kind="ExternalOutput").ap()
    if stage == "attn":
        dbg_aT = nc.dram_tensor("dbg_aT", [128, 8, TOWN], BF16, kind="ExternalOutput").ap()
    x1_d = nc.dram_tensor("x1_d", [S, D], F32, kind="Internal").ap()
    y_d = nc.dram_tensor("y_d", [TOWN, D], F32, kind="Internal").ap()
    b_x1d, b_yd, b_xs, b_out = Buf(), Buf(), Buf(), Buf()

    with ExitStack() as es:
        fw = FW(nc, es)
        C = Ctx()
        setup_consts(fw, nc, C, es, dram)
        if stage == "ffn":
            with ExitStack() as pes:
                A = ffn_alloc(fw, nc, pes)
                ffn_pass(fw, nc, C, A, dram["xs"][0:TOWN, :], out, dram["ff1_w_gate"], dram["ff1_w_up"],
                         dram["ff1_w_down"], dram["ff1_pre_g"], dram["ff1_post_g"], y_d, b_yd, b_xs, b_out)
                fw.barrier()
        if stage == "attn":
            with ExitStack() as mes:
                aT = fw.sb(mes, "m_aT", [128, 8, TOWN], BF16); b_aT = Buf()
                AA = attention_alloc(fw, nc, mes)
                with ExitStack() as mes3:
                    M = mixer_common_alloc(fw, nc, mes3)
                    M.aT, M.b_aT = aT, b_aT
                    build_hmT(fw, nc, C, M, dram["xs"], b_xs, dram["mix_pre_g"])
                    attention_proj(fw, nc, C, M, AA, dram, u_d, b_ud)
                attention_core(fw, nc, C, M, AA, dram)
                fw.dma("sp", dbg_aT[:, :, :], M.aT[:], reads=[M.b_aT], writes=[b_out])
                fw.barrier()
        if stage == "s5":
            with ExitStack() as mes:
                Yown = fw.sb(mes, "m_Yown", [128, 8, 1024], F32); b_Y = Buf()
                with ExitStack() as mes2:
                    M = mixer_common_alloc(fw, nc, mes2)
                    build_hmT(fw, nc, C, M, dram["xs"], b_xs, dram["mix_pre_g"])
                    attention_proj(fw, nc, C, M, None, dram, u_d, b_ud, only_u=True)
                s5_phase(fw, nc, C, dram, Yown, b_Y, u_d, b_ud)
                fw.dma("sp", dbg_Y[:, :, :], Yown[:], reads=[b_Y], writes=[b_out])
                fw.barrier()
        if stage in ("full", "mix"):
            if stage == "full":
                with ExitStack() as pes:
                    A = ffn_alloc(fw, nc, pes)
                    for ps_ in range(2):
                        ffn_pass(fw, nc, C, A, dram["xs"][ps_ * TOWN:(ps_ + 1) * TOWN, :],
                                 x1_d[ps_ * TOWN:(ps_ + 1) * TOWN, :], dram["ff1_w_gate"], dram["ff1_w_up"],
                                 dram["ff1_w_down"], dram["ff1_pre_g"], dram["ff1_post_g"], y_d, b_yd, b_xs, b_x1d)
                    fw.barrier()
                x1_src = x1_d
            else:
                x1_src = dram["xs"]
            with ExitStack() as mes:
                aT = fw.sb(mes, "m_aT", [128, 8, TOWN], BF16); b_aT = Buf()
                with ExitStack() as mes2:
                    AA = attention_alloc(fw, nc, mes2)
                    with ExitStack() as mes3:
                        M = mixer_common_alloc(fw, nc, mes3)
                        M.aT, M.b_aT = aT, b_aT
                        build_hmT(fw, nc, C, M, x1_src, b_x1d, dram["mix_pre_g"])
                        attention_proj(fw, nc, C, M, AA, dram, u_d, b_ud)
                    attention_core(fw, nc, C, M, AA, dram)
                Yown = fw.sb(mes, "m_Yown", [128, 8, 1024], F32); b_Y = Buf()
                s5_phase(fw, nc, C, dram, Yown, b_Y, u_d, b_ud)
                post_phase(fw, nc, C, dram, Yown, b_Y, aT, b_aT, x1_src, b_x1d, y_d, b_yd, x2_d, b_x2d)
            if stage == "full":
                with ExitStack() as pes:
                    A = ffn_alloc(fw, nc, pes)
                    ffn_pass(fw, nc, C, A, x2_d, out, dram["ff2_w_gate"], dram["ff2_w_up"],
                             dram["ff2_w_down"], dram["ff2_pre_g"], dram["ff2_post_g"], y_d, b_yd, b_x2d, b_out)
                    fw.barrier()
            else:
                with ExitStack() as pes:
                    xt = fw.sb(pes, "o_xt", [128, D], F32); b_xt = Buf()
                    for tt in range(8):
                        fw.dma("sp", xt[:], x2_d[tt * 128:(tt + 1) * 128, :], reads=[b_x2d], writes=[b_xt])
                        fw.dma("sp", out[tt * 128:(tt + 1) * 128, :], xt[:], reads=[b_xt], writes=[b_out])
                    fw.barrier()
        fw.barrier(engines=("sp",))
    return nc


def common_inputs(inp):
    m = {}
    m["c_ident"] = np.eye(128, dtype=np.float32)
    jj = np.arange(128)[:, None]
    mm = np.arange(3072)[None, :]
    m["c_alibi"] = np.abs(mm - jj - 1920).astype(np.float32)
    ex = np.zeros((128, 33), np.float32)
    j8 = np.arange(8)
    ex[:64, 0:8] = j8 + 1; ex[:64, 8:16] = 7 - j8; ex[:64, 16:24] = -1 - j8
    ex[64:, 0:8] = 8 - j8; ex[64:, 8:16] = j8; ex[64:, 16:24] = j8 - 8
    ex[:, 24:32] = 8 * (2 ** j8); ex[:, 32] = 1
    m["c_exps"] = ex
    jrow = (np.arange(128) // 16)[:, None]
    jcol = (np.arange(128) // 16)[None, :]
    m["c_maskF"] = (jcol >= jrow).astype(np.float32)
    m["c_maskB"] = (jcol <= jrow).astype(np.float32)
    m["ssm_d"] = np.ascontiguousarray(np.asarray(inp["ssm_d"], dtype=np.float32).reshape(1, -1))
    for p in ("ff1", "ff2"):
        for n in ("_pre_g", "_post_g"):
            m[p + n] = np.ascontiguousarray(np.asarray(inp[p + n], dtype=np.float32).reshape(1, -1))
        for n in ("_w_gate", "_w_up", "_w_down"):
            m[p + n] = np.ascontiguousarray(np.asarray(inp[p + n], dtype=np.float32)[0])
    for n in ("mix_pre_g", "mix_post_g", "lam_q1", "lam_k1", "lam_q2", "lam_k2", "attn_head_g"):
        m[n] = np.ascontiguousarray(np.asarray(inp[n], dtype=np.float32).reshape(1, -1))
    m["ssm_w_glu"] = np.ascontiguousarray(np.asarray(inp["ssm_w_glu"], dtype=np.float32)[0])
    for n in ("ssm_b_glu", "ssm_out_g"):
        m[n] = np.ascontiguousarray(np.asarray(inp[n], dtype=np.float32).reshape(1, -1))
    m["w_in"] = np.ascontiguousarray(np.asarray(inp["w_in"], dtype=np.float32)[0])
    m["w_out"] = np.ascontiguousarray(np.asarray(inp["w_out"], dtype=np.float32)[0])
    return m


SSM_KEYS = ("ssm_lam_re", "ssm_lam_im", "ssm_log_dt", "ssm_b_re", "ssm_b_im", "ssm_c_re", "ssm_c_im")


def ssm_inputs(inp, r):
    m = {}
    for k in SSM_KEYS:
        a = np.asarray(inp[k], dtype=np.float32)[0]
        if r == 1:
            a = a[::-1]
        m[k] = np.ascontiguousarray(a)
    return m


_NC_CACHE = {}


def kernel(**inputs):
    x = np.asarray(inputs["x"], dtype=np.float32)
    B = x.shape[0]
    common = common_inputs(inputs)
    ssm = [ssm_inputs(inputs, r) for r in range(2)]
    in_maps = []
    for core in range(8):
        b, r = core // 2, core % 2
        m = dict(common)
        m.update(ssm[r])
        xs = x[b] if r == 0 else x[b][::-1]
        m["xs"] = np.ascontiguousarray(xs)
        in_maps.append(m)
    if "full" not in _NC_CACHE:
        _NC_CACHE["full"] = build("full")
    nc = _NC_CACHE["full"]
    res = run_bass_kernel_spmd(nc, in_maps, core_ids=list(range(8)))
    out = np.empty((B, S, D), dtype=np.float32)
    for core in range(8):
        b, r = core // 2, core % 2
        o = np.asarray(res.results[core]["out"], dtype=np.float32)
        if r == 0:
            out[b, :TOWN] = o
        else:
            out[b, TOWN:] = o[::-1]
    return out
```

```python
import numpy as np
from contextlib import ExitStack
import concourse.bass as bass
import concourse.mybir as mybir
from concourse.bass_utils import run_bass_kernel_spmd

F32 = mybir.dt.float32
BF16 = mybir.dt.bfloat16
I32 = mybir.dt.int32
AF = mybir.ActivationFunctionType
ALU = mybir.AluOpType

D = 2048
S = 2048
TOWN = 1024
DFF = 5632
NFF = DFF // 128
KD = D // 128
EPS = 1e-6
NH = 8
NG = 64


class Buf:
    __slots__ = ("name", "w", "r")

    def __init__(self, name=""):
        self.name = name
        self.w = {}
        self.r = {}


def _merge(d, ev):
    sem, val = ev
    k = id(sem)
    if k not in d or d[k][1] < val:
        d[k] = (sem, val)


class FW:
    NDMA = 8

    def __init__(self, nc, es):
        self.nc = nc
        self.es = es
        self.eng = {"pe": nc.tensor, "act": nc.scalar, "dve": nc.vector,
                    "pool": nc.gpsimd, "sp": nc.sync}
        self.csem = {}
        self.ccnt = {}
        for k in ("pe", "act", "dve", "pool"):
            self.csem[k] = es.enter_context(nc.semaphore("c_" + k))
            self.ccnt[k] = 0
        self.dsem = {}
        self.dcnt = {}
        self.di = {}
        for q in ("sp", "act", "pool"):
            self.dsem[q] = [es.enter_context(nc.semaphore(f"d_{q}{i}")) for i in range(self.NDMA)]
            self.dcnt[q] = [0] * self.NDMA
            self.di[q] = 0
        self.waited = {k: {} for k in self.eng}

    def sb(self, es, name, shape, dt):
        self.nalloc = getattr(self, "nalloc", 0) + 1
        return es.enter_context(self.nc.sbuf_tensor(f"{name}_{self.nalloc}", list(shape), dt))

    def ps(self, es, name, shape, dt):
        self.nalloc = getattr(self, "nalloc", 0) + 1
        return es.enter_context(self.nc.psum_tensor(f"{name}_{self.nalloc}", list(shape), dt))

    def _wait(self, ek, ev):
        sem, val = ev
        key = id(sem)
        d = self.waited[ek]
        if d.get(key, 0) >= val:
            return
        d[key] = val
        self.eng[ek].wait_ge(sem, val)

    def _deps(self, ek, reads, writes):
        best = {}
        for b in reads:
            for ev in b.w.values():
                _merge(best, ev)
        for b in writes:
            for ev in b.w.values():
                _merge(best, ev)
            for ev in b.r.values():
                _merge(best, ev)
        for ev in best.values():
            self._wait(ek, ev)

    def _commit(self, ev, reads, writes):
        for b in reads:
            _merge(b.r, ev)
        for b in writes:
            _merge(b.w, ev)

    def op(self, ek, fn, reads=(), writes=(), signal=True):
        self._deps(ek, reads, writes)
        ins = fn()
        if signal:
            self.ccnt[ek] += 1
            ins.then_inc(self.csem[ek], 1)
            ev = (self.csem[ek], self.ccnt[ek])
            self._commit(ev, reads, writes)
            return ev
        return None

    def dma(self, q, out, in_, reads=(), writes=(), wr_only=(), **kw):
        i = self.di[q]
        self.di[q] = (i + 1) % self.NDMA
        sem = self.dsem[q][i]
        if self.dcnt[q][i] > 0:
            self._wait(q, (sem, self.dcnt[q][i]))
        self._deps(q, reads, writes)
        ins = self.eng[q].dma_start(out=out, in_=in_, **kw)
        self.dcnt[q][i] += 16
        ins.then_inc(sem, 16)
        ev = (sem, self.dcnt[q][i])
        self._commit(ev, reads, list(writes) + list(wr_only))
        return ev

    def all_events(self):
        evs = []
        for k in self.csem:
            if self.ccnt[k] > 0:
                evs.append((self.csem[k], self.ccnt[k]))
        for q in self.dsem:
            for i in range(self.NDMA):
                if self.dcnt[q][i] > 0:
                    evs.append((self.dsem[q][i], self.dcnt[q][i]))
        return evs

    def barrier(self, engines=("pe", "act", "dve", "pool", "sp")):
        evs = self.all_events()
        for ek in engines:
            for ev in evs:
                self._wait(ek, ev)


def bc_last(t, off, nparts, mid, last, pstride, mid_stride=1):
    return bass.AP(t, off, [[pstride, nparts], [mid_stride, mid], [0, last]])


class Ctx:
    pass


def setup_consts(fw, nc, C, es, dram):
    C.ident = fw.sb(es, "ident", [128, 128], BF16)
    C.b_ident = Buf()
    C.identf = fw.sb(es, "identf", [128, 128], F32)
    C.b_identf = Buf()
    fw.dma("sp", C.identf[:], dram["c_ident"][:, :], writes=[C.b_identf])
    fw.op("dve", lambda: nc.vector.tensor_copy(C.ident[:], C.identf[:]), reads=[C.b_identf], writes=[C.b_ident])
    C.ones = fw.sb(es, "ones", [128, 128], BF16)
    C.b_ones = Buf()
    fw.op("dve", lambda: nc.vector.memset(C.ones[:], 1.0), writes=[C.b_ones])
    C.bank = [fw.ps(es, f"bank{i}", [128, 512], F32) for i in range(8)]
    C.b_bank = [Buf(f"bank{i}") for i in range(8)]


def rms_stats(fw, nc, src_tile, b_src, junk, b_junk, ss_col, b_ss, rs_col, b_rs, n, mult=1.0):
    fw.op("act", lambda: nc.scalar.activation(junk, src_tile, AF.Square, accum_out=ss_col),
          reads=[b_src], writes=[b_junk, b_ss])
    fw.op("act", lambda: nc.scalar.activation(rs_col, ss_col, AF.Sqrt, bias=EPS, scale=1.0 / n),
          reads=[b_ss], writes=[b_rs])
    fw.op("dve", lambda: nc.vector.reciprocal(rs_col, rs_col), reads=[b_rs], writes=[b_rs])
    if mult != 1.0:
        fw.op("dve", lambda: nc.vector.tensor_scalar(rs_col, rs_col, float(mult), None, ALU.mult),
              reads=[b_rs], writes=[b_rs])


def load_gT(fw, nc, gT, b_gT, g_dram):
    with nc.allow_non_contiguous_dma("tiny gain transpose load"):
        fw.dma("sp", gT[:], g_dram[0, :].rearrange("(k p) -> p k", p=128), writes=[b_gT])


def norm_transpose(fw, nc, C, xt, b_xt, hb, b_hb, ss_col, b_ss, rs_col, b_rs, gT, b_gT, hT, b_hT, col0, ncols=128,
                   col_step=1):
    rms_stats(fw, nc, xt, b_xt, hb, b_hb, ss_col, b_ss, rs_col, b_rs, D)
    fw.op("dve", lambda: nc.vector.tensor_scalar(hb, xt, rs_col, None, ALU.mult),
          reads=[b_xt, b_rs], writes=[b_hb])
    for half in range(2):
        bk = C.bank[half]
        bb = C.b_bank[half]
        pst = bk[:].bitcast(BF16)
        for k8 in range(8):
            k = half * 8 + k8
            fw.op("pe", lambda k=k, k8=k8, pst=pst: nc.tensor.transpose(
                pst[:, k8 * 128:(k8 + 1) * 128], hb[:, k * 128:(k + 1) * 128], C.ident[:]),
                reads=[b_hb, C.b_ident], writes=[bb], signal=(k8 == 7))
        src3 = pst.rearrange("p (k t) -> p k t", k=8)
        if col_step == 1:
            dst3 = hT[:, half * 8:(half + 1) * 8, col0:col0 + ncols]
        else:
            dst3 = hT[:, half * 8:(half + 1) * 8, col0:col0 + ncols * col_step:col_step]
        gb = bc_last(gT, half * 8, 128, 8, 128, KD)
        fw.op("dve", lambda src3=src3, dst3=dst3, gb=gb: nc.vector.tensor_tensor(dst3, src3, gb, ALU.mult),
              reads=[bb, b_gT], writes=[b_hT])


def ffn_alloc(fw, nc, es):
    A = Ctx()
    A.hT = fw.sb(es, "f_hT", [128, KD, TOWN], BF16); A.b_hT = Buf()
    A.actT = fw.sb(es, "f_actT", [128, NFF, TOWN], BF16); A.b_actT = Buf()
    A.NW = 2
    A.wg = [fw.sb(es, f"f_wg{i}", [128, KD, 128], BF16) for i in range(A.NW)]
    A.wu = [fw.sb(es, f"f_wu{i}", [128, KD, 128], BF16) for i in range(A.NW)]
    A.b_wg = [Buf() for _ in range(A.NW)]
    A.b_wu = [Buf() for _ in range(A.NW)]
    A.NWD = 2
    A.wd = [fw.sb(es, f"f_wd{i}", [128, 11, 512], BF16) for i in range(A.NWD)]
    A.b_wd = [Buf() for _ in range(A.NWD)]
    A.xt = [fw.sb(es, f"f_xt{i}", [128, D], F32) for i in range(2)]
    A.b_xt = [Buf() for _ in range(2)]
    A.hb = [fw.sb(es, f"f_hb{i}", [128, D], BF16) for i in range(2)]
    A.b_hb = [Buf() for _ in range(2)]
    A.ss = fw.sb(es, "f_ss", [128, 64], F32); A.b_ss = Buf()
    A.rs = fw.sb(es, "f_rs", [128, 64], F32); A.b_rs = Buf()
    A.gT = fw.sb(es, "f_gT", [128, KD], F32); A.b_gT = Buf()
    A.gpost = fw.sb(es, "f_gpost", [128, D], F32); A.b_gpost = Buf()
    A.sg = [fw.sb(es, f"f_sg{i}", [128, 512], F32) for i in range(2)]
    A.b_sg = [Buf() for _ in range(2)]
    A.yev = [fw.sb(es, f"f_yev{i}", [128, 512], F32) for i in range(4)]
    A.b_yev = [Buf() for _ in range(4)]
    A.cnt = 0
    return A


def ffn_pass(fw, nc, C, A, src, dst, wg_d, wu_d, wd_d, pre_g, post_g, y_d, b_yd, b_src, b_dst):
    NT = TOWN // 128
    load_gT(fw, nc, A.gT, A.b_gT, pre_g)
    fw.dma("sp", A.gpost[:], post_g[0:1, :].partition_broadcast(128), writes=[A.b_gpost])
    for tt in range(NT):
        s = tt % 2
        fw.dma("sp", A.xt[s][:], src[tt * 128:(tt + 1) * 128, :], reads=[b_src], writes=[A.b_xt[s]])
        col = A.cnt % 64
        A.cnt += 1
        norm_transpose(fw, nc, C, A.xt[s][:], A.b_xt[s], A.hb[s][:], A.b_hb[s],
                       A.ss[:, col:col + 1], A.b_ss, A.rs[:, col:col + 1], A.b_rs,
                       A.gT, A.b_gT, A.hT, A.b_hT, tt * 128)
    wgv = wg_d.rearrange("(k p) m -> p k m", p=128)
    wuv = wu_d.rearrange("(k p) m -> p k m", p=128)
    for f in range(NFF):
        s = f % A.NW
        fw.dma("pool", A.wg[s][:], wgv[:, :, f * 128:(f + 1) * 128], writes=[A.b_wg[s]])
        fw.dma("pool", A.wu[s][:], wuv[:, :, f * 128:(f + 1) * 128], writes=[A.b_wu[s]])
        for c in range(2):
            bi = (f % 2) * 4 + c * 2
            pg, pu = C.bank[bi], C.bank[bi + 1]
            bpg, bpu = C.b_bank[bi], C.b_bank[bi + 1]
            for k in range(KD):
                fw.op("pe", lambda k=k, pg=pg, s=s, c=c: nc.tensor.matmul(
                    pg[:], A.wg[s][:, k, :], A.hT[:, k, c * 512:(c + 1) * 512], start=(k == 0), stop=(k == KD - 1)),
                    reads=[A.b_wg[s], A.b_hT], writes=[bpg], signal=(k == KD - 1))
            for k in range(KD):
                fw.op("pe", lambda k=k, pu=pu, s=s, c=c: nc.tensor.matmul(
                    pu[:], A.wu[s][:, k, :], A.hT[:, k, c * 512:(c + 1) * 512], start=(k == 0), stop=(k == KD - 1)),
                    reads=[A.b_wu[s], A.b_hT], writes=[bpu], signal=(k == KD - 1))
            sgs = c
            fw.op("act", lambda pg=pg, sgs=sgs: nc.scalar.activation(A.sg[sgs][:], pg[:], AF.Silu),
                  reads=[bpg], writes=[A.b_sg[sgs]])
            fw.op("dve", lambda pu=pu, sgs=sgs, f=f, c=c: nc.vector.tensor_tensor(
                A.actT[:, f, c * 512:(c + 1) * 512], A.sg[sgs][:], pu[:], ALU.mult),
                reads=[A.b_sg[sgs], bpu], writes=[A.b_actT])
    wdv = wd_d.rearrange("(f p) m -> p f m", p=128)
    ig = 0
    iy = 0
    fw._deps("sp", (), [b_yd])
    for n in range(4):
        for g4 in range(4):
            s = ig % A.NWD
            ig += 1
            fw.dma("pool", A.wd[s][:], wdv[:, g4 * 11:(g4 + 1) * 11, n * 512:(n + 1) * 512], writes=[A.b_wd[s]])
            for tt in range(NT):
                for fi in range(11):
                    f = g4 * 11 + fi
                    last = (g4 == 3 and fi == 10)
                    fw.op("pe", lambda tt=tt, f=f, fi=fi, s=s, g4=g4: nc.tensor.matmul(
                        C.bank[tt][:], A.actT[:, f, tt * 128:(tt + 1) * 128], A.wd[s][:, fi, :],
                        start=(g4 == 0 and fi == 0), stop=(g4 == 3 and fi == 10)),
                        reads=[A.b_actT, A.b_wd[s]], writes=[C.b_bank[tt]], signal=(fi == 10))
        for tt in range(NT):
            ys = iy % 4
            iy += 1
            ek = "act" if tt % 2 == 0 else "dve"
            if ek == "act":
                fw.op("act", lambda tt=tt, ys=ys: nc.scalar.copy(A.yev[ys][:], C.bank[tt][:]),
                      reads=[C.b_bank[tt]], writes=[A.b_yev[ys]])
            else:
                fw.op("dve", lambda tt=tt, ys=ys: nc.vector.tensor_copy(A.yev[ys][:], C.bank[tt][:]),
                      reads=[C.b_bank[tt]], writes=[A.b_yev[ys]])
            fw.dma("sp", y_d[tt * 128:(tt + 1) * 128, n * 512:(n + 1) * 512], A.yev[ys][:],
                   reads=[A.b_yev[ys]], wr_only=[b_yd])
    act32 = A.actT[:].bitcast(F32)
    NDB = 3
    dby = [act32[:, 4 * i:4 * i + 4, :].rearrange("p a b -> p (a b)") for i in range(NDB)]
    dbx = [act32[:, 4 * (NDB + i):4 * (NDB + i) + 4, :].rearrange("p a b -> p (a b)") for i in range(NDB)]
    b_dby = [Buf() for _ in range(NDB)]
    b_dbx = [Buf() for _ in range(NDB)]
    fw._deps("sp", (), [A.b_actT])
    fw._deps("pool", (), [b_dst])
    for tt in range(NT):
        s = tt % 2
        d = tt % NDB
        yt, b_yt = dby[d], b_dby[d]
        xt, b_xt = dbx[d], b_dbx[d]
        fw.dma("sp", yt, y_d[tt * 128:(tt + 1) * 128, :], reads=[b_yd], writes=[b_yt])
        fw.dma("sp", xt, src[tt * 128:(tt + 1) * 128, :], reads=[b_src], writes=[b_xt])
        col = A.cnt % 64
        A.cnt += 1
        ssc = A.ss[:, col:col + 1]
        rsc = A.rs[:, col:col + 1]
        rms_stats(fw, nc, yt, b_yt, A.hb[s][:], A.b_hb[s], ssc, A.b_ss, rsc, A.b_rs, D, mult=0.5)
        fw.op("dve", lambda rsc=rsc, yt=yt: nc.vector.scalar_tensor_tensor(
            yt, yt, rsc, A.gpost[:], ALU.mult, ALU.mult),
            reads=[b_yt, A.b_rs, A.b_gpost], writes=[b_yt])
        fw.op("pool", lambda yt=yt, xt=xt: nc.gpsimd.tensor_tensor(yt, yt, xt, ALU.add),
              reads=[b_yt, b_xt], writes=[b_yt])
        fw.dma("pool", dst[tt * 128:(tt + 1) * 128, :], yt, reads=[b_yt], wr_only=[b_dst])
    for ev in fw.all_events():
        _merge(A.b_actT.w, ev)


SLOPES = [2.0 ** (-(h + 1)) for h in range(NH)]
LAM_INIT = 0.2


def mixer_common_alloc(fw, nc, es):
    M = Ctx()
    M.hmT = fw.sb(es, "m_hmT", [128, KD, S], BF16); M.b_hmT = Buf()
    M.es = es
    M.ss = fw.sb(es, "m_ss", [128, 64], F32); M.b_ss = Buf()
    M.rs = fw.sb(es, "m_rs", [128, 64], F32); M.b_rs = Buf()
    M.gT = fw.sb(es, "m_gT", [128, KD], F32); M.b_gT = Buf()
    M.cnt = 0
    return M


def build_hmT(fw, nc, C, M, x1_src, b_x1, g_dram):
    with ExitStack() as es:
        xt = [fw.sb(es, f"h_xt{i}", [128, D], F32) for i in range(2)]
        b_xt = [Buf() for _ in range(2)]
        hb = [fw.sb(es, f"h_hb{i}", [128, D], BF16) for i in range(2)]
        b_hb = [Buf() for _ in range(2)]
        load_gT(fw, nc, M.gT, M.b_gT, g_dram)
        for tt in range(S // 128):
            s = tt % 2
            fw.dma("sp", xt[s][:], x1_src[tt * 128:(tt + 1) * 128, :], reads=[b_x1], writes=[b_xt[s]])
            col = M.cnt % 64
            M.cnt += 1
            norm_transpose(fw, nc, C, xt[s][:], b_xt[s], hb[s][:], b_hb[s],
                           M.ss[:, col:col + 1], M.b_ss, M.rs[:, col:col + 1], M.b_rs,
                           M.gT, M.b_gT, M.hmT, M.b_hmT, tt * 128)
        fw.barrier()


def attention_alloc(fw, nc, es):
    A = Ctx()
    A.QT = [fw.sb(es, f"a_QT{c}", [128, NH, TOWN], BF16) for c in range(2)]
    A.b_QT = Buf()
    A.KT = fw.sb(es, "a_KT", [128, NH, S], BF16); A.b_KT = Buf()
    A.V = fw.sb(es, "a_V", [128, S // 128, 1024], BF16); A.b_V = Buf()
    for c in range(2):
        fw.op("pool", lambda c=c: nc.gpsimd.memset(A.QT[c][:], 0.0), writes=[A.b_QT])
    return A


def attention_proj(fw, nc, C, M, A, dram, u_d, b_ud, only_u=False):
    w_in = dram["w_in"]
    if A is not None:
        QT, KT, V, b_QT, b_KT, b_V = A.QT, A.KT, A.V, A.b_QT, A.b_KT, A.b_V
    with ExitStack() as es:
        wp = [fw.sb(es, f"a_wp{i}", [128, KD, 256], BF16) for i in range(2)]
        b_wp = [Buf() for _ in range(2)]
        wv = w_in.rearrange("(k p) m -> p k m", p=128)
        iw = 0
        ib = 0
        ust = [fw.sb(es, f"a_ust{i}", [128, 16, 8, 16], BF16) for i in range(2)]
        b_ust = [Buf() for _ in range(2)]
        iu = 0
        for blk in range(4):
            s = iw % 2
            iw += 1
            fw.dma("pool", wp[s][:], wv[:, :, 3072 + blk * 256:3072 + (blk + 1) * 256], writes=[b_wp[s]])
            for hh in range(2):
                us = iu % 2
                iu += 1
                for j in range(8):
                    bi = ib % 8
                    ib += 1
                    t0 = 1024 * hh + j
                    for k in range(KD):
                        fw.op("pe", lambda k=k, s=s, t0=t0, bi=bi: nc.tensor.matmul(
                            C.bank[bi][:, 0:256], M.hmT[:, k, t0:t0 + 1017:8], wp[s][:, k, :],
                            start=(k == 0), stop=(k == KD - 1)),
                            reads=[b_wp[s], M.b_hmT], writes=[C.b_bank[bi]], signal=(k == KD - 1))
                    src = C.bank[bi][:, 0:256].rearrange("c (g q) -> c g q", g=16)
                    if j % 2 == 0:
                        fw.op("act", lambda us=us, j=j, src=src: nc.scalar.copy(ust[us][:, :, j, :], src),
                              reads=[C.b_bank[bi]], writes=[b_ust[us]])
                    else:
                        fw.op("dve", lambda us=us, j=j, src=src: nc.vector.tensor_copy(ust[us][:, :, j, :], src),
                              reads=[C.b_bank[bi]], writes=[b_ust[us]])
                fw.dma("sp", u_d[:, hh, blk * 16:(blk + 1) * 16, :],
                       ust[us][:].rearrange("c g j q -> c g (j q)"), reads=[b_ust[us]], wr_only=[b_ud])
        if only_u:
            fw.barrier()
            return
        for blk in range(8):
            s = iw % 2
            iw += 1
            fw.dma("pool", wp[s][:], wv[:, :, blk * 256:(blk + 1) * 256], writes=[b_wp[s]])
            isq = blk < 4
            nch = 2 if isq else 4
            for h4 in range(2):
                h = (blk % 4) * 2 + h4
                for ch in range(nch):
                    bi = ib % 8
                    ib += 1
                    for k in range(KD):
                        fw.op("pe", lambda k=k, s=s, h4=h4, ch=ch, bi=bi: nc.tensor.matmul(
                            C.bank[bi][:], wp[s][:, k, h4 * 128:(h4 + 1) * 128], M.hmT[:, k, ch * 512:(ch + 1) * 512],
                            start=(k == 0), stop=(k == KD - 1)),
                            reads=[b_wp[s], M.b_hmT], writes=[C.b_bank[bi]], signal=(k == KD - 1))
                    if isq:
                        for c in range(2):
                            fw.op("act", lambda h=h, ch=ch, bi=bi, c=c: nc.scalar.mul(
                                QT[c][64 * c:64 * c + 64, h, ch * 512:(ch + 1) * 512],
                                C.bank[bi][64 * c:64 * c + 64, :], 0.125),
                                reads=[C.b_bank[bi]], writes=[b_QT])
                    else:
                        fw.op("dve", lambda h=h, ch=ch, bi=bi: nc.vector.tensor_copy(
                            KT[:, h, ch * 512:(ch + 1) * 512], C.bank[bi][:]),
                            reads=[C.b_bank[bi]], writes=[b_KT])
        for blk in range(4):
            s = iw % 2
            iw += 1
            fw.dma("pool", wp[s][:], wv[:, :, 2048 + blk * 256:2048 + (blk + 1) * 256], writes=[b_wp[s]])
            for tt in range(S // 128):
                bi = ib % 8
                ib += 1
                for k in range(KD):
                    fw.op("pe", lambda k=k, s=s, tt=tt, bi=bi: nc.tensor.matmul(
                        C.bank[bi][:, 0:256], M.hmT[:, k, tt * 128:(tt + 1) * 128], wp[s][:, k, :],
                        start=(k == 0), stop=(k == KD - 1)),
                        reads=[b_wp[s], M.b_hmT], writes=[C.b_bank[bi]], signal=(k == KD - 1))
                if tt % 2 == 0:
                    fw.op("act", lambda tt=tt, blk=blk, bi=bi: nc.scalar.copy(
                        V[:, tt, blk * 256:(blk + 1) * 256], C.bank[bi][:, 0:256]),
                        reads=[C.b_bank[bi]], writes=[b_V])
                else:
                    fw.op("dve", lambda tt=tt, blk=blk, bi=bi: nc.vector.tensor_copy(
                        V[:, tt, blk * 256:(blk + 1) * 256], C.bank[bi][:, 0:256]),
                        reads=[C.b_bank[bi]], writes=[b_V])
        fw.barrier()


def attention_core(fw, nc, C, M, A, dram):
    QT, KT, V, b_QT, b_KT, b_V = A.QT, A.KT, A.V, A.b_QT, A.b_KT, A.b_V
    with ExitStack() as es:
        G = fw.sb(es, "a_G", [128, 3072], F32); b_G = Buf()
        fw.dma("sp", G[:], dram["c_alibi"][:, :], writes=[b_G])
        GD = [fw.sb(es, f"a_GD{i}", [128, 3072], BF16) for i in range(2)]
        b_GD = [Buf(), Buf()]
        lq = fw.sb(es, "a_lq", [128, 4, 64], F32); b_lq = Buf()
        for i, nm in enumerate(("lam_q1", "lam_k1", "lam_q2", "lam_k2")):
            fw.dma("sp", lq[:, i, :], dram[nm][0:1, :].partition_broadcast(128), writes=[b_lq])
        sc4 = fw.sb(es, "a_sc4", [128, 8], F32); b_sc4 = Buf()
        junk = fw.sb(es, "a_junk", [128, 64], F32); b_junk = Buf()
        fw.op("dve", lambda: nc.vector.scalar_tensor_tensor(junk[:], lq[:, 0, :], 1.0, lq[:, 1, :], ALU.mult, ALU.mult,
                                                            accum_out=sc4[:, 0:1]), reads=[b_lq], writes=[b_junk, b_sc4])
        fw.op("dve", lambda: nc.vector.scalar_tensor_tensor(junk[:], lq[:, 2, :], 1.0, lq[:, 3, :], ALU.mult, ALU.mult,
                                                            accum_out=sc4[:, 1:2]), reads=[b_lq], writes=[b_junk, b_sc4])
        fw.op("act", lambda: nc.scalar.activation(sc4[:, 2:4], sc4[:, 0:2], AF.Exp), reads=[b_sc4], writes=[b_sc4])
        fw.op("dve", lambda: nc.vector.tensor_tensor(sc4[:, 4:5], sc4[:, 3:4], sc4[:, 2:3], ALU.subtract),
              reads=[b_sc4], writes=[b_sc4])
        fw.op("dve", lambda: nc.vector.tensor_scalar(sc4[:, 5:6], sc4[:, 4:5], -LAM_INIT, None, ALU.add),
              reads=[b_sc4], writes=[b_sc4])
        neg_lam = sc4[:, 5:6]
        gh = fw.sb(es, "a_gh", [128, 2], F32); b_gh = Buf()
        with nc.allow_non_contiguous_dma("tiny"):
            fw.dma("sp", gh[:, 0:1], dram["attn_head_g"][0, :].rearrange("(p o) -> p o", o=1), writes=[b_gh])
        fw.op("dve", lambda: nc.vector.tensor_scalar(gh[:, 1:2], gh[:, 0:1], 1.0 - LAM_INIT, None, ALU.mult),
              reads=[b_gh], writes=[b_gh])
        scb = [fw.sb(es, f"a_scb{i}", [128, 512], BF16) for i in range(4)]
        b_scb = [Buf() for _ in range(4)]
        pT = [fw.sb(es, f"a_pT{i}", [128, 512], BF16) for i in range(6)]
        b_pT = [Buf() for _ in range(6)]
        rz = [fw.sb(es, f"a_rz{i}", [128, 512], F32) for i in range(2)]
        b_rz = [Buf() for _ in range(2)]
        ot = [fw.sb(es, f"a_ot{i}", [128, 512], F32) for i in range(2)]
        b_ot = [Buf() for _ in range(2)]
        an = fw.sb(es, "a_an", [128, 2 * NH, 512], F32); b_an = Buf()
        sqas = [fw.sb(es, f"a_sqa{i}", [128, 512], BF16) for i in range(2)]; b_sqas = [Buf(), Buf()]
        rns = [fw.sb(es, f"a_rn{i}", [128, 512], F32) for i in range(2)]; b_rns = [Buf(), Buf()]
        iters = [(h, qc, kb, c) for h in range(NH) for qc in range(2) for kb in range(S // 128) for c in range(2)]
        SB = [0, 1, 7, 6]
        DEPTH = 3
        NKB = S // 128

        def stage1(i):
            h, qc, kb, c = iters[i]
            off = 512 * qc - 128 * kb + 1920
            sb_i = SB[i % 4]
            sbf = i % 4
            pi = i % 6
            pbank = C.bank[sb_i]
            fw.op("pe", lambda: nc.tensor.matmul(
                pbank[:], KT[:, h, kb * 128:(kb + 1) * 128],
                QT[c][:, h, qc * 512:(qc + 1) * 512], start=True, stop=True),
                reads=[b_KT, b_QT], writes=[C.b_bank[sb_i]])
            if (h == 0 and qc == 0 and kb == 0 and c == 0) or (qc == 1 and kb == 0 and c == 0 and h + 1 < NH):
                hn = 0 if (h == 0 and qc == 0) else h + 1
                fw.op("act", lambda hn=hn: nc.scalar.activation(GD[hn % 2][:], G[:], AF.Exp, scale=-SLOPES[hn]),
                      reads=[b_G], writes=[b_GD[hn % 2]])
            fw.op("act", lambda: nc.scalar.activation(scb[sbf][:], pbank[:], AF.Exp),
                  reads=[C.b_bank[sb_i]], writes=[b_scb[sbf]])
            fw.op("dve", lambda: nc.vector.tensor_tensor(pT[pi][:], scb[sbf][:], GD[h % 2][:, off:off + 512], ALU.mult),
                  reads=[b_scb[sbf], b_GD[h % 2]], writes=[b_pT[pi]])

        def stage2(i):
            h, qc, kb, c = iters[i]
            pi = i % 6
            fw.op("pe", lambda: nc.tensor.matmul(
                C.bank[2 + c][:], V[:, kb, h * 128:(h + 1) * 128], pT[pi][:],
                start=(kb == 0), stop=(kb == NKB - 1)),
                reads=[b_V, b_pT[pi]], writes=[C.b_bank[2 + c]], signal=False)
            fw.op("pe", lambda: nc.tensor.matmul(
                C.bank[4 + c][:], C.ones[:], pT[pi][:],
                start=(kb == 0), stop=(kb == NKB - 1)),
                reads=[C.b_ones, b_pT[pi], b_V], writes=[C.b_bank[2 + c], C.b_bank[4 + c]])
            if kb == NKB - 1 and c == 1:
                for cc in range(2):
                    fw.op("dve", lambda cc=cc: nc.vector.reciprocal(rz[cc][:], C.bank[4 + cc][:]),
                          reads=[C.b_bank[4 + cc]], writes=[b_rz[cc]])
                    fw.op("dve", lambda cc=cc: nc.vector.tensor_tensor(ot[cc][:], C.bank[2 + cc][:], rz[cc][:], ALU.mult),
                          reads=[C.b_bank[2 + cc], b_rz[cc]], writes=[b_ot[cc]])
                u = h * 2 + qc
                fw.op("dve", lambda u=u: nc.vector.scalar_tensor_tensor(an[:, u, :], ot[1][:], neg_lam, ot[0][:], ALU.mult, ALU.add),
                      reads=[b_ot[0], b_ot[1], b_sc4], writes=[b_an])

        def head_norm(u):
            h, qc = u // 2, u % 2
            sq_, bsq_ = sqas[u % 2], b_sqas[u % 2]
            r_, br_ = rns[u % 2], b_rns[u % 2]
            bk = 2 + (u % 4)
            fw.op("act", lambda: nc.scalar.activation(sq_[:], an[:, u, :], AF.Square), reads=[b_an], writes=[bsq_])
            fw.op("pe", lambda: nc.tensor.matmul(C.bank[bk][:], C.ones[:], sq_[:], start=True, stop=True),
                  reads=[C.b_ones, bsq_], writes=[C.b_bank[bk]])
            fw.op("act", lambda: nc.scalar.activation(r_[:], C.bank[bk][:], AF.Sqrt, bias=EPS, scale=1.0 / 128),
                  reads=[C.b_bank[bk]], writes=[br_])
            fw.op("dve", lambda: nc.vector.reciprocal(r_[:], r_[:]), reads=[br_], writes=[br_])
            fw.op("dve", lambda: nc.vector.scalar_tensor_tensor(
                M.aT[:, h, qc * 512:(qc + 1) * 512], an[:, u, :], gh[:, 1:2], r_[:], ALU.mult, ALU.mult),
                reads=[b_an, b_gh, br_], writes=[M.b_aT])

        NI = len(iters)
        for i in range(NI + DEPTH):
            if i < NI:
                stage1(i)
            if i >= DEPTH:
                stage2(i - DEPTH)
        for u in range(2 * NH):
            head_norm(u)
        fw.barrier()


TWO_PI = 6.283185307179586
PI_C = 3.1415925


def ap4(t, off, pstride, nparts, dims):
    return bass.AP(t, off, [[pstride, nparts]] + [[st, n] for st, n in dims])


def s5_phase(fw, nc, C, dram, Yown, b_Y, u_d, b_ud):
    with ExitStack() as es:
        U2 = fw.sb(es, "s_U2", [128, 2, 64, 128], BF16); b_U2 = Buf()
        for hh in range(2):
            fw.dma("sp", U2[:, hh, :, :], u_d[:, hh, :, :], reads=[b_ud], writes=[b_U2])
        NF = 33
        PWR = fw.sb(es, "s_PWR", [128, NF, 64], F32); b_PW = Buf()
        PWI = fw.sb(es, "s_PWI", [128, NF, 64], F32)
        NAI = fw.sb(es, "s_NAI", [128, 8, 64], F32)
        BBR = fw.sb(es, "s_BBR", [128, 64, 16], F32); b_BB = Buf()
        BBI = fw.sb(es, "s_BBI", [128, 64, 16], F32)
        CTR = fw.sb(es, "s_CTR", [128, 64, 16], F32); b_CT = Buf()
        CTI = fw.sb(es, "s_CTI", [128, 64, 16], F32)
        MF = fw.sb(es, "s_MF", [128, 128], F32); b_MK = Buf()
        MB = fw.sb(es, "s_MB", [128, 128], F32)
        fw.dma("sp", MF[:], dram["c_maskF"][:, :], writes=[b_MK])
        fw.dma("sp", MB[:], dram["c_maskB"][:, :], writes=[b_MK])
        dB = fw.sb(es, "s_dB", [128, 1024], F32); b_dB = Buf()
        fw.dma("sp", dB[:], dram["ssm_d"][0:1, :].partition_broadcast(128), writes=[b_dB])
        dve = nc.vector
        with ExitStack() as pes:
            LL = fw.sb(pes, "s_LL", [64, 2, 128], F32); b_LL = Buf()
            for i, nm in enumerate(("ssm_lam_re", "ssm_lam_im")):
                fw.dma("sp", LL[:, i, :].rearrange("g (d n) -> g d n", d=2), dram[nm].rearrange("d g n -> g d n"),
                       writes=[b_LL])
            LRI = fw.sb(pes, "s_LRI", [128, 2, 64], F32); b_LRI = Buf()
            for i in range(2):
                fw.op("pe", lambda i=i: nc.tensor.transpose(C.bank[0][:, i * 64:(i + 1) * 64], LL[:, i, :],
                                                            C.identf[0:64, 0:64]),
                      reads=[b_LL, C.b_identf], writes=[C.b_bank[0]])
            fw.op("dve", lambda: dve.tensor_copy(LRI[:].rearrange("p a g -> p (a g)"), C.bank[0][:, 0:128]),
                  reads=[C.b_bank[0]], writes=[b_LRI])
            LR = LRI[:, 0, :]
            LI = LRI[:, 1, :]
            DT = fw.sb(pes, "s_DT", [128, 64], F32); b_DT = Buf()
            for d in range(2):
                fw.dma("sp", DT[64 * d:64 * d + 64, :], dram["ssm_log_dt"][d:d + 1, :].partition_broadcast(64),
                       writes=[b_DT])
            fw.op("act", lambda: nc.scalar.activation(DT[:], DT[:], AF.Exp), reads=[b_DT], writes=[b_DT])
            LD = fw.sb(pes, "s_LD", [128, 2, 64], F32); b_LD = Buf()
            for i in range(2):
                fw.op("dve", lambda i=i: dve.tensor_tensor(LD[:, i, :], LRI[:, i, :], DT[:], ALU.mult),
                      reads=[b_LRI, b_DT], writes=[b_LD])
            EXPS = fw.sb(pes, "s_EXPS", [128, NF], F32); b_EX = Buf()
            fw.dma("sp", EXPS[:], dram["c_exps"][:, :], writes=[b_EX])
            ANG = fw.sb(pes, "s_ANG", [128, NF, 64], F32); b_ANG = Buf()
            MAG = fw.sb(pes, "s_MAG", [128, NF, 64], F32); b_MAG = Buf()
            KF = fw.sb(pes, "s_KF", [128, NF, 64], F32); b_KF = Buf()
            KI = fw.sb(pes, "s_KI", [128, NF, 64], I32); b_KI = Buf()
            ex_b = ap4(EXPS, 0, NF, 128, [(1, NF), (0, 64)])
            lid_b = ap4(LD, 64, 128, 128, [(0, NF), (1, 64)])
            lrd_b = ap4(LD, 0, 128, 128, [(0, NF), (1, 64)])
            fw.op("dve", lambda: dve.tensor_tensor(ANG[:], lid_b, ex_b, ALU.mult), reads=[b_LD, b_EX], writes=[b_ANG])
            fw.op("dve", lambda: dve.tensor_tensor(MAG[:], lrd_b, ex_b, ALU.mult), reads=[b_LD, b_EX], writes=[b_MAG])
            fw.op("act", lambda: nc.scalar.activation(MAG[:], MAG[:], AF.Exp), reads=[b_MAG], writes=[b_MAG])

            def reduce_clamp(A, b_A):
                fw.op("dve", lambda: dve.tensor_scalar(KF[:], A[:], 1.0 / TWO_PI, None, ALU.mult),
                      reads=[b_A], writes=[b_KF])
                fw.op("dve", lambda: dve.tensor_copy(KI[:], KF[:]), reads=[b_KF], writes=[b_KI])
                fw.op("dve", lambda: dve.tensor_copy(KF[:], KI[:]), reads=[b_KI], writes=[b_KF])
                fw.op("dve", lambda: dve.scalar_tensor_tensor(A[:], KF[:], -TWO_PI, A[:], ALU.mult, ALU.add),
                      reads=[b_KF, b_A], writes=[b_A])
                fw.op("dve", lambda: dve.tensor_scalar(A[:], A[:], PI_C, -PI_C, ALU.min, ALU.max),
                      reads=[b_A], writes=[b_A])

            reduce_clamp(ANG, b_ANG)
            fw.op("act", lambda: nc.scalar.activation(PWI[:], ANG[:], AF.Sin), reads=[b_ANG], writes=[b_PW])
            fw.op("dve", lambda: dve.tensor_scalar(ANG[:], ANG[:], 1.5707963267948966, None, ALU.add),
                  reads=[b_ANG], writes=[b_ANG])
            reduce_clamp(ANG, b_ANG)
            fw.op("act", lambda: nc.scalar.activation(PWR[:], ANG[:], AF.Sin), reads=[b_ANG], writes=[b_PW])
            fw.op("dve", lambda: dve.tensor_tensor(PWR[:], PWR[:], MAG[:], ALU.mult), reads=[b_PW, b_MAG], writes=[b_PW])
            fw.op("dve", lambda: dve.tensor_tensor(PWI[:], PWI[:], MAG[:], ALU.mult), reads=[b_PW, b_MAG], writes=[b_PW])
            fw.op("dve", lambda: dve.tensor_scalar(NAI[:], PWI[:, 24:32, :], -1.0, None, ALU.mult),
                  reads=[b_PW], writes=[b_PW])
            FT = fw.sb(pes, "s_FT", [128, 8, 64], F32); b_FT = Buf()
            a_re = PWR[:, 32, :]
            a_im = PWI[:, 32, :]
            den, t2, nr, fre, fim, tt_ = (FT[:, i, :] for i in range(6))
            ops = [
                (den, LR, LR, ALU.mult), (t2, LI, LI, ALU.mult), (den, den, t2, ALU.add),
            ]
            for o, a, b, op_ in ops:
                fw.op("dve", lambda o=o, a=a, b=b, op_=op_: dve.tensor_tensor(o, a, b, op_),
                      reads=[b_LRI, b_FT, b_PW], writes=[b_FT])
            fw.op("dve", lambda: dve.reciprocal(den, den), reads=[b_FT], writes=[b_FT])
            fw.op("dve", lambda: dve.tensor_scalar(nr, a_re, -1.0, None, ALU.add), reads=[b_PW], writes=[b_FT])
            ops = [
                (fre, nr, LR, ALU.mult), (tt_, a_im, LI, ALU.mult), (fre, fre, tt_, ALU.add), (fre, fre, den, ALU.mult),
                (fim, a_im, LR, ALU.mult), (tt_, nr, LI, ALU.mult), (fim, fim, tt_, ALU.subtract),
                (fim, fim, den, ALU.mult),
            ]
            for o, a, b, op_ in ops:
                fw.op("dve", lambda o=o, a=a, b=b, op_=op_: dve.tensor_tensor(o, a, b, op_),
                      reads=[b_LRI, b_FT, b_PW], writes=[b_FT])
            BR = fw.sb(pes, "s_BR", [128, 64, 16], F32); b_BRI = Buf()
            BI = fw.sb(pes, "s_BI", [128, 64, 16], F32)
            TB_ = fw.sb(pes, "s_TB", [128, 64, 16], F32); b_TB = Buf()
            for d in range(2):
                fw.dma("sp", BR[64 * d:64 * d + 64, :, :], dram["ssm_b_re"][d].rearrange("g n q -> n g q"), writes=[b_BRI])
                fw.dma("sp", BI[64 * d:64 * d + 64, :, :], dram["ssm_b_im"][d].rearrange("g n q -> n g q"), writes=[b_BRI])
            fre_b = ap4(FT, 3 * 64, 8 * 64, 128, [(1, 64), (0, 16)])
            fim_b = ap4(FT, 4 * 64, 8 * 64, 128, [(1, 64), (0, 16)])
            seq = [
                (BBR[:], fre_b, BR[:], ALU.mult), (TB_[:], fim_b, BI[:], ALU.mult), (BBR[:], BBR[:], TB_[:], ALU.subtract),
                (BBI[:], fre_b, BI[:], ALU.mult), (TB_[:], fim_b, BR[:], ALU.mult), (BBI[:], BBI[:], TB_[:], ALU.add),
            ]
            for o, a, b, op_ in seq:
                fw.op("dve", lambda o=o, a=a, b=b, op_=op_: dve.tensor_tensor(o, a, b, op_),
                      reads=[b_FT, b_BRI, b_TB, b_BB], writes=[b_TB, b_BB])
            Cin = fw.sb(pes, "s_Cin", [128, 2, 8, 128], F32); b_Cin = Buf()
            for i, nm in enumerate(("ssm_c_re", "ssm_c_im")):
                for d in range(2):
                    fw.dma("sp", Cin[:, i, :, 64 * d:64 * d + 64],
                           dram[nm][d].rearrange("(gb g8) p n -> (g8 p) gb n", g8=8), writes=[b_Cin])
            for i, CT in enumerate((CTR, CTI)):
                for half in range(2):
                    bk = C.bank[half]
                    for q4 in range(4):
                        gb_ = half * 4 + q4
                        fw.op("pe", lambda i=i, gb_=gb_, q4=q4, bk=bk: nc.tensor.transpose(
                            bk[:, q4 * 128:(q4 + 1) * 128], Cin[:, i, gb_, :], C.identf[:]),
                            reads=[b_Cin, C.b_identf], writes=[C.b_bank[half]], signal=(q4 == 3))
                    fw.op("dve", lambda CT=CT, half=half, bk=bk: dve.tensor_copy(
                        CT[:].rearrange("p g q -> p (g q)")[:, half * 512:(half + 1) * 512], bk[:]),
                        reads=[C.b_bank[half]], writes=[b_CT])
            fw.barrier()
        GB = 16
        CO_re = fw.sb(es, "s_COre", [128, GB, 128], BF16); b_CO = Buf()
        CO_imn = fw.sb(es, "s_COim", [128, GB, 128], BF16)
        WX_re = fw.sb(es, "s_WXre", [128, GB, 128], BF16); b_WX = Buf()
        WX_im = fw.sb(es, "s_WXim", [128, GB, 128], BF16)
        W2_re = fw.sb(es, "s_W2re", [128, GB, 128], BF16); b_W2 = Buf()
        W2_im = fw.sb(es, "s_W2im", [128, GB, 128], BF16)
        WXT_re = fw.sb(es, "s_WXTre", [128, GB, 128], BF16); b_WXT = Buf()
        WXT_im = fw.sb(es, "s_WXTim", [128, GB, 128], BF16)
        TT = fw.sb(es, "s_TT", [128, GB, 128], BF16); b_TT = Buf()
        T1 = fw.sb(es, "s_T1", [128, GB * 128], F32); b_T1 = Buf()
        T2 = fw.sb(es, "s_T2", [128, GB * 128], F32); b_T2 = Buf()
        U8b = fw.sb(es, "s_U8b", [128, GB, 256], BF16); b_U8 = Buf()
        NI = 4
        ST = [[fw.sb(es, f"s_ST{a_}{b_}", [128, 2, 256], F32) for b_ in range(2)] for a_ in range(NI)]
        b_ST = [[Buf(), Buf()] for _ in range(NI)]
        SS = [fw.sb(es, f"s_SS{a_}", [128, 2, 256], F32) for a_ in range(NI)]
        b_SS = [Buf() for _ in range(NI)]
        E = [fw.sb(es, f"s_E{a_}", [128, 2, 128], BF16) for a_ in range(NI)]
        b_E = [Buf() for _ in range(NI)]
        for a_ in range(NI):
            for b_ in range(2):
                fw.op("dve", lambda a_=a_, b_=b_: dve.memset(ST[a_][b_][:], 0.0), writes=[b_ST[a_][b_]])
            fw.op("dve", lambda a_=a_: dve.memset(E[a_][:], 0.0), writes=[b_E[a_]])
            fw.op("dve", lambda a_=a_: dve.memset(SS[a_][:], 0.0), writes=[b_SS[a_]])
        ig = 0
        for gb in range(NG // GB):
            g0 = gb * GB

            def expand(fam, XR, XI, b_X, o_re, o_im, b_o, neg_im):
                pwr = ap4(PWR, fam * 8 * 64 + g0, NF * 64, 128, [(1, GB), (64, 8), (0, 16)])
                pwi = ap4(PWI, fam * 8 * 64 + g0, NF * 64, 128, [(1, GB), (64, 8), (0, 16)])
                xr = ap4(XR, g0 * 16, 1024, 128, [(16, GB), (0, 8), (1, 16)])
                xi = ap4(XI, g0 * 16, 1024, 128, [(16, GB), (0, 8), (1, 16)])
                t1 = T1[:].rearrange("p (g j q) -> p g j q", g=GB, j=8)
                t2 = T2[:].rearrange("p (g j q) -> p g j q", g=GB, j=8)
                ore = o_re[:].rearrange("p g (j q) -> p g j q", j=8)
                oim = o_im[:].rearrange("p g (j q) -> p g j q", j=8)
                fw.op("dve", lambda: dve.tensor_tensor(t1, pwr, xr, ALU.mult), reads=[b_PW, b_X], writes=[b_T1])
                fw.op("pool", lambda: nc.gpsimd.tensor_tensor(t2, pwi, xi, ALU.mult), reads=[b_PW, b_X], writes=[b_T2])
                fw.op("dve", lambda: dve.tensor_tensor(ore, t1, t2, ALU.subtract), reads=[b_T1, b_T2], writes=[b_o])
                fw.op("dve", lambda: dve.tensor_tensor(t1, pwr, xi, ALU.mult), reads=[b_PW, b_X], writes=[b_T1])
                fw.op("pool", lambda: nc.gpsimd.tensor_tensor(t2, pwi, xr, ALU.mult), reads=[b_PW, b_X], writes=[b_T2])
                if neg_im:
                    fw.op("dve", lambda: dve.scalar_tensor_tensor(oim, t1, -1.0, t2, ALU.mult, ALU.subtract),
                          reads=[b_T1, b_T2], writes=[b_o])
                else:
                    fw.op("dve", lambda: dve.tensor_tensor(oim, t1, t2, ALU.add), reads=[b_T1, b_T2], writes=[b_o])

            expand(0, CTR, CTI, b_CT, CO_re, CO_imn, b_CO, True)
            expand(1, BBR, BBI, b_BB, WX_re, WX_im, b_WX, False)
            expand(2, BBR, BBI, b_BB, W2_re, W2_im, b_W2, False)
            for src, dstT in ((WX_re, WXT_re), (WX_im, WXT_im)):
                for half in range(2):
                    bk = C.bank[half]
                    pst = bk[:].bitcast(BF16)
                    for q8 in range(8):
                        gl = half * 8 + q8
                        fw.op("pe", lambda src=src, gl=gl, q8=q8, pst=pst: nc.tensor.transpose(
                            pst[:, q8 * 128:(q8 + 1) * 128], src[:, gl, :], C.ident[:]),
                            reads=[b_WX, C.b_ident], writes=[C.b_bank[half]], signal=(q8 == 7))
                    fw.op("act", lambda dstT=dstT, half=half, pst=pst: nc.scalar.copy(
                        dstT[:, half * 8:(half + 1) * 8, :].rearrange("p g m -> p (g m)"), pst),
                        reads=[C.b_bank[half]], writes=[b_WXT])
            for q in range(GB // 4):
                for g4 in range(4):
                    gl = q * 4 + g4
                    for (lo, bki) in ((0, 2), (64, 3)):
                        fw.op("pe", lambda gl=gl, g4=g4, lo=lo, bki=bki: nc.tensor.matmul(
                            C.bank[bki][:, g4 * 128:(g4 + 1) * 128], W2_re[lo:lo + 64, gl, :], CO_re[lo:lo + 64, gl, :],
                            start=True, stop=False),
                            reads=[b_W2, b_CO], writes=[C.b_bank[bki]], signal=False)
                        fw.op("pe", lambda gl=gl, g4=g4, lo=lo, bki=bki: nc.tensor.matmul(
                            C.bank[bki][:, g4 * 128:(g4 + 1) * 128], W2_im[lo:lo + 64, gl, :], CO_imn[lo:lo + 64, gl, :],
                            start=False, stop=True),
                            reads=[b_W2, b_CO], writes=[C.b_bank[bki]], signal=(g4 == 3))
                mf = ap4(MF, 0, 128, 128, [(0, 4), (1, 128)])
                mb = ap4(MB, 0, 128, 128, [(0, 4), (1, 128)])
                t1v = T1[:, 0:512].rearrange("p (g m) -> p g m", g=4)
                t2v = T2[:, 0:512].rearrange("p (g m) -> p g m", g=4)
                fw.op("dve", lambda t1v=t1v, mf=mf: dve.tensor_tensor(
                    t1v, C.bank[2][:].rearrange("p (g m) -> p g m", g=4), mf, ALU.mult),
                    reads=[C.b_bank[2], b_MK], writes=[b_T1])
                fw.op("dve", lambda t2v=t2v, mb=mb: dve.tensor_tensor(
                    t2v, C.bank[3][:].rearrange("p (g m) -> p g m", g=4), mb, ALU.mult),
                    reads=[C.b_bank[3], b_MK], writes=[b_T2])
                fw.op("dve", lambda q=q, t1v=t1v, t2v=t2v: dve.tensor_tensor(
                    TT[:, q * 4:(q + 1) * 4, :], t1v, t2v, ALU.add), reads=[b_T1, b_T2], writes=[b_TT])
            for hh in range(2):
                for half in range(2):
                    bk = C.bank[half]
                    pst = bk[:].bitcast(BF16)
                    for q8 in range(8):
                        g = g0 + half * 8 + q8
                        fw.op("pe", lambda hh=hh, g=g, q8=q8, pst=pst: nc.tensor.transpose(
                            pst[:, q8 * 128:(q8 + 1) * 128], U2[:, hh, g, :], C.ident[:]),
                            reads=[b_U2, C.b_ident], writes=[C.b_bank[half]], signal=(q8 == 7))
                    fw.op("act", lambda hh=hh, half=half, pst=pst: nc.scalar.copy(
                        U8b[:, half * 8:(half + 1) * 8, hh * 128:(hh + 1) * 128],
                        pst.rearrange("p (g c) -> p g c", g=8)),
                        reads=[C.b_bank[half]], writes=[b_U8])
            for gp in range(GB // NI):
                info = []
                for st in range(NI):
                    gl = gp * NI + st
                    g = g0 + gl
                    xb = 2 + st
                    yb = 6 + ((ig // 4) % 2)
                    g4 = ig % 4
                    ig += 1
                    info.append((gl, g, xb, yb, g4))
                    fw.op("pe", lambda gl=gl, xb=xb: nc.tensor.matmul(
                        C.bank[xb][:, 0:256], WXT_re[:, gl, :], U8b[:, gl, :], start=True, stop=True),
                        reads=[b_WXT, b_U8], writes=[C.b_bank[xb]], signal=False)
                    fw.op("pe", lambda gl=gl, xb=xb: nc.tensor.matmul(
                        C.bank[xb][:, 256:512], WXT_im[:, gl, :], U8b[:, gl, :], start=True, stop=True),
                        reads=[b_WXT, b_U8], writes=[C.b_bank[xb]])
                    bx = C.bank[xb]
                    S0 = ST[st][0]
                    fw.op("act", lambda bx=bx, S0=S0: nc.scalar.copy(
                        S0[0:64, :, :], bx[0:64, :].rearrange("p (a c) -> p a c", a=2)),
                        reads=[C.b_bank[xb]], writes=[b_ST[st][0]])
                    fw.op("act", lambda bx=bx, S0=S0: nc.scalar.copy(S0[64:128, 0, :], bx[64:128, 255::-1]),
                          reads=[C.b_bank[xb]], writes=[b_ST[st][0]])
                    fw.op("act", lambda bx=bx, S0=S0: nc.scalar.copy(S0[64:128, 1, :], bx[64:128, 511:255:-1]),
                          reads=[C.b_bank[xb]], writes=[b_ST[st][0]])
                cur = 0
                for k in range(8):
                    sft = 1 << k
                    nxt = 1 - cur
                    n = 256 - sft
                    for st in range(NI):
                        gl, g, xb, yb, g4 = info[st]
                        Sc, Sn, Sx = ST[st][cur], ST[st][nxt], SS[st]
                        ar = PWR[:, 24 + k, g:g + 1]
                        ai = PWI[:, 24 + k, g:g + 1]
                        nai = NAI[:, k, g:g + 1]
                        fw.op("dve", lambda Sc=Sc, Sx=Sx, ar=ar, sft=sft, n=n: dve.scalar_tensor_tensor(
                            Sx[:, :, sft:256], Sc[:, :, 0:n], ar, Sc[:, :, sft:256], ALU.mult, ALU.add),
                            reads=[b_ST[st][cur], b_PW], writes=[b_SS[st]])
                        fw.op("dve", lambda Sc=Sc, Sn=Sn, Sx=Sx, nai=nai, sft=sft, n=n: dve.scalar_tensor_tensor(
                            Sn[:, 0, sft:256], Sc[:, 1, 0:n], nai, Sx[:, 0, sft:256], ALU.mult, ALU.add),
                            reads=[b_ST[st][cur], b_SS[st], b_PW], writes=[b_ST[st][nxt]])
                        fw.op("dve", lambda Sc=Sc, Sn=Sn, Sx=Sx, ai=ai, sft=sft, n=n: dve.scalar_tensor_tensor(
                            Sn[:, 1, sft:256], Sc[:, 0, 0:n], ai, Sx[:, 1, sft:256], ALU.mult, ALU.add),
                            reads=[b_ST[st][cur], b_SS[st], b_PW], writes=[b_ST[st][nxt]])
                        fw.op("act", lambda Sc=Sc, Sn=Sn, sft=sft: nc.scalar.copy(Sn[:, :, 0:sft], Sc[:, :, 0:sft]),
                              reads=[b_ST[st][cur]], writes=[b_ST[st][nxt]])
                    cur = nxt
                for st in range(NI):
                    gl, g, xb, yb, g4 = info[st]
                    Sf = ST[st][cur]
                    Es = E[st]
                    fw.op("act", lambda Sf=Sf, Es=Es: nc.scalar.copy(Es[0:64, :, 1:128], Sf[0:64, :, 0:127]),
                          reads=[b_ST[st][cur]], writes=[b_E[st]])
                    fw.op("pool", lambda Sf=Sf, Es=Es: nc.gpsimd.tensor_copy(Es[64:128, 0, :], Sf[64:128, 0, 254:126:-1]),
                          reads=[b_ST[st][cur]], writes=[b_E[st]])
                    fw.op("pool", lambda Sf=Sf, Es=Es: nc.gpsimd.tensor_copy(Es[64:128, 1, :], Sf[64:128, 1, 254:126:-1]),
                          reads=[b_ST[st][cur]], writes=[b_E[st]])
                    yo = C.bank[yb][:, g4 * 128:(g4 + 1) * 128]
                    fw.op("pe", lambda gl=gl, yo=yo: nc.tensor.matmul(yo, U8b[:, gl, 0:128], TT[:, gl, :], start=True, stop=False),
                          reads=[b_U8, b_TT], writes=[C.b_bank[yb]], signal=False)
                    fw.op("pe", lambda gl=gl, yo=yo, Es=Es: nc.tensor.matmul(yo, Es[:, 0, :], CO_re[:, gl, :], start=False, stop=False),
                          reads=[b_E[st], b_CO], writes=[C.b_bank[yb]], signal=False)
                    fw.op("pe", lambda gl=gl, yo=yo, Es=Es: nc.tensor.matmul(yo, Es[:, 1, :], CO_imn[:, gl, :], start=False, stop=True),
                          reads=[b_E[st], b_CO, b_U8, b_TT], writes=[C.b_bank[yb]])
                    if g4 == 3:
                        gbase = g - 3
                        fw.op("dve", lambda yb=yb, gbase=gbase: dve.tensor_copy(
                            Yown[:, :, gbase * 16:(gbase + 4) * 16].rearrange("c j (g p) -> c g j p", g=4),
                            C.bank[yb][:].rearrange("c (g j p) -> c g j p", g=4, j=8)),
                            reads=[C.b_bank[yb]], writes=[b_Y])
        for j in range(8):
            fw.op("pool", lambda j=j: nc.gpsimd.tensor_tensor(
                T1[:, 0:1024].rearrange("c (g p) -> c g p", g=64), dB[:].rearrange("c (g p) -> c g p", g=64),
                U2[:, 0, :, j * 16:(j + 1) * 16], ALU.mult),
                  reads=[b_dB, b_U2], writes=[b_T1])
            fw.op("dve", lambda j=j: dve.tensor_tensor(Yown[:, j, :], Yown[:, j, :], T1[:, 0:1024], ALU.add),
                  reads=[b_T1, b_Y], writes=[b_Y])
        fw.barrier()


GELU_K0 = 0.7978845608028654
GELU_K1 = 0.7978845608028654 * 0.044715


def post_phase(fw, nc, C, dram, Yown, b_Y, aT, b_aT, x1_d, b_x1d, y_d, b_yd, x2_d, b_x2d):
    dve = nc.vector
    with ExitStack() as es:
        sT = fw.sb(es, "p_sT", [128, 8, TOWN], BF16); b_sT = Buf()
        with ExitStack() as es1:
            wglu = fw.sb(es1, "p_wglu", [128, 8, 1024], BF16); b_wglu = Buf()
            wgv = dram["ssm_w_glu"].rearrange("(k p) m -> p k m", p=128)
            for h2 in range(2):
                fw.dma("pool", wglu[:, :, h2 * 512:(h2 + 1) * 512], wgv[:, :, h2 * 512:(h2 + 1) * 512], writes=[b_wglu])
            bglu = fw.sb(es1, "p_bglu", [128, 1024], F32); b_bg = Buf()
            outg = fw.sb(es1, "p_outg", [128, 1024], F32)
            fw.dma("sp", bglu[:], dram["ssm_b_glu"][0:1, :].partition_broadcast(128), writes=[b_bg])
            fw.dma("sp", outg[:], dram["ssm_out_g"][0:1, :].partition_broadcast(128), writes=[b_bg])
            tas = [fw.sb(es1, f"p_ta{i}", [128, 1024], F32) for i in range(2)]; b_tas = [Buf(), Buf()]
            tbs = [fw.sb(es1, f"p_tb{i}", [128, 1024], F32) for i in range(2)]; b_tbs = [Buf(), Buf()]
            g1s = [fw.sb(es1, f"p_g1{i}", [128, 1024], F32) for i in range(2)]; b_g1s = [Buf(), Buf()]
            g1bs = [fw.sb(es1, f"p_g1b{i}", [128, 1024], BF16) for i in range(2)]; b_g1bs = [Buf(), Buf()]
            g1Ts = [fw.sb(es1, f"p_g1T{i}", [128, 8, 128], BF16) for i in range(2)]; b_g1Ts = [Buf(), Buf()]
            ss = fw.sb(es1, "p_ss", [128, 16], F32); b_ss = Buf()
            rs = fw.sb(es1, "p_rs", [128, 16], F32); b_rs = Buf()
            for j in range(8):
                pj = j % 2
                ta, b_ta, tb, b_tb, g1, b_g1 = tas[pj], b_tas[pj], tbs[pj], b_tbs[pj], g1s[pj], b_g1s[pj]
                g1b, b_g1b, g1T, b_g1T = g1bs[pj], b_g1bs[pj], g1Ts[pj], b_g1Ts[pj]
                bk0 = 4 * pj
                y = Yown[:, j, :]
                fw.op("act", lambda y=y, ta=ta: nc.scalar.activation(ta[:], y, AF.Square), reads=[b_Y], writes=[b_ta])
                fw.op("dve", lambda ta=ta: dve.tensor_scalar(ta[:], ta[:], GELU_K1, GELU_K0, ALU.mult, ALU.add),
                      reads=[b_ta], writes=[b_ta])
                fw.op("dve", lambda y=y, ta=ta: dve.tensor_tensor(ta[:], ta[:], y, ALU.mult), reads=[b_ta, b_Y], writes=[b_ta])
                fw.op("act", lambda ta=ta, tb=tb: nc.scalar.activation(tb[:], ta[:], AF.Sigmoid, scale=2.0),
                      reads=[b_ta], writes=[b_tb])
                fw.op("dve", lambda y=y, tb=tb, g1=g1: dve.tensor_tensor(g1[:], y, tb[:], ALU.mult),
                      reads=[b_tb, b_Y], writes=[b_g1])
                fw.op("pool", lambda g1=g1, g1b=g1b: nc.gpsimd.tensor_copy(g1b[:], g1[:]), reads=[b_g1], writes=[b_g1b])
                pst = C.bank[bk0][:].bitcast(BF16)
                for kf in range(8):
                    fw.op("pe", lambda kf=kf, pst=pst, g1b=g1b: nc.tensor.transpose(
                        pst[:, kf * 128:(kf + 1) * 128], g1b[:, kf * 128:(kf + 1) * 128], C.ident[:]),
                        reads=[b_g1b, C.b_ident], writes=[C.b_bank[bk0]], signal=(kf == 7))
                fw.op("act", lambda pst=pst, g1T=g1T: nc.scalar.copy(g1T[:].rearrange("p k c -> p (k c)"), pst),
                      reads=[C.b_bank[bk0]], writes=[b_g1T])
                for n in range(2):
                    bk = bk0 + 2 + n
                    for kf in range(8):
                        fw.op("pe", lambda kf=kf, n=n, bk=bk, g1T=g1T: nc.tensor.matmul(
                            C.bank[bk][:], g1T[:, kf, :], wglu[:, kf, n * 512:(n + 1) * 512],
                            start=(kf == 0), stop=(kf == 7)),
                            reads=[b_g1T, b_wglu], writes=[C.b_bank[bk]], signal=(kf == 7))
                    fw.op("dve", lambda n=n, bk=bk, ta=ta: dve.tensor_tensor(
                        ta[:, n * 512:(n + 1) * 512], C.bank[bk][:], bglu[:, n * 512:(n + 1) * 512], ALU.add),
                        reads=[C.b_bank[bk], b_bg], writes=[b_ta])
                fw.op("act", lambda ta=ta, tb=tb: nc.scalar.activation(tb[:], ta[:], AF.Sigmoid), reads=[b_ta], writes=[b_tb])
                fw.op("dve", lambda g1=g1, tb=tb: dve.tensor_tensor(g1[:], g1[:], tb[:], ALU.mult),
                      reads=[b_tb, b_g1], writes=[b_g1])
                rms_stats(fw, nc, g1[:], b_g1, ta[:], b_ta, ss[:, j:j + 1], b_ss, rs[:, j:j + 1], b_rs, 1024)
                fw.op("dve", lambda j=j, g1=g1, g1b=g1b: dve.scalar_tensor_tensor(
                    g1b[:], g1[:], rs[:, j:j + 1], outg[:], ALU.mult, ALU.mult),
                    reads=[b_g1, b_rs, b_bg], writes=[b_g1b])
                pst2 = C.bank[bk0 + 1][:].bitcast(BF16)
                for kf in range(8):
                    fw.op("pe", lambda kf=kf, pst2=pst2, g1b=g1b: nc.tensor.transpose(
                        pst2[:, kf * 128:(kf + 1) * 128], g1b[:, kf * 128:(kf + 1) * 128], C.ident[:]),
                        reads=[b_g1b, C.b_ident], writes=[C.b_bank[bk0 + 1]], signal=(kf == 7))
                fw.op("act", lambda pst2=pst2, j=j: nc.scalar.copy(
                    sT[:, :, j * 128:(j + 1) * 128], pst2.rearrange("p (k c) -> p k c", k=8)),
                    reads=[C.b_bank[bk0 + 1]], writes=[b_sT])
            fw.barrier()
        wo = [fw.sb(es, f"p_wo{i}", [128, KD, 512], BF16) for i in range(2)]
        b_wo = [Buf(), Buf()]
        yev = [fw.sb(es, f"p_yev{i}", [128, 512], F32) for i in range(4)]
        b_yev = [Buf() for _ in range(4)]
        wov = dram["w_out"].rearrange("(k p) m -> p k m", p=128)
        iy = 0
        fw._deps("sp", (), [b_yd])
        for n in range(4):
            s = n % 2
            fw.dma("pool", wo[s][:], wov[:, :, n * 512:(n + 1) * 512], writes=[b_wo[s]])
            for j in range(8):
                for k in range(KD):
                    if k < 8:
                        lhsT = aT[:, k, j:j + 1017:8]
                    else:
                        lhsT = sT[:, k - 8, j * 128:(j + 1) * 128]
                    fw.op("pe", lambda k=k, j=j, s=s, lhsT=lhsT: nc.tensor.matmul(
                        C.bank[j][:], lhsT, wo[s][:, k, :], start=(k == 0), stop=(k == KD - 1)),
                        reads=[b_aT, b_sT, b_wo[s]], writes=[C.b_bank[j]], signal=(k == KD - 1))
                ys = iy % 4
                iy += 1
                if j % 2 == 0:
                    fw.op("act", lambda j=j, ys=ys: nc.scalar.copy(yev[ys][:], C.bank[j][:]),
                          reads=[C.b_bank[j]], writes=[b_yev[ys]])
                else:
                    fw.op("dve", lambda j=j, ys=ys: dve.tensor_copy(yev[ys][:], C.bank[j][:]),
                          reads=[C.b_bank[j]], writes=[b_yev[ys]])
                fw.dma("sp", y_d[j * 128:(j + 1) * 128, n * 512:(n + 1) * 512], yev[ys][:],
                       reads=[b_yev[ys]], wr_only=[b_yd])
        gpost = fw.sb(es, "p_gpost", [128, D], F32); b_gp = Buf()
        fw.dma("sp", gpost[:], dram["mix_post_g"][0:1, :].partition_broadcast(128), writes=[b_gp])
        xts = [fw.sb(es, f"p_xt{i}", [128, D], F32) for i in range(2)]; b_xts = [Buf(), Buf()]
        yts = [fw.sb(es, f"p_yt{i}", [128, D], F32) for i in range(2)]; b_yts = [Buf(), Buf()]
        junks = [fw.sb(es, f"p_junk{i}", [128, D], BF16) for i in range(2)]; b_junks = [Buf(), Buf()]
        ss2 = fw.sb(es, "p_ss2", [128, 16], F32); b_ss2 = Buf()
        rs2 = fw.sb(es, "p_rs2", [128, 16], F32); b_rs2 = Buf()
        for j in range(8):
            pj = j % 2
            xt, b_xt, yt, b_yt, junk, b_junk = xts[pj], b_xts[pj], yts[pj], b_yts[pj], junks[pj], b_junks[pj]
            fw.dma("sp", yt[:], y_d[j * 128:(j + 1) * 128, :], reads=[b_yd], writes=[b_yt])
            fw.dma("sp", xt[:], x1_d[j:j + 1017:8, :], reads=[b_x1d], writes=[b_xt])
            rms_stats(fw, nc, yt[:], b_yt, junk[:], b_junk, ss2[:, j:j + 1], b_ss2, rs2[:, j:j + 1], b_rs2, D)
            fw.op("dve", lambda j=j, yt=yt: dve.scalar_tensor_tensor(yt[:], yt[:], rs2[:, j:j + 1], gpost[:], ALU.mult, ALU.mult),
                  reads=[b_yt, b_rs2, b_gp], writes=[b_yt])
            fw.op("pool", lambda yt=yt, xt=xt: nc.gpsimd.tensor_tensor(yt[:], yt[:], xt[:], ALU.add),
                  reads=[b_yt, b_xt], writes=[b_yt])
            fw.dma("pool", x2_d[j:j + 1017:8, :], yt[:], reads=[b_yt], wr_only=[b_x2d])
        fw.barrier()

def build(stage="full"):
    nc = bass.Bass("TRN2", target_bir_lowering=False)
    dram = {}

    def din(name, shape, dt=F32):
        dram[name] = nc.dram_tensor(name, list(shape), dt, kind="ExternalInput").ap()

    din("xs", [S, D])
    din("c_ident", [128, 128])
    for p in ("ff1", "ff2"):
        din(p + "_pre_g", [1, D]); din(p + "_post_g", [1, D])
        din(p + "_w_gate", [D, DFF]); din(p + "_w_up", [D, DFF]); din(p + "_w_down", [DFF, D])
    din("c_alibi", [128, 3072])
    din("mix_pre_g", [1, D]); din("mix_post_g", [1, D])
    din("w_in", [D, 4096]); din("w_out", [D, D])
    for nm in ("lam_q1", "lam_k1", "lam_q2", "lam_k2"):
        din(nm, [1, 64])
    din("attn_head_g", [1, 128])
    din("c_exps", [128, 33]); din("c_maskF", [128, 128]); din("c_maskB", [128, 128])
    din("ssm_lam_re", [2, 64, 64]); din("ssm_lam_im", [2, 64, 64]); din("ssm_log_dt", [2, 64])
    din("ssm_b_re", [2, 64, 64, 16]); din("ssm_b_im", [2, 64, 64, 16])
    din("ssm_c_re", [2, 64, 16, 64]); din("ssm_c_im", [2, 64, 16, 64])
    din("ssm_d", [1, 1024])
    u_d = nc.dram_tensor("u_d", [128, 2, 64, 128], BF16, kind="Internal").ap()
    b_ud = Buf()
    din("ssm_w_glu", [1024, 1024]); din("ssm_b_glu", [1, 1024]); din("ssm_out_g", [1, 1024])
    x2_d = nc.dram_tensor("x2_d", [TOWN, D], F32, kind="Internal").ap()
    b_x2d = Buf()
    if stage == "s5":
        dbg_Y = nc.dram_tensor("dbg_Y", [128, 8, 1024], F32, kind="ExternalOutput").ap()
    out = nc.dram_tensor("out", [TOWN, D], F32, kind="ExternalOutput").ap()
    if stage == "attn":
        dbg_aT = nc.dram_tensor("dbg_aT", [128, 8, TOWN], BF16, kind="ExternalOutput").ap()
    x1_d = nc.dram_tensor("x1_d", [S, D], F32, kind="Internal").ap()
    y_d = nc.dram_tensor("y_d", [TOWN, D], F32, kind="Internal").ap()
    b_x1d, b_yd, b_xs, b_out = Buf(), Buf(), Buf(), Buf()

    with ExitStack() as es:
        fw = FW(nc, es)
        C = Ctx()
        setup_consts(fw, nc, C, es, dram)
        if stage == "ffn":
            with ExitStack() as pes:
                A = ffn_alloc(fw, nc, pes)
                ffn_pass(fw, nc, C, A, dram["xs"][0:TOWN, :], out, dram["ff1_w_gate"], dram["ff1_w_up"],
                         dram["ff1_w_down"], dram["ff1_pre_g"], dram["ff1_post_g"], y_d, b_yd, b_xs, b_out)
                fw.barrier()
        if stage == "attn":
            with ExitStack() as mes:
                aT = fw.sb(mes, "m_aT", [128, 8, TOWN], BF16); b_aT = Buf()
                AA = attention_alloc(fw, nc, mes)
                with ExitStack() as mes3:
                    M = mixer_common_alloc(fw, nc, mes3)
                    M.aT, M.b_aT = aT, b_aT
                    build_hmT(fw, nc, C, M, dram["xs"], b_xs, dram["mix_pre_g"])
                    attention_proj(fw, nc, C, M, AA, dram, u_d, b_ud)
                attention_core(fw, nc, C, M, AA, dram)
                fw.dma("sp", dbg_aT[:, :, :], M.aT[:], reads=[M.b_aT], writes=[b_out])
                fw.barrier()
        if stage == "s5":
            with ExitStack() as mes:
                Yown = fw.sb(mes, "m_Yown", [128, 8, 1024], F32); b_Y = Buf()
                with ExitStack() as mes2:
                    M = mixer_common_alloc(fw, nc, mes2)
                    build_hmT(fw, nc, C, M, dram["xs"], b_xs, dram["mix_pre_g"])
                    attention_proj(fw, nc, C, M, None, dram, u_d, b_ud, only_u=True)
                s5_phase(fw, nc, C, dram, Yown, b_Y, u_d, b_ud)
                fw.dma("sp", dbg_Y[:, :, :], Yown[:], reads=[b_Y], writes=[b_out])
                fw.barrier()
        if stage in ("full", "mix"):
            if stage == "full":
                with ExitStack() as pes:
                    A = ffn_alloc(fw, nc, pes)
                    for ps_ in range(2):
                        ffn_pass(fw, nc, C, A, dram["xs"][ps_ * TOWN:(ps_ + 1) * TOWN, :],
                                 x1_d[ps_ * TOWN:(ps_ + 1) * TOWN, :], dram["ff1_w_gate"], dram["ff1_w_up"],
                                 dram["ff1_w_down"], dram["ff1_pre_g"], dram["ff1_post_g"], y_d, b_yd, b_xs, b_x1d)
                    fw.barrier()
                x1_src = x1_d
            else:
                x1_src = dram["xs"]
            with ExitStack() as mes:
                aT = fw.sb(mes, "m_aT", [128, 8, TOWN], BF16); b_aT = Buf()
                with ExitStack() as mes2:
                    AA = attention_alloc(fw, nc, mes2)
                    with ExitStack() as mes3:
                        M = mixer_common_alloc(fw, nc, mes3)
                        M.aT, M.b_aT = aT, b_aT
                        build_hmT(fw, nc, C, M, x1_src, b_x1d, dram["mix_pre_g"])
                        attention_proj(fw, nc, C, M, AA, dram, u_d, b_ud)
                    attention_core(fw, nc, C, M, AA, dram)
                Yown = fw.sb(mes, "m_Yown", [128, 8, 1024], F32); b_Y = Buf()
                s5_phase(fw, nc, C, dram, Yown, b_Y, u_d, b_ud)
                post_phase(fw, nc, C, dram, Yown, b_Y, aT, b_aT, x1_src, b_x1d, y_d, b_yd, x2_d, b_x2d)
            if stage == "full":
                with ExitStack() as pes:
                    A = ffn_alloc(fw, nc, pes)
                    ffn_pass(fw, nc, C, A, x2_d, out, dram["ff2_w_gate"], dram["ff2_w_up"],
                             dram["ff2_w_down"], dram["ff2_pre_g"], dram["ff2_post_g"], y_d, b_yd, b_x2d, b_out)
                    fw.barrier()
            else:
                with ExitStack() as pes:
                    xt = fw.sb(pes, "o_xt", [128, D], F32); b_xt = Buf()
                    for tt in range(8):
                        fw.dma("sp", xt[:], x2_d[tt * 128:(tt + 1) * 128, :], reads=[b_x2d], writes=[b_xt])
                        fw.dma("sp", out[tt * 128:(tt + 1) * 128, :], xt[:], reads=[b_xt], writes=[b_out])
                    fw.barrier()
        fw.barrier(engines=("sp",))
    return nc


def common_inputs(inp):
    m = {}
    m["c_ident"] = np.eye(128, dtype=np.float32)
    jj = np.arange(128)[:, None]
    mm = np.arange(3072)[None, :]
    m["c_alibi"] = np.abs(mm - jj - 1920).astype(np.float32)
    ex = np.zeros((128, 33), np.float32)
    j8 = np.arange(8)
    ex[:64, 0:8] = j8 + 1; ex[:64, 8:16] = 7 - j8; ex[:64, 16:24] = -1 - j8
    ex[64:, 0:8] = 8 - j8; ex[64:, 8:16] = j8; ex[64:, 16:24] = j8 - 8
    ex[:, 24:32] = 8 * (2 ** j8); ex[:, 32] = 1
    m["c_exps"] = ex
    jrow = (np.arange(128) // 16)[:, None]
    jcol = (np.arange(128) // 16)[None, :]
    m["c_maskF"] = (jcol >= jrow).astype(np.float32)
    m["c_maskB"] = (jcol <= jrow).astype(np.float32)
    m["ssm_d"] = np.ascontiguousarray(np.asarray(inp["ssm_d"], dtype=np.float32).reshape(1, -1))
    for p in ("ff1", "ff2"):
        for n in ("_pre_g", "_post_g"):
            m[p + n] = np.ascontiguousarray(np.asarray(inp[p + n], dtype=np.float32).reshape(1, -1))
        for n in ("_w_gate", "_w_up", "_w_down"):
            m[p + n] = np.ascontiguousarray(np.asarray(inp[p + n], dtype=np.float32)[0])
    for n in ("mix_pre_g", "mix_post_g", "lam_q1", "lam_k1", "lam_q2", "lam_k2", "attn_head_g"):
        m[n] = np.ascontiguousarray(np.asarray(inp[n], dtype=np.float32).reshape(1, -1))
    m["ssm_w_glu"] = np.ascontiguousarray(np.asarray(inp["ssm_w_glu"], dtype=np.float32)[0])
    for n in ("ssm_b_glu", "ssm_out_g"):
        m[n] = np.ascontiguousarray(np.asarray(inp[n], dtype=np.float32).reshape(1, -1))
    m["w_in"] = np.ascontiguousarray(np.asarray(inp["w_in"], dtype=np.float32)[0])
    m["w_out"] = np.ascontiguousarray(np.asarray(inp["w_out"], dtype=np.float32)[0])
    return m


SSM_KEYS = ("ssm_lam_re", "ssm_lam_im", "ssm_log_dt", "ssm_b_re", "ssm_b_im", "ssm_c_re", "ssm_c_im")


def ssm_inputs(inp, r):
    m = {}
    for k in SSM_KEYS:
        a = np.asarray(inp[k], dtype=np.float32)[0]
        if r == 1:
            a = a[::-1]
        m[k] = np.ascontiguousarray(a)
    return m


_NC_CACHE = {}


def kernel(**inputs):
    x = np.asarray(inputs["x"], dtype=np.float32)
    B = x.shape[0]
    common = common_inputs(inputs)
    ssm = [ssm_inputs(inputs, r) for r in range(2)]
    in_maps = []
    for core in range(8):
        b, r = core // 2, core % 2
        m = dict(common)
        m.update(ssm[r])
        xs = x[b] if r == 0 else x[b][::-1]
        m["xs"] = np.ascontiguousarray(xs)
        in_maps.append(m)
    if "full" not in _NC_CACHE:
        _NC_CACHE["full"] = build("full")
    nc = _NC_CACHE["full"]
    res = run_bass_kernel_spmd(nc, in_maps, core_ids=list(range(8)))
    out = np.empty((B, S, D), dtype=np.float32)
    for core in range(8):
        b, r = core // 2, core % 2
        o = np.asarray(res.results[core]["out"], dtype=np.float32)
        if r == 0:
            out[b, :TOWN] = o
        else:
            out[b, TOWN:] = o[::-1]
    return out
```

```python
import numpy as np
from contextlib import ExitStack
import concourse.bass as bass
import concourse.mybir as mybir
from concourse.bass_utils import run_bass_kernel_spmd

F32 = mybir.dt.float32
BF16 = mybir.dt.bfloat16
I32 = mybir.dt.int32
AF = mybir.ActivationFunctionType
ALU = mybir.AluOpType

D = 2048
S = 2048
TOWN = 1024
DFF = 5632
NFF = DFF // 128
KD = D // 128
EPS = 1e-6
NH = 8
NG = 64


class Buf:
    __slots__ = ("name", "w", "r")

    def __init__(self, name=""):
        self.name = name
        self.w = {}
        self.r = {}


def _merge(d, ev):
    sem, val = ev
    k = id(sem)
    if k not in d or d[k][1] < val:
        d[k] = (sem, val)


class FW:
    NDMA = 8

    def __init__(self, nc, es):
        self.nc = nc
        self.es = es
        self.eng = {"pe": nc.tensor, "act": nc.scalar, "dve": nc.vector,
                    "pool": nc.gpsimd, "sp": nc.sync}
        self.csem = {}
        self.ccnt = {}
        for k in ("pe", "act", "dve", "pool"):
            self.csem[k] = es.enter_context(nc.semaphore("c_" + k))
            self.ccnt[k] = 0
        self.dsem = {}
        self.dcnt = {}
        self.di = {}
        for q in ("sp", "act", "pool"):
            self.dsem[q] = [es.enter_context(nc.semaphore(f"d_{q}{i}")) for i in range(self.NDMA)]
            self.dcnt[q] = [0] * self.NDMA
            self.di[q] = 0
        self.waited = {k: {} for k in self.eng}

    def sb(self, es, name, shape, dt):
        self.nalloc = getattr(self, "nalloc", 0) + 1
        return es.enter_context(self.nc.sbuf_tensor(f"{name}_{self.nalloc}", list(shape), dt))

    def ps(self, es, name, shape, dt):
        self.nalloc = getattr(self, "nalloc", 0) + 1
        return es.enter_context(self.nc.psum_tensor(f"{name}_{self.nalloc}", list(shape), dt))

    def _wait(self, ek, ev):
        sem, val = ev
        key = id(sem)
        d = self.waited[ek]
        if d.get(key, 0) >= val:
            return
        d[key] = val
        self.eng[ek].wait_ge(sem, val)

    def _deps(self, ek, reads, writes):
        best = {}
        for b in reads:
            for ev in b.w.values():
                _merge(best, ev)
        for b in writes:
            for ev in b.w.values():
                _merge(best, ev)
            for ev in b.r.values():
                _merge(best, ev)
        for ev in best.values():
            self._wait(ek, ev)

    def _commit(self, ev, reads, writes):
        for b in reads:
            _merge(b.r, ev)
        for b in writes:
            _merge(b.w, ev)

    def op(self, ek, fn, reads=(), writes=(), signal=True):
        self._deps(ek, reads, writes)
        ins = fn()
        if signal:
            self.ccnt[ek] += 1
            ins.then_inc(self.csem[ek], 1)
            ev = (self.csem[ek], self.ccnt[ek])
            self._commit(ev, reads, writes)
            return ev
        return None

    def dma(self, q, out, in_, reads=(), writes=(), wr_only=(), **kw):
        i = self.di[q]
        self.di[q] = (i + 1) % self.NDMA
        sem = self.dsem[q][i]
        if self.dcnt[q][i] > 0:
            self._wait(q, (sem, self.dcnt[q][i]))
        self._deps(q, reads, writes)
        ins = self.eng[q].dma_start(out=out, in_=in_, **kw)
        self.dcnt[q][i] += 16
        ins.then_inc(sem, 16)
        ev = (sem, self.dcnt[q][i])
        self._commit(ev, reads, list(writes) + list(wr_only))
        return ev

    def all_events(self):
        evs = []
        for k in self.csem:
            if self.ccnt[k] > 0:
                evs.append((self.csem[k], self.ccnt[k]))
        for q in self.dsem:
            for i in range(self.NDMA):
                if self.dcnt[q][i] > 0:
                    evs.append((self.dsem[q][i], self.dcnt[q][i]))
        return evs

    def barrier(self, engines=("pe", "act", "dve", "pool", "sp")):
        evs = self.all_events()
        for ek in engines:
            for ev in evs:
                self._wait(ek, ev)


def bc_last(t, off, nparts, mid, last, pstride, mid_stride=1):
    return bass.AP(t, off, [[pstride, nparts], [mid_stride, mid], [0, last]])


class Ctx:
    pass


def setup_consts(fw, nc, C, es, dram):
    C.ident = fw.sb(es, "ident", [128, 128], BF16)
    C.b_ident = Buf()
    C.identf = fw.sb(es, "identf", [128, 128], F32)
    C.b_identf = Buf()
    fw.dma("sp", C.identf[:], dram["c_ident"][:, :], writes=[C.b_identf])
    fw.op("dve", lambda: nc.vector.tensor_copy(C.ident[:], C.identf[:]), reads=[C.b_identf], writes=[C.b_ident])
    C.ones = fw.sb(es, "ones", [128, 128], BF16)
    C.b_ones = Buf()
    fw.op("dve", lambda: nc.vector.memset(C.ones[:], 1.0), writes=[C.b_ones])
    C.bank = [fw.ps(es, f"bank{i}", [128, 512], F32) for i in range(8)]
    C.b_bank = [Buf(f"bank{i}") for i in range(8)]


def rms_stats(fw, nc, src_tile, b_src, junk, b_junk, ss_col, b_ss, rs_col, b_rs, n, mult=1.0):
    fw.op("act", lambda: nc.scalar.activation(junk, src_tile, AF.Square, accum_out=ss_col),
          reads=[b_src], writes=[b_junk, b_ss])
    fw.op("act", lambda: nc.scalar.activation(rs_col, ss_col, AF.Sqrt, bias=EPS, scale=1.0 / n),
          reads=[b_ss], writes=[b_rs])
    fw.op("dve", lambda: nc.vector.reciprocal(rs_col, rs_col), reads=[b_rs], writes=[b_rs])
    if mult != 1.0:
        fw.op("dve", lambda: nc.vector.tensor_scalar(rs_col, rs_col, float(mult), None, ALU.mult),
              reads=[b_rs], writes=[b_rs])


def load_gT(fw, nc, gT, b_gT, g_dram):
    with nc.allow_non_contiguous_dma("tiny gain transpose load"):
        fw.dma("sp", gT[:], g_dram[0, :].rearrange("(k p) -> p k", p=128), writes=[b_gT])


def norm_transpose(fw, nc, C, xt, b_xt, hb, b_hb, ss_col, b_ss, rs_col, b_rs, gT, b_gT, hT, b_hT, col0, ncols=128,
                   col_step=1):
    rms_stats(fw, nc, xt, b_xt, hb, b_hb, ss_col, b_ss, rs_col, b_rs, D)
    fw.op("dve", lambda: nc.vector.tensor_scalar(hb, xt, rs_col, None, ALU.mult),
          reads=[b_xt, b_rs], writes=[b_hb])
    for half in range(2):
        bk = C.bank[half]
        bb = C.b_bank[half]
        pst = bk[:].bitcast(BF16)
        for k8 in range(8):
            k = half * 8 + k8
            fw.op("pe", lambda k=k, k8=k8, pst=pst: nc.tensor.transpose(
                pst[:, k8 * 128:(k8 + 1) * 128], hb[:, k * 128:(k + 1) * 128], C.ident[:]),
                reads=[b_hb, C.b_ident], writes=[bb], signal=(k8 == 7))
        src3 = pst.rearrange("p (k t) -> p k t", k=8)
        if col_step == 1:
            dst3 = hT[:, half * 8:(half + 1) * 8, col0:col0 + ncols]
        else:
            dst3 = hT[:, half * 8:(half + 1) * 8, col0:col0 + ncols * col_step:col_step]
        gb = bc_last(gT, half * 8, 128, 8, 128, KD)
        fw.op("dve", lambda src3=src3, dst3=dst3, gb=gb: nc.vector.tensor_tensor(dst3, src3, gb, ALU.mult),
              reads=[bb, b_gT], writes=[b_hT])


def ffn_alloc(fw, nc, es):
    A = Ctx()
    A.hT = fw.sb(es, "f_hT", [128, KD, TOWN], BF16); A.b_hT = Buf()
    A.actT = fw.sb(es, "f_actT", [128, NFF, TOWN], BF16); A.b_actT = Buf()
    A.NW = 2
    A.wg = [fw.sb(es, f"f_wg{i}", [128, KD, 128], BF16) for i in range(A.NW)]
    A.wu = [fw.sb(es, f"f_wu{i}", [128, KD, 128], BF16) for i in range(A.NW)]
    A.b_wg = [Buf() for _ in range(A.NW)]
    A.b_wu = [Buf() for _ in range(A.NW)]
    A.NWD = 2
    A.wd = [fw.sb(es, f"f_wd{i}", [128, 11, 512], BF16) for i in range(A.NWD)]
    A.b_wd = [Buf() for _ in range(A.NWD)]
    A.xt = [fw.sb(es, f"f_xt{i}", [128, D], F32) for i in range(2)]
    A.b_xt = [Buf() for _ in range(2)]
    A.hb = [fw.sb(es, f"f_hb{i}", [128, D], BF16) for i in range(2)]
    A.b_hb = [Buf() for _ in range(2)]
    A.ss = fw.sb(es, "f_ss", [128, 64], F32); A.b_ss = Buf()
    A.rs = fw.sb(es, "f_rs", [128, 64], F32); A.b_rs = Buf()
    A.gT = fw.sb(es, "f_gT", [128, KD], F32); A.b_gT = Buf()
    A.gpost = fw.sb(es, "f_gpost", [128, D], F32); A.b_gpost = Buf()
    A.sg = [fw.sb(es, f"f_sg{i}", [128, 512], F32) for i in range(2)]
    A.b_sg = [Buf() for _ in range(2)]
    A.yev = [fw.sb(es, f"f_yev{i}", [128, 512], F32) for i in range(4)]
    A.b_yev = [Buf() for _ in range(4)]
    A.cnt = 0
    return A


def ffn_pass(fw, nc, C, A, src, dst, wg_d, wu_d, wd_d, pre_g, post_g, y_d, b_yd, b_src, b_dst):
    NT = TOWN // 128
    load_gT(fw, nc, A.gT, A.b_gT, pre_g)
    fw.dma("sp", A.gpost[:], post_g[0:1, :].partition_broadcast(128), writes=[A.b_gpost])
    for tt in range(NT):
        s = tt % 2
        fw.dma("sp", A.xt[s][:], src[tt * 128:(tt + 1) * 128, :], reads=[b_src], writes=[A.b_xt[s]])
        col = A.cnt % 64
        A.cnt += 1
        norm_transpose(fw, nc, C, A.xt[s][:], A.b_xt[s], A.hb[s][:], A.b_hb[s],
                       A.ss[:, col:col + 1], A.b_ss, A.rs[:, col:col + 1], A.b_rs,
                       A.gT, A.b_gT, A.hT, A.b_hT, tt * 128)
    wgv = wg_d.rearrange("(k p) m -> p k m", p=128)
    wuv = wu_d.rearrange("(k p) m -> p k m", p=128)
    for f in range(NFF):
        s = f % A.NW
        fw.dma("pool", A.wg[s][:], wgv[:, :, f * 128:(f + 1) * 128], writes=[A.b_wg[s]])
        fw.dma("pool", A.wu[s][:], wuv[:, :, f * 128:(f + 1) * 128], writes=[A.b_wu[s]])
        for c in range(2):
            bi = (f % 2) * 4 + c * 2
            pg, pu = C.bank[bi], C.bank[bi + 1]
            bpg, bpu = C.b_bank[bi], C.b_bank[bi + 1]
            for k in range(KD):
                fw.op("pe", lambda k=k, pg=pg, s=s, c=c: nc.tensor.matmul(
                    pg[:], A.wg[s][:, k, :], A.hT[:, k, c * 512:(c + 1) * 512], start=(k == 0), stop=(k == KD - 1)),
                    reads=[A.b_wg[s], A.b_hT], writes=[bpg], signal=(k == KD - 1))
            for k in range(KD):
                fw.op("pe", lambda k=k, pu=pu, s=s, c=c: nc.tensor.matmul(
                    pu[:], A.wu[s][:, k, :], A.hT[:, k, c * 512:(c + 1) * 512], start=(k == 0), stop=(k == KD - 1)),
                    reads=[A.b_wu[s], A.b_hT], writes=[bpu], signal=(k == KD - 1))
            sgs = c
            fw.op("act", lambda pg=pg, sgs=sgs: nc.scalar.activation(A.sg[sgs][:], pg[:], AF.Silu),
                  reads=[bpg], writes=[A.b_sg[sgs]])
            fw.op("dve", lambda pu=pu, sgs=sgs, f=f, c=c: nc.vector.tensor_tensor(
                A.actT[:, f, c * 512:(c + 1) * 512], A.sg[sgs][:], pu[:], ALU.mult),
                reads=[A.b_sg[sgs], bpu], writes=[A.b_actT])
    wdv = wd_d.rearrange("(f p) m -> p f m", p=128)
    ig = 0
    iy = 0
    fw._deps("sp", (), [b_yd])
    for n in range(4):
        for g4 in range(4):
            s = ig % A.NWD
            ig += 1
            fw.dma("pool", A.wd[s][:], wdv[:, g4 * 11:(g4 + 1) * 11, n * 512:(n + 1) * 512], writes=[A.b_wd[s]])
            for tt in range(NT):
                for fi in range(11):
                    f = g4 * 11 + fi
                    last = (g4 == 3 and fi == 10)
                    fw.op("pe", lambda tt=tt, f=f, fi=fi, s=s, g4=g4: nc.tensor.matmul(
                        C.bank[tt][:], A.actT[:, f, tt * 128:(tt + 1) * 128], A.wd[s][:, fi, :],
                        start=(g4 == 0 and fi == 0), stop=(g4 == 3 and fi == 10)),
                        reads=[A.b_actT, A.b_wd[s]], writes=[C.b_bank[tt]], signal=(fi == 10))
        for tt in range(NT):
            ys = iy % 4
            iy += 1
            ek = "act" if tt % 2 == 0 else "dve"
            if ek == "act":
                fw.op("act", lambda tt=tt, ys=ys: nc.scalar.copy(A.yev[ys][:], C.bank[tt][:]),
                      reads=[C.b_bank[tt]], writes=[A.b_yev[ys]])
            else:
                fw.op("dve", lambda tt=tt, ys=ys: nc.vector.tensor_copy(A.yev[ys][:], C.bank[tt][:]),
                      reads=[C.b_bank[tt]], writes=[A.b_yev[ys]])
            fw.dma("sp", y_d[tt * 128:(tt + 1) * 128, n * 512:(n + 1) * 512], A.yev[ys][:],
                   reads=[A.b_yev[ys]], wr_only=[b_yd])
    act32 = A.actT[:].bitcast(F32)
    NDB = 3
    dby = [act32[:, 4 * i:4 * i + 4, :].rearrange("p a b -> p (a b)") for i in range(NDB)]
    dbx = [act32[:, 4 * (NDB + i):4 * (NDB + i) + 4, :].rearrange("p a b -> p (a b)") for i in range(NDB)]
    b_dby = [Buf() for _ in range(NDB)]
    b_dbx = [Buf() for _ in range(NDB)]
    fw._deps("sp", (), [A.b_actT])
    fw._deps("pool", (), [b_dst])
    for tt in range(NT):
        s = tt % 2
        d = tt % NDB
        yt, b_yt = dby[d], b_dby[d]
        xt, b_xt = dbx[d], b_dbx[d]
        fw.dma("sp", yt, y_d[tt * 128:(tt + 1) * 128, :], reads=[b_yd], writes=[b_yt])
        fw.dma("sp", xt, src[tt * 128:(tt + 1) * 128, :], reads=[b_src], writes=[b_xt])
        col = A.cnt % 64
        A.cnt += 1
        ssc = A.ss[:, col:col + 1]
        rsc = A.rs[:, col:col + 1]
        rms_stats(fw, nc, yt, b_yt, A.hb[s][:], A.b_hb[s], ssc, A.b_ss, rsc, A.b_rs, D, mult=0.5)
        fw.op("dve", lambda rsc=rsc, yt=yt: nc.vector.scalar_tensor_tensor(
            yt, yt, rsc, A.gpost[:], ALU.mult, ALU.mult),
            reads=[b_yt, A.b_rs, A.b_gpost], writes=[b_yt])
        fw.op("pool", lambda yt=yt, xt=xt: nc.gpsimd.tensor_tensor(yt, yt, xt, ALU.add),
              reads=[b_yt, b_xt], writes=[b_yt])
        fw.dma("pool", dst[tt * 128:(tt + 1) * 128, :], yt, reads=[b_yt], wr_only=[b_dst])
    for ev in fw.all_events():
        _merge(A.b_actT.w, ev)


SLOPES = [2.0 ** (-(h + 1)) for h in range(NH)]
LAM_INIT = 0.2


def mixer_common_alloc(fw, nc, es):
    M = Ctx()
    M.hmT = fw.sb(es, "m_hmT", [128, KD, S], BF16); M.b_hmT = Buf()
    M.es = es
    M.ss = fw.sb(es, "m_ss", [128, 64], F32); M.b_ss = Buf()
    M.rs = fw.sb(es, "m_rs", [128, 64], F32); M.b_rs = Buf()
    M.gT = fw.sb(es, "m_gT", [128, KD], F32); M.b_gT = Buf()
    M.cnt = 0
    return M


def build_hmT(fw, nc, C, M, x1_src, b_x1, g_dram):
    with ExitStack() as es:
        xt = [fw.sb(es, f"h_xt{i}", [128, D], F32) for i in range(2)]
        b_xt = [Buf() for _ in range(2)]
        hb = [fw.sb(es, f"h_hb{i}", [128, D], BF16) for i in range(2)]
        b_hb = [Buf() for _ in range(2)]
        load_gT(fw, nc, M.gT, M.b_gT, g_dram)
        for tt in range(S // 128):
            s = tt % 2
            fw.dma("sp", xt[s][:], x1_src[tt * 128:(tt + 1) * 128, :], reads=[b_x1], writes=[b_xt[s]])
            col = M.cnt % 64
            M.cnt += 1
            norm_transpose(fw, nc, C, xt[s][:], b_xt[s], hb[s][:], b_hb[s],
                           M.ss[:, col:col + 1], M.b_ss, M.rs[:, col:col + 1], M.b_rs,
                           M.gT, M.b_gT, M.hmT, M.b_hmT, tt * 128)
        fw.barrier()


def attention_alloc(fw, nc, es):
    A = Ctx()
    A.QT = [fw.sb(es, f"a_QT{c}", [128, NH, TOWN], BF16) for c in range(2)]
    A.b_QT = Buf()
    A.KT = fw.sb(es, "a_KT", [128, NH, S], BF16); A.b_KT = Buf()
    A.V = fw.sb(es, "a_V", [128, S // 128, 1024], BF16); A.b_V = Buf()
    for c in range(2):
        fw.op("pool", lambda c=c: nc.gpsimd.memset(A.QT[c][:], 0.0), writes=[A.b_QT])
    return A


def attention_proj(fw, nc, C, M, A, dram, u_d, b_ud, only_u=False):
    w_in = dram["w_in"]
    if A is not None:
        QT, KT, V, b_QT, b_KT, b_V = A.QT, A.KT, A.V, A.b_QT, A.b_KT, A.b_V
    with ExitStack() as es:
        wp = [fw.sb(es, f"a_wp{i}", [128, KD, 256], BF16) for i in range(2)]
        b_wp = [Buf() for _ in range(2)]
        wv = w_in.rearrange("(k p) m -> p k m", p=128)
        iw = 0
        ib = 0
        ust = [fw.sb(es, f"a_ust{i}", [128, 16, 8, 16], BF16) for i in range(2)]
        b_ust = [Buf() for _ in range(2)]
        iu = 0
        for blk in range(4):
            s = iw % 2
            iw += 1
            fw.dma("pool", wp[s][:], wv[:, :, 3072 + blk * 256:3072 + (blk + 1) * 256], writes=[b_wp[s]])
            for hh in range(2):
                us = iu % 2
                iu += 1
                for j in range(8):
                    bi = ib % 8
                    ib += 1
                    t0 = 1024 * hh + j
                    for k in range(KD):
                        fw.op("pe", lambda k=k, s=s, t0=t0, bi=bi: nc.tensor.matmul(
                            C.bank[bi][:, 0:256], M.hmT[:, k, t0:t0 + 1017:8], wp[s][:, k, :],
                            start=(k == 0), stop=(k == KD - 1)),
                            reads=[b_wp[s], M.b_hmT], writes=[C.b_bank[bi]], signal=(k == KD - 1))
                    src = C.bank[bi][:, 0:256].rearrange("c (g q) -> c g q", g=16)
                    if j % 2 == 0:
                        fw.op("act", lambda us=us, j=j, src=src: nc.scalar.copy(ust[us][:, :, j, :], src),
                              reads=[C.b_bank[bi]], writes=[b_ust[us]])
                    else:
                        fw.op("dve", lambda us=us, j=j, src=src: nc.vector.tensor_copy(ust[us][:, :, j, :], src),
                              reads=[C.b_bank[bi]], writes=[b_ust[us]])
                fw.dma("sp", u_d[:, hh, blk * 16:(blk + 1) * 16, :],
                       ust[us][:].rearrange("c g j q -> c g (j q)"), reads=[b_ust[us]], wr_only=[b_ud])
        if only_u:
            fw.barrier()
            return
        for blk in range(8):
            s = iw % 2
            iw += 1
            fw.dma("pool", wp[s][:], wv[:, :, blk * 256:(blk + 1) * 256], writes=[b_wp[s]])
            isq = blk < 4
            nch = 2 if isq else 4
            for h4 in range(2):
                h = (blk % 4) * 2 + h4
                for ch in range(nch):
                    bi = ib % 8
                    ib += 1
                    for k in range(KD):
                        fw.op("pe", lambda k=k, s=s, h4=h4, ch=ch, bi=bi: nc.tensor.matmul(
                            C.bank[bi][:], wp[s][:, k, h4 * 128:(h4 + 1) * 128], M.hmT[:, k, ch * 512:(ch + 1) * 512],
                            start=(k == 0), stop=(k == KD - 1)),
                            reads=[b_wp[s], M.b_hmT], writes=[C.b_bank[bi]], signal=(k == KD - 1))
                    if isq:
                        for c in range(2):
                            fw.op("act", lambda h=h, ch=ch, bi=bi, c=c: nc.scalar.mul(
                                QT[c][64 * c:64 * c + 64, h, ch * 512:(ch + 1) * 512],
                                C.bank[bi][64 * c:64 * c + 64, :], 0.125),
                                reads=[C.b_bank[bi]], writes=[b_QT])
                    else:
                        fw.op("dve", lambda h=h, ch=ch, bi=bi: nc.vector.tensor_copy(
                            KT[:, h, ch * 512:(ch + 1) * 512], C.bank[bi][:]),
                            reads=[C.b_bank[bi]], writes=[b_KT])
        for blk in range(4):
            s = iw % 2
            iw += 1
            fw.dma("pool", wp[s][:], wv[:, :, 2048 + blk * 256:2048 + (blk + 1) * 256], writes=[b_wp[s]])
            for tt in range(S // 128):
                bi = ib % 8
                ib += 1
                for k in range(KD):
                    fw.op("pe", lambda k=k, s=s, tt=tt, bi=bi: nc.tensor.matmul(
                        C.bank[bi][:, 0:256], M.hmT[:, k, tt * 128:(tt + 1) * 128], wp[s][:, k, :],
                        start=(k == 0), stop=(k == KD - 1)),
                        reads=[b_wp[s], M.b_hmT], writes=[C.b_bank[bi]], signal=(k == KD - 1))
                if tt % 2 == 0:
                    fw.op("act", lambda tt=tt, blk=blk, bi=bi: nc.scalar.copy(
                        V[:, tt, blk * 256:(blk + 1) * 256], C.bank[bi][:, 0:256]),
                        reads=[C.b_bank[bi]], writes=[b_V])
                else:
                    fw.op("dve", lambda tt=tt, blk=blk, bi=bi: nc.vector.tensor_copy(
                        V[:, tt, blk * 256:(blk + 1) * 256], C.bank[bi][:, 0:256]),
                        reads=[C.b_bank[bi]], writes=[b_V])
        fw.barrier()


def attention_core(fw, nc, C, M, A, dram):
    QT, KT, V, b_QT, b_KT, b_V = A.QT, A.KT, A.V, A.b_QT, A.b_KT, A.b_V
    with ExitStack() as es:
        G = fw.sb(es, "a_G", [128, 3072], F32); b_G = Buf()
        fw.dma("sp", G[:], dram["c_alibi"][:, :], writes=[b_G])
        GD = [fw.sb(es, f"a_GD{i}", [128, 3072], BF16) for i in range(2)]
        b_GD = [Buf(), Buf()]
        lq = fw.sb(es, "a_lq", [128, 4, 64], F32); b_lq = Buf()
        for i, nm in enumerate(("lam_q1", "lam_k1", "lam_q2", "lam_k2")):
            fw.dma("sp", lq[:, i, :], dram[nm][0:1, :].partition_broadcast(128), writes=[b_lq])
        sc4 = fw.sb(es, "a_sc4", [128, 8], F32); b_sc4 = Buf()
        junk = fw.sb(es, "a_junk", [128, 64], F32); b_junk = Buf()
        fw.op("dve", lambda: nc.vector.scalar_tensor_tensor(junk[:], lq[:, 0, :], 1.0, lq[:, 1, :], ALU.mult, ALU.mult,
                                                            accum_out=sc4[:, 0:1]), reads=[b_lq], writes=[b_junk, b_sc4])
        fw.op("dve", lambda: nc.vector.scalar_tensor_tensor(junk[:], lq[:, 2, :], 1.0, lq[:, 3, :], ALU.mult, ALU.mult,
                                                            accum_out=sc4[:, 1:2]), reads=[b_lq], writes=[b_junk, b_sc4])
        fw.op("act", lambda: nc.scalar.activation(sc4[:, 2:4], sc4[:, 0:2], AF.Exp), reads=[b_sc4], writes=[b_sc4])
        fw.op("dve", lambda: nc.vector.tensor_tensor(sc4[:, 4:5], sc4[:, 3:4], sc4[:, 2:3], ALU.subtract),
              reads=[b_sc4], writes=[b_sc4])
        fw.op("dve", lambda: nc.vector.tensor_scalar(sc4[:, 5:6], sc4[:, 4:5], -LAM_INIT, None, ALU.add),
              reads=[b_sc4], writes=[b_sc4])
        neg_lam = sc4[:, 5:6]
        gh = fw.sb(es, "a_gh", [128, 2], F32); b_gh = Buf()
        with nc.allow_non_contiguous_dma("tiny"):
            fw.dma("sp", gh[:, 0:1], dram["attn_head_g"][0, :].rearrange("(p o) -> p o", o=1), writes=[b_gh])
        fw.op("dve", lambda: nc.vector.tensor_scalar(gh[:, 1:2], gh[:, 0:1], 1.0 - LAM_INIT, None, ALU.mult),
              reads=[b_gh], writes=[b_gh])
        scb = [fw.sb(es, f"a_scb{i}", [128, 512], BF16) for i in range(4)]
        b_scb = [Buf() for _ in range(4)]
        pT = [fw.sb(es, f"a_pT{i}", [128, 512], BF16) for i in range(6)]
        b_pT = [Buf() for _ in range(6)]
        rz = [fw.sb(es, f"a_rz{i}", [128, 512], F32) for i in range(2)]
        b_rz = [Buf() for _ in range(2)]
        ot = [fw.sb(es, f"a_ot{i}", [128, 512], F32) for i in range(2)]
        b_ot = [Buf() for _ in range(2)]
        an = fw.sb(es, "a_an", [128, 2 * NH, 512], F32); b_an = Buf()
        sqas = [fw.sb(es, f"a_sqa{i}", [128, 512], BF16) for i in range(8)]
        b_sqas = [Buf() for _ in range(8)]
        rns = [fw.sb(es, f"a_rn{i}", [128, 512], F32) for i in range(2)]; b_rns = [Buf(), Buf()]
        iters = [(h, qc, kb, c) for h in range(NH) for qc in range(2) for kb in range(S // 128) for c in range(2)]
        SB = [0, 1, 7, 6]
        DEPTH = 3
        NKB = S // 128

        def stage1(i):
            h, qc, kb, c = iters[i]
            off = 512 * qc - 128 * kb + 1920
            sb_i = SB[i % 4]
            sbf = i % 4
            pi = i % 6
            pbank = C.bank[sb_i]
            fw.op("pe", lambda: nc.tensor.matmul(
                pbank[:], KT[:, h, kb * 128:(kb + 1) * 128],
                QT[c][:, h, qc * 512:(qc + 1) * 512], start=True, stop=True),
                reads=[b_KT, b_QT], writes=[C.b_bank[sb_i]])
            if (h == 0 and qc == 0 and kb == 0 and c == 0) or (qc == 1 and kb == 0 and c == 0 and h + 1 < NH):
                hn = 0 if (h == 0 and qc == 0) else h + 1
                fw.op("act", lambda hn=hn: nc.scalar.activation(GD[hn % 2][:], G[:], AF.Exp, scale=-SLOPES[hn]),
                      reads=[b_G], writes=[b_GD[hn % 2]])
            fw.op("act", lambda: nc.scalar.activation(scb[sbf][:], pbank[:], AF.Exp),
                  reads=[C.b_bank[sb_i]], writes=[b_scb[sbf]])
            fw.op("dve", lambda: nc.vector.tensor_tensor(pT[pi][:], scb[sbf][:], GD[h % 2][:, off:off + 512], ALU.mult),
                  reads=[b_scb[sbf], b_GD[h % 2]], writes=[b_pT[pi]])

        def stage2(i):
            h, qc, kb, c = iters[i]
            pi = i % 6
            fw.op("pe", lambda: nc.tensor.matmul(
                C.bank[2 + c][:], V[:, kb, h * 128:(h + 1) * 128], pT[pi][:],
                start=(kb == 0), stop=(kb == NKB - 1)),
                reads=[b_V, b_pT[pi]], writes=[C.b_bank[2 + c]], signal=False)
            fw.op("pe", lambda: nc.tensor.matmul(
                C.bank[4 + c][:], C.ones[:], pT[pi][:],
                start=(kb == 0), stop=(kb == NKB - 1)),
                reads=[C.b_ones, b_pT[pi], b_V], writes=[C.b_bank[2 + c], C.b_bank[4 + c]])
            if kb == NKB - 1 and c == 1:
                for cc in range(2):
                    fw.op("dve", lambda cc=cc: nc.vector.reciprocal(rz[cc][:], C.bank[4 + cc][:]),
                          reads=[C.b_bank[4 + cc]], writes=[b_rz[cc]])
                    fw.op("dve", lambda cc=cc: nc.vector.tensor_tensor(ot[cc][:], C.bank[2 + cc][:], rz[cc][:], ALU.mult),
                          reads=[C.b_bank[2 + cc], b_rz[cc]], writes=[b_ot[cc]])
                u = h * 2 + qc
                fw.op("dve", lambda u=u: nc.vector.scalar_tensor_tensor(an[:, u, :], ot[1][:], neg_lam, ot[0][:], ALU.mult, ALU.add),
                      reads=[b_ot[0], b_ot[1], b_sc4], writes=[b_an])

        def head_sq(u):
            sq_, bsq_ = sqas[u % 8], b_sqas[u % 8]
            fw.op("act", lambda: nc.scalar.activation(sq_[:], an[:, u, :], AF.Square), reads=[b_an], writes=[bsq_])

        def head_norm(u):
            h, qc = u // 2, u % 2
            sq_, bsq_ = sqas[u % 8], b_sqas[u % 8]
            r_, br_ = rns[u % 2], b_rns[u % 2]
            bk = 2 + (u % 4)
            fw.op("pe", lambda: nc.tensor.matmul(C.bank[bk][:], C.ones[:], sq_[:], start=True, stop=True),
                  reads=[C.b_ones, bsq_], writes=[C.b_bank[bk]])
            fw.op("act", lambda: nc.scalar.activation(r_[:], C.bank[bk][:], AF.Sqrt, bias=EPS, scale=1.0 / 128),
                  reads=[C.b_bank[bk]], writes=[br_])
            fw.op("dve", lambda: nc.vector.reciprocal(r_[:], r_[:]), reads=[br_], writes=[br_])
            fw.op("dve", lambda: nc.vector.scalar_tensor_tensor(
                M.aT[:, h, qc * 512:(qc + 1) * 512], an[:, u, :], gh[:, 1:2], r_[:], ALU.mult, ALU.mult),
                reads=[b_an, b_gh, br_], writes=[M.b_aT])

        NI = len(iters)
        for i in range(NI + DEPTH):
            if i < NI:
                stage1(i)
            if i >= DEPTH:
                stage2(i - DEPTH)
        for u0 in range(0, 2 * NH, 8):
            for u in range(u0, u0 + 8):
                head_sq(u)
            for u in range(u0, u0 + 8):
                head_norm(u)
        fw.barrier()


TWO_PI = 6.283185307179586
PI_C = 3.1415925


def ap4(t, off, pstride, nparts, dims):
    return bass.AP(t, off, [[pstride, nparts]] + [[st, n] for st, n in dims])


def s5_phase(fw, nc, C, dram, Yown, b_Y, u_d, b_ud):
    with ExitStack() as es:
        U2 = fw.sb(es, "s_U2", [128, 2, 64, 128], BF16); b_U2 = Buf()
        for hh in range(2):
            fw.dma("sp", U2[:, hh, :, :], u_d[:, hh, :, :], reads=[b_ud], writes=[b_U2])
        NF = 33
        PWR = fw.sb(es, "s_PWR", [128, NF, 64], F32); b_PW = Buf()
        PWI = fw.sb(es, "s_PWI", [128, NF, 64], F32)
        NAI = fw.sb(es, "s_NAI", [128, 8, 64], F32)
        BBR = fw.sb(es, "s_BBR", [128, 64, 16], F32); b_BB = Buf()
        BBI = fw.sb(es, "s_BBI", [128, 64, 16], F32)
        CTR = fw.sb(es, "s_CTR", [128, 64, 16], F32); b_CT = Buf()
        CTI = fw.sb(es, "s_CTI", [128, 64, 16], F32)
        MF = fw.sb(es, "s_MF", [128, 128], F32); b_MK = Buf()
        MB = fw.sb(es, "s_MB", [128, 128], F32)
        fw.dma("sp", MF[:], dram["c_maskF"][:, :], writes=[b_MK])
        fw.dma("sp", MB[:], dram["c_maskB"][:, :], writes=[b_MK])
        dve = nc.vector
        with ExitStack() as pes:
            LL = fw.sb(pes, "s_LL", [64, 2, 128], F32); b_LL = Buf()
            for i, nm in enumerate(("ssm_lam_re", "ssm_lam_im")):
                fw.dma("sp", LL[:, i, :].rearrange("g (d n) -> g d n", d=2), dram[nm].rearrange("d g n -> g d n"),
                       writes=[b_LL])
            LRI = fw.sb(pes, "s_LRI", [128, 2, 64], F32); b_LRI = Buf()
            for i in range(2):
                fw.op("pe", lambda i=i: nc.tensor.transpose(C.bank[0][:, i * 64:(i + 1) * 64], LL[:, i, :],
                                                            C.identf[0:64, 0:64]),
                      reads=[b_LL, C.b_identf], writes=[C.b_bank[0]])
            fw.op("dve", lambda: dve.tensor_copy(LRI[:].rearrange("p a g -> p (a g)"), C.bank[0][:, 0:128]),
                  reads=[C.b_bank[0]], writes=[b_LRI])
            LR = LRI[:, 0, :]
            LI = LRI[:, 1, :]
            DT = fw.sb(pes, "s_DT", [128, 64], F32); b_DT = Buf()
            for d in range(2):
                fw.dma("sp", DT[64 * d:64 * d + 64, :], dram["ssm_log_dt"][d:d + 1, :].partition_broadcast(64),
                       writes=[b_DT])
            fw.op("act", lambda: nc.scalar.activation(DT[:], DT[:], AF.Exp), reads=[b_DT], writes=[b_DT])
            LD = fw.sb(pes, "s_LD", [128, 2, 64], F32); b_LD = Buf()
            for i in range(2):
                fw.op("dve", lambda i=i: dve.tensor_tensor(LD[:, i, :], LRI[:, i, :], DT[:], ALU.mult),
                      reads=[b_LRI, b_DT], writes=[b_LD])
            EXPS = fw.sb(pes, "s_EXPS", [128, NF], F32); b_EX = Buf()
            fw.dma("sp", EXPS[:], dram["c_exps"][:, :], writes=[b_EX])
            ANG = fw.sb(pes, "s_ANG", [128, NF, 64], F32); b_ANG = Buf()
            MAG = fw.sb(pes, "s_MAG", [128, NF, 64], F32); b_MAG = Buf()
            KF = fw.sb(pes, "s_KF", [128, NF, 64], F32); b_KF = Buf()
            KI = fw.sb(pes, "s_KI", [128, NF, 64], I32); b_KI = Buf()
            ex_b = ap4(EXPS, 0, NF, 128, [(1, NF), (0, 64)])
            lid_b = ap4(LD, 64, 128, 128, [(0, NF), (1, 64)])
            lrd_b = ap4(LD, 0, 128, 128, [(0, NF), (1, 64)])
            fw.op("dve", lambda: dve.tensor_tensor(ANG[:], lid_b, ex_b, ALU.mult), reads=[b_LD, b_EX], writes=[b_ANG])
            fw.op("dve", lambda: dve.tensor_tensor(MAG[:], lrd_b, ex_b, ALU.mult), reads=[b_LD, b_EX], writes=[b_MAG])
            fw.op("act", lambda: nc.scalar.activation(MAG[:], MAG[:], AF.Exp), reads=[b_MAG], writes=[b_MAG])

            def reduce_clamp(A, b_A):
                fw.op("dve", lambda: dve.tensor_scalar(KF[:], A[:], 1.0 / TWO_PI, None, ALU.mult),
                      reads=[b_A], writes=[b_KF])
                fw.op("dve", lambda: dve.tensor_copy(KI[:], KF[:]), reads=[b_KF], writes=[b_KI])
                fw.op("dve", lambda: dve.tensor_copy(KF[:], KI[:]), reads=[b_KI], writes=[b_KF])
                fw.op("dve", lambda: dve.scalar_tensor_tensor(A[:], KF[:], -TWO_PI, A[:], ALU.mult, ALU.add),
                      reads=[b_KF, b_A], writes=[b_A])
                fw.op("dve", lambda: dve.tensor_scalar(A[:], A[:], PI_C, -PI_C, ALU.min, ALU.max),
                      reads=[b_A], writes=[b_A])

            reduce_clamp(ANG, b_ANG)
            fw.op("act", lambda: nc.scalar.activation(PWI[:], ANG[:], AF.Sin), reads=[b_ANG], writes=[b_PW])
            fw.op("dve", lambda: dve.tensor_scalar(ANG[:], ANG[:], 1.5707963267948966, None, ALU.add),
                  reads=[b_ANG], writes=[b_ANG])
            reduce_clamp(ANG, b_ANG)
            fw.op("act", lambda: nc.scalar.activation(PWR[:], ANG[:], AF.Sin), reads=[b_ANG], writes=[b_PW])
            fw.op("dve", lambda: dve.tensor_tensor(PWR[:], PWR[:], MAG[:], ALU.mult), reads=[b_PW, b_MAG], writes=[b_PW])
            fw.op("dve", lambda: dve.tensor_tensor(PWI[:], PWI[:], MAG[:], ALU.mult), reads=[b_PW, b_MAG], writes=[b_PW])
            fw.op("dve", lambda: dve.tensor_scalar(NAI[:], PWI[:, 24:32, :], -1.0, None, ALU.mult),
                  reads=[b_PW], writes=[b_PW])
            FT = fw.sb(pes, "s_FT", [128, 8, 64], F32); b_FT = Buf()
            a_re = PWR[:, 32, :]
            a_im = PWI[:, 32, :]
            den, t2, nr, fre, fim, tt_ = (FT[:, i, :] for i in range(6))
            ops = [
                (den, LR, LR, ALU.mult), (t2, LI, LI, ALU.mult), (den, den, t2, ALU.add),
            ]
            for o, a, b, op_ in ops:
                fw.op("dve", lambda o=o, a=a, b=b, op_=op_: dve.tensor_tensor(o, a, b, op_),
                      reads=[b_LRI, b_FT, b_PW], writes=[b_FT])
            fw.op("dve", lambda: dve.reciprocal(den, den), reads=[b_FT], writes=[b_FT])
            fw.op("dve", lambda: dve.tensor_scalar(nr, a_re, -1.0, None, ALU.add), reads=[b_PW], writes=[b_FT])
            ops = [
                (fre, nr, LR, ALU.mult), (tt_, a_im, LI, ALU.mult), (fre, fre, tt_, ALU.add), (fre, fre, den, ALU.mult),
                (fim, a_im, LR, ALU.mult), (tt_, nr, LI, ALU.mult), (fim, fim, tt_, ALU.subtract),
                (fim, fim, den, ALU.mult),
            ]
            for o, a, b, op_ in ops:
                fw.op("dve", lambda o=o, a=a, b=b, op_=op_: dve.tensor_tensor(o, a, b, op_),
                      reads=[b_LRI, b_FT, b_PW], writes=[b_FT])
            BR = fw.sb(pes, "s_BR", [128, 64, 16], F32); b_BRI = Buf()
            BI = fw.sb(pes, "s_BI", [128, 64, 16], F32)
            TB_ = fw.sb(pes, "s_TB", [128, 64, 16], F32); b_TB = Buf()
            for d in range(2):
                fw.dma("sp", BR[64 * d:64 * d + 64, :, :], dram["ssm_b_re"][d].rearrange("g n q -> n g q"), writes=[b_BRI])
                fw.dma("sp", BI[64 * d:64 * d + 64, :, :], dram["ssm_b_im"][d].rearrange("g n q -> n g q"), writes=[b_BRI])
            fre_b = ap4(FT, 3 * 64, 8 * 64, 128, [(1, 64), (0, 16)])
            fim_b = ap4(FT, 4 * 64, 8 * 64, 128, [(1, 64), (0, 16)])
            seq = [
                (BBR[:], fre_b, BR[:], ALU.mult), (TB_[:], fim_b, BI[:], ALU.mult), (BBR[:], BBR[:], TB_[:], ALU.subtract),
                (BBI[:], fre_b, BI[:], ALU.mult), (TB_[:], fim_b, BR[:], ALU.mult), (BBI[:], BBI[:], TB_[:], ALU.add),
            ]
            for o, a, b, op_ in seq:
                fw.op("dve", lambda o=o, a=a, b=b, op_=op_: dve.tensor_tensor(o, a, b, op_),
                      reads=[b_FT, b_BRI, b_TB, b_BB], writes=[b_TB, b_BB])
            Cin = fw.sb(pes, "s_Cin", [128, 2, 8, 128], F32); b_Cin = Buf()
            for i, nm in enumerate(("ssm_c_re", "ssm_c_im")):
                for d in range(2):
                    fw.dma("sp", Cin[:, i, :, 64 * d:64 * d + 64],
                           dram[nm][d].rearrange("(gb g8) p n -> (g8 p) gb n", g8=8), writes=[b_Cin])
            for i, CT in enumerate((CTR, CTI)):
                for half in range(2):
                    bk = C.bank[half]
                    for q4 in range(4):
                        gb_ = half * 4 + q4
                        fw.op("pe", lambda i=i, gb_=gb_, q4=q4, bk=bk: nc.tensor.transpose(
                            bk[:, q4 * 128:(q4 + 1) * 128], Cin[:, i, gb_, :], C.identf[:]),
                            reads=[b_Cin, C.b_identf], writes=[C.b_bank[half]], signal=(q4 == 3))
                    fw.op("dve", lambda CT=CT, half=half, bk=bk: dve.tensor_copy(
                        CT[:].rearrange("p g q -> p (g q)")[:, half * 512:(half + 1) * 512], bk[:]),
                        reads=[C.b_bank[half]], writes=[b_CT])
            fw.barrier()
        GB = 8
        CO_re = fw.sb(es, "s_COre", [128, GB, 128], BF16); b_CO = Buf()
        CO_imn = fw.sb(es, "s_COim", [128, GB, 128], BF16)
        WX_re = fw.sb(es, "s_WXre", [128, GB, 128], BF16); b_WX = Buf()
        WX_im = fw.sb(es, "s_WXim", [128, GB, 128], BF16)
        W2_re = fw.sb(es, "s_W2re", [128, GB, 128], BF16); b_W2 = Buf()
        W2_im = fw.sb(es, "s_W2im", [128, GB, 128], BF16)
        WXT_re = fw.sb(es, "s_WXTre", [128, GB, 128], BF16); b_WXT = Buf()
        WXT_im = fw.sb(es, "s_WXTim", [128, GB, 128], BF16)
        TT = fw.sb(es, "s_TT", [128, GB, 128], BF16); b_TT = Buf()
        T1 = fw.sb(es, "s_T1", [128, GB * 128], F32); b_T1 = Buf()
        T2 = fw.sb(es, "s_T2", [128, GB * 128], F32); b_T2 = Buf()
        U8b = fw.sb(es, "s_U8b", [128, GB, 256], BF16); b_U8 = Buf()
        NI = 4
        ST = [[fw.sb(es, f"s_ST{a_}{b_}", [128, 2, 256], F32) for b_ in range(3)] for a_ in range(NI)]
        b_ST = [[Buf(), Buf(), Buf()] for _ in range(NI)]
        SS = [fw.sb(es, f"s_SS{a_}", [128, 2, 256], F32) for a_ in range(NI)]
        b_SS = [Buf() for _ in range(NI)]
        E = [fw.sb(es, f"s_E{a_}", [128, 2, 128], BF16) for a_ in range(NI)]
        b_E = [Buf() for _ in range(NI)]
        for a_ in range(NI):
            for b_ in range(3):
                fw.op("dve", lambda a_=a_, b_=b_: dve.memset(ST[a_][b_][:], 0.0), writes=[b_ST[a_][b_]])
            fw.op("dve", lambda a_=a_: dve.memset(E[a_][:], 0.0), writes=[b_E[a_]])
            fw.op("dve", lambda a_=a_: dve.memset(SS[a_][:], 0.0), writes=[b_SS[a_]])
        ig = 0
        for gb in range(NG // GB):
            g0 = gb * GB

            def expand(fam, XR, XI, b_X, o_re, o_im, b_o, neg_im):
                pwr = ap4(PWR, fam * 8 * 64 + g0, NF * 64, 128, [(1, GB), (64, 8), (0, 16)])
                pwi = ap4(PWI, fam * 8 * 64 + g0, NF * 64, 128, [(1, GB), (64, 8), (0, 16)])
                xr = ap4(XR, g0 * 16, 1024, 128, [(16, GB), (0, 8), (1, 16)])
                xi = ap4(XI, g0 * 16, 1024, 128, [(16, GB), (0, 8), (1, 16)])
                t1 = T1[:].rearrange("p (g j q) -> p g j q", g=GB, j=8)
                t2 = T2[:].rearrange("p (g j q) -> p g j q", g=GB, j=8)
                ore = o_re[:].rearrange("p g (j q) -> p g j q", j=8)
                oim = o_im[:].rearrange("p g (j q) -> p g j q", j=8)
                fw.op("dve", lambda: dve.tensor_tensor(t1, pwr, xr, ALU.mult), reads=[b_PW, b_X], writes=[b_T1])
                fw.op("pool", lambda: nc.gpsimd.tensor_tensor(t2, pwi, xi, ALU.mult), reads=[b_PW, b_X], writes=[b_T2])
                fw.op("dve", lambda: dve.tensor_tensor(ore, t1, t2, ALU.subtract), reads=[b_T1, b_T2], writes=[b_o])
                fw.op("dve", lambda: dve.tensor_tensor(t1, pwr, xi, ALU.mult), reads=[b_PW, b_X], writes=[b_T1])
                fw.op("pool", lambda: nc.gpsimd.tensor_tensor(t2, pwi, xr, ALU.mult), reads=[b_PW, b_X], writes=[b_T2])
                if neg_im:
                    fw.op("dve", lambda: dve.scalar_tensor_tensor(oim, t1, -1.0, t2, ALU.mult, ALU.subtract),
                          reads=[b_T1, b_T2], writes=[b_o])
                else:
                    fw.op("dve", lambda: dve.tensor_tensor(oim, t1, t2, ALU.add), reads=[b_T1, b_T2], writes=[b_o])

            expand(0, CTR, CTI, b_CT, CO_re, CO_imn, b_CO, True)
            expand(1, BBR, BBI, b_BB, WX_re, WX_im, b_WX, False)
            expand(2, BBR, BBI, b_BB, W2_re, W2_im, b_W2, False)
            for src, dstT in ((WX_re, WXT_re), (WX_im, WXT_im)):
                for half in range(GB // 8):
                    bk = C.bank[half]
                    pst = bk[:].bitcast(BF16)
                    for q8 in range(8):
                        gl = half * 8 + q8
                        fw.op("pe", lambda src=src, gl=gl, q8=q8, pst=pst: nc.tensor.transpose(
                            pst[:, q8 * 128:(q8 + 1) * 128], src[:, gl, :], C.ident[:]),
                            reads=[b_WX, C.b_ident], writes=[C.b_bank[half]], signal=(q8 == 7))
                    fw.op("act", lambda dstT=dstT, half=half, pst=pst: nc.scalar.copy(
                        dstT[:, half * 8:(half + 1) * 8, :].rearrange("p g m -> p (g m)"), pst),
                        reads=[C.b_bank[half]], writes=[b_WXT])
            for q in range(GB // 4):
                for g4 in range(4):
                    gl = q * 4 + g4
                    for (lo, bki) in ((0, 2), (64, 3)):
                        fw.op("pe", lambda gl=gl, g4=g4, lo=lo, bki=bki: nc.tensor.matmul(
                            C.bank[bki][:, g4 * 128:(g4 + 1) * 128], W2_re[lo:lo + 64, gl, :], CO_re[lo:lo + 64, gl, :],
                            start=True, stop=False),
                            reads=[b_W2, b_CO], writes=[C.b_bank[bki]], signal=False)
                        fw.op("pe", lambda gl=gl, g4=g4, lo=lo, bki=bki: nc.tensor.matmul(
                            C.bank[bki][:, g4 * 128:(g4 + 1) * 128], W2_im[lo:lo + 64, gl, :], CO_imn[lo:lo + 64, gl, :],
                            start=False, stop=True),
                            reads=[b_W2, b_CO], writes=[C.b_bank[bki]], signal=(g4 == 3))
                mf = ap4(MF, 0, 128, 128, [(0, 4), (1, 128)])
                mb = ap4(MB, 0, 128, 128, [(0, 4), (1, 128)])
                t1v = T1[:, 0:512].rearrange("p (g m) -> p g m", g=4)
                t2v = T2[:, 0:512].rearrange("p (g m) -> p g m", g=4)
                fw.op("dve", lambda t1v=t1v, mf=mf: dve.tensor_tensor(
                    t1v, C.bank[2][:].rearrange("p (g m) -> p g m", g=4), mf, ALU.mult),
                    reads=[C.b_bank[2], b_MK], writes=[b_T1])
                fw.op("dve", lambda t2v=t2v, mb=mb: dve.tensor_tensor(
                    t2v, C.bank[3][:].rearrange("p (g m) -> p g m", g=4), mb, ALU.mult),
                    reads=[C.b_bank[3], b_MK], writes=[b_T2])
                fw.op("dve", lambda q=q, t1v=t1v, t2v=t2v: dve.tensor_tensor(
                    TT[:, q * 4:(q + 1) * 4, :], t1v, t2v, ALU.add), reads=[b_T1, b_T2], writes=[b_TT])
            for hh in range(2):
                for half in range(GB // 8):
                    bk = C.bank[(half + hh) % 2]
                    pst = bk[:].bitcast(BF16)
                    for q8 in range(8):
                        g = g0 + half * 8 + q8
                        fw.op("pe", lambda hh=hh, g=g, q8=q8, pst=pst: nc.tensor.transpose(
                            pst[:, q8 * 128:(q8 + 1) * 128], U2[:, hh, g, :], C.ident[:]),
                            reads=[b_U2, C.b_ident], writes=[C.b_bank[(half + hh) % 2]], signal=(q8 == 7))
                    fw.op("act", lambda hh=hh, half=half, pst=pst: nc.scalar.copy(
                        U8b[:, half * 8:(half + 1) * 8, hh * 128:(hh + 1) * 128],
                        pst.rearrange("p (g c) -> p g c", g=8)),
                        reads=[C.b_bank[(half + hh) % 2]], writes=[b_U8])
            def load_set(gp):
                info = []
                for st in range(NI):
                    gl = gp * NI + st
                    g = g0 + gl
                    xb = 2 + st
                    info.append((gl, g, xb))
                    fw.op("pe", lambda gl=gl, xb=xb: nc.tensor.matmul(
                        C.bank[xb][:, 0:256], WXT_re[:, gl, :], U8b[:, gl, :], start=True, stop=True),
                        reads=[b_WXT, b_U8], writes=[C.b_bank[xb]], signal=False)
                    fw.op("pe", lambda gl=gl, xb=xb: nc.tensor.matmul(
                        C.bank[xb][:, 256:512], WXT_im[:, gl, :], U8b[:, gl, :], start=True, stop=True),
                        reads=[b_WXT, b_U8], writes=[C.b_bank[xb]])
                    bx = C.bank[xb]
                    S0 = ST[st][2]
                    fw.op("act", lambda bx=bx, S0=S0: nc.scalar.copy(
                        S0[0:64, :, :], bx[0:64, :].rearrange("p (a c) -> p a c", a=2)),
                        reads=[C.b_bank[xb]], writes=[b_ST[st][2]])
                    fw.op("act", lambda bx=bx, S0=S0: nc.scalar.copy(S0[64:128, 0, :], bx[64:128, 255::-1]),
                          reads=[C.b_bank[xb]], writes=[b_ST[st][2]])
                    fw.op("act", lambda bx=bx, S0=S0: nc.scalar.copy(S0[64:128, 1, :], bx[64:128, 511:255:-1]),
                          reads=[C.b_bank[xb]], writes=[b_ST[st][2]])
                return info

            def scan_step(info, k, cur, nxt):
                sft = 1 << k
                n = 256 - sft
                for st in range(NI):
                    gl, g, xb = info[st]
                    Sc, Sn, Sx = ST[st][cur], ST[st][nxt], SS[st]
                    ar = PWR[:, 24 + k, g:g + 1]
                    ai = PWI[:, 24 + k, g:g + 1]
                    nai = NAI[:, k, g:g + 1]
                    fw.op("dve", lambda Sc=Sc, Sx=Sx, ar=ar: dve.scalar_tensor_tensor(
                        Sx[:, :, sft:256], Sc[:, :, 0:n], ar, Sc[:, :, sft:256], ALU.mult, ALU.add),
                        reads=[b_ST[st][cur], b_PW], writes=[b_SS[st]])
                    fw.op("dve", lambda Sc=Sc, Sn=Sn, Sx=Sx, nai=nai: dve.scalar_tensor_tensor(
                        Sn[:, 0, sft:256], Sc[:, 1, 0:n], nai, Sx[:, 0, sft:256], ALU.mult, ALU.add),
                        reads=[b_ST[st][cur], b_SS[st], b_PW], writes=[b_ST[st][nxt]])
                    fw.op("dve", lambda Sc=Sc, Sn=Sn, Sx=Sx, ai=ai: dve.scalar_tensor_tensor(
                        Sn[:, 1, sft:256], Sc[:, 0, 0:n], ai, Sx[:, 1, sft:256], ALU.mult, ALU.add),
                        reads=[b_ST[st][cur], b_SS[st], b_PW], writes=[b_ST[st][nxt]])
                    fw.op("act", lambda Sc=Sc, Sn=Sn: nc.scalar.copy(Sn[:, :, 0:sft], Sc[:, :, 0:sft]),
                          reads=[b_ST[st][cur]], writes=[b_ST[st][nxt]])

            NSET = GB // NI
            nxt_info = load_set(0)
            for gp in range(NSET):
                info3 = nxt_info
                info = []
                for st in range(NI):
                    gl, g, xb = info3[st]
                    yb = 6 + ((ig // 4) % 2)
                    g4 = ig % 4
                    ig += 1
                    info.append((gl, g, xb, yb, g4))
                seq = [2, 1, 0, 1, 0, 1, 0, 1, 0]
                scan_step(info3, 0, seq[0], seq[1])
                if gp + 1 < NSET:
                    nxt_info = load_set(gp + 1)
                for k in range(1, 8):
                    scan_step(info3, k, seq[k], seq[k + 1])
                cur = 0
                for st in range(NI):
                    gl, g, xb, yb, g4 = info[st]
                    Sf = ST[st][cur]
                    Es = E[st]
                    fw.op("act", lambda Sf=Sf, Es=Es: nc.scalar.copy(Es[0:64, :, 1:128], Sf[0:64, :, 0:127]),
                          reads=[b_ST[st][cur]], writes=[b_E[st]])
                    fw.op("pool", lambda Sf=Sf, Es=Es: nc.gpsimd.tensor_copy(Es[64:128, 0, :], Sf[64:128, 0, 254:126:-1]),
                          reads=[b_ST[st][cur]], writes=[b_E[st]])
                    fw.op("pool", lambda Sf=Sf, Es=Es: nc.gpsimd.tensor_copy(Es[64:128, 1, :], Sf[64:128, 1, 254:126:-1]),
                          reads=[b_ST[st][cur]], writes=[b_E[st]])
                    yo = C.bank[yb][:, g4 * 128:(g4 + 1) * 128]
                    fw.op("pe", lambda gl=gl, yo=yo: nc.tensor.matmul(yo, U8b[:, gl, 0:128], TT[:, gl, :], start=True, stop=False),
                          reads=[b_U8, b_TT], writes=[C.b_bank[yb]], signal=False)
                    fw.op("pe", lambda gl=gl, yo=yo, Es=Es: nc.tensor.matmul(yo, Es[:, 0, :], CO_re[:, gl, :], start=False, stop=False),
                          reads=[b_E[st], b_CO], writes=[C.b_bank[yb]], signal=False)
                    fw.op("pe", lambda gl=gl, yo=yo, Es=Es: nc.tensor.matmul(yo, Es[:, 1, :], CO_imn[:, gl, :], start=False, stop=True),
                          reads=[b_E[st], b_CO, b_U8, b_TT], writes=[C.b_bank[yb]])
                    if g4 == 3:
                        gbase = g - 3
                        fw.op("dve", lambda yb=yb, gbase=gbase: dve.tensor_copy(
                            Yown[:, :, gbase * 16:(gbase + 4) * 16].rearrange("c j (g p) -> c g j p", g=4),
                            C.bank[yb][:].rearrange("c (g j p) -> c g j p", g=4, j=8)),
                            reads=[C.b_bank[yb]], writes=[b_Y])
        dB = T2[:, 0:1024]
        b_dB = b_T2
        fw.dma("sp", dB, dram["ssm_d"][0:1, :].partition_broadcast(128), writes=[b_T2])
        for j in range(8):
            fw.op("pool", lambda j=j: nc.gpsimd.tensor_tensor(
                T1[:, 0:1024].rearrange("c (g p) -> c g p", g=64), dB.rearrange("c (g p) -> c g p", g=64),
                U2[:, 0, :, j * 16:(j + 1) * 16], ALU.mult),
                  reads=[b_dB, b_U2], writes=[b_T1])
            fw.op("dve", lambda j=j: dve.tensor_tensor(Yown[:, j, :], Yown[:, j, :], T1[:, 0:1024], ALU.add),
                  reads=[b_T1, b_Y], writes=[b_Y])
        fw.barrier()


GELU_K0 = 0.7978845608028654
GELU_K1 = 0.7978845608028654 * 0.044715


def post_phase(fw, nc, C, dram, Yown, b_Y, aT, b_aT, x1_d, b_x1d, y_d, b_yd, x2_d, b_x2d):
    dve = nc.vector
    with ExitStack() as es:
        sT = fw.sb(es, "p_sT", [128, 8, TOWN], BF16); b_sT = Buf()
        with ExitStack() as es1:
            wglu = fw.sb(es1, "p_wglu", [128, 8, 1024], BF16); b_wglu = Buf()
            wgv = dram["ssm_w_glu"].rearrange("(k p) m -> p k m", p=128)
            for h2 in range(2):
                fw.dma("pool", wglu[:, :, h2 * 512:(h2 + 1) * 512], wgv[:, :, h2 * 512:(h2 + 1) * 512], writes=[b_wglu])
            bglu = fw.sb(es1, "p_bglu", [128, 1024], F32); b_bg = Buf()
            outg = fw.sb(es1, "p_outg", [128, 1024], F32)
            fw.dma("sp", bglu[:], dram["ssm_b_glu"][0:1, :].partition_broadcast(128), writes=[b_bg])
            fw.dma("sp", outg[:], dram["ssm_out_g"][0:1, :].partition_broadcast(128), writes=[b_bg])
            tas = [fw.sb(es1, f"p_ta{i}", [128, 1024], F32) for i in range(2)]; b_tas = [Buf(), Buf()]
            tbs = [fw.sb(es1, f"p_tb{i}", [128, 1024], F32) for i in range(2)]; b_tbs = [Buf(), Buf()]
            g1s = [fw.sb(es1, f"p_g1{i}", [128, 1024], F32) for i in range(2)]; b_g1s = [Buf(), Buf()]
            g1bs = [fw.sb(es1, f"p_g1b{i}", [128, 1024], BF16) for i in range(2)]; b_g1bs = [Buf(), Buf()]
            g1Ts = [fw.sb(es1, f"p_g1T{i}", [128, 8, 128], BF16) for i in range(2)]; b_g1Ts = [Buf(), Buf()]
            ss = fw.sb(es1, "p_ss", [128, 16], F32); b_ss = Buf()
            rs = fw.sb(es1, "p_rs", [128, 16], F32); b_rs = Buf()
            for j in range(8):
                pj = j % 2
                ta, b_ta, tb, b_tb, g1, b_g1 = tas[pj], b_tas[pj], tbs[pj], b_tbs[pj], g1s[pj], b_g1s[pj]
                g1b, b_g1b, g1T, b_g1T = g1bs[pj], b_g1bs[pj], g1Ts[pj], b_g1Ts[pj]
                bk0 = 4 * pj
                y = Yown[:, j, :]
                fw.op("act", lambda y=y, ta=ta: nc.scalar.activation(ta[:], y, AF.Square), reads=[b_Y], writes=[b_ta])
                fw.op("dve", lambda ta=ta: dve.tensor_scalar(ta[:], ta[:], GELU_K1, GELU_K0, ALU.mult, ALU.add),
                      reads=[b_ta], writes=[b_ta])
                fw.op("dve", lambda y=y, ta=ta: dve.tensor_tensor(ta[:], ta[:], y, ALU.mult), reads=[b_ta, b_Y], writes=[b_ta])
                fw.op("act", lambda ta=ta, tb=tb: nc.scalar.activation(tb[:], ta[:], AF.Sigmoid, scale=2.0),
                      reads=[b_ta], writes=[b_tb])
                fw.op("dve", lambda y=y, tb=tb, g1=g1: dve.tensor_tensor(g1[:], y, tb[:], ALU.mult),
                      reads=[b_tb, b_Y], writes=[b_g1])
                fw.op("pool", lambda g1=g1, g1b=g1b: nc.gpsimd.tensor_copy(g1b[:], g1[:]), reads=[b_g1], writes=[b_g1b])
                pst = C.bank[bk0][:].bitcast(BF16)
                for kf in range(8):
                    fw.op("pe", lambda kf=kf, pst=pst, g1b=g1b: nc.tensor.transpose(
                        pst[:, kf * 128:(kf + 1) * 128], g1b[:, kf * 128:(kf + 1) * 128], C.ident[:]),
                        reads=[b_g1b, C.b_ident], writes=[C.b_bank[bk0]], signal=(kf == 7))
                fw.op("act", lambda pst=pst, g1T=g1T: nc.scalar.copy(g1T[:].rearrange("p k c -> p (k c)"), pst),
                      reads=[C.b_bank[bk0]], writes=[b_g1T])
                for n in range(2):
                    bk = bk0 + 2 + n
                    for kf in range(8):
                        fw.op("pe", lambda kf=kf, n=n, bk=bk, g1T=g1T: nc.tensor.matmul(
                            C.bank[bk][:], g1T[:, kf, :], wglu[:, kf, n * 512:(n + 1) * 512],
                            start=(kf == 0), stop=(kf == 7)),
                            reads=[b_g1T, b_wglu], writes=[C.b_bank[bk]], signal=(kf == 7))
                    fw.op("dve", lambda n=n, bk=bk, ta=ta: dve.tensor_tensor(
                        ta[:, n * 512:(n + 1) * 512], C.bank[bk][:], bglu[:, n * 512:(n + 1) * 512], ALU.add),
                        reads=[C.b_bank[bk], b_bg], writes=[b_ta])
                fw.op("act", lambda ta=ta, tb=tb: nc.scalar.activation(tb[:], ta[:], AF.Sigmoid), reads=[b_ta], writes=[b_tb])
                fw.op("dve", lambda g1=g1, tb=tb: dve.tensor_tensor(g1[:], g1[:], tb[:], ALU.mult),
                      reads=[b_tb, b_g1], writes=[b_g1])
                rms_stats(fw, nc, g1[:], b_g1, ta[:], b_ta, ss[:, j:j + 1], b_ss, rs[:, j:j + 1], b_rs, 1024)
                fw.op("dve", lambda j=j, g1=g1, g1b=g1b: dve.scalar_tensor_tensor(
                    g1b[:], g1[:], rs[:, j:j + 1], outg[:], ALU.mult, ALU.mult),
                    reads=[b_g1, b_rs, b_bg], writes=[b_g1b])
                pst2 = C.bank[bk0 + 1][:].bitcast(BF16)
                for kf in range(8):
                    fw.op("pe", lambda kf=kf, pst2=pst2, g1b=g1b: nc.tensor.transpose(
                        pst2[:, kf * 128:(kf + 1) * 128], g1b[:, kf * 128:(kf + 1) * 128], C.ident[:]),
                        reads=[b_g1b, C.b_ident], writes=[C.b_bank[bk0 + 1]], signal=(kf == 7))
                fw.op("act", lambda pst2=pst2, j=j: nc.scalar.copy(
                    sT[:, :, j * 128:(j + 1) * 128], pst2.rearrange("p (k c) -> p k c", k=8)),
                    reads=[C.b_bank[bk0 + 1]], writes=[b_sT])
            fw.barrier()
        wo = [fw.sb(es, f"p_wo{i}", [128, KD, 512], BF16) for i in range(2)]
        b_wo = [Buf(), Buf()]
        yev = [fw.sb(es, f"p_yev{i}", [128, 512], F32) for i in range(4)]
        b_yev = [Buf() for _ in range(4)]
        wov = dram["w_out"].rearrange("(k p) m -> p k m", p=128)
        iy = 0
        fw._deps("sp", (), [b_yd])
        for n in range(4):
            s = n % 2
            fw.dma("pool", wo[s][:], wov[:, :, n * 512:(n + 1) * 512], writes=[b_wo[s]])
            for j in range(8):
                for k in range(KD):
                    if k < 8:
                        lhsT = aT[:, k, j:j + 1017:8]
                    else:
                        lhsT = sT[:, k - 8, j * 128:(j + 1) * 128]
                    fw.op("pe", lambda k=k, j=j, s=s, lhsT=lhsT: nc.tensor.matmul(
                        C.bank[j][:], lhsT, wo[s][:, k, :], start=(k == 0), stop=(k == KD - 1)),
                        reads=[b_aT, b_sT, b_wo[s]], writes=[C.b_bank[j]], signal=(k == KD - 1))
                ys = iy % 4
                iy += 1
                if j % 2 == 0:
                    fw.op("act", lambda j=j, ys=ys: nc.scalar.copy(yev[ys][:], C.bank[j][:]),
                          reads=[C.b_bank[j]], writes=[b_yev[ys]])
                else:
                    fw.op("dve", lambda j=j, ys=ys: dve.tensor_copy(yev[ys][:], C.bank[j][:]),
                          reads=[C.b_bank[j]], writes=[b_yev[ys]])
                fw.dma("sp", y_d[j * 128:(j + 1) * 128, n * 512:(n + 1) * 512], yev[ys][:],
                       reads=[b_yev[ys]], wr_only=[b_yd])
        gpost = fw.sb(es, "p_gpost", [128, D], F32); b_gp = Buf()
        fw.dma("sp", gpost[:], dram["mix_post_g"][0:1, :].partition_broadcast(128), writes=[b_gp])
        xts = [fw.sb(es, f"p_xt{i}", [128, D], F32) for i in range(2)]; b_xts = [Buf(), Buf()]
        yts = [fw.sb(es, f"p_yt{i}", [128, D], F32) for i in range(2)]; b_yts = [Buf(), Buf()]
        junks = [fw.sb(es, f"p_junk{i}", [128, D], BF16) for i in range(2)]; b_junks = [Buf(), Buf()]
        ss2 = fw.sb(es, "p_ss2", [128, 16], F32); b_ss2 = Buf()
        rs2 = fw.sb(es, "p_rs2", [128, 16], F32); b_rs2 = Buf()
        for j in range(8):
            pj = j % 2
            xt, b_xt, yt, b_yt, junk, b_junk = xts[pj], b_xts[pj], yts[pj], b_yts[pj], junks[pj], b_junks[pj]
            fw.dma("sp", yt[:], y_d[j * 128:(j + 1) * 128, :], reads=[b_yd], writes=[b_yt])
            fw.dma("sp", xt[:], x1_d[j:j + 1017:8, :], reads=[b_x1d], writes=[b_xt])
            rms_stats(fw, nc, yt[:], b_yt, junk[:], b_junk, ss2[:, j:j + 1], b_ss2, rs2[:, j:j + 1], b_rs2, D)
            fw.op("dve", lambda j=j, yt=yt: dve.scalar_tensor_tensor(yt[:], yt[:], rs2[:, j:j + 1], gpost[:], ALU.mult, ALU.mult),
                  reads=[b_yt, b_rs2, b_gp], writes=[b_yt])
            fw.op("pool", lambda yt=yt, xt=xt: nc.gpsimd.tensor_tensor(yt[:], yt[:], xt[:], ALU.add),
                  reads=[b_yt, b_xt], writes=[b_yt])
            fw.dma("pool", x2_d[j:j + 1017:8, :], yt[:], reads=[b_yt], wr_only=[b_x2d])
        fw.barrier()

def build(stage="full"):
    nc = bass.Bass("TRN2", target_bir_lowering=False)
    dram = {}

    def din(name, shape, dt=F32):
        dram[name] = nc.dram_tensor(name, list(shape), dt, kind="ExternalInput").ap()

    din("xs", [S, D])
    din("c_ident", [128, 128])
    for p in ("ff1", "ff2"):
        din(p + "_pre_g", [1, D]); din(p + "_post_g", [1, D])
        din(p + "_w_gate", [D, DFF]); din(p + "_w_up", [D, DFF]); din(p + "_w_down", [DFF, D])
    din("c_alibi", [128, 3072])
    din("mix_pre_g", [1, D]); din("mix_post_g", [1, D])
    din("w_in", [D, 4096]); din("w_out", [D, D])
    for nm in ("lam_q1", "lam_k1", "lam_q2", "lam_k2"):
        din(nm, [1, 64])
    din("attn_head_g", [1, 128])
    din("c_exps", [128, 33]); din("c_maskF", [128, 128]); din("c_maskB", [128, 128])
    din("ssm_lam_re", [2, 64, 64]); din("ssm_lam_im", [2, 64, 64]); din("ssm_log_dt", [2, 64])
    din("ssm_b_re", [2, 64, 64, 16]); din("ssm_b_im", [2, 64, 64, 16])
    din("ssm_c_re", [2, 64, 16, 64]); din("ssm_c_im", [2, 64, 16, 64])
    din("ssm_d", [1, 1024])
    u_d = nc.dram_tensor("u_d", [128, 2, 64, 128], BF16, kind="Internal").ap()
    b_ud = Buf()
    din("ssm_w_glu", [1024, 1024]); din("ssm_b_glu", [1, 1024]); din("ssm_out_g", [1, 1024])
    x2_d = nc.dram_tensor("x2_d", [TOWN, D], F32, kind="Internal").ap()
    b_x2d = Buf()
    if stage == "s5":
        dbg_Y = nc.dram_tensor("dbg_Y", [128, 8, 1024], F32, kind="ExternalOutput").ap()
    out = nc.dram_tensor("out", [TOWN, D], F32, kind="ExternalOutput").ap()
    if stage == "attn":
        dbg_aT = nc.dram_tensor("dbg_aT", [128, 8, TOWN], BF16, kind="ExternalOutput").ap()
    x1_d = nc.dram_tensor("x1_d", [S, D], F32, kind="Internal").ap()
    y_d = nc.dram_tensor("y_d", [TOWN, D], F32, kind="Internal").ap()
    b_x1d, b_yd, b_xs, b_out = Buf(), Buf(), Buf(), Buf()

    with ExitStack() as es:
        fw = FW(nc, es)
        C = Ctx()
        setup_consts(fw, nc, C, es, dram)
        if stage == "ffn":
            with ExitStack() as pes:
                A = ffn_alloc(fw, nc, pes)
                ffn_pass(fw, nc, C, A, dram["xs"][0:TOWN, :], out, dram["ff1_w_gate"], dram["ff1_w_up"],
                         dram["ff1_w_down"], dram["ff1_pre_g"], dram["ff1_post_g"], y_d, b_yd, b_xs, b_out)
                fw.barrier()
        if stage == "attn":
            with ExitStack() as mes:
                aT = fw.sb(mes, "m_aT", [128, 8, TOWN], BF16); b_aT = Buf()
                AA = attention_alloc(fw, nc, mes)
                with ExitStack() as mes3:
                    M = mixer_common_alloc(fw, nc, mes3)
                    M.aT, M.b_aT = aT, b_aT
                    build_hmT(fw, nc, C, M, dram["xs"], b_xs, dram["mix_pre_g"])
                    attention_proj(fw, nc, C, M, AA, dram, u_d, b_ud)
                attention_core(fw, nc, C, M, AA, dram)
                fw.dma("sp", dbg_aT[:, :, :], M.aT[:], reads=[M.b_aT], writes=[b_out])
                fw.barrier()
        if stage == "s5":
            with ExitStack() as mes:
                Yown = fw.sb(mes, "m_Yown", [128, 8, 1024], F32); b_Y = Buf()
                with ExitStack() as mes2:
                    M = mixer_common_alloc(fw, nc, mes2)
                    build_hmT(fw, nc, C, M, dram["xs"], b_xs, dram["mix_pre_g"])
                    attention_proj(fw, nc, C, M, None, dram, u_d, b_ud, only_u=True)
                s5_phase(fw, nc, C, dram, Yown, b_Y, u_d, b_ud)
                fw.dma("sp", dbg_Y[:, :, :], Yown[:], reads=[b_Y], writes=[b_out])
                fw.barrier()
        if stage in ("full", "mix"):
            if stage == "full":
                with ExitStack() as pes:
                    A = ffn_alloc(fw, nc, pes)
                    for ps_ in range(2):
                        ffn_pass(fw, nc, C, A, dram["xs"][ps_ * TOWN:(ps_ + 1) * TOWN, :],
                                 x1_d[ps_ * TOWN:(ps_ + 1) * TOWN, :], dram["ff1_w_gate"], dram["ff1_w_up"],
                                 dram["ff1_w_down"], dram["ff1_pre_g"], dram["ff1_post_g"], y_d, b_yd, b_xs, b_x1d)
                    fw.barrier()
                x1_src = x1_d
            else:
                x1_src = dram["xs"]
            with ExitStack() as mes:
                aT = fw.sb(mes, "m_aT", [128, 8, TOWN], BF16); b_aT = Buf()
                with ExitStack() as mes2:
                    AA = attention_alloc(fw, nc, mes2)
                    with ExitStack() as mes3:
                        M = mixer_common_alloc(fw, nc, mes3)
                        M.aT, M.b_aT = aT, b_aT
                        build_hmT(fw, nc, C, M, x1_src, b_x1d, dram["mix_pre_g"])
                        attention_proj(fw, nc, C, M, AA, dram, u_d, b_ud)
                    attention_core(fw, nc, C, M, AA, dram)
                Yown = fw.sb(mes, "m_Yown", [128, 8, 1024], F32); b_Y = Buf()
                s5_phase(fw, nc, C, dram, Yown, b_Y, u_d, b_ud)
                post_phase(fw, nc, C, dram, Yown, b_Y, aT, b_aT, x1_src, b_x1d, y_d, b_yd, x2_d, b_x2d)
            if stage == "full":
                with ExitStack() as pes:
                    A = ffn_alloc(fw, nc, pes)
                    ffn_pass(fw, nc, C, A, x2_d, out, dram["ff2_w_gate"], dram["ff2_w_up"],
                             dram["ff2_w_down"], dram["ff2_pre_g"], dram["ff2_post_g"], y_d, b_yd, b_x2d, b_out)
                    fw.barrier()
            else:
                with ExitStack() as pes:
                    xt = fw.sb(pes, "o_xt", [128, D], F32); b_xt = Buf()
                    for tt in range(8):
                        fw.dma("sp", xt[:], x2_d[tt * 128:(tt + 1) * 128, :], reads=[b_x2d], writes=[b_xt])
                        fw.dma("sp", out[tt * 128:(tt + 1) * 128, :], xt[:], reads=[b_xt], writes=[b_out])
                    fw.barrier()
        fw.barrier(engines=("sp",))
    return nc


def common_inputs(inp):
    m = {}
    m["c_ident"] = np.eye(128, dtype=np.float32)
    jj = np.arange(128)[:, None]
    mm = np.arange(3072)[None, :]
    m["c_alibi"] = np.abs(mm - jj - 1920).astype(np.float32)
    ex = np.zeros((128, 33), np.float32)
    j8 = np.arange(8)
    ex[:64, 0:8] = j8 + 1; ex[:64, 8:16] = 7 - j8; ex[:64, 16:24] = -1 - j8
    ex[64:, 0:8] = 8 - j8; ex[64:, 8:16] = j8; ex[64:, 16:24] = j8 - 8
    ex[:, 24:32] = 8 * (2 ** j8); ex[:, 32] = 1
    m["c_exps"] = ex
    jrow = (np.arange(128) // 16)[:, None]
    jcol = (np.arange(128) // 16)[None, :]
    m["c_maskF"] = (jcol >= jrow).astype(np.float32)
    m["c_maskB"] = (jcol <= jrow).astype(np.float32)
    m["ssm_d"] = np.ascontiguousarray(np.asarray(inp["ssm_d"], dtype=np.float32).reshape(1, -1))
    for p in ("ff1", "ff2"):
        for n in ("_pre_g", "_post_g"):
            m[p + n] = np.ascontiguousarray(np.asarray(inp[p + n], dtype=np.float32).reshape(1, -1))
        for n in ("_w_gate", "_w_up", "_w_down"):
            m[p + n] = np.ascontiguousarray(np.asarray(inp[p + n], dtype=np.float32)[0])
    for n in ("mix_pre_g", "mix_post_g", "lam_q1", "lam_k1", "lam_q2", "lam_k2", "attn_head_g"):
        m[n] = np.ascontiguousarray(np.asarray(inp[n], dtype=np.float32).reshape(1, -1))
    m["ssm_w_glu"] = np.ascontiguousarray(np.asarray(inp["ssm_w_glu"], dtype=np.float32)[0])
    for n in ("ssm_b_glu", "ssm_out_g"):
        m[n] = np.ascontiguousarray(np.asarray(inp[n], dtype=np.float32).reshape(1, -1))
    m["w_in"] = np.ascontiguousarray(np.asarray(inp["w_in"], dtype=np.float32)[0])
    m["w_out"] = np.ascontiguousarray(np.asarray(inp["w_out"], dtype=np.float32)[0])
    return m


SSM_KEYS = ("ssm_lam_re", "ssm_lam_im", "ssm_log_dt", "ssm_b_re", "ssm_b_im", "ssm_c_re", "ssm_c_im")


def ssm_inputs(inp, r):
    m = {}
    for k in SSM_KEYS:
        a = np.asarray(inp[k], dtype=np.float32)[0]
        if r == 1:
            a = a[::-1]
        m[k] = np.ascontiguousarray(a)
    return m


_NC_CACHE = {}


def kernel(**inputs):
    x = np.asarray(inputs["x"], dtype=np.float32)
    B = x.shape[0]
    common = common_inputs(inputs)
    ssm = [ssm_inputs(inputs, r) for r in range(2)]
    in_maps = []
    for core in range(8):
        b, r = core // 2, core % 2
        m = dict(common)
        m.update(ssm[r])
        xs = x[b] if r == 0 else x[b][::-1]
        m["xs"] = np.ascontiguousarray(xs)
        in_maps.append(m)
    if "full" not in _NC_CACHE:
        _NC_CACHE["full"] = build("full")
    nc = _NC_CACHE["full"]
    res = run_bass_kernel_spmd(nc, in_maps, core_ids=list(range(8)))
    out = np.empty((B, S, D), dtype=np.float32)
    for core in range(8):
        b, r = core // 2, core % 2
        o = np.asarray(res.results[core]["out"], dtype=np.float32)
        if r == 0:
            out[b, :TOWN] = o
        else:
            out[b, TOWN:] = o[::-1]
    return out
```

```python
import numpy as np
from contextlib import ExitStack
import concourse.bass as bass
import concourse.mybir as mybir
from concourse.bass_utils import run_bass_kernel_spmd

F32 = mybir.dt.float32
BF16 = mybir.dt.bfloat16
I32 = mybir.dt.int32
AF = mybir.ActivationFunctionType
ALU = mybir.AluOpType

D = 2048
S = 2048
TOWN = 1024
DFF = 5632
NFF = DFF // 128
KD = D // 128
EPS = 1e-6
NH = 8
NG = 64


class Buf:
    __slots__ = ("name", "w", "r")

    def __init__(self, name=""):
        self.name = name
        self.w = {}
        self.r = {}


def _merge(d, ev):
    sem, val = ev
    k = id(sem)
    if k not in d or d[k][1] < val:
        d[k] = (sem, val)


class FW:
    NDMA = 8

    def __init__(self, nc, es):
        self.nc = nc
        self.es = es
        self.eng = {"pe": nc.tensor, "act": nc.scalar, "dve": nc.vector,
                    "pool": nc.gpsimd, "sp": nc.sync}
        self.csem = {}
        self.ccnt = {}
        for k in ("pe", "act", "dve", "pool"):
            self.csem[k] = es.enter_context(nc.semaphore("c_" + k))
            self.ccnt[k] = 0
        self.dsem = {}
        self.dcnt = {}
        self.di = {}
        for q in ("sp", "act", "pool"):
            self.dsem[q] = [es.enter_context(nc.semaphore(f"d_{q}{i}")) for i in range(self.NDMA)]
            self.dcnt[q] = [0] * self.NDMA
            self.di[q] = 0
        self.waited = {k: {} for k in self.eng}

    def sb(self, es, name, shape, dt):
        self.nalloc = getattr(self, "nalloc", 0) + 1
        return es.enter_context(self.nc.sbuf_tensor(f"{name}_{self.nalloc}", list(shape), dt))

    def ps(self, es, name, shape, dt):
        self.nalloc = getattr(self, "nalloc", 0) + 1
        return es.enter_context(self.nc.psum_tensor(f"{name}_{self.nalloc}", list(shape), dt))

    def _wait(self, ek, ev):
        sem, val = ev
        key = id(sem)
        d = self.waited[ek]
        if d.get(key, 0) >= val:
            return
        d[key] = val
        self.eng[ek].wait_ge(sem, val)

    def _deps(self, ek, reads, writes):
        best = {}
        for b in reads:
            for ev in b.w.values():
                _merge(best, ev)
        for b in writes:
            for ev in b.w.values():
                _merge(best, ev)
            for ev in b.r.values():
                _merge(best, ev)
        for ev in best.values():
            self._wait(ek, ev)

    def _commit(self, ev, reads, writes):
        for b in reads:
            _merge(b.r, ev)
        for b in writes:
            _merge(b.w, ev)

    def op(self, ek, fn, reads=(), writes=(), signal=True):
        self._deps(ek, reads, writes)
        ins = fn()
        if signal:
            self.ccnt[ek] += 1
            ins.then_inc(self.csem[ek], 1)
            ev = (self.csem[ek], self.ccnt[ek])
            self._commit(ev, reads, writes)
            return ev
        return None

    def dma(self, q, out, in_, reads=(), writes=(), wr_only=(), **kw):
        i = self.di[q]
        self.di[q] = (i + 1) % self.NDMA
        sem = self.dsem[q][i]
        if self.dcnt[q][i] > 0:
            self._wait(q, (sem, self.dcnt[q][i]))
        self._deps(q, reads, writes)
        ins = self.eng[q].dma_start(out=out, in_=in_, **kw)
        self.dcnt[q][i] += 16
        ins.then_inc(sem, 16)
        ev = (sem, self.dcnt[q][i])
        self._commit(ev, reads, list(writes) + list(wr_only))
        return ev

    def all_events(self):
        evs = []
        for k in self.csem:
            if self.ccnt[k] > 0:
                evs.append((self.csem[k], self.ccnt[k]))
        for q in self.dsem:
            for i in range(self.NDMA):
                if self.dcnt[q][i] > 0:
                    evs.append((self.dsem[q][i], self.dcnt[q][i]))
        return evs

    def barrier(self, engines=("pe", "act", "dve", "pool", "sp")):
        evs = self.all_events()
        for ek in engines:
            for ev in evs:
                self._wait(ek, ev)


def bc_last(t, off, nparts, mid, last, pstride, mid_stride=1):
    return bass.AP(t, off, [[pstride, nparts], [mid_stride, mid], [0, last]])


class Ctx:
    pass


def setup_consts(fw, nc, C, es, dram):
    C.ident = fw.sb(es, "ident", [128, 128], BF16)
    C.b_ident = Buf()
    C.identf = fw.sb(es, "identf", [128, 128], F32)
    C.b_identf = Buf()
    fw.dma("sp", C.identf[:], dram["c_ident"][:, :], writes=[C.b_identf])
    fw.op("dve", lambda: nc.vector.tensor_copy(C.ident[:], C.identf[:]), reads=[C.b_identf], writes=[C.b_ident])
    C.ones = fw.sb(es, "ones", [128, 128], BF16)
    C.b_ones = Buf()
    fw.op("dve", lambda: nc.vector.memset(C.ones[:], 1.0), writes=[C.b_ones])
    C.mhalf = fw.sb(es, "mhalf", [128, 2], F32)
    C.b_mhalf = Buf()
    fw.op("dve", lambda: nc.vector.memset(C.mhalf[:], -0.5), writes=[C.b_mhalf])
    fw.C = C
    C.bank = [fw.ps(es, f"bank{i}", [128, 512], F32) for i in range(8)]
    C.b_bank = [Buf(f"bank{i}") for i in range(8)]


def rms_stats(fw, nc, src_tile, b_src, junk, b_junk, ss_col, b_ss, rs_col, b_rs, n, mult=1.0):
    C = fw.C
    fw.op("act", lambda: nc.scalar.activation(junk, src_tile, AF.Square, accum_out=ss_col),
          reads=[b_src], writes=[b_junk, b_ss])
    fw.op("dve", lambda: nc.vector.tensor_scalar(ss_col, ss_col, 1.0 / n, EPS, ALU.mult, ALU.add),
          reads=[b_ss], writes=[b_ss])
    fw.op("pool", lambda: nc.gpsimd.tensor_tensor(rs_col, ss_col, C.mhalf[:, 0:1], ALU.pow),
          reads=[b_ss, C.b_mhalf], writes=[b_rs])
    if mult != 1.0:
        fw.op("dve", lambda: nc.vector.tensor_scalar(rs_col, rs_col, float(mult), None, ALU.mult),
              reads=[b_rs], writes=[b_rs])


def load_gT(fw, nc, gT, b_gT, g_dram):
    with nc.allow_non_contiguous_dma("tiny gain transpose load"):
        fw.dma("sp", gT[:], g_dram[0, :].rearrange("(k p) -> p k", p=128), writes=[b_gT])


def norm_transpose(fw, nc, C, xt, b_xt, hb, b_hb, ss_col, b_ss, rs_col, b_rs, gT, b_gT, hT, b_hT, col0, ncols=128,
                   col_step=1):
    rms_stats(fw, nc, xt, b_xt, hb, b_hb, ss_col, b_ss, rs_col, b_rs, D)
    fw.op("dve", lambda: nc.vector.tensor_scalar(hb, xt, rs_col, None, ALU.mult),
          reads=[b_xt, b_rs], writes=[b_hb])
    for half in range(2):
        bk = C.bank[half]
        bb = C.b_bank[half]
        pst = bk[:].bitcast(BF16)
        for k8 in range(8):
            k = half * 8 + k8
            fw.op("pe", lambda k=k, k8=k8, pst=pst: nc.tensor.transpose(
                pst[:, k8 * 128:(k8 + 1) * 128], hb[:, k * 128:(k + 1) * 128], C.ident[:]),
                reads=[b_hb, C.b_ident], writes=[bb], signal=(k8 == 7))
        src3 = pst.rearrange("p (k t) -> p k t", k=8)
        if col_step == 1:
            dst3 = hT[:, half * 8:(half + 1) * 8, col0:col0 + ncols]
        else:
            dst3 = hT[:, half * 8:(half + 1) * 8, col0:col0 + ncols * col_step:col_step]
        gb = bc_last(gT, half * 8, 128, 8, 128, KD)
        fw.op("dve", lambda src3=src3, dst3=dst3, gb=gb: nc.vector.tensor_tensor(dst3, src3, gb, ALU.mult),
              reads=[bb, b_gT], writes=[b_hT])


def ffn_alloc(fw, nc, es):
    A = Ctx()
    A.hT = fw.sb(es, "f_hT", [128, KD, TOWN], BF16); A.b_hT = Buf()
    A.actT = fw.sb(es, "f_actT", [128, NFF, TOWN], BF16); A.b_actT = Buf()
    A.NW = 2
    A.wg = [fw.sb(es, f"f_wg{i}", [128, KD, 128], BF16) for i in range(A.NW)]
    A.wu = [fw.sb(es, f"f_wu{i}", [128, KD, 128], BF16) for i in range(A.NW)]
    A.b_wg = [Buf() for _ in range(A.NW)]
    A.b_wu = [Buf() for _ in range(A.NW)]
    A.NWD = 2
    A.wd = [fw.sb(es, f"f_wd{i}", [128, 11, 512], BF16) for i in range(A.NWD)]
    A.b_wd = [Buf() for _ in range(A.NWD)]
    A.xt = [fw.sb(es, f"f_xt{i}", [128, D], F32) for i in range(2)]
    A.b_xt = [Buf() for _ in range(2)]
    A.hb = [fw.sb(es, f"f_hb{i}", [128, D], BF16) for i in range(2)]
    A.b_hb = [Buf() for _ in range(2)]
    A.ss = fw.sb(es, "f_ss", [128, 64], F32); A.b_ss = Buf()
    A.rs = fw.sb(es, "f_rs", [128, 64], F32); A.b_rs = Buf()
    A.gT = fw.sb(es, "f_gT", [128, KD], F32); A.b_gT = Buf()
    A.gpost = fw.sb(es, "f_gpost", [128, D], F32); A.b_gpost = Buf()
    A.sg = [fw.sb(es, f"f_sg{i}", [128, 512], F32) for i in range(2)]
    A.b_sg = [Buf() for _ in range(2)]
    A.yev = [fw.sb(es, f"f_yev{i}", [128, 512], F32) for i in range(4)]
    A.b_yev = [Buf() for _ in range(4)]
    A.cnt = 0
    return A


def ffn_pass(fw, nc, C, A, src, dst, wg_d, wu_d, wd_d, pre_g, post_g, y_d, b_yd, b_src, b_dst):
    NT = TOWN // 128
    load_gT(fw, nc, A.gT, A.b_gT, pre_g)
    fw.dma("sp", A.gpost[:], post_g[0:1, :].partition_broadcast(128), writes=[A.b_gpost])
    for tt in range(NT):
        s = tt % 2
        fw.dma("sp", A.xt[s][:], src[tt * 128:(tt + 1) * 128, :], reads=[b_src], writes=[A.b_xt[s]])
        col = A.cnt % 64
        A.cnt += 1
        norm_transpose(fw, nc, C, A.xt[s][:], A.b_xt[s], A.hb[s][:], A.b_hb[s],
                       A.ss[:, col:col + 1], A.b_ss, A.rs[:, col:col + 1], A.b_rs,
                       A.gT, A.b_gT, A.hT, A.b_hT, tt * 128)
    wgv = wg_d.rearrange("(k p) m -> p k m", p=128)
    wuv = wu_d.rearrange("(k p) m -> p k m", p=128)
    for f in range(NFF):
        s = f % A.NW
        fw.dma("pool", A.wg[s][:], wgv[:, :, f * 128:(f + 1) * 128], writes=[A.b_wg[s]])
        fw.dma("pool", A.wu[s][:], wuv[:, :, f * 128:(f + 1) * 128], writes=[A.b_wu[s]])
        for c in range(2):
            bi = (f % 2) * 4 + c * 2
            pg, pu = C.bank[bi], C.bank[bi + 1]
            bpg, bpu = C.b_bank[bi], C.b_bank[bi + 1]
            for k in range(KD):
                fw.op("pe", lambda k=k, pg=pg, s=s, c=c: nc.tensor.matmul(
                    pg[:], A.wg[s][:, k, :], A.hT[:, k, c * 512:(c + 1) * 512], start=(k == 0), stop=(k == KD - 1)),
                    reads=[A.b_wg[s], A.b_hT], writes=[bpg], signal=(k == KD - 1))
            for k in range(KD):
                fw.op("pe", lambda k=k, pu=pu, s=s, c=c: nc.tensor.matmul(
                    pu[:], A.wu[s][:, k, :], A.hT[:, k, c * 512:(c + 1) * 512], start=(k == 0), stop=(k == KD - 1)),
                    reads=[A.b_wu[s], A.b_hT], writes=[bpu], signal=(k == KD - 1))
            sgs = c
            fw.op("act", lambda pg=pg, sgs=sgs: nc.scalar.activation(A.sg[sgs][:], pg[:], AF.Silu),
                  reads=[bpg], writes=[A.b_sg[sgs]])
            fw.op("dve", lambda pu=pu, sgs=sgs, f=f, c=c: nc.vector.tensor_tensor(
                A.actT[:, f, c * 512:(c + 1) * 512], A.sg[sgs][:], pu[:], ALU.mult),
                reads=[A.b_sg[sgs], bpu], writes=[A.b_actT])
    wdv = wd_d.rearrange("(f p) m -> p f m", p=128)
    ig = 0
    iy = 0
    fw._deps("sp", (), [b_yd])
    for n in range(4):
        for g4 in range(4):
            s = ig % A.NWD
            ig += 1
            fw.dma("pool", A.wd[s][:], wdv[:, g4 * 11:(g4 + 1) * 11, n * 512:(n + 1) * 512], writes=[A.b_wd[s]])
            for tt in range(NT):
                for fi in range(11):
                    f = g4 * 11 + fi
                    last = (g4 == 3 and fi == 10)
                    fw.op("pe", lambda tt=tt, f=f, fi=fi, s=s, g4=g4: nc.tensor.matmul(
                        C.bank[tt][:], A.actT[:, f, tt * 128:(tt + 1) * 128], A.wd[s][:, fi, :],
                        start=(g4 == 0 and fi == 0), stop=(g4 == 3 and fi == 10)),
                        reads=[A.b_actT, A.b_wd[s]], writes=[C.b_bank[tt]], signal=(fi == 10))
        for tt in range(NT):
            ys = iy % 4
            iy += 1
            ek = "act" if tt % 2 == 0 else "dve"
            if ek == "act":
                fw.op("act", lambda tt=tt, ys=ys: nc.scalar.copy(A.yev[ys][:], C.bank[tt][:]),
                      reads=[C.b_bank[tt]], writes=[A.b_yev[ys]])
            else:
                fw.op("dve", lambda tt=tt, ys=ys: nc.vector.tensor_copy(A.yev[ys][:], C.bank[tt][:]),
                      reads=[C.b_bank[tt]], writes=[A.b_yev[ys]])
            fw.dma("sp", y_d[tt * 128:(tt + 1) * 128, n * 512:(n + 1) * 512], A.yev[ys][:],
                   reads=[A.b_yev[ys]], wr_only=[b_yd])
    act32 = A.actT[:].bitcast(F32)
    NDB = 3
    dby = [act32[:, 4 * i:4 * i + 4, :].rearrange("p a b -> p (a b)") for i in range(NDB)]
    dbx = [act32[:, 4 * (NDB + i):4 * (NDB + i) + 4, :].rearrange("p a b -> p (a b)") for i in range(NDB)]
    b_dby = [Buf() for _ in range(NDB)]
    b_dbx = [Buf() for _ in range(NDB)]
    fw._deps("sp", (), [A.b_actT])
    fw._deps("pool", (), [b_dst])
    for tt in range(NT):
        s = tt % 2
        d = tt % NDB
        yt, b_yt = dby[d], b_dby[d]
        xt, b_xt = dbx[d], b_dbx[d]
        fw.dma("sp", yt, y_d[tt * 128:(tt + 1) * 128, :], reads=[b_yd], writes=[b_yt])
        fw.dma("sp", xt, src[tt * 128:(tt + 1) * 128, :], reads=[b_src], writes=[b_xt])
        col = A.cnt % 64
        A.cnt += 1
        ssc = A.ss[:, col:col + 1]
        rsc = A.rs[:, col:col + 1]
        rms_stats(fw, nc, yt, b_yt, A.hb[s][:], A.b_hb[s], ssc, A.b_ss, rsc, A.b_rs, D, mult=0.5)
        fw.op("dve", lambda rsc=rsc, yt=yt: nc.vector.scalar_tensor_tensor(
            yt, yt, rsc, A.gpost[:], ALU.mult, ALU.mult),
            reads=[b_yt, A.b_rs, A.b_gpost], writes=[b_yt])
        fw.op("pool", lambda yt=yt, xt=xt: nc.gpsimd.tensor_tensor(yt, yt, xt, ALU.add),
              reads=[b_yt, b_xt], writes=[b_yt])
        fw.dma("pool", dst[tt * 128:(tt + 1) * 128, :], yt, reads=[b_yt], wr_only=[b_dst])
    for ev in fw.all_events():
        _merge(A.b_actT.w, ev)


SLOPES = [2.0 ** (-(h + 1)) for h in range(NH)]
LAM_INIT = 0.2


def mixer_common_alloc(fw, nc, es):
    M = Ctx()
    M.hmT = fw.sb(es, "m_hmT", [128, KD, S], BF16); M.b_hmT = Buf()
    M.es = es
    M.ss = fw.sb(es, "m_ss", [128, 64], F32); M.b_ss = Buf()
    M.rs = fw.sb(es, "m_rs", [128, 64], F32); M.b_rs = Buf()
    M.gT = fw.sb(es, "m_gT", [128, KD], F32); M.b_gT = Buf()
    M.cnt = 0
    return M


def build_hmT(fw, nc, C, M, x1_src, b_x1, g_dram):
    with ExitStack() as es:
        xt = [fw.sb(es, f"h_xt{i}", [128, D], F32) for i in range(2)]
        b_xt = [Buf() for _ in range(2)]
        hb = [fw.sb(es, f"h_hb{i}", [128, D], BF16) for i in range(2)]
        b_hb = [Buf() for _ in range(2)]
        load_gT(fw, nc, M.gT, M.b_gT, g_dram)
        for tt in range(S // 128):
            s = tt % 2
            fw.dma("sp", xt[s][:], x1_src[tt * 128:(tt + 1) * 128, :], reads=[b_x1], writes=[b_xt[s]])
            col = M.cnt % 64
            M.cnt += 1
            norm_transpose(fw, nc, C, xt[s][:], b_xt[s], hb[s][:], b_hb[s],
                           M.ss[:, col:col + 1], M.b_ss, M.rs[:, col:col + 1], M.b_rs,
                           M.gT, M.b_gT, M.hmT, M.b_hmT, tt * 128)
        fw.barrier()


def attention_alloc(fw, nc, es):
    A = Ctx()
    A.QT = [fw.sb(es, f"a_QT{c}", [128, NH, TOWN], BF16) for c in range(2)]
    A.b_QT = Buf()
    A.KT = fw.sb(es, "a_KT", [128, NH, S], BF16); A.b_KT = Buf()
    A.V = fw.sb(es, "a_V", [128, S // 128, 1024], BF16); A.b_V = Buf()
    for c in range(2):
        fw.op("pool", lambda c=c: nc.gpsimd.memset(A.QT[c][:], 0.0), writes=[A.b_QT])
    return A


def attention_proj(fw, nc, C, M, A, dram, u_d, b_ud, only_u=False):
    w_in = dram["w_in"]
    if A is not None:
        QT, KT, V, b_QT, b_KT, b_V = A.QT, A.KT, A.V, A.b_QT, A.b_KT, A.b_V
    with ExitStack() as es:
        wp = [fw.sb(es, f"a_wp{i}", [128, KD, 256], BF16) for i in range(2)]
        b_wp = [Buf() for _ in range(2)]
        wv = w_in.rearrange("(k p) m -> p k m", p=128)
        iw = 0
        ib = 0
        ust = [fw.sb(es, f"a_ust{i}", [128, 16, 8, 16], BF16) for i in range(2)]
        b_ust = [Buf() for _ in range(2)]
        iu = 0
        for blk in range(4):
            s = iw % 2
            iw += 1
            fw.dma("pool", wp[s][:], wv[:, :, 3072 + blk * 256:3072 + (blk + 1) * 256], writes=[b_wp[s]])
            for hh in range(2):
                us = iu % 2
                iu += 1
                for j in range(8):
                    bi = ib % 8
                    ib += 1
                    t0 = 1024 * hh + j
                    for k in range(KD):
                        fw.op("pe", lambda k=k, s=s, t0=t0, bi=bi: nc.tensor.matmul(
                            C.bank[bi][:, 0:256], M.hmT[:, k, t0:t0 + 1017:8], wp[s][:, k, :],
                            start=(k == 0), stop=(k == KD - 1)),
                            reads=[b_wp[s], M.b_hmT], writes=[C.b_bank[bi]], signal=(k == KD - 1))
                    src = C.bank[bi][:, 0:256].rearrange("c (g q) -> c g q", g=16)
                    if j % 2 == 0:
                        fw.op("act", lambda us=us, j=j, src=src: nc.scalar.copy(ust[us][:, :, j, :], src),
                              reads=[C.b_bank[bi]], writes=[b_ust[us]])
                    else:
                        fw.op("dve", lambda us=us, j=j, src=src: nc.vector.tensor_copy(ust[us][:, :, j, :], src),
                              reads=[C.b_bank[bi]], writes=[b_ust[us]])
                fw.dma("sp", u_d[:, hh, blk * 16:(blk + 1) * 16, :],
                       ust[us][:].rearrange("c g j q -> c g (j q)"), reads=[b_ust[us]], wr_only=[b_ud])
        if only_u:
            fw.barrier()
            return
        for blk in range(8):
            s = iw % 2
            iw += 1
            fw.dma("pool", wp[s][:], wv[:, :, blk * 256:(blk + 1) * 256], writes=[b_wp[s]])
            isq = blk < 4
            nch = 2 if isq else 4
            for h4 in range(2):
                h = (blk % 4) * 2 + h4
                for ch in range(nch):
                    bi = ib % 8
                    ib += 1
                    for k in range(KD):
                        fw.op("pe", lambda k=k, s=s, h4=h4, ch=ch, bi=bi: nc.tensor.matmul(
                            C.bank[bi][:], wp[s][:, k, h4 * 128:(h4 + 1) * 128], M.hmT[:, k, ch * 512:(ch + 1) * 512],
                            start=(k == 0), stop=(k == KD - 1)),
                            reads=[b_wp[s], M.b_hmT], writes=[C.b_bank[bi]], signal=(k == KD - 1))
                    if isq:
                        for c in range(2):
                            fw.op("act", lambda h=h, ch=ch, bi=bi, c=c: nc.scalar.mul(
                                QT[c][64 * c:64 * c + 64, h, ch * 512:(ch + 1) * 512],
                                C.bank[bi][64 * c:64 * c + 64, :], 0.125),
                                reads=[C.b_bank[bi]], writes=[b_QT])
                    else:
                        fw.op("dve", lambda h=h, ch=ch, bi=bi: nc.vector.tensor_copy(
                            KT[:, h, ch * 512:(ch + 1) * 512], C.bank[bi][:]),
                            reads=[C.b_bank[bi]], writes=[b_KT])
        for blk in range(4):
            s = iw % 2
            iw += 1
            fw.dma("pool", wp[s][:], wv[:, :, 2048 + blk * 256:2048 + (blk + 1) * 256], writes=[b_wp[s]])
            for tt in range(S // 128):
                bi = ib % 8
                ib += 1
                for k in range(KD):
                    fw.op("pe", lambda k=k, s=s, tt=tt, bi=bi: nc.tensor.matmul(
                        C.bank[bi][:, 0:256], M.hmT[:, k, tt * 128:(tt + 1) * 128], wp[s][:, k, :],
                        start=(k == 0), stop=(k == KD - 1)),
                        reads=[b_wp[s], M.b_hmT], writes=[C.b_bank[bi]], signal=(k == KD - 1))
                if tt % 2 == 0:
                    fw.op("act", lambda tt=tt, blk=blk, bi=bi: nc.scalar.copy(
                        V[:, tt, blk * 256:(blk + 1) * 256], C.bank[bi][:, 0:256]),
                        reads=[C.b_bank[bi]], writes=[b_V])
                else:
                    fw.op("dve", lambda tt=tt, blk=blk, bi=bi: nc.vector.tensor_copy(
                        V[:, tt, blk * 256:(blk + 1) * 256], C.bank[bi][:, 0:256]),
                        reads=[C.b_bank[bi]], writes=[b_V])
        fw.barrier()


def attention_core(fw, nc, C, M, A, dram):
    QT, KT, V, b_QT, b_KT, b_V = A.QT, A.KT, A.V, A.b_QT, A.b_KT, A.b_V
    with ExitStack() as es:
        G = fw.sb(es, "a_G", [128, 3072], F32); b_G = Buf()
        fw.dma("sp", G[:], dram["c_alibi"][:, :], writes=[b_G])
        GD = [fw.sb(es, f"a_GD{i}", [128, 3072], BF16) for i in range(2)]
        b_GD = [Buf(), Buf()]
        lq = fw.sb(es, "a_lq", [128, 4, 64], F32); b_lq = Buf()
        for i, nm in enumerate(("lam_q1", "lam_k1", "lam_q2", "lam_k2")):
            fw.dma("sp", lq[:, i, :], dram[nm][0:1, :].partition_broadcast(128), writes=[b_lq])
        sc4 = fw.sb(es, "a_sc4", [128, 8], F32); b_sc4 = Buf()
        junk = fw.sb(es, "a_junk", [128, 64], F32); b_junk = Buf()
        fw.op("dve", lambda: nc.vector.scalar_tensor_tensor(junk[:], lq[:, 0, :], 1.0, lq[:, 1, :], ALU.mult, ALU.mult,
                                                            accum_out=sc4[:, 0:1]), reads=[b_lq], writes=[b_junk, b_sc4])
        fw.op("dve", lambda: nc.vector.scalar_tensor_tensor(junk[:], lq[:, 2, :], 1.0, lq[:, 3, :], ALU.mult, ALU.mult,
                                                            accum_out=sc4[:, 1:2]), reads=[b_lq], writes=[b_junk, b_sc4])
        fw.op("act", lambda: nc.scalar.activation(sc4[:, 2:4], sc4[:, 0:2], AF.Exp), reads=[b_sc4], writes=[b_sc4])
        fw.op("dve", lambda: nc.vector.tensor_tensor(sc4[:, 4:5], sc4[:, 3:4], sc4[:, 2:3], ALU.subtract),
              reads=[b_sc4], writes=[b_sc4])
        fw.op("dve", lambda: nc.vector.tensor_scalar(sc4[:, 5:6], sc4[:, 4:5], -LAM_INIT, None, ALU.add),
              reads=[b_sc4], writes=[b_sc4])
        neg_lam = sc4[:, 5:6]
        gh = fw.sb(es, "a_gh", [128, 2], F32); b_gh = Buf()
        with nc.allow_non_contiguous_dma("tiny"):
            fw.dma("sp", gh[:, 0:1], dram["attn_head_g"][0, :].rearrange("(p o) -> p o", o=1), writes=[b_gh])
        fw.op("dve", lambda: nc.vector.tensor_scalar(gh[:, 1:2], gh[:, 0:1], 1.0 - LAM_INIT, None, ALU.mult),
              reads=[b_gh], writes=[b_gh])
        scb = [fw.sb(es, f"a_scb{i}", [128, 512], BF16) for i in range(4)]
        b_scb = [Buf() for _ in range(4)]
        pT = [fw.sb(es, f"a_pT{i}", [128, 512], BF16) for i in range(6)]
        b_pT = [Buf() for _ in range(6)]
        rz = [fw.sb(es, f"a_rz{i}", [128, 512], F32) for i in range(2)]
        b_rz = [Buf() for _ in range(2)]
        ot = [fw.sb(es, f"a_ot{i}", [128, 512], F32) for i in range(2)]
        b_ot = [Buf() for _ in range(2)]
        an = fw.sb(es, "a_an", [128, 2 * NH, 512], F32); b_an = Buf()
        sqas = [fw.sb(es, f"a_sqa{i}", [128, 512], BF16) for i in range(8)]
        b_sqas = [Buf() for _ in range(8)]
        rns = [fw.sb(es, f"a_rn{i}", [128, 512], F32) for i in range(2)]; b_rns = [Buf(), Buf()]
        iters = [(h, qc, kb, c) for h in range(NH) for qc in range(2) for kb in range(S // 128) for c in range(2)]
        SB = [0, 1, 7, 6]
        DEPTH = 3
        NKB = S // 128

        def stage1(i):
            h, qc, kb, c = iters[i]
            off = 512 * qc - 128 * kb + 1920
            sb_i = SB[i % 4]
            sbf = i % 4
            pi = i % 6
            pbank = C.bank[sb_i]
            fw.op("pe", lambda: nc.tensor.matmul(
                pbank[:], KT[:, h, kb * 128:(kb + 1) * 128],
                QT[c][:, h, qc * 512:(qc + 1) * 512], start=True, stop=True),
                reads=[b_KT, b_QT], writes=[C.b_bank[sb_i]])
            if (h == 0 and qc == 0 and kb == 0 and c == 0) or (qc == 1 and kb == 0 and c == 0 and h + 1 < NH):
                hn = 0 if (h == 0 and qc == 0) else h + 1
                fw.op("act", lambda hn=hn: nc.scalar.activation(GD[hn % 2][:], G[:], AF.Exp, scale=-SLOPES[hn]),
                      reads=[b_G], writes=[b_GD[hn % 2]])
            fw.op("act", lambda: nc.scalar.activation(scb[sbf][:], pbank[:], AF.Exp),
                  reads=[C.b_bank[sb_i]], writes=[b_scb[sbf]])
            fw.op("dve", lambda: nc.vector.tensor_tensor(pT[pi][:], scb[sbf][:], GD[h % 2][:, off:off + 512], ALU.mult),
                  reads=[b_scb[sbf], b_GD[h % 2]], writes=[b_pT[pi]])

        def stage2(i):
            h, qc, kb, c = iters[i]
            pi = i % 6
            fw.op("pe", lambda: nc.tensor.matmul(
                C.bank[2 + c][:], V[:, kb, h * 128:(h + 1) * 128], pT[pi][:],
                start=(kb == 0), stop=(kb == NKB - 1)),
                reads=[b_V, b_pT[pi]], writes=[C.b_bank[2 + c]], signal=False)
            fw.op("pe", lambda: nc.tensor.matmul(
                C.bank[4 + c][:], C.ones[:], pT[pi][:],
                start=(kb == 0), stop=(kb == NKB - 1)),
                reads=[C.b_ones, b_pT[pi], b_V], writes=[C.b_bank[2 + c], C.b_bank[4 + c]])
            if kb == NKB - 1 and c == 1:
                for cc in range(2):
                    fw.op("dve", lambda cc=cc: nc.vector.reciprocal(rz[cc][:], C.bank[4 + cc][:]),
                          reads=[C.b_bank[4 + cc]], writes=[b_rz[cc]])
                    fw.op("dve", lambda cc=cc: nc.vector.tensor_tensor(ot[cc][:], C.bank[2 + cc][:], rz[cc][:], ALU.mult),
                          reads=[C.b_bank[2 + cc], b_rz[cc]], writes=[b_ot[cc]])
                u = h * 2 + qc
                fw.op("dve", lambda u=u: nc.vector.scalar_tensor_tensor(an[:, u, :], ot[1][:], neg_lam, ot[0][:], ALU.mult, ALU.add),
                      reads=[b_ot[0], b_ot[1], b_sc4], writes=[b_an])

        def head_sq(u):
            sq_, bsq_ = sqas[u % 8], b_sqas[u % 8]
            fw.op("act", lambda: nc.scalar.activation(sq_[:], an[:, u, :], AF.Square), reads=[b_an], writes=[bsq_])

        def head_norm(u):
            h, qc = u // 2, u % 2
            sq_, bsq_ = sqas[u % 8], b_sqas[u % 8]
            r_, br_ = rns[u % 2], b_rns[u % 2]
            bk = 2 + (u % 4)
            fw.op("pe", lambda: nc.tensor.matmul(C.bank[bk][:], C.ones[:], sq_[:], start=True, stop=True),
                  reads=[C.b_ones, bsq_], writes=[C.b_bank[bk]])
            fw.op("act", lambda: nc.scalar.activation(r_[:], C.bank[bk][:], AF.Sqrt, bias=EPS, scale=1.0 / 128),
                  reads=[C.b_bank[bk]], writes=[br_])
            fw.op("dve", lambda: nc.vector.reciprocal(r_[:], r_[:]), reads=[br_], writes=[br_])
            fw.op("dve", lambda: nc.vector.scalar_tensor_tensor(
                M.aT[:, h, qc * 512:(qc + 1) * 512], an[:, u, :], gh[:, 1:2], r_[:], ALU.mult, ALU.mult),
                reads=[b_an, b_gh, br_], writes=[M.b_aT])

        NI = len(iters)
        for i in range(NI + DEPTH):
            if i < NI:
                stage1(i)
            if i >= DEPTH:
                stage2(i - DEPTH)
        for u0 in range(0, 2 * NH, 8):
            for u in range(u0, u0 + 8):
                head_sq(u)
            for u in range(u0, u0 + 8):
                head_norm(u)
        fw.barrier()


TWO_PI = 6.283185307179586
PI_C = 3.1415925


def ap4(t, off, pstride, nparts, dims):
    return bass.AP(t, off, [[pstride, nparts]] + [[st, n] for st, n in dims])


def s5_phase(fw, nc, C, dram, Yown, b_Y, u_d, b_ud):
    with ExitStack() as es:
        U2 = fw.sb(es, "s_U2", [128, 2, 64, 128], BF16); b_U2 = Buf()
        for hh in range(2):
            fw.dma("sp", U2[:, hh, :, :], u_d[:, hh, :, :], reads=[b_ud], writes=[b_U2])
        NF = 33
        PWR = fw.sb(es, "s_PWR", [128, NF, 64], F32); b_PW = Buf()
        PWI = fw.sb(es, "s_PWI", [128, NF, 64], F32)
        NAI = fw.sb(es, "s_NAI", [128, 8, 64], F32)
        BBR = fw.sb(es, "s_BBR", [128, 64, 16], F32); b_BB = Buf()
        BBI = fw.sb(es, "s_BBI", [128, 64, 16], F32)
        CTR = fw.sb(es, "s_CTR", [128, 64, 16], F32); b_CT = Buf()
        CTI = fw.sb(es, "s_CTI", [128, 64, 16], F32)
        MF = fw.sb(es, "s_MF", [128, 128], F32); b_MK = Buf()
        MB = fw.sb(es, "s_MB", [128, 128], F32)
        fw.dma("sp", MF[:], dram["c_maskF"][:, :], writes=[b_MK])
        fw.dma("sp", MB[:], dram["c_maskB"][:, :], writes=[b_MK])
        dve = nc.vector
        with ExitStack() as pes:
            LL = fw.sb(pes, "s_LL", [64, 2, 128], F32); b_LL = Buf()
            for i, nm in enumerate(("ssm_lam_re", "ssm_lam_im")):
                fw.dma("sp", LL[:, i, :].rearrange("g (d n) -> g d n", d=2), dram[nm].rearrange("d g n -> g d n"),
                       writes=[b_LL])
            LRI = fw.sb(pes, "s_LRI", [128, 2, 64], F32); b_LRI = Buf()
            for i in range(2):
                fw.op("pe", lambda i=i: nc.tensor.transpose(C.bank[0][:, i * 64:(i + 1) * 64], LL[:, i, :],
                                                            C.identf[0:64, 0:64]),
                      reads=[b_LL, C.b_identf], writes=[C.b_bank[0]])
            fw.op("dve", lambda: dve.tensor_copy(LRI[:].rearrange("p a g -> p (a g)"), C.bank[0][:, 0:128]),
                  reads=[C.b_bank[0]], writes=[b_LRI])
            LR = LRI[:, 0, :]
            LI = LRI[:, 1, :]
            DT = fw.sb(pes, "s_DT", [128, 64], F32); b_DT = Buf()
            for d in range(2):
                fw.dma("sp", DT[64 * d:64 * d + 64, :], dram["ssm_log_dt"][d:d + 1, :].partition_broadcast(64),
                       writes=[b_DT])
            fw.op("act", lambda: nc.scalar.activation(DT[:], DT[:], AF.Exp), reads=[b_DT], writes=[b_DT])
            LD = fw.sb(pes, "s_LD", [128, 2, 64], F32); b_LD = Buf()
            for i in range(2):
                fw.op("dve", lambda i=i: dve.tensor_tensor(LD[:, i, :], LRI[:, i, :], DT[:], ALU.mult),
                      reads=[b_LRI, b_DT], writes=[b_LD])
            EXPS = fw.sb(pes, "s_EXPS", [128, NF], F32); b_EX = Buf()
            fw.dma("sp", EXPS[:], dram["c_exps"][:, :], writes=[b_EX])
            ANG = fw.sb(pes, "s_ANG", [128, NF, 64], F32); b_ANG = Buf()
            MAG = fw.sb(pes, "s_MAG", [128, NF, 64], F32); b_MAG = Buf()
            KF = fw.sb(pes, "s_KF", [128, NF, 64], F32); b_KF = Buf()
            KI = fw.sb(pes, "s_KI", [128, NF, 64], I32); b_KI = Buf()
            ex_b = ap4(EXPS, 0, NF, 128, [(1, NF), (0, 64)])
            lid_b = ap4(LD, 64, 128, 128, [(0, NF), (1, 64)])
            lrd_b = ap4(LD, 0, 128, 128, [(0, NF), (1, 64)])
            fw.op("dve", lambda: dve.tensor_tensor(ANG[:], lid_b, ex_b, ALU.mult), reads=[b_LD, b_EX], writes=[b_ANG])
            fw.op("dve", lambda: dve.tensor_tensor(MAG[:], lrd_b, ex_b, ALU.mult), reads=[b_LD, b_EX], writes=[b_MAG])
            fw.op("act", lambda: nc.scalar.activation(MAG[:], MAG[:], AF.Exp), reads=[b_MAG], writes=[b_MAG])

            def reduce_clamp(A, b_A):
                fw.op("dve", lambda: dve.tensor_scalar(KF[:], A[:], 1.0 / TWO_PI, None, ALU.mult),
                      reads=[b_A], writes=[b_KF])
                fw.op("dve", lambda: dve.tensor_copy(KI[:], KF[:]), reads=[b_KF], writes=[b_KI])
                fw.op("dve", lambda: dve.tensor_copy(KF[:], KI[:]), reads=[b_KI], writes=[b_KF])
                fw.op("dve", lambda: dve.scalar_tensor_tensor(A[:], KF[:], -TWO_PI, A[:], ALU.mult, ALU.add),
                      reads=[b_KF, b_A], writes=[b_A])
                fw.op("dve", lambda: dve.tensor_scalar(A[:], A[:], PI_C, -PI_C, ALU.min, ALU.max),
                      reads=[b_A], writes=[b_A])

            reduce_clamp(ANG, b_ANG)
            fw.op("act", lambda: nc.scalar.activation(PWI[:], ANG[:], AF.Sin), reads=[b_ANG], writes=[b_PW])
            fw.op("dve", lambda: dve.tensor_scalar(ANG[:], ANG[:], 1.5707963267948966, None, ALU.add),
                  reads=[b_ANG], writes=[b_ANG])
            reduce_clamp(ANG, b_ANG)
            fw.op("act", lambda: nc.scalar.activation(PWR[:], ANG[:], AF.Sin), reads=[b_ANG], writes=[b_PW])
            fw.op("dve", lambda: dve.tensor_tensor(PWR[:], PWR[:], MAG[:], ALU.mult), reads=[b_PW, b_MAG], writes=[b_PW])
            fw.op("dve", lambda: dve.tensor_tensor(PWI[:], PWI[:], MAG[:], ALU.mult), reads=[b_PW, b_MAG], writes=[b_PW])
            fw.op("dve", lambda: dve.tensor_scalar(NAI[:], PWI[:, 24:32, :], -1.0, None, ALU.mult),
                  reads=[b_PW], writes=[b_PW])
            FT = fw.sb(pes, "s_FT", [128, 8, 64], F32); b_FT = Buf()
            a_re = PWR[:, 32, :]
            a_im = PWI[:, 32, :]
            den, t2, nr, fre, fim, tt_ = (FT[:, i, :] for i in range(6))
            ops = [
                (den, LR, LR, ALU.mult), (t2, LI, LI, ALU.mult), (den, den, t2, ALU.add),
            ]
            for o, a, b, op_ in ops:
                fw.op("dve", lambda o=o, a=a, b=b, op_=op_: dve.tensor_tensor(o, a, b, op_),
                      reads=[b_LRI, b_FT, b_PW], writes=[b_FT])
            fw.op("dve", lambda: dve.reciprocal(den, den), reads=[b_FT], writes=[b_FT])
            fw.op("dve", lambda: dve.tensor_scalar(nr, a_re, -1.0, None, ALU.add), reads=[b_PW], writes=[b_FT])
            ops = [
                (fre, nr, LR, ALU.mult), (tt_, a_im, LI, ALU.mult), (fre, fre, tt_, ALU.add), (fre, fre, den, ALU.mult),
                (fim, a_im, LR, ALU.mult), (tt_, nr, LI, ALU.mult), (fim, fim, tt_, ALU.subtract),
                (fim, fim, den, ALU.mult),
            ]
            for o, a, b, op_ in ops:
                fw.op("dve", lambda o=o, a=a, b=b, op_=op_: dve.tensor_tensor(o, a, b, op_),
                      reads=[b_LRI, b_FT, b_PW], writes=[b_FT])
            BR = fw.sb(pes, "s_BR", [128, 64, 16], F32); b_BRI = Buf()
            BI = fw.sb(pes, "s_BI", [128, 64, 16], F32)
            TB_ = fw.sb(pes, "s_TB", [128, 64, 16], F32); b_TB = Buf()
            for d in range(2):
                fw.dma("sp", BR[64 * d:64 * d + 64, :, :], dram["ssm_b_re"][d].rearrange("g n q -> n g q"), writes=[b_BRI])
                fw.dma("sp", BI[64 * d:64 * d + 64, :, :], dram["ssm_b_im"][d].rearrange("g n q -> n g q"), writes=[b_BRI])
            fre_b = ap4(FT, 3 * 64, 8 * 64, 128, [(1, 64), (0, 16)])
            fim_b = ap4(FT, 4 * 64, 8 * 64, 128, [(1, 64), (0, 16)])
            seq = [
                (BBR[:], fre_b, BR[:], ALU.mult), (TB_[:], fim_b, BI[:], ALU.mult), (BBR[:], BBR[:], TB_[:], ALU.subtract),
                (BBI[:], fre_b, BI[:], ALU.mult), (TB_[:], fim_b, BR[:], ALU.mult), (BBI[:], BBI[:], TB_[:], ALU.add),
            ]
            for o, a, b, op_ in seq:
                fw.op("dve", lambda o=o, a=a, b=b, op_=op_: dve.tensor_tensor(o, a, b, op_),
                      reads=[b_FT, b_BRI, b_TB, b_BB], writes=[b_TB, b_BB])
            Cin = fw.sb(pes, "s_Cin", [128, 2, 8, 128], F32); b_Cin = Buf()
            for i, nm in enumerate(("ssm_c_re", "ssm_c_im")):
                for d in range(2):
                    fw.dma("sp", Cin[:, i, :, 64 * d:64 * d + 64],
                           dram[nm][d].rearrange("(gb g8) p n -> (g8 p) gb n", g8=8), writes=[b_Cin])
            for i, CT in enumerate((CTR, CTI)):
                for half in range(2):
                    bk = C.bank[half]
                    for q4 in range(4):
                        gb_ = half * 4 + q4
                        fw.op("pe", lambda i=i, gb_=gb_, q4=q4, bk=bk: nc.tensor.transpose(
                            bk[:, q4 * 128:(q4 + 1) * 128], Cin[:, i, gb_, :], C.identf[:]),
                            reads=[b_Cin, C.b_identf], writes=[C.b_bank[half]], signal=(q4 == 3))
                    fw.op("dve", lambda CT=CT, half=half, bk=bk: dve.tensor_copy(
                        CT[:].rearrange("p g q -> p (g q)")[:, half * 512:(half + 1) * 512], bk[:]),
                        reads=[C.b_bank[half]], writes=[b_CT])
            fw.barrier()
        GB = 8
        CO_re = fw.sb(es, "s_COre", [128, GB, 128], BF16); b_CO = Buf()
        CO_imn = fw.sb(es, "s_COim", [128, GB, 128], BF16)
        WX_re = fw.sb(es, "s_WXre", [128, GB, 128], BF16); b_WX = Buf()
        WX_im = fw.sb(es, "s_WXim", [128, GB, 128], BF16)
        W2_re = fw.sb(es, "s_W2re", [128, GB, 128], BF16); b_W2 = Buf()
        W2_im = fw.sb(es, "s_W2im", [128, GB, 128], BF16)
        WXT_re = fw.sb(es, "s_WXTre", [128, GB, 128], BF16); b_WXT = Buf()
        WXT_im = fw.sb(es, "s_WXTim", [128, GB, 128], BF16)
        TT = fw.sb(es, "s_TT", [128, GB, 128], BF16); b_TT = Buf()
        T1 = fw.sb(es, "s_T1", [128, GB * 128], F32); b_T1 = Buf()
        T2 = fw.sb(es, "s_T2", [128, GB * 128], F32); b_T2 = Buf()
        U8b = fw.sb(es, "s_U8b", [128, GB, 256], BF16); b_U8 = Buf()
        NI = 4
        ST = [[fw.sb(es, f"s_ST{a_}{b_}", [128, 2, 256], F32) for b_ in range(3)] for a_ in range(NI)]
        b_ST = [[Buf(), Buf(), Buf()] for _ in range(NI)]
        SS = [fw.sb(es, f"s_SS{a_}", [128, 2, 256], F32) for a_ in range(NI)]
        b_SS = [Buf() for _ in range(NI)]
        E = [fw.sb(es, f"s_E{a_}", [128, 2, 128], BF16) for a_ in range(NI)]
        b_E = [Buf() for _ in range(NI)]
        for a_ in range(NI):
            for b_ in range(3):
                fw.op("dve", lambda a_=a_, b_=b_: dve.memset(ST[a_][b_][:], 0.0), writes=[b_ST[a_][b_]])
            fw.op("dve", lambda a_=a_: dve.memset(E[a_][:], 0.0), writes=[b_E[a_]])
            fw.op("dve", lambda a_=a_: dve.memset(SS[a_][:], 0.0), writes=[b_SS[a_]])
        ig = 0
        for gb in range(NG // GB):
            g0 = gb * GB

            def expand(fam, XR, XI, b_X, o_re, o_im, b_o, neg_im):
                pwr = ap4(PWR, fam * 8 * 64 + g0, NF * 64, 128, [(1, GB), (64, 8), (0, 16)])
                pwi = ap4(PWI, fam * 8 * 64 + g0, NF * 64, 128, [(1, GB), (64, 8), (0, 16)])
                xr = ap4(XR, g0 * 16, 1024, 128, [(16, GB), (0, 8), (1, 16)])
                xi = ap4(XI, g0 * 16, 1024, 128, [(16, GB), (0, 8), (1, 16)])
                t1 = T1[:].rearrange("p (g j q) -> p g j q", g=GB, j=8)
                t2 = T2[:].rearrange("p (g j q) -> p g j q", g=GB, j=8)
                ore = o_re[:].rearrange("p g (j q) -> p g j q", j=8)
                oim = o_im[:].rearrange("p g (j q) -> p g j q", j=8)
                fw.op("dve", lambda: dve.tensor_tensor(t1, pwr, xr, ALU.mult), reads=[b_PW, b_X], writes=[b_T1])
                fw.op("pool", lambda: nc.gpsimd.tensor_tensor(t2, pwi, xi, ALU.mult), reads=[b_PW, b_X], writes=[b_T2])
                fw.op("dve", lambda: dve.tensor_tensor(ore, t1, t2, ALU.subtract), reads=[b_T1, b_T2], writes=[b_o])
                fw.op("dve", lambda: dve.tensor_tensor(t1, pwr, xi, ALU.mult), reads=[b_PW, b_X], writes=[b_T1])
                fw.op("pool", lambda: nc.gpsimd.tensor_tensor(t2, pwi, xr, ALU.mult), reads=[b_PW, b_X], writes=[b_T2])
                if neg_im:
                    fw.op("dve", lambda: dve.scalar_tensor_tensor(oim, t1, -1.0, t2, ALU.mult, ALU.subtract),
                          reads=[b_T1, b_T2], writes=[b_o])
                else:
                    fw.op("dve", lambda: dve.tensor_tensor(oim, t1, t2, ALU.add), reads=[b_T1, b_T2], writes=[b_o])

            expand(0, CTR, CTI, b_CT, CO_re, CO_imn, b_CO, True)
            expand(1, BBR, BBI, b_BB, WX_re, WX_im, b_WX, False)
            expand(2, BBR, BBI, b_BB, W2_re, W2_im, b_W2, False)
            for src, dstT in ((WX_re, WXT_re), (WX_im, WXT_im)):
                for half in range(GB // 8):
                    bk = C.bank[half]
                    pst = bk[:].bitcast(BF16)
                    for q8 in range(8):
                        gl = half * 8 + q8
                        fw.op("pe", lambda src=src, gl=gl, q8=q8, pst=pst: nc.tensor.transpose(
                            pst[:, q8 * 128:(q8 + 1) * 128], src[:, gl, :], C.ident[:]),
                            reads=[b_WX, C.b_ident], writes=[C.b_bank[half]], signal=(q8 == 7))
                    fw.op("act", lambda dstT=dstT, half=half, pst=pst: nc.scalar.copy(
                        dstT[:, half * 8:(half + 1) * 8, :].rearrange("p g m -> p (g m)"), pst),
                        reads=[C.b_bank[half]], writes=[b_WXT])
            for q in range(GB // 4):
                for g4 in range(4):
                    gl = q * 4 + g4
                    for (lo, bki) in ((0, 2), (64, 3)):
                        fw.op("pe", lambda gl=gl, g4=g4, lo=lo, bki=bki: nc.tensor.matmul(
                            C.bank[bki][:, g4 * 128:(g4 + 1) * 128], W2_re[lo:lo + 64, gl, :], CO_re[lo:lo + 64, gl, :],
                            start=True, stop=False),
                            reads=[b_W2, b_CO], writes=[C.b_bank[bki]], signal=False)
                        fw.op("pe", lambda gl=gl, g4=g4, lo=lo, bki=bki: nc.tensor.matmul(
                            C.bank[bki][:, g4 * 128:(g4 + 1) * 128], W2_im[lo:lo + 64, gl, :], CO_imn[lo:lo + 64, gl, :],
                            start=False, stop=True),
                            reads=[b_W2, b_CO], writes=[C.b_bank[bki]], signal=(g4 == 3))
                mf = ap4(MF, 0, 128, 128, [(0, 4), (1, 128)])
                mb = ap4(MB, 0, 128, 128, [(0, 4), (1, 128)])
                t1v = T1[:, 0:512].rearrange("p (g m) -> p g m", g=4)
                t2v = T2[:, 0:512].rearrange("p (g m) -> p g m", g=4)
                fw.op("dve", lambda t1v=t1v, mf=mf: dve.tensor_tensor(
                    t1v, C.bank[2][:].rearrange("p (g m) -> p g m", g=4), mf, ALU.mult),
                    reads=[C.b_bank[2], b_MK], writes=[b_T1])
                fw.op("dve", lambda t2v=t2v, mb=mb: dve.tensor_tensor(
                    t2v, C.bank[3][:].rearrange("p (g m) -> p g m", g=4), mb, ALU.mult),
                    reads=[C.b_bank[3], b_MK], writes=[b_T2])
                fw.op("dve", lambda q=q, t1v=t1v, t2v=t2v: dve.tensor_tensor(
                    TT[:, q * 4:(q + 1) * 4, :], t1v, t2v, ALU.add), reads=[b_T1, b_T2], writes=[b_TT])
            for hh in range(2):
                for half in range(GB // 8):
                    bk = C.bank[(half + hh) % 2]
                    pst = bk[:].bitcast(BF16)
                    for q8 in range(8):
                        g = g0 + half * 8 + q8
                        fw.op("pe", lambda hh=hh, g=g, q8=q8, pst=pst: nc.tensor.transpose(
                            pst[:, q8 * 128:(q8 + 1) * 128], U2[:, hh, g, :], C.ident[:]),
                            reads=[b_U2, C.b_ident], writes=[C.b_bank[(half + hh) % 2]], signal=(q8 == 7))
                    fw.op("act", lambda hh=hh, half=half, pst=pst: nc.scalar.copy(
                        U8b[:, half * 8:(half + 1) * 8, hh * 128:(hh + 1) * 128],
                        pst.rearrange("p (g c) -> p g c", g=8)),
                        reads=[C.b_bank[(half + hh) % 2]], writes=[b_U8])
            def load_set(gp):
                info = []
                for st in range(NI):
                    gl = gp * NI + st
                    g = g0 + gl
                    xb = 2 + st
                    info.append((gl, g, xb))
                    fw.op("pe", lambda gl=gl, xb=xb: nc.tensor.matmul(
                        C.bank[xb][:, 0:256], WXT_re[:, gl, :], U8b[:, gl, :], start=True, stop=True),
                        reads=[b_WXT, b_U8], writes=[C.b_bank[xb]], signal=False)
                    fw.op("pe", lambda gl=gl, xb=xb: nc.tensor.matmul(
                        C.bank[xb][:, 256:512], WXT_im[:, gl, :], U8b[:, gl, :], start=True, stop=True),
                        reads=[b_WXT, b_U8], writes=[C.b_bank[xb]])
                    bx = C.bank[xb]
                    S0 = ST[st][2]
                    fw.op("act", lambda bx=bx, S0=S0: nc.scalar.copy(
                        S0[0:64, :, :], bx[0:64, :].rearrange("p (a c) -> p a c", a=2)),
                        reads=[C.b_bank[xb]], writes=[b_ST[st][2]])
                    fw.op("act", lambda bx=bx, S0=S0: nc.scalar.copy(S0[64:128, 0, :], bx[64:128, 255::-1]),
                          reads=[C.b_bank[xb]], writes=[b_ST[st][2]])
                    fw.op("act", lambda bx=bx, S0=S0: nc.scalar.copy(S0[64:128, 1, :], bx[64:128, 511:255:-1]),
                          reads=[C.b_bank[xb]], writes=[b_ST[st][2]])
                return info

            def scan_step(info, k, cur, nxt):
                sft = 1 << k
                n = 256 - sft
                for st in range(NI):
                    gl, g, xb = info[st]
                    Sc, Sn, Sx = ST[st][cur], ST[st][nxt], SS[st]
                    ar = PWR[:, 24 + k, g:g + 1]
                    ai = PWI[:, 24 + k, g:g + 1]
                    nai = NAI[:, k, g:g + 1]
                    fw.op("dve", lambda Sc=Sc, Sx=Sx, ar=ar: dve.scalar_tensor_tensor(
                        Sx[:, :, sft:256], Sc[:, :, 0:n], ar, Sc[:, :, sft:256], ALU.mult, ALU.add),
                        reads=[b_ST[st][cur], b_PW], writes=[b_SS[st]])
                    fw.op("dve", lambda Sc=Sc, Sn=Sn, Sx=Sx, nai=nai: dve.scalar_tensor_tensor(
                        Sn[:, 0, sft:256], Sc[:, 1, 0:n], nai, Sx[:, 0, sft:256], ALU.mult, ALU.add),
                        reads=[b_ST[st][cur], b_SS[st], b_PW], writes=[b_ST[st][nxt]])
                    fw.op("dve", lambda Sc=Sc, Sn=Sn, Sx=Sx, ai=ai: dve.scalar_tensor_tensor(
                        Sn[:, 1, sft:256], Sc[:, 0, 0:n], ai, Sx[:, 1, sft:256], ALU.mult, ALU.add),
                        reads=[b_ST[st][cur], b_SS[st], b_PW], writes=[b_ST[st][nxt]])
                    fw.op("act", lambda Sc=Sc, Sn=Sn: nc.scalar.copy(Sn[:, :, 0:sft], Sc[:, :, 0:sft]),
                          reads=[b_ST[st][cur]], writes=[b_ST[st][nxt]])

            NSET = GB // NI
            nxt_info = load_set(0)
            for gp in range(NSET):
                info3 = nxt_info
                info = []
                for st in range(NI):
                    gl, g, xb = info3[st]
                    yb = 6 + ((ig // 4) % 2)
                    g4 = ig % 4
                    ig += 1
                    info.append((gl, g, xb, yb, g4))
                seq = [2, 1, 0, 1, 0, 1, 0, 1, 0]
                scan_step(info3, 0, seq[0], seq[1])
                if gp + 1 < NSET:
                    nxt_info = load_set(gp + 1)
                for k in range(1, 8):
                    scan_step(info3, k, seq[k], seq[k + 1])
                cur = 0
                for st in range(NI):
                    gl, g, xb, yb, g4 = info[st]
                    Sf = ST[st][cur]
                    Es = E[st]
                    fw.op("act", lambda Sf=Sf, Es=Es: nc.scalar.copy(Es[0:64, :, 1:128], Sf[0:64, :, 0:127]),
                          reads=[b_ST[st][cur]], writes=[b_E[st]])
                    fw.op("pool", lambda Sf=Sf, Es=Es: nc.gpsimd.tensor_copy(Es[64:128, 0, :], Sf[64:128, 0, 254:126:-1]),
                          reads=[b_ST[st][cur]], writes=[b_E[st]])
                    fw.op("pool", lambda Sf=Sf, Es=Es: nc.gpsimd.tensor_copy(Es[64:128, 1, :], Sf[64:128, 1, 254:126:-1]),
                          reads=[b_ST[st][cur]], writes=[b_E[st]])
                    yo = C.bank[yb][:, g4 * 128:(g4 + 1) * 128]
                    fw.op("pe", lambda gl=gl, yo=yo: nc.tensor.matmul(yo, U8b[:, gl, 0:128], TT[:, gl, :], start=True, stop=False),
                          reads=[b_U8, b_TT], writes=[C.b_bank[yb]], signal=False)
                    fw.op("pe", lambda gl=gl, yo=yo, Es=Es: nc.tensor.matmul(yo, Es[:, 0, :], CO_re[:, gl, :], start=False, stop=False),
                          reads=[b_E[st], b_CO], writes=[C.b_bank[yb]], signal=False)
                    fw.op("pe", lambda gl=gl, yo=yo, Es=Es: nc.tensor.matmul(yo, Es[:, 1, :], CO_imn[:, gl, :], start=False, stop=True),
                          reads=[b_E[st], b_CO, b_U8, b_TT], writes=[C.b_bank[yb]])
                    if g4 == 3:
                        gbase = g - 3
                        fw.op("dve", lambda yb=yb, gbase=gbase: dve.tensor_copy(
                            Yown[:, :, gbase * 16:(gbase + 4) * 16].rearrange("c j (g p) -> c g j p", g=4),
                            C.bank[yb][:].rearrange("c (g j p) -> c g j p", g=4, j=8)),
                            reads=[C.b_bank[yb]], writes=[b_Y])
        dB = T2[:, 0:1024]
        b_dB = b_T2
        fw.dma("sp", dB, dram["ssm_d"][0:1, :].partition_broadcast(128), writes=[b_T2])
        for j in range(8):
            fw.op("pool", lambda j=j: nc.gpsimd.tensor_tensor(
                T1[:, 0:1024].rearrange("c (g p) -> c g p", g=64), dB.rearrange("c (g p) -> c g p", g=64),
                U2[:, 0, :, j * 16:(j + 1) * 16], ALU.mult),
                  reads=[b_dB, b_U2], writes=[b_T1])
            fw.op("dve", lambda j=j: dve.tensor_tensor(Yown[:, j, :], Yown[:, j, :], T1[:, 0:1024], ALU.add),
                  reads=[b_T1, b_Y], writes=[b_Y])
        fw.barrier()


GELU_K0 = 0.7978845608028654
GELU_K1 = 0.7978845608028654 * 0.044715


def post_phase(fw, nc, C, dram, Yown, b_Y, aT, b_aT, x1_d, b_x1d, y_d, b_yd, x2_d, b_x2d):
    dve = nc.vector
    with ExitStack() as es:
        sT = fw.sb(es, "p_sT", [128, 8, TOWN], BF16); b_sT = Buf()
        with ExitStack() as es1:
            wglu = fw.sb(es1, "p_wglu", [128, 8, 1024], BF16); b_wglu = Buf()
            wgv = dram["ssm_w_glu"].rearrange("(k p) m -> p k m", p=128)
            for h2 in range(2):
                fw.dma("pool", wglu[:, :, h2 * 512:(h2 + 1) * 512], wgv[:, :, h2 * 512:(h2 + 1) * 512], writes=[b_wglu])
            bglu = fw.sb(es1, "p_bglu", [128, 1024], F32); b_bg = Buf()
            outg = fw.sb(es1, "p_outg", [128, 1024], F32)
            fw.dma("sp", bglu[:], dram["ssm_b_glu"][0:1, :].partition_broadcast(128), writes=[b_bg])
            fw.dma("sp", outg[:], dram["ssm_out_g"][0:1, :].partition_broadcast(128), writes=[b_bg])
            tas = [fw.sb(es1, f"p_ta{i}", [128, 1024], F32) for i in range(2)]; b_tas = [Buf(), Buf()]
            tbs = [fw.sb(es1, f"p_tb{i}", [128, 1024], F32) for i in range(2)]; b_tbs = [Buf(), Buf()]
            g1s = [fw.sb(es1, f"p_g1{i}", [128, 1024], F32) for i in range(2)]; b_g1s = [Buf(), Buf()]
            g1bs = [fw.sb(es1, f"p_g1b{i}", [128, 1024], BF16) for i in range(2)]; b_g1bs = [Buf(), Buf()]
            g1Ts = [fw.sb(es1, f"p_g1T{i}", [128, 8, 128], BF16) for i in range(2)]; b_g1Ts = [Buf(), Buf()]
            ss = fw.sb(es1, "p_ss", [128, 16], F32); b_ss = Buf()
            rs = fw.sb(es1, "p_rs", [128, 16], F32); b_rs = Buf()
            mhalf = fw.sb(es1, "p_mhalf", [128, 2], F32); b_mh = Buf()
            fw.op("dve", lambda: dve.memset(mhalf[:], -0.5), writes=[b_mh])
            for j in range(8):
                pj = j % 2
                ta, b_ta, tb, b_tb, g1, b_g1 = tas[pj], b_tas[pj], tbs[pj], b_tbs[pj], g1s[pj], b_g1s[pj]
                g1b, b_g1b, g1T, b_g1T = g1bs[pj], b_g1bs[pj], g1Ts[pj], b_g1Ts[pj]
                bk0 = 4 * pj
                y = Yown[:, j, :]
                fw.op("dve", lambda y=y, ta=ta: dve.tensor_tensor(ta[:], y, y, ALU.mult), reads=[b_Y], writes=[b_ta])
                fw.op("dve", lambda ta=ta: dve.tensor_scalar(ta[:], ta[:], GELU_K1, GELU_K0, ALU.mult, ALU.add),
                      reads=[b_ta], writes=[b_ta])
                fw.op("dve", lambda y=y, ta=ta: dve.tensor_tensor(ta[:], ta[:], y, ALU.mult), reads=[b_ta, b_Y], writes=[b_ta])
                fw.op("act", lambda ta=ta, tb=tb: nc.scalar.activation(tb[:], ta[:], AF.Sigmoid, scale=2.0),
                      reads=[b_ta], writes=[b_tb])
                fw.op("dve", lambda y=y, tb=tb, g1=g1: dve.tensor_tensor(g1[:], y, tb[:], ALU.mult),
                      reads=[b_tb, b_Y], writes=[b_g1])
                fw.op("pool", lambda g1=g1, g1b=g1b: nc.gpsimd.tensor_copy(g1b[:], g1[:]), reads=[b_g1], writes=[b_g1b])
                pst = C.bank[bk0][:].bitcast(BF16)
                for kf in range(8):
                    fw.op("pe", lambda kf=kf, pst=pst, g1b=g1b: nc.tensor.transpose(
                        pst[:, kf * 128:(kf + 1) * 128], g1b[:, kf * 128:(kf + 1) * 128], C.ident[:]),
                        reads=[b_g1b, C.b_ident], writes=[C.b_bank[bk0]], signal=(kf == 7))
                fw.op("dve", lambda pst=pst, g1T=g1T: dve.tensor_copy(g1T[:].rearrange("p k c -> p (k c)"), pst),
                      reads=[C.b_bank[bk0]], writes=[b_g1T])
                for n in range(2):
                    bk = bk0 + 2 + n
                    for kf in range(8):
                        fw.op("pe", lambda kf=kf, n=n, bk=bk, g1T=g1T: nc.tensor.matmul(
                            C.bank[bk][:], g1T[:, kf, :], wglu[:, kf, n * 512:(n + 1) * 512],
                            start=(kf == 0), stop=(kf == 7)),
                            reads=[b_g1T, b_wglu], writes=[C.b_bank[bk]], signal=(kf == 7))
                    fw.op("dve", lambda n=n, bk=bk, ta=ta: dve.tensor_tensor(
                        ta[:, n * 512:(n + 1) * 512], C.bank[bk][:], bglu[:, n * 512:(n + 1) * 512], ALU.add),
                        reads=[C.b_bank[bk], b_bg], writes=[b_ta])
                fw.op("act", lambda ta=ta, tb=tb: nc.scalar.activation(tb[:], ta[:], AF.Sigmoid), reads=[b_ta], writes=[b_tb])
                fw.op("dve", lambda g1=g1, tb=tb: dve.tensor_tensor(g1[:], g1[:], tb[:], ALU.mult),
                      reads=[b_tb, b_g1], writes=[b_g1])
                fw.op("dve", lambda g1=g1, ta=ta, j=j: dve.scalar_tensor_tensor(
                    ta[:], g1[:], 1.0 / 1024, g1[:], ALU.mult, ALU.mult, accum_out=ss[:, j:j + 1]),
                    reads=[b_g1], writes=[b_ta, b_ss])
                fw.op("dve", lambda j=j: dve.tensor_scalar(ss[:, j:j + 1], ss[:, j:j + 1], EPS, None, ALU.add),
                      reads=[b_ss], writes=[b_ss])
                fw.op("pool", lambda j=j: nc.gpsimd.tensor_tensor(rs[:, j:j + 1], ss[:, j:j + 1], mhalf[:, 0:1], ALU.pow),
                      reads=[b_ss, b_mh], writes=[b_rs])
                fw.op("dve", lambda j=j, g1=g1, g1b=g1b: dve.scalar_tensor_tensor(
                    g1b[:], g1[:], rs[:, j:j + 1], outg[:], ALU.mult, ALU.mult),
                    reads=[b_g1, b_rs, b_bg], writes=[b_g1b])
                pst2 = C.bank[bk0 + 1][:].bitcast(BF16)
                for kf in range(8):
                    fw.op("pe", lambda kf=kf, pst2=pst2, g1b=g1b: nc.tensor.transpose(
                        pst2[:, kf * 128:(kf + 1) * 128], g1b[:, kf * 128:(kf + 1) * 128], C.ident[:]),
                        reads=[b_g1b, C.b_ident], writes=[C.b_bank[bk0 + 1]], signal=(kf == 7))
                fw.op("dve", lambda pst2=pst2, j=j: dve.tensor_copy(
                    sT[:, :, j * 128:(j + 1) * 128], pst2.rearrange("p (k c) -> p k c", k=8)),
                    reads=[C.b_bank[bk0 + 1]], writes=[b_sT])
            fw.barrier()
        wo = [fw.sb(es, f"p_wo{i}", [128, KD, 512], BF16) for i in range(2)]
        b_wo = [Buf(), Buf()]
        yev = [fw.sb(es, f"p_yev{i}", [128, 512], F32) for i in range(4)]
        b_yev = [Buf() for _ in range(4)]
        wov = dram["w_out"].rearrange("(k p) m -> p k m", p=128)
        iy = 0
        fw._deps("sp", (), [b_yd])
        for n in range(4):
            s = n % 2
            fw.dma("pool", wo[s][:], wov[:, :, n * 512:(n + 1) * 512], writes=[b_wo[s]])
            for j in range(8):
                for k in range(KD):
                    if k < 8:
                        lhsT = aT[:, k, j:j + 1017:8]
                    else:
                        lhsT = sT[:, k - 8, j * 128:(j + 1) * 128]
                    fw.op("pe", lambda k=k, j=j, s=s, lhsT=lhsT: nc.tensor.matmul(
                        C.bank[j][:], lhsT, wo[s][:, k, :], start=(k == 0), stop=(k == KD - 1)),
                        reads=[b_aT, b_sT, b_wo[s]], writes=[C.b_bank[j]], signal=(k == KD - 1))
                ys = iy % 4
                iy += 1
                if j % 2 == 0:
                    fw.op("act", lambda j=j, ys=ys: nc.scalar.copy(yev[ys][:], C.bank[j][:]),
                          reads=[C.b_bank[j]], writes=[b_yev[ys]])
                else:
                    fw.op("dve", lambda j=j, ys=ys: dve.tensor_copy(yev[ys][:], C.bank[j][:]),
                          reads=[C.b_bank[j]], writes=[b_yev[ys]])
                fw.dma("sp", y_d[j * 128:(j + 1) * 128, n * 512:(n + 1) * 512], yev[ys][:],
                       reads=[b_yev[ys]], wr_only=[b_yd])
        gpost = fw.sb(es, "p_gpost", [128, D], F32); b_gp = Buf()
        fw.dma("sp", gpost[:], dram["mix_post_g"][0:1, :].partition_broadcast(128), writes=[b_gp])
        xts = [fw.sb(es, f"p_xt{i}", [128, D], F32) for i in range(2)]; b_xts = [Buf(), Buf()]
        yts = [fw.sb(es, f"p_yt{i}", [128, D], F32) for i in range(2)]; b_yts = [Buf(), Buf()]
        junks = [fw.sb(es, f"p_junk{i}", [128, D], BF16) for i in range(2)]; b_junks = [Buf(), Buf()]
        ss2 = fw.sb(es, "p_ss2", [128, 16], F32); b_ss2 = Buf()
        rs2 = fw.sb(es, "p_rs2", [128, 16], F32); b_rs2 = Buf()
        for j in range(8):
            pj = j % 2
            xt, b_xt, yt, b_yt, junk, b_junk = xts[pj], b_xts[pj], yts[pj], b_yts[pj], junks[pj], b_junks[pj]
            fw.dma("sp", yt[:], y_d[j * 128:(j + 1) * 128, :], reads=[b_yd], writes=[b_yt])
            fw.dma("sp", xt[:], x1_d[j:j + 1017:8, :], reads=[b_x1d], writes=[b_xt])
            rms_stats(fw, nc, yt[:], b_yt, junk[:], b_junk, ss2[:, j:j + 1], b_ss2, rs2[:, j:j + 1], b_rs2, D)
            fw.op("dve", lambda j=j, yt=yt: dve.scalar_tensor_tensor(yt[:], yt[:], rs2[:, j:j + 1], gpost[:], ALU.mult, ALU.mult),
                  reads=[b_yt, b_rs2, b_gp], writes=[b_yt])
            fw.op("pool", lambda yt=yt, xt=xt: nc.gpsimd.tensor_tensor(yt[:], yt[:], xt[:], ALU.add),
                  reads=[b_yt, b_xt], writes=[b_yt])
            fw.dma("pool", x2_d[j:j + 1017:8, :], yt[:], reads=[b_yt], wr_only=[b_x2d])
        fw.barrier()

def build(stage="full"):
    nc = bass.Bass("TRN2", target_bir_lowering=False)
    dram = {}

    def din(name, shape, dt=F32):
        dram[name] = nc.dram_tensor(name, list(shape), dt, kind="ExternalInput").ap()

    din("xs", [S, D])
    din("c_ident", [128, 128])
    for p in ("ff1", "ff2"):
        din(p + "_pre_g", [1, D]); din(p + "_post_g", [1, D])
        din(p + "_w_gate", [D, DFF]); din(p + "_w_up", [D, DFF]); din(p + "_w_down", [DFF, D])
    din("c_alibi", [128, 3072])
    din("mix_pre_g", [1, D]); din("mix_post_g", [1, D])
    din("w_in", [D, 4096]); din("w_out", [D, D])
    for nm in ("lam_q1", "lam_k1", "lam_q2", "lam_k2"):
        din(nm, [1, 64])
    din("attn_head_g", [1, 128])
    din("c_exps", [128, 33]); din("c_maskF", [128, 128]); din("c_maskB", [128, 128])
    din("ssm_lam_re", [2, 64, 64]); din("ssm_lam_im", [2, 64, 64]); din("ssm_log_dt", [2, 64])
    din("ssm_b_re", [2, 64, 64, 16]); din("ssm_b_im", [2, 64, 64, 16])
    din("ssm_c_re", [2, 64, 16, 64]); din("ssm_c_im", [2, 64, 16, 64])
    din("ssm_d", [1, 1024])
    u_d = nc.dram_tensor("u_d", [128, 2, 64, 128], BF16, kind="Internal").ap()
    b_ud = Buf()
    din("ssm_w_glu", [1024, 1024]); din("ssm_b_glu", [1, 1024]); din("ssm_out_g", [1, 1024])
    x2_d = nc.dram_tensor("x2_d", [TOWN, D], F32, kind="Internal").ap()
    b_x2d = Buf()
    if stage == "s5":
        dbg_Y = nc.dram_tensor("dbg_Y", [128, 8, 1024], F32, kind="ExternalOutput").ap()
    out = nc.dram_tensor("out", [TOWN, D], F32, kind="ExternalOutput").ap()
    if stage == "attn":
        dbg_aT = nc.dram_tensor("dbg_aT", [128, 8, TOWN], BF16, kind="ExternalOutput").ap()
    x1_d = nc.dram_tensor("x1_d", [S, D], F32, kind="Internal").ap()
    y_d = nc.dram_tensor("y_d", [TOWN, D], F32, kind="Internal").ap()
    b_x1d, b_yd, b_xs, b_out = Buf(), Buf(), Buf(), Buf()

    with ExitStack() as es:
        fw = FW(nc, es)
        C = Ctx()
        setup_consts(fw, nc, C, es, dram)
        if stage == "ffn":
            with ExitStack() as pes:
                A = ffn_alloc(fw, nc, pes)
                ffn_pass(fw, nc, C, A, dram["xs"][0:TOWN, :], out, dram["ff1_w_gate"], dram["ff1_w_up"],
                         dram["ff1_w_down"], dram["ff1_pre_g"], dram["ff1_post_g"], y_d, b_yd, b_xs, b_out)
                fw.barrier()
        if stage == "attn":
            with ExitStack() as mes:
                aT = fw.sb(mes, "m_aT", [128, 8, TOWN], BF16); b_aT = Buf()
                AA = attention_alloc(fw, nc, mes)
                with ExitStack() as mes3:
                    M = mixer_common_alloc(fw, nc, mes3)
                    M.aT, M.b_aT = aT, b_aT
                    build_hmT(fw, nc, C, M, dram["xs"], b_xs, dram["mix_pre_g"])
                    attention_proj(fw, nc, C, M, AA, dram, u_d, b_ud)
                attention_core(fw, nc, C, M, AA, dram)
                fw.dma("sp", dbg_aT[:, :, :], M.aT[:], reads=[M.b_aT], writes=[b_out])
                fw.barrier()
        if stage == "s5":
            with ExitStack() as mes:
                Yown = fw.sb(mes, "m_Yown", [128, 8, 1024], F32); b_Y = Buf()
                with ExitStack() as mes2:
                    M = mixer_common_alloc(fw, nc, mes2)
                    build_hmT(fw, nc, C, M, dram["xs"], b_xs, dram["mix_pre_g"])
                    attention_proj(fw, nc, C, M, None, dram, u_d, b_ud, only_u=True)
                s5_phase(fw, nc, C, dram, Yown, b_Y, u_d, b_ud)
                fw.dma("sp", dbg_Y[:, :, :], Yown[:], reads=[b_Y], writes=[b_out])
                fw.barrier()
        if stage in ("full", "mix"):
            if stage == "full":
                with ExitStack() as pes:
                    A = ffn_alloc(fw, nc, pes)
                    for ps_ in range(2):
                        ffn_pass(fw, nc, C, A, dram["xs"][ps_ * TOWN:(ps_ + 1) * TOWN, :],
                                 x1_d[ps_ * TOWN:(ps_ + 1) * TOWN, :], dram["ff1_w_gate"], dram["ff1_w_up"],
                                 dram["ff1_w_down"], dram["ff1_pre_g"], dram["ff1_post_g"], y_d, b_yd, b_xs, b_x1d)
                    fw.barrier()
                x1_src = x1_d
            else:
                x1_src = dram["xs"]
            with ExitStack() as mes:
                aT = fw.sb(mes, "m_aT", [128, 8, TOWN], BF16); b_aT = Buf()
                with ExitStack() as mes2:
                    AA = attention_alloc(fw, nc, mes2)
                    with ExitStack() as mes3:
                        M = mixer_common_alloc(fw, nc, mes3)
                        M.aT, M.b_aT = aT, b_aT
                        build_hmT(fw, nc, C, M, x1_src, b_x1d, dram["mix_pre_g"])
                        attention_proj(fw, nc, C, M, AA, dram, u_d, b_ud)
                    attention_core(fw, nc, C, M, AA, dram)
                Yown = fw.sb(mes, "m_Yown", [128, 8, 1024], F32); b_Y = Buf()
                s5_phase(fw, nc, C, dram, Yown, b_Y, u_d, b_ud)
                post_phase(fw, nc, C, dram, Yown, b_Y, aT, b_aT, x1_src, b_x1d, y_d, b_yd, x2_d, b_x2d)
            if stage == "full":
                with ExitStack() as pes:
                    A = ffn_alloc(fw, nc, pes)
                    ffn_pass(fw, nc, C, A, x2_d, out, dram["ff2_w_gate"], dram["ff2_w_up"],
                             dram["ff2_w_down"], dram["ff2_pre_g"], dram["ff2_post_g"], y_d, b_yd, b_x2d, b_out)
                    fw.barrier()
            else:
                with ExitStack() as pes:
                    xt = fw.sb(pes, "o_xt", [128, D], F32); b_xt = Buf()
                    for tt in range(8):
                        fw.dma("sp", xt[:], x2_d[tt * 128:(tt + 1) * 128, :], reads=[b_x2d], writes=[b_xt])
                        fw.dma("sp", out[tt * 128:(tt + 1) * 128, :], xt[:], reads=[b_xt], writes=[b_out])
                    fw.barrier()
        fw.barrier(engines=("sp",))
    return nc


def common_inputs(inp):
    m = {}
    m["c_ident"] = np.eye(128, dtype=np.float32)
    jj = np.arange(128)[:, None]
    mm = np.arange(3072)[None, :]
    m["c_alibi"] = np.abs(mm - jj - 1920).astype(np.float32)
    ex = np.zeros((128, 33), np.float32)
    j8 = np.arange(8)
    ex[:64, 0:8] = j8 + 1; ex[:64, 8:16] = 7 - j8; ex[:64, 16:24] = -1 - j8
    ex[64:, 0:8] = 8 - j8; ex[64:, 8:16] = j8; ex[64:, 16:24] = j8 - 8
    ex[:, 24:32] = 8 * (2 ** j8); ex[:, 32] = 1
    m["c_exps"] = ex
    jrow = (np.arange(128) // 16)[:, None]
    jcol = (np.arange(128) // 16)[None, :]
    m["c_maskF"] = (jcol >= jrow).astype(np.float32)
    m["c_maskB"] = (jcol <= jrow).astype(np.float32)
    m["ssm_d"] = np.ascontiguousarray(np.asarray(inp["ssm_d"], dtype=np.float32).reshape(1, -1))
    for p in ("ff1", "ff2"):
        for n in ("_pre_g", "_post_g"):
            m[p + n] = np.ascontiguousarray(np.asarray(inp[p + n], dtype=np.float32).reshape(1, -1))
        for n in ("_w_gate", "_w_up", "_w_down"):
            m[p + n] = np.ascontiguousarray(np.asarray(inp[p + n], dtype=np.float32)[0])
    for n in ("mix_pre_g", "mix_post_g", "lam_q1", "lam_k1", "lam_q2", "lam_k2", "attn_head_g"):
        m[n] = np.ascontiguousarray(np.asarray(inp[n], dtype=np.float32).reshape(1, -1))
    m["ssm_w_glu"] = np.ascontiguousarray(np.asarray(inp["ssm_w_glu"], dtype=np.float32)[0])
    for n in ("ssm_b_glu", "ssm_out_g"):
        m[n] = np.ascontiguousarray(np.asarray(inp[n], dtype=np.float32).reshape(1, -1))
    m["w_in"] = np.ascontiguousarray(np.asarray(inp["w_in"], dtype=np.float32)[0])
    m["w_out"] = np.ascontiguousarray(np.asarray(inp["w_out"], dtype=np.float32)[0])
    return m


SSM_KEYS = ("ssm_lam_re", "ssm_lam_im", "ssm_log_dt", "ssm_b_re", "ssm_b_im", "ssm_c_re", "ssm_c_im")


def ssm_inputs(inp, r):
    m = {}
    for k in SSM_KEYS:
        a = np.asarray(inp[k], dtype=np.float32)[0]
        if r == 1:
            a = a[::-1]
        m[k] = np.ascontiguousarray(a)
    return m


_NC_CACHE = {}


def kernel(**inputs):
    x = np.asarray(inputs["x"], dtype=np.float32)
    B = x.shape[0]
    common = common_inputs(inputs)
    ssm = [ssm_inputs(inputs, r) for r in range(2)]
    in_maps = []
    for core in range(8):
        b, r = core // 2, core % 2
        m = dict(common)
        m.update(ssm[r])
        xs = x[b] if r == 0 else x[b][::-1]
        m["xs"] = np.ascontiguousarray(xs)
        in_maps.append(m)
    if "full" not in _NC_CACHE:
        _NC_CACHE["full"] = build("full")
    nc = _NC_CACHE["full"]
    res = run_bass_kernel_spmd(nc, in_maps, core_ids=list(range(8)))
    out = np.empty((B, S, D), dtype=np.float32)
    for core in range(8):
        b, r = core // 2, core % 2
        o = np.asarray(res.results[core]["out"], dtype=np.float32)
        if r == 0:
            out[b, :TOWN] = o
        else:
            out[b, TOWN:] = o[::-1]
    return out
```
